# Optimizing a Trainium2 kernel written in Bass

```python
import math
import jax, jax.numpy as jnp
from jax import lax
import numpy as np

D_MODEL = 1024
BATCH = 2
SEQ = 16384
DEPTH = 2

N_MIXERS = 2
N_LAYERS_A = (DEPTH + 1) // 2
N_LAYERS_B = DEPTH // 2
Q_BLOCK = 128
EPS = 1e-6
MASK_VALUE = -1e30
FORCED_SCORE = 1e6

NSA_HEADS = 16
NSA_KV_GROUPS = 4
NSA_HEADS_PER_GROUP = NSA_HEADS // NSA_KV_GROUPS
NSA_HEAD_DIM = D_MODEL // NSA_HEADS
CMP_STRIDE = 16
CMP_BLOCK = 2 * CMP_STRIDE
CMP_HIDDEN = 4 * NSA_HEAD_DIM
SEL_BLOCK = 64
SEL_TOP_N = 16
WINDOW = 512
NSA_Q_COLS = NSA_HEADS * NSA_HEAD_DIM
NSA_KV_COLS = NSA_KV_GROUPS * NSA_HEAD_DIM
NSA_GATE_COLS = 3 * NSA_HEADS
NSA_SPLITS = [int(v) for v in np.cumsum([NSA_Q_COLS] + [NSA_KV_COLS] * 6)]
NSA_PROJ_COLS = NSA_Q_COLS + 6 * NSA_KV_COLS + NSA_GATE_COLS

DIFF_HEADS = 8
DIFF_HEAD_DIM = D_MODEL // (2 * DIFF_HEADS)
DIFF_QK_COLS = DIFF_HEADS * 2 * DIFF_HEAD_DIM
DIFF_V_COLS = DIFF_HEADS * 2 * DIFF_HEAD_DIM
DIFF_PROJ_COLS = 2 * DIFF_QK_COLS + DIFF_V_COLS

D_FF = 2816
CONV_WIDTH = 3

kernel_name = 'hybrid_nsa_diffattn_convffn'


def rms_norm(x, g):
    xf = x.astype(jnp.float32)
    y = xf * lax.rsqrt(jnp.mean(xf * xf, axis=-1, keepdims=True) + EPS)
    return (y * g.astype(jnp.float32)).astype(x.dtype)


def alibi_slopes(n):
    return jnp.exp2(-8.0 * jnp.arange(1, n + 1, dtype=jnp.float32) / n)


def masked_softmax(s, mask):
    s = jnp.where(mask, s, MASK_VALUE)
    m = jnp.max(s, axis=-1, keepdims=True)
    e = jnp.where(mask, jnp.exp(s - m), 0.0)
    return e / jnp.maximum(jnp.sum(e, axis=-1, keepdims=True), 1e-30)


def compress_blocks(kv, pe, w1, w2):
    B, S, G, Dh = kv.shape
    chunks = kv.reshape(B, S // CMP_STRIDE, CMP_STRIDE, G, Dh)
    blocks = jnp.concatenate([chunks[:, :-1], chunks[:, 1:]], axis=2)
    blocks = blocks + pe[:, None, :]
    n_cmp = blocks.shape[1]
    flat = blocks.transpose(0, 1, 3, 2, 4).reshape(B, n_cmp, G, CMP_BLOCK * Dh)
    return jax.nn.gelu(flat @ w1) @ w2


def nsa_mixer(h, w_in, cmp_k_pe, cmp_k_w1, cmp_k_w2, cmp_v_pe, cmp_v_w1, cmp_v_w2, w_out):
    B, S, _ = h.shape
    G, P, Dh = NSA_KV_GROUPS, NSA_HEADS_PER_GROUP, NSA_HEAD_DIM
    proj = h @ w_in
    q, kc, vc, ks, vs, kw, vw, gates = jnp.split(proj, NSA_SPLITS, axis=-1)
    q = q.reshape(B, S, G, P, Dh) * (Dh ** -0.5)
    kc, vc, ks, vs, kw, vw = [a.reshape(B, S, G, Dh) for a in (kc, vc, ks, vs, kw, vw)]
    gates = jax.nn.sigmoid(gates.reshape(B, S, G, P, 3))

    k_cmp = compress_blocks(kc, cmp_k_pe, cmp_k_w1, cmp_k_w2)
    v_cmp = compress_blocks(vc, cmp_v_pe, cmp_v_w1, cmp_v_w2)
    n_cmp = k_cmp.shape[1]
    cmp_end = jnp.arange(n_cmp) * CMP_STRIDE + CMP_BLOCK - 1

    n_sel = S // SEL_BLOCK
    k_sel = ks.reshape(B, n_sel, SEL_BLOCK, G, Dh).transpose(0, 3, 1, 2, 4)
    v_sel = vs.reshape(B, n_sel, SEL_BLOCK, G, Dh).transpose(0, 3, 1, 2, 4)
    n_top = min(SEL_TOP_N, n_sel)
    ratio = SEL_BLOCK // CMP_STRIDE
    back_pad = ratio * n_sel - n_cmp

    kw_pad = jnp.pad(kw, ((0, 0), (WINDOW, 0), (0, 0), (0, 0)))
    vw_pad = jnp.pad(vw, ((0, 0), (WINDOW, 0), (0, 0), (0, 0)))

    slopes = alibi_slopes(NSA_HEADS).reshape(G, P)[None, :, :, None, None]
    b_ix = jnp.arange(B)[:, None, None, None]
    g_ix = jnp.arange(G)[None, :, None, None]
    blk_ids = jnp.arange(n_sel)

    def block_fn(blk):
        q0 = blk * Q_BLOCK
        t = q0 + jnp.arange(Q_BLOCK)
        qb = lax.dynamic_slice_in_dim(q, q0, Q_BLOCK, axis=1)
        gb = lax.dynamic_slice_in_dim(gates, q0, Q_BLOCK, axis=1)

        dist_c = (t[:, None] - cmp_end[None, :]).astype(jnp.float32)
        s_c = jnp.einsum('btgpd,bjgd->bgptj', qb, k_cmp).astype(jnp.float32) - slopes * dist_c
        p_cmp = masked_softmax(s_c, dist_c >= 0)
        o_cmp = jnp.einsum('bgptj,bjgd->btgpd', p_cmp.astype(v_cmp.dtype), v_cmp)

        p_grp = jnp.sum(p_cmp, axis=2)
        p_pad = jnp.pad(p_grp, ((0, 0), (0, 0), (0, 0), (1, back_pad)))
        imp = p_pad[..., 0:ratio * n_sel:ratio]
        for r in range(1, ratio + 1):
            imp = imp + p_pad[..., r:r + ratio * n_sel:ratio]
        cur = t // SEL_BLOCK
        valid = blk_ids[None, :] <= cur[:, None]
        forced = (blk_ids[None, :] == 0) | (valid & (blk_ids[None, :] >= cur[:, None] - 1))
        imp = jnp.where(forced, FORCED_SCORE, jnp.where(valid, imp, -1.0))
        _, sel = lax.top_k(imp, n_top)

        k_g = k_sel[b_ix, g_ix, sel].reshape(B, G, Q_BLOCK, n_top * SEL_BLOCK, Dh)
        v_g = v_sel[b_ix, g_ix, sel].reshape(B, G, Q_BLOCK, n_top * SEL_BLOCK, Dh)
        pos = (sel[..., None] * SEL_BLOCK + jnp.arange(SEL_BLOCK)).reshape(B, G, Q_BLOCK, n_top * SEL_BLOCK)
        dist_s = (t[None, None, :, None] - pos).astype(jnp.float32)[:, :, None]
        s_s = jnp.einsum('btgpd,bgtkd->bgptk', qb, k_g).astype(jnp.float32) - slopes * dist_s
        p_s = masked_softmax(s_s, dist_s >= 0)
        o_sel = jnp.einsum('bgptk,bgtkd->btgpd', p_s.astype(v_g.dtype), v_g)

        kwb = lax.dynamic_slice_in_dim(kw_pad, q0, WINDOW + Q_BLOCK, axis=1)
        vwb = lax.dynamic_slice_in_dim(vw_pad, q0, WINDOW + Q_BLOCK, axis=1)
        spos = q0 - WINDOW + jnp.arange(WINDOW + Q_BLOCK)
        dist_w = t[:, None] - spos[None, :]
        mask_w = (dist_w >= 0) & (dist_w < WINDOW) & (spos[None, :] >= 0)
        s_w = jnp.einsum('btgpd,bkgd->bgptk', qb, kwb).astype(jnp.float32) - slopes * dist_w.astype(jnp.float32)
        p_w = masked_softmax(s_w, mask_w)
        o_win = jnp.einsum('bgptk,bkgd->btgpd', p_w.astype(vwb.dtype), vwb)

        o = gb[..., 0:1] * o_cmp + gb[..., 1:2] * o_sel + gb[..., 2:3] * o_win
        return o.reshape(B, Q_BLOCK, NSA_HEADS * Dh)

    out = lax.map(block_fn, jnp.arange(S // Q_BLOCK))
    out = out.transpose(1, 0, 2, 3).reshape(B, S, NSA_HEADS * Dh)
    return out @ w_out


def diff_mixer(h, w_in, lam_q1, lam_k1, lam_q2, lam_k2, subln_g, w_out, lambda_init):
    B, S, _ = h.shape
    H, d = DIFF_HEADS, DIFF_HEAD_DIM
    proj = h @ w_in
    q, k, v = jnp.split(proj, [DIFF_QK_COLS, 2 * DIFF_QK_COLS], axis=-1)
    q = q.reshape(B, S, H, 2, d) * (d ** -0.5)
    k = k.reshape(B, S, H, 2, d)
    v = v.reshape(B, S, H, 2 * d)
    f32 = jnp.float32
    lam = (jnp.exp(jnp.sum(lam_q1.astype(f32) * lam_k1.astype(f32)))
           - jnp.exp(jnp.sum(lam_q2.astype(f32) * lam_k2.astype(f32))) + lambda_init)
    slopes = alibi_slopes(H)[None, None, :, None, None]
    kpos = jnp.arange(S)

    def block_fn(blk):
        q0 = blk * Q_BLOCK
        t = q0 + jnp.arange(Q_BLOCK)
        qb = lax.dynamic_slice_in_dim(q, q0, Q_BLOCK, axis=1)
        dist = (t[:, None] - kpos[None, :]).astype(f32)
        s = jnp.einsum('bthcd,bshcd->bchts', qb, k).astype(f32) - slopes * dist
        p = masked_softmax(s, dist >= 0)
        a = p[:, 0] - lam * p[:, 1]
        return jnp.einsum('bhts,bshe->bthe', a.astype(v.dtype), v)

    o = lax.map(block_fn, jnp.arange(S // Q_BLOCK))
    o = o.transpose(1, 0, 2, 3, 4).reshape(B, S, H, 2 * d)
    o = rms_norm(o, subln_g) * (1.0 - lambda_init)
    return o.reshape(B, S, H * 2 * d) @ w_out


def conv_ffn(h, w_up, conv_w, conv_b, w_down):
    u = h @ w_up
    c = u.shape[-1]
    u = lax.conv_general_dilated(u, conv_w[:, None, :], window_strides=(1,),
                                 padding=[(CONV_WIDTH - 1, 0)],
                                 dimension_numbers=('NWC', 'WIO', 'NWC'),
                                 feature_group_count=c) + conv_b
    gate, val = jnp.split(u, 2, axis=-1)
    return (jax.nn.silu(gate) * val) @ w_down


def setup_inputs(seed: int = 0) -> dict:
    key = jax.random.key(seed)
    ks = jax.random.split(key, 24)
    f32 = jnp.float32

    def w(k, shape, fan_in):
        return jax.random.normal(k, shape, f32) * (fan_in ** -0.5)

    def gain(k, shape):
        return 1.0 + 0.01 * jax.random.normal(k, shape, f32)

    Dh = NSA_HEAD_DIM
    d = DIFF_HEAD_DIM
    return {
        'x': jax.random.normal(ks[0], (BATCH, SEQ, D_MODEL), f32),
        'norm_mix_g': gain(ks[1], (DEPTH, D_MODEL)),
        'norm_ffn_g': gain(ks[2], (DEPTH, D_MODEL)),
        'final_norm_g': gain(ks[3], (D_MODEL,)),
        'nsa_w_in': w(ks[4], (N_LAYERS_A, D_MODEL, NSA_PROJ_COLS), D_MODEL),
        'nsa_cmp_k_pe': 0.1 * jax.random.normal(ks[5], (N_LAYERS_A, CMP_BLOCK, Dh), f32),
        'nsa_cmp_k_w1': w(ks[6], (N_LAYERS_A, CMP_BLOCK * Dh, CMP_HIDDEN), CMP_BLOCK * Dh),
        'nsa_cmp_k_w2': w(ks[7], (N_LAYERS_A, CMP_HIDDEN, Dh), CMP_HIDDEN),
        'nsa_cmp_v_pe': 0.1 * jax.random.normal(ks[8], (N_LAYERS_A, CMP_BLOCK, Dh), f32),
        'nsa_cmp_v_w1': w(ks[9], (N_LAYERS_A, CMP_BLOCK * Dh, CMP_HIDDEN), CMP_BLOCK * Dh),
        'nsa_cmp_v_w2': w(ks[10], (N_LAYERS_A, CMP_HIDDEN, Dh), CMP_HIDDEN),
        'nsa_w_out': w(ks[11], (N_LAYERS_A, NSA_HEADS * Dh, D_MODEL), NSA_HEADS * Dh),
        'diff_w_in': w(ks[12], (N_LAYERS_B, D_MODEL, DIFF_PROJ_COLS), D_MODEL),
        'diff_lam_q1': 0.1 * jax.random.normal(ks[13], (N_LAYERS_B, d), f32),
        'diff_lam_k1': 0.1 * jax.random.normal(ks[14], (N_LAYERS_B, d), f32),
        'diff_lam_q2': 0.1 * jax.random.normal(ks[15], (N_LAYERS_B, d), f32),
        'diff_lam_k2': 0.1 * jax.random.normal(ks[16], (N_LAYERS_B, d), f32),
        'diff_subln_g': gain(ks[17], (N_LAYERS_B, 2 * d)),
        'diff_w_out': w(ks[18], (N_LAYERS_B, DIFF_V_COLS, D_MODEL), DIFF_V_COLS),
        'ffn_w_up': w(ks[19], (DEPTH, D_MODEL, 2 * D_FF), D_MODEL),
        'ffn_conv_w': w(ks[20], (DEPTH, CONV_WIDTH, 2 * D_FF), CONV_WIDTH),
        'ffn_conv_b': 0.01 * jax.random.normal(ks[21], (DEPTH, 2 * D_FF), f32),
        'ffn_w_down': w(ks[22], (DEPTH, D_FF, D_MODEL), D_FF),
    }


def reference(x, norm_mix_g, norm_ffn_g, final_norm_g,
              nsa_w_in, nsa_cmp_k_pe, nsa_cmp_k_w1, nsa_cmp_k_w2,
              nsa_cmp_v_pe, nsa_cmp_v_w1, nsa_cmp_v_w2, nsa_w_out,
              diff_w_in, diff_lam_q1, diff_lam_k1, diff_lam_q2, diff_lam_k2,
              diff_subln_g, diff_w_out,
              ffn_w_up, ffn_conv_w, ffn_conv_b, ffn_w_down):
    for i in range(DEPTH):
        h = rms_norm(x, norm_mix_g[i])
        j = i // N_MIXERS
        if i % N_MIXERS == 0:
            mix = nsa_mixer(h, nsa_w_in[j], nsa_cmp_k_pe[j], nsa_cmp_k_w1[j], nsa_cmp_k_w2[j],
                            nsa_cmp_v_pe[j], nsa_cmp_v_w1[j], nsa_cmp_v_w2[j], nsa_w_out[j])
        else:
            lambda_init = 0.8 - 0.6 * math.exp(-0.3 * i)
            mix = diff_mixer(h, diff_w_in[j], diff_lam_q1[j], diff_lam_k1[j], diff_lam_q2[j],
                             diff_lam_k2[j], diff_subln_g[j], diff_w_out[j], lambda_init)
        x = x + mix
        h = rms_norm(x, norm_ffn_g[i])
        x = x + conv_ffn(h, ffn_w_up[i], ffn_conv_w[i], ffn_conv_b[i], ffn_w_down[i])
    return rms_norm(x, final_norm_g)
```

```python
import numpy as np
import ml_dtypes
import concourse.bass as bass
import concourse.mybir as mybir
from concourse.bass_utils import run_bass_kernel_spmd

F32 = mybir.dt.float32
BF16 = mybir.dt.bfloat16
AF = mybir.ActivationFunctionType
ALU = mybir.AluOpType
AX = mybir.AxisListType
NPBF = ml_dtypes.bfloat16

NCORES = 8
D_MODEL = 1024
BATCH = 2
SEQ = 16384
EPS = 1e-6


class Buf:
    __slots__ = ("name", "w", "r", "sem", "cnt")

    def __init__(self, name):
        self.name = name
        self.w = None
        self.r = {}
        self.sem = None
        self.cnt = 0


class Ctx:
    SEM_ROLL = 30000

    def __init__(self, nc, same_engine_sync=True):
        self.nc = nc
        self.same = same_engine_sync
        self.nsem = 0
        self.E = {}
        for n, e in [("pe", nc.tensor), ("act", nc.scalar), ("dve", nc.vector),
                     ("pool", nc.gpsimd), ("sp", nc.sync)]:
            self.E[n] = {"eng": e, "sem": self._newsem("e_" + n), "cnt": 0, "waited": {}}
        self.stores = []
        self.es = None
        self.pfx = ""
        self.phase_bufs = []
        self.sempool = []
        self.ccbuf = None
        self.dummy = self.nc.alloc_sbuf_tensor("bar_dummy", [128, 8], F32)
        self.bdummy = Buf("bar_dummy")

    def _newsem(self, name):
        self.nsem += 1
        s = self.nc.alloc_semaphore("%s_%d" % (name, self.nsem))
        return (s, self.nsem)

    def buf(self, name):
        return Buf(name)

    def begin_phase(self, pfx):
        from contextlib import ExitStack
        self.es = ExitStack()
        self.pfx = pfx

    def sb(self, name, shape, dt):
        return self.es.enter_context(self.nc.sbuf_tensor(self.pfx + name, shape, dt))

    def ps(self, name, shape, dt=F32):
        return self.es.enter_context(self.nc.psum_tensor(self.pfx + name, shape, dt))

    def _own_sem(self, own):
        if own.sem is None or own.cnt >= self.SEM_ROLL:
            if self.sempool and own.sem is None:
                sem, cnt = self.sempool.pop()
                own.sem = sem
                own.cnt = cnt
            else:
                own.sem = self._newsem("d")
                own.cnt = 0
            self.phase_bufs.append(own)

    def barrier(self):
        toks = []
        for n, E in self.E.items():
            if E["cnt"] > 0:
                toks.append((E["sem"], E["cnt"], n))
        for b in self.phase_bufs:
            toks.append((b.sem, b.cnt, "dma"))
        if self.ccbuf is not None and self.ccbuf.cnt > 0:
            toks.append((self.ccbuf.sem, self.ccbuf.cnt, "dma"))
        self._wait("pool", toks)
        self.op("pool", lambda e: e.memset(self.dummy[:], 0.0), w=[self.bdummy])
        for n in self.E:
            if n != "pool":
                self._wait(n, [self.bdummy.w])

    def end_phase(self):
        self.barrier()
        self.es.close()
        self.es = None
        seen = set()
        for b in self.phase_bufs:
            if b.sem[1] not in seen and b.cnt < self.SEM_ROLL // 2:
                seen.add(b.sem[1])
                self.sempool.append((b.sem, b.cnt))
        self.phase_bufs = []

    def collective_async(self, src_ap, dst_ap, groups, deps):
        if self.ccbuf is None:
            self.ccbuf = Buf("cc_async")
            self.ccbuf.sem = self._newsem("cca")
        self._wait("pool", deps)
        inst = self.nc.gpsimd.collective_compute("AllGather", ALU.bypass, replica_groups=groups,
                                                 ins=[src_ap.opt()], outs=[dst_ap.opt()])
        inst.then_inc(self.ccbuf.sem[0], 1)
        self.ccbuf.cnt += 1

    def all_gather_chunks(self, pairs, groups):
        self.barrier()
        cb = Buf("cc")
        cb.sem = self._newsem("cc")
        for (src_ap, dst_ap) in pairs:
            inst = self.nc.gpsimd.collective_compute("AllGather", ALU.bypass, replica_groups=groups,
                                                     ins=[src_ap.opt()], outs=[dst_ap.opt()])
            inst.then_inc(cb.sem[0], 1)
            cb.cnt += 1
        self.phase_bufs.append(cb)
        self.barrier()
        self.phase_bufs.remove(cb)

    def all_gather(self, src_ap, dst_ap, groups):
        self.barrier()
        cb = Buf("cc")
        cb.sem = self._newsem("cc")
        inst = self.nc.gpsimd.collective_compute("AllGather", ALU.bypass, replica_groups=groups,
                                                 ins=[src_ap.opt()], outs=[dst_ap.opt()])
        inst.then_inc(cb.sem[0], 1)
        cb.cnt = 1
        self.phase_bufs.append(cb)
        self.barrier()
        self.phase_bufs.remove(cb)

    def bufs(self, name, n):
        return [Buf("%s%d" % (name, i)) for i in range(n)]

    def _wait(self, en, toks):
        E = self.E[en]
        need = {}
        for t in toks:
            if t is None:
                continue
            (sem, key), val, src = t
            if src == en and (en == "pe" or not self.same):
                continue
            if need.get(key, (None, 0))[1] < val:
                need[key] = (sem, val)
        for key, (sem, val) in need.items():
            if E["waited"].get(key, 0) < val:
                E["eng"].wait_ge(sem, val)
                E["waited"][key] = val

    def _collect(self, r, w):
        toks = []
        for b in r:
            toks.append(b.w)
        for b in w:
            toks.append(b.w)
            toks.extend(b.r.values())
        return toks

    def op(self, en, fn, r=(), w=()):
        E = self.E[en]
        if E["cnt"] >= self.SEM_ROLL:
            E["sem"] = self._newsem("e_" + en)
            E["cnt"] = 0
        self._wait(en, self._collect(r, w))
        inst = fn(E["eng"])
        E["cnt"] += 1
        inst.then_inc(E["sem"][0], 1)
        tok = (E["sem"], E["cnt"], en)
        for b in r:
            b.r[en] = tok
        for b in w:
            b.w = tok
            b.r = {}
        return inst

    def dma(self, qn, out, in_, r=(), w=(), own=None, **kw):
        E = self.E[qn]
        if own is None:
            own = w[0] if len(w) else r[0]
        self._own_sem(own)
        self._wait(qn, self._collect(r, w))
        inst = E["eng"].dma_start(out=out, in_=in_, **kw)
        own.cnt += 16
        inst.then_inc(own.sem[0], 16)
        tok = (own.sem, own.cnt, "dma")
        for b in r:
            b.r["dma_%d" % own.sem[1]] = tok
        for b in w:
            b.w = tok
            b.r = {}
        return tok

    def store(self, qn, out, in_, r, **kw):
        tok = self.dma(qn, out, in_, r=r, w=(), **kw)
        self.stores.append(tok)

    def finish(self):
        self._wait("sp", self.stores)
        self.stores = []
        if self.es is not None:
            self.es.close()
            self.es = None


def new_nc():
    return bass.Bass("TRN2", target_bir_lowering=False)


def emit_k1(k, nc, T, C, xv, g_in, w_in, fm_specs, fm_route, tok_cols, tok_route):
    NT = T // 512
    W = k.sb("W", [128, 8, C], BF16)
    G = k.sb("G", [128, 8], F32)
    ONES = k.sb("ONES", [128, 128], F32)
    X = [k.sb("X%d" % i, [128, 8, 512], F32) for i in range(2)]
    SQ = k.sb("SQ", [128, 8, 512], F32)
    RS = k.sb("RS", [128, 512], F32)
    HT = k.sb("HT", [128, 8, 512], BF16)
    NOB = 4
    OB = [k.sb("OB%d" % i, [128, 512], BF16) for i in range(NOB)]
    PS = [k.ps("PS%d" % i, [128, 512], F32) for i in range(6)]
    PSS = k.ps("PSS", [128, 512], F32)

    bW = k.bufs("W", 8)
    bG = k.buf("G")
    bONES = k.buf("ONES")
    bX = k.bufs("X", 2)
    bSQ = k.buf("SQ")
    bRS = k.buf("RS")
    bHT = k.buf("HT")
    bOB = k.bufs("OB", NOB)
    bPS = k.bufs("PS", 6)
    bPSS = k.buf("PSS")

    wv = w_in.rearrange("(dc p) c -> p dc c", p=128)
    for dc in range(8):
        for c0 in range(0, C, 1024):
            c1 = min(C, c0 + 1024)
            k.dma("pool", W[:, dc, c0:c1], wv[:, dc, c0:c1], w=[bW[dc]])
    k.dma("sp", G[:], g_in, w=[bG])
    k.op("dve", lambda e: e.memset(ONES[:], 1.0), w=[bONES])

    pi = 0
    oi = 0
    for it in range(NT):
        t0 = it * 512
        xb = it % 2
        k.dma("sp", X[xb][:], xv[:, :, t0:t0 + 512], w=[bX[xb]])
        k.op("act", lambda e: e.activation(out=SQ[:], in_=X[xb][:], func=AF.Square),
             r=[bX[xb]], w=[bSQ])
        for dc in range(8):
            k.op("pe", lambda e: e.matmul(PSS[:], lhsT=ONES[:], rhs=SQ[:, dc, :],
                                           start=(dc == 0), stop=(dc == 7)),
                 r=[bONES, bSQ], w=[bPSS])
        k.op("act", lambda e: e.activation(out=RS[:], in_=PSS[:], func=AF.Sqrt,
                                            scale=1.0 / D_MODEL, bias=EPS),
             r=[bPSS], w=[bRS])
        k.op("dve", lambda e: e.reciprocal(out=RS[:], in_=RS[:]), r=[bRS], w=[bRS])
        for dc in range(8):
            k.op("dve", lambda e: e.scalar_tensor_tensor(
                out=HT[:, dc, :], in0=X[xb][:, dc, :], scalar=G[:, dc:dc + 1], in1=RS[:],
                op0=ALU.mult, op1=ALU.mult), r=[bX[xb], bG, bRS], w=[bHT])
        for (s0, s1, scale, func) in fm_specs:
            for c0 in range(s0, s1, 128):
                c1 = min(s1, c0 + 128)
                m = c1 - c0
                p = pi % 6
                pi += 1
                for dc in range(8):
                    k.op("pe", lambda e: e.matmul(PS[p][:m, :], lhsT=W[:, dc, c0:c1],
                                                   rhs=HT[:, dc, :], start=(dc == 0),
                                                   stop=(dc == 7)),
                         r=[bW[dc], bHT], w=[bPS[p]])
                o = oi % NOB
                oi += 1
                k.op("act", lambda e: e.activation(out=OB[o][:m, :], in_=PS[p][:m, :],
                                                    func=func, scale=scale),
                     r=[bPS[p]], w=[bOB[o]])
                for (ro, nr, dst) in fm_route(c0, c1, t0):
                    k.store("sp", dst, OB[o][ro:ro + nr, :], r=[bOB[o]])
        for (c0, c1) in tok_cols:
            m = c1 - c0
            for tb in range(4):
                p = pi % 6
                pi += 1
                for dc in range(8):
                    k.op("pe", lambda e: e.matmul(PS[p][:, :m],
                                                   lhsT=HT[:, dc, tb * 128:(tb + 1) * 128],
                                                   rhs=W[:, dc, c0:c1], start=(dc == 0),
                                                   stop=(dc == 7)),
                         r=[bW[dc], bHT], w=[bPS[p]])
                o = oi % NOB
                oi += 1
                k.op("dve", lambda e: e.tensor_copy(out=OB[o][:, :m], in_=PS[p][:, :m]),
                     r=[bPS[p]], w=[bOB[o]])
                for (co, ncl, dst) in tok_route(c0, c1, t0, tb):
                    k.store("sp", dst, OB[o][:, co:co + ncl], r=[bOB[o]])


def emit_norm(k, nc, T, xv, g_in, h_dst, tile_done=None):
    NT = T // 512
    G = k.sb("G", [128, 8], F32)
    ONES = k.sb("ONES", [128, 128], F32)
    X = [k.sb("X%d" % i, [128, 8, 512], F32) for i in range(2)]
    SQ = k.sb("SQ", [128, 8, 512], F32)
    RS = k.sb("RS", [128, 512], F32)
    HT = [k.sb("HT%d" % i, [128, 8, 512], BF16) for i in range(2)]
    PSS = k.ps("PSS", [128, 512], F32)
    bG = k.buf("G"); bONES = k.buf("ONES"); bX = k.bufs("X", 2); bSQ = k.buf("SQ"); bRS = k.buf("RS")
    bHT = k.bufs("HT", 2); bPSS = k.buf("PSS")
    k.dma("sp", G[:], g_in, w=[bG])
    k.op("dve", lambda e: e.memset(ONES[:], 1.0), w=[bONES])
    for it in range(NT):
        t0 = it * 512
        xb = it % 2
        k.dma("sp", X[xb][:], xv[:, :, t0:t0 + 512], w=[bX[xb]])
        k.op("act", lambda e: e.activation(out=SQ[:], in_=X[xb][:], func=AF.Square), r=[bX[xb]], w=[bSQ])
        for dc in range(8):
            k.op("pe", lambda e: e.matmul(PSS[:], lhsT=ONES[:], rhs=SQ[:, dc, :], start=(dc == 0), stop=(dc == 7)),
                 r=[bONES, bSQ], w=[bPSS])
        k.op("act", lambda e: e.activation(out=RS[:], in_=PSS[:], func=AF.Sqrt, scale=1.0 / D_MODEL, bias=EPS),
             r=[bPSS], w=[bRS])
        k.op("dve", lambda e: e.reciprocal(out=RS[:], in_=RS[:]), r=[bRS], w=[bRS])
        for dc in range(8):
            k.op("dve", lambda e: e.scalar_tensor_tensor(
                out=HT[xb][:, dc, :], in0=X[xb][:, dc, :], scalar=G[:, dc:dc + 1], in1=RS[:],
                op0=ALU.mult, op1=ALU.mult), r=[bX[xb], bG, bRS], w=[bHT[xb]])
        n0 = len(k.stores)
        for dc in range(8):
            k.store("pool", h_dst(dc, t0), HT[xb][:, dc, :], r=[bHT[xb]])
        if tile_done is not None:
            tile_done(it, k.stores[n0:])


def emit_proj(k, nc, T, C, ht_src, w_in, fm_specs, fm_route, tok_cols, tok_route):
    NT = T // 512
    W = k.sb("W", [128, 8, C], BF16)
    HT = [k.sb("HT%d" % i, [128, 8, 512], BF16) for i in range(2)]
    NOB = 4
    OB = [k.sb("OB%d" % i, [128, 512], BF16) for i in range(NOB)]
    PS = [k.ps("PS%d" % i, [128, 512], F32) for i in range(6)]
    bW = k.bufs("W", 8); bHT = k.bufs("HT", 2); bOB = k.bufs("OB", NOB); bPS = k.bufs("PS", 6)
    wv = w_in.rearrange("(dc p) c -> p dc c", p=128)
    for dc in range(8):
        for c0 in range(0, C, 1024):
            c1 = min(C, c0 + 1024)
            k.dma("pool", W[:, dc, c0:c1], wv[:, dc, c0:c1], w=[bW[dc]])
    pi = 0
    oi = 0
    for it in range(NT):
        t0 = it * 512
        hb = it % 2
        k.dma("sp", HT[hb][:], ht_src(t0), w=[bHT[hb]])
        for (s0, s1, scale, func) in fm_specs:
            for c0 in range(s0, s1, 128):
                c1 = min(s1, c0 + 128)
                m = c1 - c0
                p = pi % 6
                pi += 1
                for dc in range(8):
                    k.op("pe", lambda e: e.matmul(PS[p][:m, :], lhsT=W[:, dc, c0:c1], rhs=HT[hb][:, dc, :],
                                                   start=(dc == 0), stop=(dc == 7)),
                         r=[bW[dc], bHT[hb]], w=[bPS[p]])
                o = oi % NOB
                oi += 1
                k.op("act", lambda e: e.activation(out=OB[o][:m, :], in_=PS[p][:m, :], func=func, scale=scale),
                     r=[bPS[p]], w=[bOB[o]])
                for (ro, nr, dst) in fm_route(c0, c1, t0):
                    k.store("pool", dst, OB[o][ro:ro + nr, :], r=[bOB[o]])
        for (c0, c1) in tok_cols:
            m = c1 - c0
            for tb in range(4):
                p = pi % 6
                pi += 1
                for dc in range(8):
                    k.op("pe", lambda e: e.matmul(PS[p][:, :m], lhsT=HT[hb][:, dc, tb * 128:(tb + 1) * 128],
                                                   rhs=W[:, dc, c0:c1], start=(dc == 0), stop=(dc == 7)),
                         r=[bW[dc], bHT[hb]], w=[bPS[p]])
                o = oi % NOB
                oi += 1
                k.op("dve", lambda e: e.tensor_copy(out=OB[o][:, :m], in_=PS[p][:, :m]), r=[bPS[p]], w=[bOB[o]])
                for (co, ncl, dst) in tok_route(c0, c1, t0, tb):
                    k.store("pool", dst, OB[o][:, co:co + ncl], r=[bOB[o]])


def build_k1(T, C, fm_specs, tok_cols):
    nc = new_nc()
    CV = sum(c1 - c0 for c0, c1 in tok_cols)
    xT = nc.dram_tensor("xT", [D_MODEL, T], F32, kind="ExternalInput").ap()
    g_in = nc.dram_tensor("g", [128, 8], F32, kind="ExternalInput").ap()
    w_in = nc.dram_tensor("w", [D_MODEL, C], F32, kind="ExternalInput").ap()
    projT = nc.dram_tensor("projT", [C, T], BF16, kind="ExternalOutput").ap()
    vtok = nc.dram_tensor("vtok", [T, max(CV, 1)], BF16, kind="ExternalOutput").ap()
    voff = {}
    vo = 0
    for (c0, c1) in tok_cols:
        voff[c0] = vo
        vo += c1 - c0
    k = Ctx(nc)
    k.begin_phase("")
    emit_k1(k, nc, T, C, xT.rearrange("(dc p) t -> p dc t", p=128), g_in, w_in, fm_specs,
            lambda c0, c1, t0: [(0, c1 - c0, projT[c0:c1, t0:t0 + 512])],
            tok_cols,
            lambda c0, c1, t0, tb: [(0, c1 - c0, vtok[t0 + tb * 128:t0 + (tb + 1) * 128,
                                                     voff[c0]:voff[c0] + c1 - c0])])
    k.finish()
    return nc


def g_layout(g):
    return np.ascontiguousarray(np.asarray(g, np.float32).reshape(8, 128).T)


def run_k1(xT_shards, g, w, fm_specs, tok_cols):
    T = xT_shards[0].shape[1]
    C = w.shape[1]
    nc = build_k1(T, C, fm_specs, tok_cols)
    gl = g_layout(g)
    w = np.ascontiguousarray(w, dtype=np.float32)
    in_maps = [{"xT": np.ascontiguousarray(s), "g": gl, "w": w} for s in xT_shards]
    res = run_bass_kernel_spmd(nc, in_maps, core_ids=list(range(NCORES)))
    return res.results


BIG = 29952.0


def alibi_slopes(n):
    return np.exp2(-8.0 * np.arange(1, n + 1, dtype=np.float64) / n)


def split3(v):
    v = np.asarray(v, np.float64)
    a = v.astype(NPBF)
    r = v - a.astype(np.float64)
    b = r.astype(NPBF)
    r = r - b.astype(np.float64)
    c = r.astype(NPBF)
    return np.stack([a, b, c], 0)


def q_aug_rows(m, S):
    tl = np.arange(S) % 512
    return split3(-m * tl)


def bt_table(m):
    sl = np.arange(128)[:, None]
    delta = np.arange(-3, 125)[None, :]
    return (m * sl - m * 128.0 * delta).astype(np.float32)


def btc_table(m):
    jl = np.arange(128)[:, None]
    dd = np.arange(-28, 32)[None, :]
    return (16.0 * m * jl - m * (512.0 * dd - 31.0)).astype(np.float32)


def dm_table():
    sl = np.arange(128)[:, None, None]
    dd = np.arange(4)[None, :, None]
    tl = np.arange(512)[None, None, :]
    return np.where(128 * dd + sl > tl, -1.0, 0.0).astype(NPBF)


def wm_table():
    sl = np.arange(128)[:, None, None]
    dd = np.arange(4)[None, :, None]
    tl = np.arange(512)[None, None, :]
    return np.where(tl - 128 * dd - sl >= 0, -1.0, 0.0).astype(NPBF)


def cm_table():
    jl = np.arange(128)[:, None, None]
    dd = np.arange(5)[None, :, None]
    tl = np.arange(512)[None, None, :]
    return np.where(16 * jl + 31 > 512 * dd + tl, -1.0, 0.0).astype(NPBF)


def bigi_table():
    return (np.eye(128) * BIG).astype(NPBF)


def bcast128(v):
    v = np.asarray(v, np.float32).reshape(1, -1)
    return np.ascontiguousarray(np.broadcast_to(v, (128, v.shape[1])))


def emit_sqmax(k, nc, fetch, ntiles, SQT, bSQT, ONESB, bONESB, psM, bpsM, MX, bMX, OUT, bOUT):
    nxt = fetch(0)
    for i in range(ntiles):
        rb, src = nxt
        if i + 1 < ntiles:
            nxt = fetch(i + 1)
        a = i % 2
        w_ = src.shape[-1]
        k.op("dve", lambda e: e.tensor_tensor(out=SQT[a][0:64, :w_], in0=src, in1=src, op=ALU.mult),
             r=rb, w=[bSQT[a]])
        k.op("pe", lambda e: e.matmul(psM[a][:, :w_], lhsT=ONESB[0:64, :], rhs=SQT[a][0:64, :w_],
                                       start=True, stop=True), r=[bSQT[a], bONESB], w=[bpsM[a]])
        k.op("dve", lambda e: e.reduce_max(out=MX[:, i:i + 1], in_=psM[a][:, :w_], axis=AX.X),
             r=[bpsM[a]], w=[bMX])
    k.op("dve", lambda e: e.reduce_max(out=OUT, in_=MX[:, 0:ntiles], axis=AX.X),
         r=[bMX], w=[bOUT])


class IOK2bStandalone:
    def __init__(self, nc, S, NH):
        d = lambda n, sh, dt: nc.dram_tensor(n, sh, dt, kind="ExternalInput").ap()
        self.qa = d("qa", [NH * 2, 64, S], BF16)
        self.ka = d("ka", [NH * 2, 64, S], BF16)
        self.v_in = d("v", [S, NH * 128], BF16)
        self.qaug = d("qaug", [NH, 3, 512], BF16)
        self.bt_in = d("bt", [NH, 128, 128], F32)
        self.dm_in = d("dm", [128, 4, 512], BF16)
        self.bigi_in = d("bigi", [128, 128], BF16)
        self.lam_in = d("lam", [128, 4, 64], F32)
        self.sg_in = d("sg", [128, 1], F32)
        self.oT = nc.dram_tensor("oT", [NH * 128, S], BF16, kind="ExternalOutput").ap()

    def q(self, hh, c, t0, t1):
        return [(self.qa[hh * 2 + c, :, t0:t1], t0, t1)]

    def k(self, hh, c, t0, t1):
        return [(self.ka[hh * 2 + c, :, t0:t1], t0, t1)]

    def v(self, hh, t0, t1):
        return [(self.v_in[t0:t1, hh * 128:(hh + 1) * 128], t0, t1)]

    def out(self, hh, I):
        return self.oT[hh * 128:(hh + 1) * 128, I * 512:(I + 1) * 512]


def build_k2b(S, lambda_init, NH=2):
    nc = new_nc()
    io = IOK2bStandalone(nc, S, NH)
    k = Ctx(nc)
    k.begin_phase("")
    emit_k2b(k, nc, S, lambda_init, io, NH)
    k.finish()
    return nc


def emit_k2b(k, nc, S, lambda_init, io, NH=2):
    NQ = S // 512
    NB = S // 128
    bt_in, dm_in, bigi_in, lam_in, sg_in = io.bt_in, io.dm_in, io.bigi_in, io.lam_in, io.sg_in

    KA = [k.sb("KA%d" % c, [67, S], BF16) for c in range(2)]
    V = k.sb("V", [128, NB, 128], BF16)
    QT = [k.sb("QT%d" % i, [67, 512], BF16) for i in range(4)]
    BT = k.sb("BT", [128, 128], F32)
    BTC = [k.sb("BTC%d" % c, [128, 128], F32) for c in range(2)]
    DM = k.sb("DM", [128, 4, 512], BF16)
    BIGI = k.sb("BIGI", [128, 128], BF16)
    ONESB = k.sb("ONESB", [128, 128], BF16)
    ONESF = k.sb("ONESF", [128, 128], F32)
    LAM = k.sb("LAM", [128, 4, 64], F32)
    LT = k.sb("LT", [128, 2, 64], F32)
    LS = k.sb("LS", [128, 4], F32)
    SG = k.sb("SG", [128, 1], F32)
    SQT = [k.sb("SQT%d" % i, [128, 512], BF16) for i in range(2)]
    MX = k.sb("MX", [128, 64], F32)
    Q2 = k.sb("Q2", [128, 4], F32)
    CC = k.sb("CC", [128, 4], F32)
    NPT = 3
    PT = [k.sb("PT%d" % i, [128, 1024], BF16) for i in range(NPT)]
    RL = k.sb("RL", [128, 512], F32)
    OC = [k.sb("OC%d" % c, [128, 512], F32) for c in range(2)]
    OD = k.sb("OD", [128, 512], F32)
    ACP = [k.sb("ACP%d" % c, [128, 512], F32) for c in range(2)]
    OSQ = k.sb("OSQ", [128, 512], F32)
    RS = k.sb("RS", [128, 512], F32)
    OUT = [k.sb("OUT%d" % i, [128, 512], BF16) for i in range(2)]
    psS2 = [k.ps("psS2_%d" % i, [128, 1024], F32) for i in range(2)]
    psS = [psS2[0][:, 0:512], psS2[0][:, 512:1024], psS2[1][:, 0:512], psS2[1][:, 512:1024]]
    psO = [k.ps("psO%d" % i, [128, 512], F32) for i in range(2)]
    psL = [k.ps("psL%d" % i, [128, 512], F32) for i in range(2)]
    psM = psS2[0][:, 0:512]

    bKA = k.bufs("KA", 2); bV = k.buf("V"); bQT = k.bufs("QT", 4); bBT = k.buf("BT")
    bBTC = k.bufs("BTC", 2); bDM = k.buf("DM"); bBIGI = k.buf("BIGI"); bONESB = k.buf("ONESB")
    bONESF = k.buf("ONESF"); bLAM = k.buf("LAM"); bLT = k.buf("LT"); bLS = k.buf("LS")
    bSG = k.buf("SG"); bSQT = k.bufs("SQT", 2); bMX = k.buf("MX"); bQ2 = k.buf("Q2"); bCC = k.buf("CC")
    bPT = k.bufs("PT", NPT); bRL = k.buf("RL"); bOC = k.bufs("OC", 2); bOD = k.buf("OD")
    bOSQ = k.buf("OSQ"); bRS = k.buf("RS"); bOUT = k.bufs("OUT", 2); bACP = k.bufs("ACP", 2)
    bpsS = k.bufs("psS", 4); bpsO = k.bufs("psO", 2); bpsL = k.bufs("psL", 2); bpsM = bpsS[0]

    k.dma("sp", DM[:], dm_in, w=[bDM])
    k.dma("sp", BIGI[:], bigi_in, w=[bBIGI])
    k.dma("sp", LAM[:], lam_in, w=[bLAM])
    k.dma("sp", SG[:], sg_in, w=[bSG])
    k.op("dve", lambda e: e.memset(ONESB[:], 1.0), w=[bONESB])
    k.op("dve", lambda e: e.memset(ONESF[:], 1.0), w=[bONESF])
    for j in range(2):
        k.op("dve", lambda e: e.tensor_tensor(out=LT[:, j, :], in0=LAM[:, 2 * j, :],
                                               in1=LAM[:, 2 * j + 1, :], op=ALU.mult),
             r=[bLAM], w=[bLT])
        k.op("dve", lambda e: e.reduce_sum(out=LS[:, j:j + 1], in_=LT[:, j, :], axis=AX.X),
             r=[bLT], w=[bLS])
    k.op("act", lambda e: e.activation(out=LS[:, 0:2], in_=LS[:, 0:2], func=AF.Exp),
         r=[bLS], w=[bLS])
    k.op("dve", lambda e: e.tensor_tensor(out=LS[:, 2:3], in0=LS[:, 1:2], in1=LS[:, 0:1],
                                           op=ALU.subtract), r=[bLS], w=[bLS])
    k.op("dve", lambda e: e.tensor_scalar(out=LS[:, 3:4], in0=LS[:, 2:3], scalar1=-lambda_init,
                                           scalar2=None, op0=ALU.add), r=[bLS], w=[bLS])
    k.op("dve", lambda e: e.tensor_scalar(out=SG[:], in0=SG[:], scalar1=1.0 - lambda_init,
                                           scalar2=None, op0=ALU.mult), r=[bSG], w=[bSG])

    qti = 0
    pti = 0
    psi = 0
    oi = 0
    deferred = []
    for hh in range(NH):
        for c in range(2):
            for s0 in range(0, S, 4096):
                s1 = min(S, s0 + 4096)
                for (ap, lo, hi) in io.k(hh, c, s0, s1):
                    k.dma("sp", KA[c][0:64, lo:hi], ap, w=[bKA[c]])
            k.op("pool", lambda e: e.memset(KA[c][64:67, :], 1.0), w=[bKA[c]])
        for b0 in range(0, NB, 32):
            b1 = min(NB, b0 + 32)
            for (ap, lo, hi) in io.v(hh, b0 * 128, b1 * 128):
                k.dma("sp", V[:, lo // 128:hi // 128, :], ap.rearrange("(nb p) c -> p nb c", p=128), w=[bV])
        k.dma("sp", BT[:], bt_in[hh], w=[bBT])
        for qi_ in range(4):
            k.dma("sp", QT[qi_][64:67, :], io.qaug[hh], w=[bQT[qi_]])
        for c in range(2):
            def fetch_q(i, c=c):
                nonlocal qti
                qb = qti % 4
                qti += 1
                for (ap, lo, hi) in io.q(hh, c, i * 512, (i + 1) * 512):
                    k.dma("sp", QT[qb][0:64, :], ap, w=[bQT[qb]])
                return [bQT[qb]], QT[qb][0:64, :]
            emit_sqmax(k, nc, fetch_q, NQ, SQT, bSQT, ONESB, bONESB, [psS2[0], psS2[1]], [bpsS[0], bpsS[2]], MX, bMX,
                       Q2[:, 0:1], bQ2)
            emit_sqmax(k, nc, lambda i, c=c: ([bKA[c]], KA[c][0:64, i * 512:(i + 1) * 512]), NQ,
                       SQT, bSQT, ONESB, bONESB, [psS2[0], psS2[1]], [bpsS[0], bpsS[2]], MX, bMX, Q2[:, 1:2], bQ2)
            k.op("dve", lambda e: e.tensor_tensor(out=Q2[:, 2:3], in0=Q2[:, 0:1], in1=Q2[:, 1:2],
                                                   op=ALU.mult), r=[bQ2], w=[bQ2])
            k.op("act", lambda e: e.activation(out=Q2[:, 3:4], in_=Q2[:, 2:3], func=AF.Sqrt),
                 r=[bQ2], w=[bQ2])
            k.op("dve", lambda e: e.tensor_copy(out=CC[:, c:c + 1], in_=Q2[:, 3:4]), r=[bQ2], w=[bCC])
        k.op("dve", lambda e: e.tensor_tensor(out=CC[:, 2:3], in0=CC[:, 0:1], in1=CC[:, 1:2], op=ALU.max),
             r=[bCC], w=[bCC])
        for c in range(2):
            k.op("dve", lambda e: e.tensor_scalar(out=BTC[c][:], in0=BT[:], scalar1=CC[:, 2:3],
                                                   scalar2=None, op0=ALU.subtract),
                 r=[bBT, bCC], w=[bBTC[c]])
        for I in range(NQ):
            nkb = 4 * I + 4
            qbs = []
            for c in range(2):
                qb = qti % 4
                qti += 1
                for (ap, lo, hi) in io.q(hh, c, I * 512, (I + 1) * 512):
                    k.dma("sp", QT[qb][0:64, :], ap, w=[bQT[qb]])
                qbs.append(qb)
            staged = []

            def stage_a(jb):
                nonlocal psi, pti
                pts = []
                diag = jb >= 4 * I
                idx = 4 * I - jb + 3
                pair = (psi // 2) % 2
                for c in range(2):
                    ps = psi % 4
                    psi += 1
                    k.op("pe", lambda e: e.matmul(psS[ps], lhsT=KA[c][:, jb * 128:(jb + 1) * 128],
                                                   rhs=QT[qbs[c]][:], start=True, stop=not diag),
                         r=[bKA[c], bQT[qbs[c]]], w=[bpsS[ps]])
                    if diag:
                        dd = jb - 4 * I
                        k.op("pe", lambda e: e.matmul(psS[ps], lhsT=BIGI[:], rhs=DM[:, dd, :],
                                                       start=False, stop=True),
                             r=[bBIGI, bDM], w=[bpsS[ps]])
                pt = pti % NPT
                pti += 1
                k.op("act", lambda e: e.activation(out=PT[pt][:], in_=psS2[pair][:], func=AF.Exp,
                                                    bias=BTC[0][:, idx:idx + 1], scale=1.0),
                     r=[bpsS[2 * pair], bpsS[2 * pair + 1], bBTC[0]], w=[bPT[pt]])
                return [pt, pt]

            def stage_b(jb, pts):
                for c in range(2):
                    k.op("pe", lambda e: e.matmul(psO[c][:], lhsT=V[:, jb, :], rhs=PT[pts[c]][:, c * 512:(c + 1) * 512],
                                                   start=(jb == 0), stop=(jb == nkb - 1)),
                         r=[bV, bPT[pts[c]]], w=[bpsO[c]])
                for c in range(2):
                    if jb % 3 == 2:
                        if jb == 2:
                            k.op("pool", lambda e: e.tensor_copy(out=ACP[c][:], in_=PT[pts[c]][:, c * 512:(c + 1) * 512]),
                                 r=[bPT[pts[c]]], w=[bACP[c]])
                        else:
                            k.op("pool", lambda e: e.tensor_tensor(out=ACP[c][:], in0=ACP[c][:],
                                                                    in1=PT[pts[c]][:, c * 512:(c + 1) * 512], op=ALU.add),
                                 r=[bACP[c], bPT[pts[c]]], w=[bACP[c]])
                    elif jb == 0:
                        k.op("dve", lambda e: e.tensor_copy(out=psL[c][:], in_=PT[pts[c]][:, c * 512:(c + 1) * 512]),
                             r=[bPT[pts[c]]], w=[bpsL[c]])
                    else:
                        k.op("dve", lambda e: e.tensor_tensor(out=psL[c][:], in0=psL[c][:], in1=PT[pts[c]][:, c * 512:(c + 1) * 512],
                                                               op=ALU.add),
                             r=[bpsL[c], bPT[pts[c]]], w=[bpsL[c]])

            for jb in range(nkb):
                staged.append((jb, stage_a(jb)))
                if jb == 1:
                    while deferred:
                        deferred.pop(0)()
                if len(staged) > 1:
                    stage_b(*staged.pop(0))
            while staged:
                stage_b(*staged.pop(0))
            def tile_epilogue(hh=hh, I=I):
                nonlocal psi, oi
                for c in range(2):
                    k.op("dve", lambda e: e.tensor_tensor(out=OSQ[:], in0=psL[c][:], in1=ACP[c][:], op=ALU.add),
                         r=[bpsL[c], bACP[c]], w=[bOSQ])
                    ps = psi % 4
                    psi += 1
                    k.op("pe", lambda e: e.matmul(psS[ps], lhsT=ONESF[:], rhs=OSQ[:], start=True, stop=True),
                         r=[bONESF, bOSQ], w=[bpsS[ps]])
                    k.op("dve", lambda e: e.tensor_scalar(out=RL[:], in0=psS[ps], scalar1=1e-30,
                                                           scalar2=None, op0=ALU.max),
                         r=[bpsS[ps]], w=[bRL])
                    k.op("dve", lambda e: e.reciprocal(out=RL[:], in_=RL[:]), r=[bRL], w=[bRL])
                    k.op("dve", lambda e: e.tensor_tensor(out=OC[c][:], in0=psO[c][:], in1=RL[:], op=ALU.mult),
                         r=[bpsO[c], bRL], w=[bOC[c]])
                k.op("dve", lambda e: e.scalar_tensor_tensor(out=OD[:], in0=OC[1][:], scalar=LS[:, 3:4],
                                                              in1=OC[0][:], op0=ALU.mult, op1=ALU.add),
                     r=[bOC[0], bOC[1], bLS], w=[bOD])
                k.op("act", lambda e: e.activation(out=OSQ[:], in_=OD[:], func=AF.Square),
                     r=[bOD], w=[bOSQ])
                k.op("pe", lambda e: e.matmul(psM, lhsT=ONESF[:], rhs=OSQ[:], start=True, stop=True),
                     r=[bONESF, bOSQ], w=[bpsM])
                k.op("act", lambda e: e.activation(out=RS[:], in_=psM, func=AF.Sqrt,
                                                    scale=1.0 / 128.0, bias=EPS), r=[bpsM], w=[bRS])
                k.op("dve", lambda e: e.reciprocal(out=RS[:], in_=RS[:]), r=[bRS], w=[bRS])
                ob = oi % 2
                oi += 1
                k.op("dve", lambda e: e.scalar_tensor_tensor(out=OUT[ob][:], in0=OD[:], scalar=SG[:, 0:1],
                                                              in1=RS[:], op0=ALU.mult, op1=ALU.mult),
                     r=[bOD, bSG, bRS], w=[bOUT[ob]])
                n0 = len(k.stores)
                k.store("sp", io.out(hh, I), OUT[ob][:], r=[bOUT[ob]])
                if getattr(io, "out_done", None) is not None:
                    io.out_done(I, hh, k.stores[n0:])
            deferred.append(tile_epilogue)
    while deferred:
        deferred.pop(0)()


D_FF = 2816


class IOK3Standalone:
    def __init__(self, nc, T):
        d = lambda n, sh, dt: nc.dram_tensor(n, sh, dt, kind="ExternalInput").ap()
        NCH = 2 * D_FF // 128
        self.xT = d("xT", [D_MODEL, 2 + T], F32)
        self.aT = d("aT", [D_MODEL, 2 + T], BF16)
        self.wo_in = d("wo", [D_MODEL, D_MODEL], F32)
        self.wu_in = d("wu", [D_MODEL, 2 * D_FF], F32)
        self.wd_in = d("wd", [D_FF, D_MODEL], F32)
        self.cw_in = d("cw", [128, NCH, 3], F32)
        self.cb_in = d("cb", [128, NCH], F32)
        self.g_in = d("g", [128, 8], F32)
        self.gf_in = d("gf", [128, 8], F32)
        self.yT = nc.dram_tensor("yT", [D_MODEL, T], F32, kind="ExternalOutput").ap()
        self.halo_scale = None
        self.tail_dst = None

    def x_src(self, col0, n):
        return self.xT.rearrange("(dc p) t -> p dc t", p=128)[:, :, col0:col0 + n]

    def a_src(self, col0, n):
        return [(0, 1, self.aT.rearrange("(dc p) t -> p dc t", p=128)[:, :, col0:col0 + n])]

    def y_dst(self, oc, tcol, n):
        return self.yT[oc * 128:(oc + 1) * 128, tcol:tcol + n]

    def make_scratch(self, nc, T):
        self.X1D = nc.dram_tensor("X1D", [D_MODEL, T], F32).ap()
        self.AFFD = nc.dram_tensor("AFFD", [D_FF, T], BF16).ap()

    def x1_dst(self, dc, tcol, n):
        return self.X1D[dc * 128:(dc + 1) * 128, tcol:tcol + n]

    def x1_src(self, tcol, n):
        return self.X1D.rearrange("(dc p) t -> p dc t", p=128)[:, :, tcol:tcol + n]

    def aff_dst(self, gch, tcol, n):
        return self.AFFD[gch * 128:(gch + 1) * 128, tcol:tcol + n]

    def aff_src(self, tcol, n):
        return self.AFFD.rearrange("(kc p) t -> p kc t", p=128)[:, :, tcol:tcol + n]


def build_k3_split(T, final_norm):
    nc = new_nc()
    io = IOK3Standalone(nc, T)
    io.make_scratch(nc, T)
    k = Ctx(nc)
    k.begin_phase("a_")
    emit_k3(k, nc, T, final_norm, io, 512, "a")
    k.end_phase()
    k.begin_phase("b_")
    emit_k3(k, nc, T, final_norm, io, 512, "b")
    k.finish()
    return nc


def build_k3(T, final_norm, N=256):
    nc = new_nc()
    io = IOK3Standalone(nc, T)
    k = Ctx(nc)
    k.begin_phase("")
    emit_k3(k, nc, T, final_norm, io, N)
    k.finish()
    return nc


def emit_k3(k, nc, T, final_norm, io, N=256, part="ab"):
    A_, B_ = ("a" in part), ("b" in part)
    NT = T // N
    NCH = 2 * D_FF // 128
    NG = NCH // 2
    wo_in, wu_in, wd_in, cw_in, cb_in, g_in, gf_in = (io.wo_in, io.wu_in, io.wd_in, io.cw_in, io.cb_in,
                                                      io.g_in, io.gf_in)

    WO = k.sb("WO", [128, 8, D_MODEL], BF16) if A_ else None
    WU = k.sb("WU", [128, 8, 2 * D_FF], BF16) if A_ else None
    WD = k.sb("WD", [128, NG, D_MODEL], BF16) if B_ else None
    CW = k.sb("CW", [128, NCH, 3], F32)
    CB = k.sb("CB", [128, NCH], F32)
    G = k.sb("G", [128, 8], F32)
    GF = k.sb("GF", [128, 8], F32)
    ONES = k.sb("ONES", [128, 128], F32)
    HALO = k.sb("HALO", [128, NCH, 2], F32) if A_ else None
    NX1 = 2 if part == "b" else 1
    X1s = [k.sb("X1_%d" % i, [128, 8, N], F32) for i in range(NX1)]
    X1 = X1s[0]
    AT = k.sb("AT", [128, 8, N], BF16) if A_ else None
    SQ = [k.sb("SQ%d" % i, [128, N], F32) for i in range(2)]
    RS = k.sb("RS", [128, N], F32)
    HT = k.sb("HT", [128, 8, N], BF16) if A_ else None
    NAF = {"ab": 1, "a": 1, "b": 2}[part]
    AFFs = [k.sb("AFF%d" % i, [128, NG if B_ else 2, N], BF16) for i in range(NAF)]
    AFF = AFFs[0]
    NU, NY, NSG = 3, 4, 2
    U = [k.sb("U%d" % i, [128, 2 + N], F32) for i in range(NU)] if A_ else None
    Y = [k.sb("Y%d" % i, [128, N], F32) for i in range(NY)] if A_ else None
    SGT = [k.sb("SGT%d" % i, [128, N], F32) for i in range(NSG)] if A_ else None
    OUT = [k.sb("OUT%d" % i, [128, N], F32) for i in range(2)] if B_ else None
    mid = B_ and getattr(io, "h_dst", None) is not None
    assert not (mid and final_norm)
    if mid:
        OUTH = [k.sb("OUTH%d" % i, [128, N], BF16) for i in range(2)]
        bOUTH = k.bufs("OUTH", 2)
    PS = [k.ps("PS%d" % i, [128, 512], F32) for i in range(6)]
    PSS = k.ps("PSS", [128, 512], F32)

    bWO = k.buf("WO"); bWU = k.bufs("WU", 8); bWD = k.buf("WD"); bCW = k.buf("CW"); bCB = k.buf("CB")
    bG = k.buf("G"); bGF = k.buf("GF"); bONES = k.buf("ONES"); bHALO = k.bufs("HALO", NCH)
    bX1s = [k.bufs("X1", 8) for _ in range(NX1)]; bX1 = bX1s[0]; bAT = k.buf("AT"); bSQ = k.bufs("SQ", 2); bRS = k.buf("RS"); bHT = k.buf("HT")
    bAFFs = [k.bufs("AFF", NG) for _ in range(NAF)]; bAFF = bAFFs[0]; bU = k.bufs("U", NU); bY = k.bufs("Y", NY); bSGT = k.bufs("SGT", NSG)
    bOUT = k.bufs("OUT", 2); bPS = k.bufs("PS", 6); bPSS = k.buf("PSS")

    if A_:
        k.dma("sp", CW[:], cw_in, w=[bCW])
        k.dma("sp", CB[:], cb_in, w=[bCB])
        k.dma("sp", G[:], g_in, w=[bG])
    k.dma("sp", GF[:], gf_in, w=[bGF])
    k.op("dve", lambda e: e.memset(ONES[:], 1.0), w=[bONES])
    pre = getattr(io, "wb", None)
    if pre is not None:
        wov = pre[0].rearrange("(dc p) c -> p dc c", p=128)
        wuv = pre[1].rearrange("(dc p) c -> p dc c", p=128)
        wdv = pre[2].rearrange("(kc p) c -> p kc c", p=128)
        for dc in range(8 if A_ else 0):
            k.dma("sp", WU[:, dc, :], wuv[:, dc, :], w=[bWU[dc]])
        if A_:
            k.dma("sp", WO[:], wov, w=[bWO])
        for kc in range(0, NG if B_ else 0, 2):
            k.dma("sp", WD[:, kc:kc + 2, :], wdv[:, kc:kc + 2, :], w=[bWD])
    else:
        wov = wo_in.rearrange("(dc p) c -> p dc c", p=128)
        for dc in range(8 if A_ else 0):
            k.dma("pool", WO[:, dc, :], wov[:, dc, :], w=[bWO])
        wuv = wu_in.rearrange("(dc p) c -> p dc c", p=128)
        for dc in range(8 if A_ else 0):
            for c0 in range(0, 2 * D_FF, 1024):
                c1 = min(2 * D_FF, c0 + 1024)
                k.dma("pool", WU[:, dc, c0:c1], wuv[:, dc, c0:c1], w=[bWU[dc]])
        wdv = wd_in.rearrange("(kc p) c -> p kc c", p=128)
        for kc in range(NG if B_ else 0):
            k.dma("pool", WD[:, kc, :], wdv[:, kc, :], w=[bWD])

    st = {"pi": 0, "ui": 0, "yi": 0, "oi": 0, "sq": 0}
    if A_ and io.halo_scale is not None:
        M0 = k.sb("M0", [128, 1], F32)
        bM0 = k.buf("M0")
        k.dma("sp", M0[:], io.halo_scale, w=[bM0])

    def rms(n, gtile, inv_d):
        for dc in range(8):
            s = st["sq"] % 2
            st["sq"] += 1
            k.op("act", lambda e: e.activation(out=SQ[s][:, :n], in_=X1[:, dc, :n], func=AF.Square),
                 r=[bX1[dc]], w=[bSQ[s]])
            k.op("pe", lambda e: e.matmul(PSS[:, :n], lhsT=ONES[:], rhs=SQ[s][:, :n],
                                           start=(dc == 0), stop=(dc == 7)),
                 r=[bONES, bSQ[s]], w=[bPSS])
        k.op("act", lambda e: e.activation(out=RS[:, :n], in_=PSS[:, :n], func=AF.Sqrt,
                                            scale=inv_d, bias=EPS), r=[bPSS], w=[bRS])
        k.op("dve", lambda e: e.reciprocal(out=RS[:, :n], in_=RS[:, :n]), r=[bRS], w=[bRS])

    def tile(col0, n, halo_only, it=0):
        nonlocal X1, bX1, AFF, bAFF
        X1, bX1 = X1s[it % NX1], bX1s[it % NX1]
        AFF, bAFF = AFFs[it % NAF], bAFFs[it % NAF]
        if part == "b":
            afv = io.aff_src(col0 - 2, n)
            for h0 in (0, NG // 2):
                k.dma("sp", AFF[:, h0:h0 + NG // 2, :n], afv[:, h0:h0 + NG // 2, :], w=bAFF[h0:h0 + NG // 2])
            k.dma("sp", X1[:, :, :n], io.x1_src(col0 - 2, n), w=bX1)
        else:
            tile_a(col0, n, halo_only)
        if halo_only or part == "a":
            return
        tile_b(col0, n)

    def tile_a(col0, n, halo_only):
        k.dma("sp", X1[:, :, :n], io.x_src(col0, n), w=bX1)
        for (dc0, dstep, ap_) in io.a_src(col0, n):
            ndc = ap_.shape[1]
            k.dma("sp", AT[:, dc0:dc0 + (ndc - 1) * dstep + 1:dstep, :n], ap_, w=[bAT])
        if halo_only and io.halo_scale is not None:
            k.op("dve", lambda e: e.tensor_scalar(out=X1[:, :, :n], in0=X1[:, :, :n], scalar1=M0[:, 0:1],
                                                   scalar2=None, op0=ALU.mult), r=bX1 + [bM0], w=bX1)
            k.op("dve", lambda e: e.tensor_scalar(out=AT[:, :, :n], in0=AT[:, :, :n], scalar1=M0[:, 0:1],
                                                   scalar2=None, op0=ALU.mult), r=[bAT, bM0], w=[bAT])
        for oc in range(8):
            p = st["pi"] % 6
            st["pi"] += 1
            for dc in range(8):
                k.op("pe", lambda e: e.matmul(PS[p][:, :n], lhsT=WO[:, dc, oc * 128:(oc + 1) * 128],
                                               rhs=AT[:, dc, :n], start=(dc == 0), stop=(dc == 7)),
                     r=[bWO, bAT], w=[bPS[p]])
            k.op("dve", lambda e: e.tensor_tensor(out=X1[:, oc, :n], in0=X1[:, oc, :n], in1=PS[p][:, :n],
                                                   op=ALU.add), r=[bX1[oc], bPS[p]], w=[bX1[oc]])
        if part == "a" and not halo_only:
            for dc in range(8):
                k.store("pool", io.x1_dst(dc, col0 - 2, n), X1[:, dc, :n], r=[bX1[dc]])
        rms(n, None, 1.0 / D_MODEL)
        for dc in range(8):
            k.op("dve", lambda e: e.scalar_tensor_tensor(
                out=HT[:, dc, :n], in0=X1[:, dc, :n], scalar=G[:, dc:dc + 1], in1=RS[:, :n],
                op0=ALU.mult, op1=ALU.mult), r=[bX1[dc], bG, bRS], w=[bHT])
        for cc in range(NCH):
            c = (cc // 2) + (NG if cc % 2 else 0)
            p = st["pi"] % 6
            st["pi"] += 1
            for dc in range(8):
                k.op("pe", lambda e: e.matmul(PS[p][:, :n], lhsT=WU[:, dc, c * 128:(c + 1) * 128],
                                               rhs=HT[:, dc, :n], start=(dc == 0), stop=(dc == 7)),
                     r=[bWU[dc], bHT], w=[bPS[p]])
            if halo_only:
                k.op("act", lambda e: e.copy(out=HALO[:, c, :], in_=PS[p][:, :n]),
                     r=[bPS[p]], w=[bHALO[c]])
                continue
            u = st["ui"] % NU
            st["ui"] += 1
            k.op("pool", lambda e: e.tensor_copy(out=U[u][:, 0:2], in_=HALO[:, c, :]),
                 r=[bHALO[c]], w=[bU[u]])
            k.op("act", lambda e: e.copy(out=U[u][:, 2:2 + n], in_=PS[p][:, :n]),
                 r=[bPS[p]], w=[bU[u]])
            k.op("pool", lambda e: e.tensor_copy(out=HALO[:, c, :], in_=U[u][:, n:n + 2]),
                 r=[bU[u]], w=[bHALO[c]])
            y = st["yi"] % NY
            st["yi"] += 1
            k.op("dve", lambda e: e.tensor_scalar(out=Y[y][:, :n], in0=U[u][:, 2:2 + n],
                                                   scalar1=CW[:, c, 2:3], scalar2=CB[:, c:c + 1],
                                                   op0=ALU.mult, op1=ALU.add),
                 r=[bU[u], bCW, bCB], w=[bY[y]])
            k.op("dve", lambda e: e.scalar_tensor_tensor(out=Y[y][:, :n], in0=U[u][:, 1:1 + n],
                                                          scalar=CW[:, c, 1:2], in1=Y[y][:, :n],
                                                          op0=ALU.mult, op1=ALU.add),
                 r=[bU[u], bCW, bY[y]], w=[bY[y]])
            k.op("dve", lambda e: e.scalar_tensor_tensor(out=Y[y][:, :n], in0=U[u][:, 0:n],
                                                          scalar=CW[:, c, 0:1], in1=Y[y][:, :n],
                                                          op0=ALU.mult, op1=ALU.add),
                 r=[bU[u], bCW, bY[y]], w=[bY[y]])
            sg = (cc // 2) % NSG
            if cc % 2 == 0:
                k.op("act", lambda e: e.activation(out=SGT[sg][:, :n], in_=Y[y][:, :n], func=AF.Silu),
                     r=[bY[y]], w=[bSGT[sg]])
            else:
                gch = cc // 2
                asl = gch if part == "ab" else gch % 2
                k.op("pool", lambda e: e.tensor_tensor(out=AFF[:, asl, :n], in0=SGT[sg][:, :n],
                                                        in1=Y[y][:, :n], op=ALU.mult),
                     r=[bSGT[sg], bY[y]], w=[bAFF[asl]])
                if part == "a":
                    k.store("pool", io.aff_dst(gch, col0 - 2, n), AFF[:, asl, :n], r=[bAFF[asl]])

    def tile_b(col0, n):
        for oc in range(8):
            p = st["pi"] % 6
            st["pi"] += 1
            for kc in range(NG):
                k.op("pe", lambda e: e.matmul(PS[p][:, :n], lhsT=WD[:, kc, oc * 128:(oc + 1) * 128],
                                               rhs=AFF[:, kc, :n], start=(kc == 0), stop=(kc == NG - 1)),
                     r=[bWD, bAFF[kc]], w=[bPS[p]])
            if final_norm or mid:
                k.op("dve", lambda e: e.tensor_tensor(out=X1[:, oc, :n], in0=X1[:, oc, :n],
                                                       in1=PS[p][:, :n], op=ALU.add),
                     r=[bX1[oc], bPS[p]], w=[bX1[oc]])
            else:
                o = st["oi"] % 2
                st["oi"] += 1
                k.op("dve", lambda e: e.tensor_tensor(out=OUT[o][:, :n], in0=X1[:, oc, :n],
                                                       in1=PS[p][:, :n], op=ALU.add),
                     r=[bX1[oc], bPS[p]], w=[bOUT[o]])
                k.store("pool", io.y_dst(oc, col0 - 2, n), OUT[o][:, :n], r=[bOUT[o]])
                if io.tail_dst is not None and col0 - 2 + n == T:
                    k.store("pool", io.tail_dst(oc), OUT[o][:, n - 2:n], r=[bOUT[o]])
        if mid:
            rms(n, None, 1.0 / D_MODEL)
            for oc in range(8):
                k.store("pool", io.y_dst(oc, col0 - 2, n), X1[:, oc, :n], r=[bX1[oc]])
                if col0 - 2 + n == T:
                    k.store("pool", io.tail_dst(oc), X1[:, oc, n - 2:n], r=[bX1[oc]])
                o = st["oi"] % 2
                st["oi"] += 1
                k.op("dve", lambda e: e.scalar_tensor_tensor(
                    out=OUTH[o][:, :n], in0=X1[:, oc, :n], scalar=GF[:, oc:oc + 1], in1=RS[:, :n],
                    op0=ALU.mult, op1=ALU.mult), r=[bX1[oc], bGF, bRS], w=[bOUTH[o]])
                k.store("pool", io.h_dst(oc, col0 - 2, n), OUTH[o][:, :n], r=[bOUTH[o]])
        if final_norm:
            rms(n, None, 1.0 / D_MODEL)
            for oc in range(8):
                o = st["oi"] % 2
                st["oi"] += 1
                k.op("dve", lambda e: e.scalar_tensor_tensor(
                    out=OUT[o][:, :n], in0=X1[:, oc, :n], scalar=GF[:, oc:oc + 1], in1=RS[:, :n],
                    op0=ALU.mult, op1=ALU.mult), r=[bX1[oc], bGF, bRS], w=[bOUT[o]])
                k.store("pool", io.y_dst(oc, col0 - 2, n), OUT[o][:, :n], r=[bOUT[o]])

    if A_:
        tile(0, 2, True)
    for it in range(NT):
        n0 = len(k.stores)
        tile(2 + it * N, N, False, it)
        if B_ and getattr(io, "tile_done", None) is not None:
            io.tile_done(it, k.stores[n0:], N)


def conv_layouts(conv_w, conv_b):
    cw = np.ascontiguousarray(np.asarray(conv_w, np.float32).reshape(3, 44, 128).transpose(2, 1, 0))
    cb = np.ascontiguousarray(np.asarray(conv_b, np.float32).reshape(44, 128).T)
    return cw, cb


def esel_table():
    n = np.arange(128)[:, None, None]
    jj = np.arange(64)[None, :, None]
    s = np.arange(128)[None, None, :]
    return np.where(n == 2 * jj + (s >= 64), BIG, 0.0).astype(NPBF)


def itab_table(m):
    tl = np.arange(128)[:, None]
    jp = np.arange(1024)[None, :] - 1016
    d = tl - 16 * jp - 31
    return np.where(d >= 0, -m * d, -1e30).astype(np.float32)


def ab_tables():
    tl = np.arange(128)[:, None]
    npr = np.arange(256)[None, :] - 254
    cc = (tl >= 64).astype(np.int64)
    V = npr <= cc
    Fn = V & (npr >= cc - 1)
    A = V.astype(np.float32)
    B = (V.astype(np.float32) - 1.0) + 1e6 * Fn.astype(np.float32)
    return A, B.astype(np.float32)


def selg_table():
    t = np.zeros((12, 12, 64), np.float32)
    for r in range(12):
        t[r, r, :] = 1.0
    return t.astype(NPBF)


K2A_STATIC = [("pek", [64, 32], F32), ("pev", [64, 32], F32), ("w1k", [2048, 256], F32),
              ("w2k", [256, 64], F32), ("w1v", [2048, 256], F32), ("w2v", [256, 64], F32),
              ("bt", [128, 4, 128], F32), ("btc", [128, 4, 60], F32), ("dm", [128, 4, 512], BF16),
              ("wm", [128, 4, 512], BF16), ("cm", [128, 5, 512], BF16), ("bigi", [128, 128], BF16),
              ("idb", [128, 128], F32), ("esel", [128, 64, 128], BF16), ("itab", [128, 4, 1024], F32),
              ("atab", [128, 256], F32), ("btab", [128, 256], F32), ("selg", [12, 12, 64], BF16),
              ("qaug", [4, 3, 512], BF16)]


class IOK2aStandalone:
    def __init__(self, nc, S):
        d = lambda n, sh, dt: nc.dram_tensor(n, sh, dt, kind="ExternalInput").ap()
        self.t = {"q%d" % p: None for p in range(4)}
        qa = d("qa", [4, 64, S], BF16)
        for p in range(4):
            self.t["q%d" % p] = qa[p]
        self.t["kc"] = d("kca", [64, S], BF16)
        self.t["vc"] = d("vca", [64, S], BF16)
        self.t["ks"] = d("ksa", [64, S], BF16)
        self.t["kw"] = d("kwa", [64, S], BF16)
        self.t["gt"] = d("gt", [12, S], BF16)
        self.t["vs"] = d("vs", [S, 64], BF16)
        self.t["vw"] = d("vw", [S, 64], BF16)
        self.st = {n: d(n, sh, dt) for (n, sh, dt) in K2A_STATIC}
        self.oT = nc.dram_tensor("oT", [256, S], BF16, kind="ExternalOutput").ap()

    def fm(self, name, t0, t1):
        return [(self.t[name][:, t0:t1], t0, t1)]

    def tok(self, name, t0, t1):
        return [(self.t[name][t0:t1, :], t0, t1)]

    def out(self, p, I):
        return self.oT[p * 64:(p + 1) * 64, I * 512:(I + 1) * 512]


def build_k2a(S):
    nc = new_nc()
    io = IOK2aStandalone(nc, S)
    k = Ctx(nc)
    k.begin_phase("")
    emit_k2a(k, nc, S, io)
    k.finish()
    return nc


def emit_k2a(k, nc, S, io):
    NQ = S // 512
    NB = S // 128
    NCMP = S // 16 - 1
    CW_ = S // 16
    NCB = (CW_ + 127) // 128
    stc = io.st
    pek_in, pev_in, w1k_in, w2k_in, w1v_in, w2v_in = (stc["pek"], stc["pev"], stc["w1k"], stc["w2k"],
                                                      stc["w1v"], stc["w2v"])
    bt_in, btc_in, dm_in, wm_in, cm_in, bigi_in, idb_in = (stc["bt"], stc["btc"], stc["dm"], stc["wm"],
                                                           stc["cm"], stc["bigi"], stc["idb"])
    esel_in, itab_in, atab_in, btab_in, selg_in, qaug_in = (stc["esel"], stc["itab"], stc["atab"],
                                                            stc["btab"], stc["selg"], stc["qaug"])
    A = k.sb
    P = k.ps

    def ld_fm(dst_fn, name, t0, t1, w):
        for (ap, lo, hi) in io.fm(name, t0, t1):
            k.dma("sp", dst_fn(lo - t0, hi - t0), ap, w=w)

    def ld_tok(dst_fn, name, t0, t1, w):
        for (ap, lo, hi) in io.tok(name, t0, t1):
            k.dma("sp", dst_fn((lo - t0) // 128, (hi - t0) // 128),
                  ap.rearrange("(nb p) d -> p nb d", p=128), w=w)

    KS = A("KS", [67, S], BF16); bKS = k.buf("KS")
    VS = A("VS", [128, NB, 65], BF16); bVS = k.buf("VS")
    KCMP = A("KCMP", [67, NCB * 128], BF16); bKCMP = k.buf("KCMP")
    VC = A("VC", [128, NCB, 65], BF16); bVC = k.buf("VC")
    BT = A("BT", [128, 4, 128], F32); bBT = k.buf("BT")
    BTCs = A("BTCs", [128, 4, 128], F32); bBTCs = k.buf("BTCs")
    BTCw = A("BTCw", [128, 4, 8], F32); bBTCw = k.buf("BTCw")
    BTC0 = A("BTC0", [128, 4, 60], F32); bBTC0 = k.buf("BTC0")
    BTCc = A("BTCc", [128, 4, 60], F32); bBTCc = k.buf("BTCc")
    DM = A("DM", [128, 4, 512], BF16); bDM = k.buf("DM")
    WM = A("WM", [128, 4, 512], BF16); bWM = k.buf("WM")
    CM = A("CM", [128, 5, 512], BF16); bCM = k.buf("CM")
    BIGI = A("BIGI", [128, 128], BF16); bBIGI = k.buf("BIGI")
    IDB = A("IDB", [128, 128], F32); bIDB = k.buf("IDB")
    ESEL = A("ESEL", [128, 64, 128], BF16); bESEL = k.buf("ESEL")
    ITAB = A("ITAB", [128, 4, 1024], F32); bITAB = k.buf("ITAB")
    ATAB = A("ATAB", [128, 256], F32); bATAB = k.buf("ATAB")
    BTAB = A("BTAB", [128, 256], F32); bBTAB = k.buf("BTAB")
    SELG = A("SELG", [12, 12, 64], BF16); bSELG = k.buf("SELG")
    ONESB = A("ONESB", [128, 128], BF16); bONESB = k.buf("ONESB")
    ONESF = A("ONESF", [128, 64], F32); bONESF = k.buf("ONESF")
    QT = [A("QT%d" % i, [67, 4, 512], BF16) for i in range(2)]; bQT = k.bufs("QT", 2)
    KW = [A("KW%d" % i, [67, 1024], BF16) for i in range(2)]; bKW = k.bufs("KW", 2)
    VW = [A("VW%d" % i, [128, 8, 65], BF16) for i in range(2)]; bVW = k.bufs("VW", 2)
    GT = [A("GT%d" % i, [12, 512], BF16) for i in range(2)]; bGT = k.bufs("GT", 2)
    SQT = [A("SQT%d" % i, [128, 512], BF16) for i in range(2)]; bSQT = k.bufs("SQT", 2)
    MX = A("MX", [128, 64], F32); bMX = k.buf("MX")
    ST = A("ST", [128, 16], F32); bST = k.buf("ST")
    SX = A("SX", [128, 1024], F32); bSX = k.buf("SX")
    EX = A("EX", [128, 1024], F32); bEX = k.buf("EX")
    PG = A("PG", [128, 1028], F32); bPG = k.buf("PG")
    SC = A("SC", [128, 8], F32); bSC = k.buf("SC")
    IMP = A("IMP", [128, 256], F32); bIMP = k.buf("IMP")
    I2 = A("I2", [128, 256], F32); bI2 = k.buf("I2")
    I3 = A("I3", [128, 256], F32); bI3 = k.buf("I3")
    M8 = A("M8", [128, 16], F32); bM8 = k.buf("M8")
    MQ = A("MQ", [128, 256], F32); bMQ = k.buf("MQ")
    MT = [A("MT%d" % i, [128, 2, 512], BF16) for i in range(2)]; bMT = k.bufs("MT", 2)
    NPT = 6
    PT = [A("PT%d" % i, [128, 512], BF16) for i in range(NPT)]; bPT = k.bufs("PT", NPT)
    RL = A("RL", [128, 512], F32); bRL = k.buf("RL")
    OBF = A("OBF", [64, 512], F32); bOBF = k.buf("OBF")
    TT = A("TT", [64, 512], F32); bTT = k.buf("TT")
    ACC = [A("ACC%d" % i, [64, 512], F32) for i in range(2)]; bACC = k.bufs("ACC", 2)
    OUTB = [A("OUTB%d" % i, [64, 512], BF16) for i in range(2)]; bOUTB = k.bufs("OUTB", 2)
    psS = [P("psS%d" % i, [128, 512], F32) for i in range(4)]; bpsS = k.bufs("psS", 4)
    psO = [P("psO%d" % i, [128, 512], F32) for i in range(2)]; bpsO = k.bufs("psO", 2)
    psA = P("psA", [128, 512], F32); bpsA = k.buf("psA")
    psX = [P("psX0", [128, 512], F32), psS[0]]; bpsX = [k.buf("psX0"), bpsS[0]]

    for (dst, src, b) in [(BT, bt_in, bBT), (BTC0, btc_in, bBTC0), (DM, dm_in, bDM), (WM, wm_in, bWM),
                          (CM, cm_in, bCM), (BIGI, bigi_in, bBIGI), (IDB, idb_in, bIDB),
                          (ITAB, itab_in, bITAB), (ATAB, atab_in, bATAB), (BTAB, btab_in, bBTAB),
                          (SELG, selg_in, bSELG)]:
        k.dma("sp", dst[:], src, w=[b])
    for j0 in range(0, 64, 16):
        k.dma("sp", ESEL[:, j0:j0 + 16, :], esel_in[:, j0:j0 + 16, :], w=[bESEL])
    k.op("dve", lambda e: e.memset(ONESB[:], 1.0), w=[bONESB])
    k.op("dve", lambda e: e.memset(ONESF[:], 1.0), w=[bONESF])
    k.op("dve", lambda e: e.memset(PG[:], 0.0), w=[bPG])
    k.op("dve", lambda e: e.memset(I2[:], -1.0), w=[bI2])
    k.op("dve", lambda e: e.memset(RL[:], 1.0), w=[bRL])
    k.op("pool", lambda e: e.memset(KCMP[:], 0.0), w=[bKCMP])
    k.op("pool", lambda e: e.memset(KCMP[64:67, :], 1.0), w=[bKCMP])
    k.op("pool", lambda e: e.memset(VC[:], 0.0), w=[bVC])
    for s0 in range(0, S, 4096):
        s1 = min(S, s0 + 4096)
        ld_fm(lambda a, b, s0=s0: KS[0:64, s0 + a:s0 + b], "ks", s0, s1, [bKS])
    k.op("pool", lambda e: e.memset(KS[64:67, :], 1.0), w=[bKS])
    for b0 in range(0, NB, 8):
        ld_tok(lambda a, b, b0=b0: VS[:, b0 + a:b0 + b, 0:64], "vs", b0 * 128, (b0 + 8) * 128, [bVS])
    k.op("pool", lambda e: e.memset(VS[:, :, 64:65], 1.0), w=[bVS])
    for i in range(2):
        k.op("pool", lambda e: e.memset(VW[i][:, :, 64:65], 1.0), w=[bVW[i]])
        k.op("pool", lambda e: e.memset(KW[i][64:67, :], 1.0), w=[bKW[i]])
        k.dma("sp", QT[i][64:67, :, :], qaug_in.rearrange("h r t -> r h t"), w=[bQT[i]])

    with nc.sbuf_tensor("KCH", [64, 8208], BF16) as KCH, \
            nc.sbuf_tensor("W1", [64, 32, 256], BF16) as W1, \
            nc.sbuf_tensor("W2", [128, 2, 64], BF16) as W2, \
            nc.sbuf_tensor("PEF", [64, 32], F32) as PEF, \
            nc.sbuf_tensor("PEB", [64, 32], BF16) as PEB, \
            nc.sbuf_tensor("BH", [128, 2], F32) as BH, \
            nc.sbuf_tensor("HX", [128, 512], F32) as HX, \
            nc.sbuf_tensor("H2", [128, 512], F32) as H2, \
            nc.sbuf_tensor("HID", [128, 2, 512], BF16) as HID:
        bKCH = k.buf("KCH"); bW1 = k.buf("W1"); bW2 = k.buf("W2"); bPEF = k.buf("PEF"); bPEB = k.buf("PEB")
        bBH = k.buf("BH"); bHX = k.buf("HX"); bH2 = k.buf("H2"); bHID = k.bufs("HID", 2)
        for which, (src, pe_in, w1_in, w2_in) in enumerate([("kc", pek_in, w1k_in, w2k_in),
                                                           ("vc", pev_in, w1v_in, w2v_in)]):
            w1v_ = w1_in.rearrange("(p d) h -> d p h", d=64)
            for p0 in range(0, 32, 8):
                k.dma("pool", W1[:, p0:p0 + 8, :], w1v_[:, p0:p0 + 8, :], w=[bW1])
            k.dma("pool", W2[:], w2_in.rearrange("(c p) d -> p c d", p=128), w=[bW2])
            k.dma("sp", PEF[:], pe_in, w=[bPEF])
            k.op("dve", lambda e: e.tensor_copy(out=PEB[:], in_=PEF[:]), r=[bPEF], w=[bPEB])
            for hc in range(2):
                for pos in range(32):
                    k.op("pe", lambda e: e.matmul(psX[0][:, hc:hc + 1], lhsT=W1[:, pos, hc * 128:(hc + 1) * 128],
                                                   rhs=PEB[:, pos:pos + 1], start=(pos == 0), stop=(pos == 31)),
                         r=[bW1, bPEB], w=[bpsX[0]])
            k.op("dve", lambda e: e.tensor_copy(out=BH[:], in_=psX[0][:, 0:2]), r=[bpsX[0]], w=[bBH])
            for j0 in range(0, NCMP, 512):
                n = min(512, NCMP - j0)
                t0 = 16 * j0
                t1 = min(S, t0 + 16 * n + 16)
                ld_fm(lambda a, b: KCH[:, a:b], src, t0, t1, [bKCH])
                for hc in range(2):
                    px = psX[1]
                    for pos in range(32):
                        k.op("pe", lambda e: e.matmul(px[:, :n], lhsT=W1[:, pos, hc * 128:(hc + 1) * 128],
                                                       rhs=KCH[:, pos:pos + 16 * (n - 1) + 1:16],
                                                       start=(pos == 0), stop=(pos == 31)),
                             r=[bW1, bKCH], w=[bpsX[1]])
                    k.op("act", lambda e: e.activation(out=HX[:, :n], in_=px[:, :n], func=AF.Identity,
                                                        bias=BH[:, hc:hc + 1], scale=1.0),
                         r=[bpsX[1], bBH], w=[bHX])
                    k.op("dve", lambda e: e.tensor_tensor(out=H2[:, :n], in0=HX[:, :n], in1=HX[:, :n],
                                                           op=ALU.mult), r=[bHX], w=[bH2])
                    k.op("dve", lambda e: e.tensor_scalar(out=H2[:, :n], in0=H2[:, :n], scalar1=0.044715,
                                                           scalar2=1.0, op0=ALU.mult, op1=ALU.add),
                         r=[bH2], w=[bH2])
                    k.op("dve", lambda e: e.tensor_tensor(out=H2[:, :n], in0=H2[:, :n], in1=HX[:, :n],
                                                           op=ALU.mult), r=[bH2, bHX], w=[bH2])
                    k.op("act", lambda e: e.activation(out=H2[:, :n], in_=H2[:, :n], func=AF.Tanh,
                                                        scale=0.7978845608028654), r=[bH2], w=[bH2])
                    k.op("dve", lambda e: e.tensor_scalar(out=H2[:, :n], in0=H2[:, :n], scalar1=0.5,
                                                           scalar2=0.5, op0=ALU.mult, op1=ALU.add),
                         r=[bH2], w=[bH2])
                    k.op("dve", lambda e: e.tensor_tensor(out=HID[:, hc, :n], in0=H2[:, :n], in1=HX[:, :n],
                                                           op=ALU.mult), r=[bH2, bHX], w=[bHID[hc]])
                if which == 0:
                    for hc in range(2):
                        k.op("pe", lambda e: e.matmul(psX[0][0:64, :n], lhsT=W2[:, hc, :], rhs=HID[:, hc, :n],
                                                       start=(hc == 0), stop=(hc == 1)),
                             r=[bW2, bHID[hc]], w=[bpsX[0]])
                    k.op("act", lambda e: e.copy(out=KCMP[0:64, j0:j0 + n], in_=psX[0][0:64, :n]),
                         r=[bpsX[0]], w=[bKCMP])
                else:
                    for jb in range((n + 127) // 128):
                        m = min(128, n - jb * 128)
                        for hc in range(2):
                            k.op("pe", lambda e: e.matmul(psX[0][0:m, 0:64], lhsT=HID[:, hc, jb * 128:jb * 128 + m],
                                                           rhs=W2[:, hc, :], start=(hc == 0), stop=(hc == 1)),
                                 r=[bW2, bHID[hc]], w=[bpsX[0]])
                        gb = j0 // 128 + jb
                        k.op("act", lambda e: e.copy(out=VC[0:m, gb, 0:64], in_=psX[0][0:m, 0:64]),
                             r=[bpsX[0]], w=[bVC])
                        k.op("pool", lambda e: e.memset(VC[0:m, gb, 64:65], 1.0), w=[bVC])

    st = {"q": 0, "kw": 0, "ps": 0, "pt": 0, "out": 0}
    NT5 = S // 512

    def load_q(I):
        qb = st["q"] % 2
        st["q"] += 1
        for p_ in range(4):
            ld_fm(lambda a, b, p_=p_: QT[qb][0:64, p_, a:b], "q%d" % p_, I * 512, (I + 1) * 512, [bQT[qb]])
        return qb

    ncw = (NCB * 128 + 511) // 512
    emit_sqmax(k, nc, lambda i: ([bKCMP], KCMP[0:64, i * 512:min(NCB * 128, (i + 1) * 512)]), ncw,
               SQT, bSQT, ONESB, bONESB, psS[1:3], bpsS[1:3], MX, bMX, ST[:, 4:5], bST)
    emit_sqmax(k, nc, lambda i: ([bKS], KS[0:64, i * 512:(i + 1) * 512]), NT5,
               SQT, bSQT, ONESB, bONESB, psS[1:3], bpsS[1:3], MX, bMX, ST[:, 5:6], bST)

    def fetch_kw(i):
        wb = st["kw"] % 2
        st["kw"] += 1
        ld_fm(lambda a, b: KW[wb][0:64, a:b], "kw", i * 512, (i + 1) * 512, [bKW[wb]])
        return [bKW[wb]], KW[wb][0:64, 0:512]
    emit_sqmax(k, nc, fetch_kw, NT5, SQT, bSQT, ONESB, bONESB, psS[1:3], bpsS[1:3], MX, bMX, ST[:, 6:7], bST)
    MXQ = A("MXQ", [128, 4, 32], F32); bMXQ = k.buf("MXQ")
    it_ = 0
    qb_nx = load_q(0)
    for i in range(NT5):
        qb_ = qb_nx
        if i + 1 < NT5:
            qb_nx = load_q(i + 1)
        for p in range(4):
            a = it_ % 2
            it_ += 1
            k.op("dve", lambda e: e.tensor_tensor(out=SQT[a][0:64, :], in0=QT[qb_][0:64, p, :],
                                                   in1=QT[qb_][0:64, p, :], op=ALU.mult),
                 r=[bQT[qb_]], w=[bSQT[a]])
            k.op("pe", lambda e: e.matmul(psS[1 + a][:], lhsT=ONESB[0:64, :], rhs=SQT[a][0:64, :],
                                           start=True, stop=True), r=[bSQT[a], bONESB], w=[bpsS[1 + a]])
            k.op("dve", lambda e: e.reduce_max(out=MXQ[:, p, i:i + 1], in_=psS[1 + a][:], axis=AX.X),
                 r=[bpsS[1 + a]], w=[bMXQ])
    for p in range(4):
        k.op("dve", lambda e: e.reduce_max(out=ST[:, p:p + 1], in_=MXQ[:, p, 0:NT5], axis=AX.X),
             r=[bMXQ], w=[bST])
    for p in range(4):
        k.op("dve", lambda e: e.tensor_scalar(out=ST[:, 8:11], in0=ST[:, 4:7], scalar1=ST[:, p:p + 1],
                                               scalar2=None, op0=ALU.mult), r=[bST], w=[bST])
        k.op("act", lambda e: e.activation(out=ST[:, 8:11], in_=ST[:, 8:11], func=AF.Sqrt), r=[bST], w=[bST])
        k.op("dve", lambda e: e.tensor_scalar(out=BTCc[:, p, :], in0=BTC0[:, p, :], scalar1=ST[:, 8:9],
                                               scalar2=None, op0=ALU.subtract), r=[bST, bBTC0], w=[bBTCc])
        k.op("dve", lambda e: e.tensor_scalar(out=BTCs[:, p, :], in0=BT[:, p, :], scalar1=ST[:, 9:10],
                                               scalar2=None, op0=ALU.subtract), r=[bST, bBT], w=[bBTCs])
        k.op("dve", lambda e: e.tensor_scalar(out=BTCw[:, p, :], in0=BT[:, p, 0:8], scalar1=ST[:, 10:11],
                                               scalar2=None, op0=ALU.subtract), r=[bST, bBT], w=[bBTCw])


    def importance_block(I, qb, mb, qi):
        for u in importance_units(I, qb, mb, qi):
            u()

    def importance_units(I, qb, mb, qi):
        return [lambda p=p: importance_head(I, qb, qi, p) for p in range(4)] + [lambda: importance_topk(I, mb, qi)]

    def importance_head(I, qb, qi, p):
        i = 4 * I + qi
        ncols = min(8 * (i + 1), NCB * 128)
        nb = 2 * (i + 1)
        if True:
            io_ = 1016 - 8 * i
            for c0 in range(0, ncols, 512):
                c1 = min(ncols, c0 + 512)
                k.op("pe", lambda e: e.matmul(psA[:, 0:c1 - c0], lhsT=QT[qb][0:64, p, qi * 128:(qi + 1) * 128],
                                               rhs=KCMP[0:64, c0:c1], start=True, stop=True),
                     r=[bQT[qb], bKCMP], w=[bpsA])
                k.op("dve", lambda e: e.tensor_tensor(out=SX[:, c0:c1], in0=psA[:, 0:c1 - c0],
                                                       in1=ITAB[:, p, io_ + c0:io_ + c1], op=ALU.add),
                     r=[bpsA, bITAB], w=[bSX])
            k.op("dve", lambda e: e.reduce_max(out=SC[:, 0:1], in_=SX[:, :ncols], axis=AX.X),
                 r=[bSX], w=[bSC])
            k.op("dve", lambda e: e.tensor_scalar(out=SC[:, 1:2], in0=SC[:, 0:1], scalar1=-1e20,
                                                   scalar2=-1.0, op0=ALU.max, op1=ALU.mult),
                 r=[bSC], w=[bSC])
            k.op("act", lambda e: e.activation(out=EX[:, :ncols], in_=SX[:, :ncols], func=AF.Exp,
                                                bias=SC[:, 1:2], scale=1.0, accum_out=SC[:, 2:3]),
                 r=[bSX, bSC], w=[bEX, bSC])
            k.op("dve", lambda e: e.tensor_scalar(out=SC[:, 3:4], in0=SC[:, 2:3], scalar1=1e-30,
                                                   scalar2=None, op0=ALU.max), r=[bSC], w=[bSC])
            k.op("dve", lambda e: e.reciprocal(out=SC[:, 3:4], in_=SC[:, 3:4]), r=[bSC], w=[bSC])
            if p == 0:
                k.op("dve", lambda e: e.tensor_scalar(out=PG[:, 1:1 + ncols], in0=EX[:, :ncols],
                                                       scalar1=SC[:, 3:4], scalar2=None, op0=ALU.mult),
                     r=[bEX, bSC], w=[bPG])
            else:
                k.op("dve", lambda e: e.scalar_tensor_tensor(out=PG[:, 1:1 + ncols], in0=EX[:, :ncols],
                                                              scalar=SC[:, 3:4], in1=PG[:, 1:1 + ncols],
                                                              op0=ALU.mult, op1=ALU.add),
                     r=[bEX, bSC, bPG], w=[bPG])

    def importance_topk(I, mb, qi):
        i = 4 * I + qi
        nb = 2 * (i + 1)
        k.op("dve", lambda e: e.reduce_sum(out=IMP[:, :nb],
                                            in_=PG[:, 0:4 * nb].rearrange("p (n r) -> p n r", r=4),
                                            axis=AX.X), r=[bPG], w=[bIMP])
        k.op("dve", lambda e: e.tensor_tensor(out=IMP[:, :nb], in0=IMP[:, :nb],
                                               in1=PG[:, 4:4 * nb + 1:4], op=ALU.add),
             r=[bPG, bIMP], w=[bIMP])
        k.op("dve", lambda e: e.tensor_tensor(out=I2[:, :nb], in0=IMP[:, :nb], in1=ATAB[:, 256 - nb:256],
                                               op=ALU.mult), r=[bIMP, bATAB], w=[bI2])
        k.op("dve", lambda e: e.tensor_tensor(out=I2[:, :nb], in0=I2[:, :nb], in1=BTAB[:, 256 - nb:256],
                                               op=ALU.add), r=[bI2, bBTAB], w=[bI2])
        k.op("dve", lambda e: e.memset(I2[:, 0:1], 1e6), w=[bI2])
        k.op("dve", lambda e: e.max(out=M8[:, 0:8], in_=I2[:]), r=[bI2], w=[bM8])
        k.op("dve", lambda e: e.match_replace(out=I3[:], in_to_replace=M8[:, 0:8], in_values=I2[:],
                                               imm_value=-2.0), r=[bI2, bM8], w=[bI3])
        k.op("dve", lambda e: e.max(out=M8[:, 8:16], in_=I3[:]), r=[bI3], w=[bM8])
        k.op("dve", lambda e: e.tensor_scalar(out=MQ[:], in0=I2[:], scalar1=M8[:, 15:16], scalar2=1.0,
                                               op0=ALU.is_ge, op1=ALU.subtract), r=[bI2, bM8], w=[bMQ])
        nch = 2 if nb > 128 else 1
        for ch in range(nch):
            k.op("pe", lambda e: e.transpose(out=psX[0][:, 0:128], in_=MQ[:, ch * 128:(ch + 1) * 128],
                                              identity=IDB[:]), r=[bMQ, bIDB], w=[bpsX[0]])
            k.op("act", lambda e: e.copy(out=MT[mb][:, ch, qi * 128:(qi + 1) * 128], in_=psX[0][:, 0:128]),
                 r=[bpsX[0]], w=[bMT[mb]])

    def branch(I, qb, heads, br, steps, bias_fn, first, hook=None):
        nst = len(steps)
        staged = []

        def stage_a(n_):
            lk, kb, extra, bidx, vl, vb = steps[n_]
            pts = []
            for hi_, p in enumerate(heads):
                ps = st["ps"] % 4
                st["ps"] += 1
                pts.append((ps, None))
            for hi_, p in enumerate(heads):
                ps = pts[hi_][0]
                k.op("pe", lambda e: e.matmul(psS[ps][:], lhsT=lk, rhs=QT[qb][:, p, :], start=True,
                                               stop=(len(extra) == 0)), r=kb + [bQT[qb]], w=[bpsS[ps]])
            for xi, (xl, xr, xb) in enumerate(extra):
                for hi_, p in enumerate(heads):
                    ps = pts[hi_][0]
                    k.op("pe", lambda e: e.matmul(psS[ps][:], lhsT=xl, rhs=xr, start=False,
                                                   stop=(xi == len(extra) - 1)), r=xb, w=[bpsS[ps]])
            out = []
            for hi_, p in enumerate(heads):
                ps = pts[hi_][0]
                pt = st["pt"] % NPT
                st["pt"] += 1
                bias, bbuf = bias_fn(p, bidx)
                k.op("act", lambda e: e.activation(out=PT[pt][:], in_=psS[ps][:], func=AF.Exp, bias=bias,
                                                    scale=1.0), r=[bpsS[ps], bbuf], w=[bPT[pt]])
                out.append(pt)
            return out

        def stage_b(n_, pts):
            lk, kb, extra, bidx, vl, vb = steps[n_]
            for hi_, p in enumerate(heads):
                k.op("pe", lambda e: e.matmul(psO[hi_][0:65, :], lhsT=vl, rhs=PT[pts[hi_]][:], start=(n_ == 0),
                                               stop=(n_ == nst - 1)), r=[vb, bPT[pts[hi_]]], w=[bpsO[hi_]])

        for n_ in range(nst):
            staged.append((n_, stage_a(n_)))
            if n_ == min(1, nst - 1):
                while deferred:
                    deferred.pop(0)()
            if hook is not None:
                hook(n_, nst)
            if len(staged) > 1:
                stage_b(*staged.pop(0))
        while staged:
            stage_b(*staged.pop(0))
        deferred.append(lambda: epilogue(qb, heads, br, first))

    def epilogue(qb, heads, br, first):
        for hi_, p in enumerate(heads):
            po = psO[hi_]
            k.op("dve", lambda e: e.tensor_scalar(out=RL[64:65, :], in0=po[64:65, :], scalar1=1e-30, scalar2=None,
                                                   op0=ALU.max), r=[bpsO[hi_]], w=[bRL])
            k.op("dve", lambda e: e.reciprocal(out=RL[64:65, :], in_=RL[64:65, :]), r=[bRL], w=[bRL])
            k.op("act", lambda e: e.copy(out=OBF[:], in_=po[0:64, :]), r=[bpsO[hi_]], w=[bOBF])
            k.op("pe", lambda e: e.matmul(psX[0][0:64, :], lhsT=ONESF[64:65, :], rhs=RL[64:65, :], start=True,
                                           stop=True), r=[bONESF, bRL], w=[bpsX[0]])
            k.op("dve", lambda e: e.tensor_tensor(out=TT[:], in0=OBF[:], in1=psX[0][0:64, :], op=ALU.mult),
                 r=[bOBF, bpsX[0]], w=[bTT])
            gr = p * 3 + br
            k.op("pe", lambda e: e.matmul(psX[0][0:64, :], lhsT=SELG[:, gr, :], rhs=GT[qb][:, :], start=True,
                                           stop=True), r=[bSELG, bGT[qb]], w=[bpsX[0]])
            if first:
                k.op("dve", lambda e: e.tensor_tensor(out=ACC[hi_][:], in0=TT[:], in1=psX[0][0:64, :], op=ALU.mult),
                     r=[bTT, bpsX[0]], w=[bACC[hi_]])
            else:
                k.op("dve", lambda e: e.tensor_tensor(out=TT[:], in0=TT[:], in1=psX[0][0:64, :], op=ALU.mult),
                     r=[bTT, bpsX[0]], w=[bTT])
                k.op("dve", lambda e: e.tensor_tensor(out=ACC[hi_][:], in0=ACC[hi_][:], in1=TT[:], op=ALU.add),
                     r=[bTT, bACC[hi_]], w=[bACC[hi_]])

    def load_tile(I):
        qb = load_q(I)
        ld_fm(lambda a, b: GT[qb][:, a:b], "gt", I * 512, (I + 1) * 512, [bGT[qb]])
        wb = I % 2
        jlo = max(0, 4 * I - 4)
        lo = jlo - (4 * I - 4)
        ld_fm(lambda a, b: KW[wb][0:64, lo * 128 + a:lo * 128 + b], "kw", jlo * 128, (4 * I + 4) * 128,
              [bKW[wb]])
        ld_tok(lambda a, b: VW[wb][:, lo + a:lo + b, 0:64], "vw", jlo * 128, (4 * I + 4) * 128, [bVW[wb]])
        return qb

    if getattr(io, "after_prologue", None) is not None:
        io.after_prologue()
    deferred = []
    qb_next = load_tile(0)
    for qi in range(4):
        importance_block(0, qb_next, 0, qi)
    for I in range(NQ):
        qb = qb_next
        mb = I % 2
        wb = I % 2
        jlo = max(0, 4 * I - 4)
        pend = []
        while deferred:
            deferred.pop(0)()
        if I + 1 < NQ:
            qb_next = load_tile(I + 1)
            for qi in range(4):
                pend += importance_units(I + 1, qb_next, (I + 1) % 2, qi)
        gap = max(1, (2 * (4 * I + 4)) // 22)
        half = [10]

        def hook(n_, nst):
            if pend and half[0] > 0 and n_ >= 1 and (n_ - 1) % gap == 0:
                pend.pop(0)()
                half[0] -= 1

        for hp in range(2):
            heads = (2 * hp, 2 * hp + 1)
            steps = []
            for jb in range(NCB):
                dd = I - 4 * jb
                if dd < 0:
                    continue
                extra = []
                if dd <= 4:
                    extra.append((BIGI[:], CM[:, dd, :], [bBIGI, bCM]))
                steps.append((KCMP[:, jb * 128:(jb + 1) * 128], [bKCMP], extra, dd + 28, VC[:, jb, :], bVC))
            branch(I, qb, heads, 0, steps, lambda p, ix: (BTCc[:, p, ix:ix + 1], bBTCc), True)
            steps = []
            for jb in range(4 * I + 4):
                extra = [(ESEL[:, jb % 64, :], MT[mb][:, jb // 64, :], [bESEL, bMT[mb]])]
                if jb >= 4 * I:
                    extra.append((BIGI[:], DM[:, jb - 4 * I, :], [bBIGI, bDM]))
                steps.append((KS[:, jb * 128:(jb + 1) * 128], [bKS], extra, 4 * I - jb + 3, VS[:, jb, :], bVS))
            half[0] = 10
            branch(I, qb, heads, 1, steps, lambda p, ix: (BTCs[:, p, ix:ix + 1], bBTCs), False, hook)
            steps = []
            for jb in range(jlo, 4 * I + 4):
                lw = jb - (4 * I - 4)
                if lw < 4:
                    extra = [(BIGI[:], WM[:, lw, :], [bBIGI, bWM])]
                else:
                    extra = [(BIGI[:], DM[:, lw - 4, :], [bBIGI, bDM])]
                steps.append((KW[wb][:, lw * 128:(lw + 1) * 128], [bKW[wb]], extra, 4 * I - jb + 3,
                              VW[wb][:, lw, :], bVW[wb]))
            branch(I, qb, heads, 2, steps, lambda p, ix: (BTCw[:, p, ix:ix + 1], bBTCw), False)
            def emit_out(I=I, hp=hp, heads=heads):
                n0 = len(k.stores)
                for hi_, p in enumerate(heads):
                    ob = st["out"] % 2
                    st["out"] += 1
                    k.op("act", lambda e: e.copy(out=OUTB[ob][:], in_=ACC[hi_][:]), r=[bACC[hi_]], w=[bOUTB[ob]])
                    k.store("sp", io.out(p, I), OUTB[ob][:], r=[bOUTB[ob]])
                if getattr(io, "out_done", None) is not None:
                    io.out_done(I, hp, k.stores[n0:])
            deferred.append(emit_out)
        while pend:
            pend.pop(0)()
    while deferred:
        deferred.pop(0)()


def k2a_consts(S):
    sl16 = alibi_slopes(16)
    A_, B_ = ab_tables()
    c = {"dm": dm_table(), "wm": wm_table(), "cm": cm_table(), "bigi": bigi_table(),
         "idb": np.eye(128).astype(np.float32), "esel": esel_table(), "atab": A_, "btab": B_,
         "selg": selg_table()}
    per_g = []
    for g in range(4):
        ms = sl16[g * 4:(g + 1) * 4]
        per_g.append({
            "bt": np.ascontiguousarray(np.stack([bt_table(m) for m in ms], 1)),
            "btc": np.ascontiguousarray(np.stack([btc_table(m) for m in ms], 1)),
            "itab": np.ascontiguousarray(np.stack([itab_table(m) for m in ms], 1)),
            "qaug": np.stack([q_aug_rows(m, 512) for m in ms], 0),
        })
    return c, per_g


def prep_k2a(pT, vt, g, consts, per_g, wts, S):
    d = dict(consts)
    d.update(per_g[g])
    d.update(wts)
    d["qa"] = np.ascontiguousarray(pT[g * 256:(g + 1) * 256].reshape(4, 64, S))
    d["kca"] = np.ascontiguousarray(pT[1024 + g * 64:1024 + (g + 1) * 64])
    d["vca"] = np.ascontiguousarray(pT[1280 + g * 64:1280 + (g + 1) * 64])
    d["ksa"] = np.ascontiguousarray(pT[1536 + g * 64:1536 + (g + 1) * 64])
    d["kwa"] = np.ascontiguousarray(pT[2048 + g * 64:2048 + (g + 1) * 64])
    d["vs"] = np.ascontiguousarray(vt[:, g * 64:(g + 1) * 64])
    d["vw"] = np.ascontiguousarray(vt[:, 256 + g * 64:256 + (g + 1) * 64])
    d["gt"] = np.ascontiguousarray(pT[2560 + g * 12:2560 + (g + 1) * 12])
    return d


def k2a_weights(pek, w1k, w2k, pev, w1v, w2v):
    f = lambda a: np.ascontiguousarray(np.asarray(a, np.float32))
    return {"pek": f(np.asarray(pek).T), "pev": f(np.asarray(pev).T), "w1k": f(w1k), "w2k": f(w2k),
            "w1v": f(w1v), "w2v": f(w2v)}


TPC = BATCH * SEQ // NCORES
CPB = NCORES // BATCH


def _run(nc, in_maps):
    res = run_bass_kernel_spmd(nc, in_maps, core_ids=list(range(NCORES)))
    return res.results


def _with_halo(full_T, c):
    b, j = divmod(c, CPB)
    a = full_T[b]
    out = np.zeros((a.shape[0], 2 + TPC), a.dtype)
    lo = j * TPC
    if j > 0:
        out[:, 0:2] = a[:, lo - 2:lo]
    out[:, 2:] = a[:, lo:lo + TPC]
    return out


def kernel_unfused(x, norm_mix_g, norm_ffn_g, final_norm_g,
           nsa_w_in, nsa_cmp_k_pe, nsa_cmp_k_w1, nsa_cmp_k_w2,
           nsa_cmp_v_pe, nsa_cmp_v_w1, nsa_cmp_v_w2, nsa_w_out,
           diff_w_in, diff_lam_q1, diff_lam_k1, diff_lam_q2, diff_lam_k2,
           diff_subln_g, diff_w_out,
           ffn_w_up, ffn_conv_w, ffn_conv_b, ffn_w_down):
    f32 = lambda a: np.ascontiguousarray(np.asarray(a, dtype=np.float32))
    x = f32(x)
    S = SEQ
    xT = [np.ascontiguousarray(x[b].T) for b in range(BATCH)]

    def tok_shards(full_T):
        return [np.ascontiguousarray(full_T[c // CPB][:, (c % CPB) * TPC:(c % CPB + 1) * TPC])
                for c in range(NCORES)]

    fm = [(0, 1024, 0.125, AF.Copy), (1024, 2560, 1.0, AF.Copy), (2560, 2608, 1.0, AF.Sigmoid)]
    tok = [(1792, 2048), (2304, 2560)]
    nc1 = build_k1(TPC, 2608, fm, tok)
    gl = g_layout(norm_mix_g[0])
    w = f32(nsa_w_in[0])
    r1 = _run(nc1, [{"xT": s, "g": gl, "w": w} for s in tok_shards(xT)])
    pT = [np.concatenate([r1[b * CPB + j]["projT"] for j in range(CPB)], axis=1) for b in range(BATCH)]
    vt = [np.concatenate([r1[b * CPB + j]["vtok"] for j in range(CPB)], axis=0) for b in range(BATCH)]
    del r1
    consts, per_g = k2a_consts(S)
    wts = k2a_weights(nsa_cmp_k_pe[0], nsa_cmp_k_w1[0], nsa_cmp_k_w2[0],
                      nsa_cmp_v_pe[0], nsa_cmp_v_w1[0], nsa_cmp_v_w2[0])
    nc2 = build_k2a(S)
    r2 = _run(nc2, [prep_k2a(pT[c // CPB], vt[c // CPB], c % CPB, consts, per_g, wts, S)
                    for c in range(NCORES)])
    aT = [np.concatenate([r2[b * CPB + g]["oT"] for g in range(CPB)], axis=0) for b in range(BATCH)]
    del r2, pT, vt
    cw, cb = conv_layouts(ffn_conv_w[0], ffn_conv_b[0])
    nc3 = build_k3(TPC, False)
    ins = [{"xT": _with_halo(xT, c), "aT": _with_halo(aT, c), "wo": f32(nsa_w_out[0]),
            "wu": f32(ffn_w_up[0]), "wd": f32(ffn_w_down[0]), "cw": cw, "cb": cb,
            "g": g_layout(norm_ffn_g[0]), "gf": g_layout(final_norm_g)} for c in range(NCORES)]
    r3 = _run(nc3, ins)
    xT = [np.concatenate([r3[b * CPB + j]["yT"] for j in range(CPB)], axis=1) for b in range(BATCH)]
    del r3, ins, aT

    lambda_init = 0.8 - 0.6 * float(np.exp(-0.3 * 1))
    fm = [(0, 1024, 0.125, AF.Copy), (1024, 2048, 1.0, AF.Copy)]
    tok = [(2048, 2560), (2560, 3072)]
    nc4 = build_k1(TPC, 3072, fm, tok)
    gl = g_layout(norm_mix_g[1])
    w = f32(diff_w_in[0])
    r4 = _run(nc4, [{"xT": s, "g": gl, "w": w} for s in tok_shards(xT)])
    pT = [np.concatenate([r4[b * CPB + j]["projT"] for j in range(CPB)], axis=1) for b in range(BATCH)]
    vt = [np.concatenate([r4[b * CPB + j]["vtok"] for j in range(CPB)], axis=0) for b in range(BATCH)]
    del r4
    sl8 = alibi_slopes(8)
    dmt, bigit = dm_table(), bigi_table()
    lam = np.stack([f32(diff_lam_q1[0]), f32(diff_lam_k1[0]), f32(diff_lam_q2[0]), f32(diff_lam_k2[0])], 0)
    lam = np.ascontiguousarray(np.broadcast_to(lam[None], (128, 4, 64)))
    sg = f32(diff_subln_g[0]).reshape(128, 1)
    ins = []
    for c in range(NCORES):
        b, hp = divmod(c, CPB)
        qa = np.ascontiguousarray(pT[b][hp * 256:(hp + 1) * 256].reshape(4, 64, S))
        ka = np.ascontiguousarray(pT[b][1024 + hp * 256:1024 + (hp + 1) * 256].reshape(4, 64, S))
        bt = np.stack([bt_table(sl8[hp * 2 + hh]) for hh in range(2)], 0)
        qaug = np.stack([q_aug_rows(sl8[hp * 2 + hh], 512) for hh in range(2)], 0)
        ins.append({"qa": qa, "ka": ka, "qaug": qaug,
                    "v": np.ascontiguousarray(vt[b][:, hp * 256:(hp + 1) * 256]),
                    "bt": bt, "dm": dmt, "bigi": bigit, "lam": lam, "sg": sg})
    nc5 = build_k2b(S, lambda_init)
    r5 = _run(nc5, ins)
    aT = [np.concatenate([r5[b * CPB + hp]["oT"] for hp in range(CPB)], axis=0) for b in range(BATCH)]
    del r5, ins, pT, vt
    cw, cb = conv_layouts(ffn_conv_w[1], ffn_conv_b[1])
    nc6 = build_k3(TPC, True)
    ins = [{"xT": _with_halo(xT, c), "aT": _with_halo(aT, c), "wo": f32(diff_w_out[0]),
            "wu": f32(ffn_w_up[1]), "wd": f32(ffn_w_down[1]), "cw": cw, "cb": cb,
            "g": g_layout(norm_ffn_g[1]), "gf": g_layout(final_norm_g)} for c in range(NCORES)]
    r6 = _run(nc6, ins)
    out = np.empty((BATCH, SEQ, D_MODEL), np.float32)
    for c in range(NCORES):
        b, j = divmod(c, CPB)
        out[b, j * TPC:(j + 1) * TPC, :] = r6[c]["yT"].T
    return out

TPC = BATCH * SEQ // NCORES
CPB = NCORES // BATCH
GROUPS = [[0, 1, 2, 3], [4, 5, 6, 7]]
RB1 = 640
RB4 = 512


def _chunks(t0, t1):
    j = t0 // TPC
    while j * TPC < t1:
        lo, hi = max(t0, j * TPC), min(t1, (j + 1) * TPC)
        yield j, lo, hi
        j += 1


class IOK2aFused:
    ROW = {"q0": 0, "q1": 64, "q2": 128, "q3": 192, "kc": 256, "vc": 320, "ks": 384, "kw": 448, "gt": 512}

    def __init__(self, L1F, L1T, SND2, st, k=None, G2=None):
        self.WF, self.WT, self.SND2, self.st = L1F, L1T, SND2, st
        self.k, self.G2, self.pend = k, G2, {}

    def fm(self, name, t0, t1):
        row0 = self.ROW[name]
        nr = 12 if name == "gt" else 64
        rr = lambda j: ((row0 // 128) * 4 + j) * 128 + row0 % 128
        return [(self.WF[rr(j):rr(j) + nr, lo - j * TPC:hi - j * TPC], lo, hi)
                for (j, lo, hi) in _chunks(t0, t1)]

    def tok(self, name, t0, t1):
        c0 = 0 if name == "vs" else 64
        return [(self.WT[lo:hi, c0:c0 + 64], lo, hi) for (j, lo, hi) in _chunks(t0, t1)]

    def out(self, p, I):
        j, i8 = divmod(I, 8)
        return self.SND2[j * 256 + p * 64:j * 256 + (p + 1) * 64, i8 * 512:(i8 + 1) * 512]

    def out_done(self, I, hp, toks):
        self.pend.setdefault(hp, []).extend(toks)
        if I % 8 == 7:
            i = (I // 8) * 2 + hp
            self.k.collective_async(self.SND2[i * 128:(i + 1) * 128, :], self.G2[i * 512:(i + 1) * 512, :],
                                    GROUPS, self.pend.pop(hp))


class IOK2bFused:
    def __init__(self, L4F, L4T, SND5, d, k=None, G5=None):
        self.WF, self.WT, self.SND5 = L4F, L4T, SND5
        self.ctx, self.G5, self.pend = k, G5, {}
        self.qaug, self.bt_in, self.dm_in, self.bigi_in, self.lam_in, self.sg_in = (
            d["b_qaug"], d["b_bt"], d["a_dm"], d["a_bigi"], d["b_lam"], d["b_sg"])

    def _fm(self, row0, t0, t1):
        rr = lambda j: ((row0 // 128) * 4 + j) * 128 + row0 % 128
        return [(self.WF[rr(j):rr(j) + 64, lo - j * TPC:hi - j * TPC], lo, hi)
                for (j, lo, hi) in _chunks(t0, t1)]

    def q(self, hh, c, t0, t1):
        return self._fm((hh * 2 + c) * 64, t0, t1)

    def k(self, hh, c, t0, t1):
        return self._fm(256 + (hh * 2 + c) * 64, t0, t1)

    def v(self, hh, t0, t1):
        out = []
        for (j, lo, hi) in _chunks(t0, t1):
            a = lo
            while a < hi:
                tl = a - j * TPC
                b = min(hi, j * TPC + (tl // 2048 + 1) * 2048)
                row = ((tl // 2048) * 4 + j) * 2048 + tl % 2048
                out.append((self.WT[row:row + (b - a), hh * 128:(hh + 1) * 128], a, b))
                a = b
        return out

    def out(self, hh, I):
        j, i8 = divmod(I, 8)
        return self.SND5[j * 256 + hh * 128:j * 256 + (hh + 1) * 128, i8 * 512:(i8 + 1) * 512]

    def out_done(self, I, hh, toks):
        self.pend.setdefault(hh, []).extend(toks)
        if I % 8 == 7:
            i = (I // 8) * 2 + hh
            self.ctx.collective_async(self.SND5[i * 128:(i + 1) * 128, :], self.G5[i * 512:(i + 1) * 512, :],
                                    GROUPS, self.pend.pop(hh))


class IOK3Fused:
    def __init__(self, layer, d, xh0, X2, LA, LHA, LH, SND3, yT, SNDH=None, k=None, GH=None):
        self.layer, self.xh0, self.X2, self.SND3, self.yT = layer, xh0, X2, SND3, yT
        self.WA = LA.rearrange("(h g p) t -> h p g t", h=2, g=4, p=128)
        self.WHA = LHA.rearrange("(h g p) t -> h p g t", h=2, g=4, p=128)
        self.WH = LH.rearrange("(dc p) t -> p dc t", p=128)
        sfx = str(layer)
        self.wo_in, self.wu_in, self.wd_in = d["wo" + sfx], d["wu" + sfx], d["wd" + sfx]
        self.cw_in, self.cb_in, self.g_in, self.gf_in = d["cw" + sfx], d["cb" + sfx], d["g_ffn" + sfx], d["g_fin"]
        self.halo_scale = d["m0"]
        self.tail_dst = (lambda oc: SND3[oc * 128:(oc + 1) * 128, 0:2]) if layer == 0 else None
        if layer == 0:
            self.gf_in = d["g_mix1"]
            self.h_dst = lambda oc, tcol, n: SNDH[(tcol // 512) * 1024 + oc * 128:(tcol // 512) * 1024 + (oc + 1) * 128,
                                                  tcol % 512:tcol % 512 + n]
            self.k, self.GH, self.SNDH, self.pend = k, GH, SNDH, []

    def tile_done(self, it, toks, N=256):
        if self.layer != 0:
            return
        self.pend.extend(toks)
        per = 512 // N
        if it % per == per - 1:
            c = it // per
            self.k.collective_async(self.SNDH[c * 1024:(c + 1) * 1024, :], self.GH[c * 4096:(c + 1) * 4096, :],
                                    GROUPS, self.pend)
            self.pend = []

    def x_src(self, col0, n):
        if self.layer == 0:
            return self.xh0.rearrange("(dc p) t -> p dc t", p=128)[:, :, col0:col0 + n]
        if col0 == 0:
            assert n == 2
            return self.WH
        return self.X2.rearrange("(dc p) t -> p dc t", p=128)[:, :, col0 - 2:col0 - 2 + n]

    def a_src(self, col0, n):
        if col0 == 0:
            assert n == 2
            return [(h, 2, self.WHA[h]) for h in range(2)]
        return [(h, 2, self.WA[h][:, :, col0 - 2:col0 - 2 + n]) for h in range(2)]

    def y_dst(self, oc, tcol, n):
        dst = self.X2 if self.layer == 0 else self.yT
        return dst[oc * 128:(oc + 1) * 128, tcol:tcol + n]

    def x1_dst(self, dc, tcol, n):
        return self.X1D[dc * 128:(dc + 1) * 128, tcol:tcol + n]

    def x1_src(self, tcol, n):
        return self.X1D.rearrange("(dc p) t -> p dc t", p=128)[:, :, tcol:tcol + n]

    def aff_dst(self, gch, tcol, n):
        return self.AFFD[gch * 128:(gch + 1) * 128, tcol:tcol + n]

    def aff_src(self, tcol, n):
        return self.AFFD.rearrange("(kc p) t -> p kc t", p=128)[:, :, tcol:tcol + n]


FUSED_INPUTS = [("xh0", [D_MODEL, 2 + TPC], F32), ("w_in0g", [D_MODEL, 652], F32), ("w_in1g", [D_MODEL, 768], F32),
                ("g_mix0", [128, 8], F32), ("g_mix1", [128, 8], F32), ("g_ffn0", [128, 8], F32),
                ("g_ffn1", [128, 8], F32), ("g_fin", [128, 8], F32), ("m0", [128, 1], F32),
                ("b_qaug", [2, 3, 512], BF16), ("b_bt", [2, 128, 128], F32), ("b_lam", [128, 4, 64], F32),
                ("b_sg", [128, 1], F32)]
for _l in range(2):
    FUSED_INPUTS += [("wo%d" % _l, [D_MODEL, D_MODEL], F32), ("wu%d" % _l, [D_MODEL, 2 * D_FF], F32),
                     ("wd%d" % _l, [D_FF, D_MODEL], F32), ("cw%d" % _l, [128, 44, 3], F32),
                     ("cb%d" % _l, [128, 44], F32)]
FUSED_INPUTS += [("a_" + n, sh, dt) for (n, sh, dt) in K2A_STATIC]


def build_fused(lambda_init, upto=None):
    nc = new_nc()
    S, T = SEQ, TPC
    d = {n: nc.dram_tensor(n, sh, dt, kind="ExternalInput").ap() for (n, sh, dt) in FUSED_INPUTS}
    yT = nc.dram_tensor("yT", [D_MODEL, T], F32, kind="ExternalOutput").ap()
    sc = lambda n, sh, dt: nc.dram_tensor(n, sh, dt).ap()
    SND1F = sc("SND1F", [4 * RB1, T], BF16); G1F = sc("G1F", [16 * RB1, T], BF16)
    SND1T = sc("SND1T", [4 * T, 128], BF16); G1T = sc("G1T", [16 * T, 128], BF16)
    SND2 = sc("SND2", [4 * 256, T], BF16); G2 = sc("G2", [16 * 256, T], BF16)
    X2 = sc("X2", [D_MODEL, T], F32)
    SND3 = sc("SND3", [D_MODEL, 2], F32); G3 = sc("G3", [4 * D_MODEL, 2], F32)
    SND4F = sc("SND4F", [4 * RB4, T], BF16); G4F = sc("G4F", [16 * RB4, T], BF16)
    SND4T = sc("SND4T", [4 * T, 256], BF16); G4T = sc("G4T", [16 * T, 256], BF16)
    SND5 = sc("SND5", [4 * 256, T], BF16); G5 = sc("G5", [16 * 256, T], BF16)
    L1F = sc("L1F", [4 * RB1, T], BF16); L1T = sc("L1T", [4 * T, 128], BF16)
    L4F = sc("L4F", [4 * RB4, T], BF16); L4T = sc("L4T", [4 * T, 256], BF16)
    LA2 = sc("LA2", [1024, T], BF16); LA5 = sc("LA5", [1024, T], BF16)
    LHA2 = sc("LHA2", [1024, 2], BF16); LHA5 = sc("LHA5", [1024, 2], BF16)
    LH = sc("LH", [D_MODEL, 2], F32)
    k = Ctx(nc)
    r = nc.sync.partition_id() % 4

    dbg = nc.dram_tensor("dbg", [2560, 4096], BF16, kind="ExternalOutput").ap() if upto else None

    def stop_here(tag, src_ap):
        if upto != tag:
            return False
        k.barrier()
        db = Buf("dbg")
        k.dma("sp", dbg[0:src_ap.shape[0], 0:src_ap.shape[1]], src_ap, w=[db], own=db)
        k.stores.append(db.w)
        k.finish()
        return True

    def gather_rows(SND, G, rows):
        n = SND.shape[0] // rows
        k.all_gather_chunks([(SND[i * rows:(i + 1) * rows, :], G[i * 4 * rows:(i + 1) * 4 * rows, :])
                             for i in range(n)], GROUPS)

    def extract_window(G, L):
        n = L.shape[0] * L.shape[1]
        a = n // 16384
        assert a * 16384 == n and G.shape[0] * G.shape[1] == 4 * n
        gf = G.rearrange("r t -> (r t)").rearrange("(q a l) -> q a l", q=4, a=a, l=16384)
        lf = L.rearrange("r t -> (r t)").rearrange("(a l) -> a l", a=a, l=16384)
        bX = Buf("extract")
        k.dma("sp", lf, gf[bass.ds(r, 1)].rearrange("o a l -> (o a) l"), w=[bX], own=bX)

    def extract_att(G, L, LHA_):
        extract_window(G, L)
        gq = G.rearrange("(q x) t -> q x t", q=4)
        bX2 = Buf("extract")
        with nc.allow_non_contiguous_dma(reason="2-column conv halo"):
            k.dma("sp", LHA_, gq[bass.ds((r + 3) % 4, 1), :, T - 2:T].rearrange("o x t -> (o x) t"), w=[bX2],
                  own=bX2)

    SNDH = sc("SNDH", [8 * D_MODEL, 512], BF16)
    GH = sc("GH", [32 * D_MODEL, 512], BF16)
    GHv = GH.rearrange("(it j dc p) t -> p it j dc t", it=8, j=4, dc=8, p=128)

    def ht_src(t0):
        j, tl = divmod(t0, T)
        return GHv[:, tl // 512, j, :, :]

    k.begin_phase("A_")
    emit_norm(k, nc, T, d["xh0"].rearrange("(dc p) t -> p dc t", p=128)[:, :, 2:2 + T], d["g_mix0"],
              lambda dc, t0: SNDH[(t0 // 512) * 1024 + dc * 128:(t0 // 512) * 1024 + (dc + 1) * 128, 0:512],
              lambda it, toks: k.collective_async(SNDH[it * 1024:(it + 1) * 1024, :],
                                                  GH[it * 4096:(it + 1) * 4096, :], GROUPS, toks))
    k.end_phase()

    k.begin_phase("P_")

    def fm_route_a(c0, c1, t0):
        j, tl = divmod(t0, T)
        row = ((c0 // 128) * 4 + j) * 128 + c0 % 128
        return [(0, c1 - c0, L1F[row:row + c1 - c0, tl:tl + 512])]

    def tok_route_a(c0, c1, t0, tb):
        return [(0, 128, L1T[t0 + tb * 128:t0 + (tb + 1) * 128, 0:128])]

    emit_proj(k, nc, S, 652, ht_src, d["w_in0g"],
              [(0, 256, 0.125, AF.Copy), (256, 512, 1.0, AF.Copy), (512, 524, 1.0, AF.Sigmoid)], fm_route_a,
              [(524, 652)], tok_route_a)
    k.end_phase()

    WB = [(sc("WOB%d" % l, [D_MODEL, D_MODEL], BF16), sc("WUB%d" % l, [D_MODEL, 2 * D_FF], BF16),
           sc("WDB%d" % l, [D_FF, D_MODEL], BF16)) for l in range(2)]

    def precast_weights():
        for l in range(2):
            for (src, dst) in ((d["wo%d" % l], WB[l][0]), (d["wu%d" % l], WB[l][1]), (d["wd%d" % l], WB[l][2])):
                R_, C_ = src.shape
                bw = Buf("wcast")
                for r0 in range(0, R_, 128):
                    for c0 in range(0, C_, 1024):
                        c1 = min(C_, c0 + 1024)
                        k.dma("pool", dst[r0:r0 + 128, c0:c1], src[r0:r0 + 128, c0:c1], w=[bw], own=bw)

    k.begin_phase("B_")
    io_b = IOK2aFused(L1F, L1T, SND2, {n: d["a_" + n] for (n, _, _) in K2A_STATIC}, k, G2)
    io_b.after_prologue = precast_weights
    emit_k2a(k, nc, S, io_b)
    k.end_phase()
    extract_att(G2, LA2, LHA2)
    k.barrier()
    if stop_here("B2", LA2) or stop_here("B2H", LHA2):
        return nc

    k.begin_phase("C_")
    X1D = sc("X1D", [D_MODEL, T], F32)
    AFFD = sc("AFFD", [D_FF, T], BF16)
    io_c = IOK3Fused(0, d, d["xh0"], X2, LA2, LHA2, LH, SND3, yT, SNDH, k, GH)
    io_c.X1D, io_c.AFFD = X1D, AFFD
    io_c.wb = WB[0]
    emit_k3(k, nc, T, False, io_c, 512, "a")
    k.end_phase()
    k.begin_phase("Cb_")
    emit_k3(k, nc, T, False, io_c, 512, "b")
    k.end_phase()
    if upto == "C":
        k.barrier()
        k.finish()
        return nc
    k.all_gather_chunks([(SND3, G3)], GROUPS)
    bXh = Buf("extract")
    k.dma("sp", LH, G3[bass.ds(((r + 3) % 4) * D_MODEL, D_MODEL), :], w=[bXh], own=bXh)

    k.begin_phase("D_")

    def fm_route_d(c0, c1, t0):
        j, tl = divmod(t0, T)
        row = ((c0 // 128) * 4 + j) * 128
        return [(0, 128, L4F[row:row + 128, tl:tl + 512])]

    def tok_route_d(c0, c1, t0, tb):
        j, tl = divmod(t0 + tb * 128, T)
        row = ((tl // 2048) * 4 + j) * 2048 + tl % 2048
        return [(0, 256, L4T[row:row + 128, 0:256])]

    emit_proj(k, nc, S, 768, ht_src, d["w_in1g"],
              [(0, 256, 0.125, AF.Copy), (256, 512, 1.0, AF.Copy)], fm_route_d, [(512, 768)], tok_route_d)
    k.end_phase()

    k.begin_phase("E_")
    emit_k2b(k, nc, S, lambda_init, IOK2bFused(L4F, L4T, SND5, d, k, G5))
    k.end_phase()
    if upto == "E":
        k.barrier()
        k.finish()
        return nc
    extract_att(G5, LA5, LHA5)
    k.barrier()

    k.begin_phase("F_")
    io_f = IOK3Fused(1, d, d["xh0"], X2, LA5, LHA5, LH, SND3, yT)
    io_f.X1D, io_f.AFFD = X1D, AFFD
    io_f.wb = WB[1]
    emit_k3(k, nc, T, True, io_f, 512, "a")
    k.end_phase()
    k.begin_phase("Fb_")
    emit_k3(k, nc, T, True, io_f, 512, "b")
    k.barrier()
    k.finish()
    print("[fused] semaphores used:", k.nsem)
    return nc


def fused_inputs(x, norm_mix_g, norm_ffn_g, final_norm_g,
                 nsa_w_in, nsa_cmp_k_pe, nsa_cmp_k_w1, nsa_cmp_k_w2,
                 nsa_cmp_v_pe, nsa_cmp_v_w1, nsa_cmp_v_w2, nsa_w_out,
                 diff_w_in, diff_lam_q1, diff_lam_k1, diff_lam_q2, diff_lam_k2,
                 diff_subln_g, diff_w_out,
                 ffn_w_up, ffn_conv_w, ffn_conv_b, ffn_w_down):
    f32 = lambda a: np.ascontiguousarray(np.asarray(a, dtype=np.float32))
    x = f32(x)
    xT = [np.ascontiguousarray(x[b].T) for b in range(BATCH)]
    consts, per_g = k2a_consts(SEQ)
    wts = k2a_weights(nsa_cmp_k_pe[0], nsa_cmp_k_w1[0], nsa_cmp_k_w2[0],
                      nsa_cmp_v_pe[0], nsa_cmp_v_w1[0], nsa_cmp_v_w2[0])
    sl8 = alibi_slopes(8)
    lam = np.stack([f32(diff_lam_q1[0]), f32(diff_lam_k1[0]), f32(diff_lam_q2[0]), f32(diff_lam_k2[0])], 0)
    lam = np.ascontiguousarray(np.broadcast_to(lam[None], (128, 4, 64)))
    common = {"g_mix0": g_layout(norm_mix_g[0]), "g_mix1": g_layout(norm_mix_g[1]),
              "g_ffn0": g_layout(norm_ffn_g[0]), "g_ffn1": g_layout(norm_ffn_g[1]),
              "g_fin": g_layout(final_norm_g), "b_lam": lam, "b_sg": f32(diff_subln_g[0]).reshape(128, 1),
              "wo0": f32(nsa_w_out[0]), "wo1": f32(diff_w_out[0])}
    for l in range(2):
        cw, cb = conv_layouts(ffn_conv_w[l], ffn_conv_b[l])
        common.update({"wu%d" % l: f32(ffn_w_up[l]), "wd%d" % l: f32(ffn_w_down[l]), "cw%d" % l: cw,
                       "cb%d" % l: cb})
    for n_, v_ in list(consts.items()) + list(wts.items()):
        common["a_" + n_] = v_
    in_maps = []
    for c in range(NCORES):
        r = c % CPB
        m = dict(common)
        for n_, v_ in per_g[r].items():
            m["a_" + n_] = v_
        m["xh0"] = _with_halo(xT, c)
        w0, w1 = f32(nsa_w_in[0]), f32(diff_w_in[0])
        cols0 = (list(range(r * 256, (r + 1) * 256)) + [o_ + r * 64 + i for o_ in (1024, 1280, 1536, 2048)
                                                         for i in range(64)]
                 + list(range(2560 + r * 12, 2560 + (r + 1) * 12))
                 + [o_ + r * 64 + i for o_ in (1792, 2304) for i in range(64)])
        m["w_in0g"] = np.ascontiguousarray(w0[:, cols0])
        cols1 = [o_ + r * 256 + i for o_ in (0, 1024, 2048) for i in range(256)]
        m["w_in1g"] = np.ascontiguousarray(w1[:, cols1])
        m["m0"] = np.full((128, 1), 1.0 if r > 0 else 0.0, np.float32)
        m["b_qaug"] = np.stack([q_aug_rows(sl8[r * 2 + hh], 512) for hh in range(2)], 0)
        m["b_bt"] = np.stack([bt_table(sl8[r * 2 + hh]) for hh in range(2)], 0)
        in_maps.append(m)
    return in_maps


def kernel(**inputs):
    lambda_init = 0.8 - 0.6 * float(np.exp(-0.3 * 1))
    nc = build_fused(lambda_init)
    in_maps = fused_inputs(**inputs)
    res = run_bass_kernel_spmd(nc, in_maps, core_ids=list(range(NCORES))).results
    out = np.empty((BATCH, SEQ, D_MODEL), np.float32)
    for c in range(NCORES):
        b, j = divmod(c, CPB)
        out[b, j * TPC:(j + 1) * TPC, :] = res[c]["yT"].T
    return out
```

```python
import numpy as np
import ml_dtypes
import concourse.bass as bass
import concourse.mybir as mybir
from concourse.bass_utils import run_bass_kernel_spmd

F32 = mybir.dt.float32
BF16 = mybir.dt.bfloat16
AF = mybir.ActivationFunctionType
ALU = mybir.AluOpType
AX = mybir.AxisListType
NPBF = ml_dtypes.bfloat16

NCORES = 8
D_MODEL = 1024
BATCH = 2
SEQ = 16384
EPS = 1e-6


class Buf:
    __slots__ = ("name", "w", "r", "sem", "cnt")

    def __init__(self, name):
        self.name = name
        self.w = None
        self.r = {}
        self.sem = None
        self.cnt = 0


class Ctx:
    SEM_ROLL = 30000

    def __init__(self, nc, same_engine_sync=True):
        self.nc = nc
        self.same = same_engine_sync
        self.nsem = 0
        self.E = {}
        for n, e in [("pe", nc.tensor), ("act", nc.scalar), ("dve", nc.vector),
                     ("pool", nc.gpsimd), ("sp", nc.sync)]:
            self.E[n] = {"eng": e, "sem": self._newsem("e_" + n), "cnt": 0, "waited": {}}
        self.stores = []
        self.es = None
        self.pfx = ""
        self.phase_bufs = []
        self.sempool = []
        self.ccbuf = None
        self.dummy = self.nc.alloc_sbuf_tensor("bar_dummy", [128, 8], F32)
        self.bdummy = Buf("bar_dummy")

    def _newsem(self, name):
        self.nsem += 1
        s = self.nc.alloc_semaphore("%s_%d" % (name, self.nsem))
        return (s, self.nsem)

    def buf(self, name):
        return Buf(name)

    def begin_phase(self, pfx):
        from contextlib import ExitStack
        self.es = ExitStack()
        self.pfx = pfx

    def sb(self, name, shape, dt):
        return self.es.enter_context(self.nc.sbuf_tensor(self.pfx + name, shape, dt))

    def ps(self, name, shape, dt=F32):
        return self.es.enter_context(self.nc.psum_tensor(self.pfx + name, shape, dt))

    def _own_sem(self, own):
        if own.sem is None or own.cnt >= self.SEM_ROLL:
            if self.sempool and own.sem is None:
                sem, cnt = self.sempool.pop()
                own.sem = sem
                own.cnt = cnt
            else:
                own.sem = self._newsem("d")
                own.cnt = 0
            self.phase_bufs.append(own)

    def barrier(self):
        toks = []
        for n, E in self.E.items():
            if E["cnt"] > 0:
                toks.append((E["sem"], E["cnt"], n))
        for b in self.phase_bufs:
            toks.append((b.sem, b.cnt, "dma"))
        if self.ccbuf is not None and self.ccbuf.cnt > 0:
            toks.append((self.ccbuf.sem, self.ccbuf.cnt, "dma"))
        self._wait("pool", toks)
        self.op("pool", lambda e: e.memset(self.dummy[:], 0.0), w=[self.bdummy])
        for n in self.E:
            if n != "pool":
                self._wait(n, [self.bdummy.w])

    def end_phase(self):
        self.barrier()
        self.es.close()
        self.es = None
        seen = set()
        for b in self.phase_bufs:
            if b.sem[1] not in seen and b.cnt < self.SEM_ROLL // 2:
                seen.add(b.sem[1])
                self.sempool.append((b.sem, b.cnt))
        self.phase_bufs = []

    def collective_async(self, src_ap, dst_ap, groups, deps):
        if self.ccbuf is None:
            self.ccbuf = Buf("cc_async")
            self.ccbuf.sem = self._newsem("cca")
        self._wait("pool", deps)
        inst = self.nc.gpsimd.collective_compute("AllGather", ALU.bypass, replica_groups=groups,
                                                 ins=[src_ap.opt()], outs=[dst_ap.opt()])
        inst.then_inc(self.ccbuf.sem[0], 1)
        self.ccbuf.cnt += 1

    def all_gather_chunks(self, pairs, groups):
        self.barrier()
        cb = Buf("cc")
        cb.sem = self._newsem("cc")
        for (src_ap, dst_ap) in pairs:
            inst = self.nc.gpsimd.collective_compute("AllGather", ALU.bypass, replica_groups=groups,
                                                     ins=[src_ap.opt()], outs=[dst_ap.opt()])
            inst.then_inc(cb.sem[0], 1)
            cb.cnt += 1
        self.phase_bufs.append(cb)
        self.barrier()
        self.phase_bufs.remove(cb)

    def all_gather(self, src_ap, dst_ap, groups):
        self.barrier()
        cb = Buf("cc")
        cb.sem = self._newsem("cc")
        inst = self.nc.gpsimd.collective_compute("AllGather", ALU.bypass, replica_groups=groups,
                                                 ins=[src_ap.opt()], outs=[dst_ap.opt()])
        inst.then_inc(cb.sem[0], 1)
        cb.cnt = 1
        self.phase_bufs.append(cb)
        self.barrier()
        self.phase_bufs.remove(cb)

    def bufs(self, name, n):
        return [Buf("%s%d" % (name, i)) for i in range(n)]

    def _wait(self, en, toks):
        E = self.E[en]
        need = {}
        for t in toks:
            if t is None:
                continue
            (sem, key), val, src = t
            if src == en and (en == "pe" or not self.same):
                continue
            if need.get(key, (None, 0))[1] < val:
                need[key] = (sem, val)
        for key, (sem, val) in need.items():
            if E["waited"].get(key, 0) < val:
                E["eng"].wait_ge(sem, val)
                E["waited"][key] = val

    def _collect(self, r, w):
        toks = []
        for b in r:
            toks.append(b.w)
        for b in w:
            toks.append(b.w)
            toks.extend(b.r.values())
        return toks

    def op(self, en, fn, r=(), w=()):
        E = self.E[en]
        if E["cnt"] >= self.SEM_ROLL:
            E["sem"] = self._newsem("e_" + en)
            E["cnt"] = 0
        self._wait(en, self._collect(r, w))
        inst = fn(E["eng"])
        E["cnt"] += 1
        inst.then_inc(E["sem"][0], 1)
        tok = (E["sem"], E["cnt"], en)
        for b in r:
            b.r[en] = tok
        for b in w:
            b.w = tok
            b.r = {}
        return inst

    def dma(self, qn, out, in_, r=(), w=(), own=None, **kw):
        E = self.E[qn]
        if own is None:
            own = w[0] if len(w) else r[0]
        self._own_sem(own)
        self._wait(qn, self._collect(r, w))
        inst = E["eng"].dma_start(out=out, in_=in_, **kw)
        own.cnt += 16
        inst.then_inc(own.sem[0], 16)
        tok = (own.sem, own.cnt, "dma")
        for b in r:
            b.r["dma_%d" % own.sem[1]] = tok
        for b in w:
            b.w = tok
            b.r = {}
        return tok

    def store(self, qn, out, in_, r, **kw):
        tok = self.dma(qn, out, in_, r=r, w=(), **kw)
        self.stores.append(tok)

    def finish(self):
        self._wait("sp", self.stores)
        self.stores = []
        if self.es is not None:
            self.es.close()
            self.es = None


def new_nc():
    return bass.Bass("TRN2", target_bir_lowering=False)


def emit_k1(k, nc, T, C, xv, g_in, w_in, fm_specs, fm_route, tok_cols, tok_route):
    NT = T // 512
    W = k.sb("W", [128, 8, C], BF16)
    G = k.sb("G", [128, 8], F32)
    ONES = k.sb("ONES", [128, 128], F32)
    X = [k.sb("X%d" % i, [128, 8, 512], F32) for i in range(2)]
    SQ = k.sb("SQ", [128, 8, 512], F32)
    RS = k.sb("RS", [128, 512], F32)
    HT = k.sb("HT", [128, 8, 512], BF16)
    NOB = 4
    OB = [k.sb("OB%d" % i, [128, 512], BF16) for i in range(NOB)]
    PS = [k.ps("PS%d" % i, [128, 512], F32) for i in range(6)]
    PSS = k.ps("PSS", [128, 512], F32)

    bW = k.bufs("W", 8)
    bG = k.buf("G")
    bONES = k.buf("ONES")
    bX = k.bufs("X", 2)
    bSQ = k.buf("SQ")
    bRS = k.buf("RS")
    bHT = k.buf("HT")
    bOB = k.bufs("OB", NOB)
    bPS = k.bufs("PS", 6)
    bPSS = k.buf("PSS")

    wv = w_in.rearrange("(dc p) c -> p dc c", p=128)
    for dc in range(8):
        for c0 in range(0, C, 1024):
            c1 = min(C, c0 + 1024)
            k.dma("pool", W[:, dc, c0:c1], wv[:, dc, c0:c1], w=[bW[dc]])
    k.dma("sp", G[:], g_in, w=[bG])
    k.op("dve", lambda e: e.memset(ONES[:], 1.0), w=[bONES])

    pi = 0
    oi = 0
    for it in range(NT):
        t0 = it * 512
        xb = it % 2
        k.dma("sp", X[xb][:], xv[:, :, t0:t0 + 512], w=[bX[xb]])
        k.op("act", lambda e: e.activation(out=SQ[:], in_=X[xb][:], func=AF.Square),
             r=[bX[xb]], w=[bSQ])
        for dc in range(8):
            k.op("pe", lambda e: e.matmul(PSS[:], lhsT=ONES[:], rhs=SQ[:, dc, :],
                                           start=(dc == 0), stop=(dc == 7)),
                 r=[bONES, bSQ], w=[bPSS])
        k.op("act", lambda e: e.activation(out=RS[:], in_=PSS[:], func=AF.Sqrt,
                                            scale=1.0 / D_MODEL, bias=EPS),
             r=[bPSS], w=[bRS])
        k.op("dve", lambda e: e.reciprocal(out=RS[:], in_=RS[:]), r=[bRS], w=[bRS])
        for dc in range(8):
            k.op("dve", lambda e: e.scalar_tensor_tensor(
                out=HT[:, dc, :], in0=X[xb][:, dc, :], scalar=G[:, dc:dc + 1], in1=RS[:],
                op0=ALU.mult, op1=ALU.mult), r=[bX[xb], bG, bRS], w=[bHT])
        for (s0, s1, scale, func) in fm_specs:
            for c0 in range(s0, s1, 128):
                c1 = min(s1, c0 + 128)
                m = c1 - c0
                p = pi % 6
                pi += 1
                for dc in range(8):
                    k.op("pe", lambda e: e.matmul(PS[p][:m, :], lhsT=W[:, dc, c0:c1],
                                                   rhs=HT[:, dc, :], start=(dc == 0),
                                                   stop=(dc == 7)),
                         r=[bW[dc], bHT], w=[bPS[p]])
                o = oi % NOB
                oi += 1
                k.op("act", lambda e: e.activation(out=OB[o][:m, :], in_=PS[p][:m, :],
                                                    func=func, scale=scale),
                     r=[bPS[p]], w=[bOB[o]])
                for (ro, nr, dst) in fm_route(c0, c1, t0):
                    k.store("sp", dst, OB[o][ro:ro + nr, :], r=[bOB[o]])
        for (c0, c1) in tok_cols:
            m = c1 - c0
            for tb in range(4):
                p = pi % 6
                pi += 1
                for dc in range(8):
                    k.op("pe", lambda e: e.matmul(PS[p][:, :m],
                                                   lhsT=HT[:, dc, tb * 128:(tb + 1) * 128],
                                                   rhs=W[:, dc, c0:c1], start=(dc == 0),
                                                   stop=(dc == 7)),
                         r=[bW[dc], bHT], w=[bPS[p]])
                o = oi % NOB
                oi += 1
                k.op("dve", lambda e: e.tensor_copy(out=OB[o][:, :m], in_=PS[p][:, :m]),
                     r=[bPS[p]], w=[bOB[o]])
                for (co, ncl, dst) in tok_route(c0, c1, t0, tb):
                    k.store("sp", dst, OB[o][:, co:co + ncl], r=[bOB[o]])


def emit_norm(k, nc, T, xv, g_in, h_dst, tile_done=None):
    NT = T // 512
    G = k.sb("G", [128, 8], F32)
    ONES = k.sb("ONES", [128, 128], F32)
    X = [k.sb("X%d" % i, [128, 8, 512], F32) for i in range(2)]
    SQ = k.sb("SQ", [128, 8, 512], F32)
    RS = k.sb("RS", [128, 512], F32)
    HT = [k.sb("HT%d" % i, [128, 8, 512], BF16) for i in range(2)]
    PSS = k.ps("PSS", [128, 512], F32)
    bG = k.buf("G"); bONES = k.buf("ONES"); bX = k.bufs("X", 2); bSQ = k.buf("SQ"); bRS = k.buf("RS")
    bHT = k.bufs("HT", 2); bPSS = k.buf("PSS")
    k.dma("sp", G[:], g_in, w=[bG])
    k.op("dve", lambda e: e.memset(ONES[:], 1.0), w=[bONES])
    for it in range(NT):
        t0 = it * 512
        xb = it % 2
        k.dma("sp", X[xb][:], xv[:, :, t0:t0 + 512], w=[bX[xb]])
        k.op("act", lambda e: e.activation(out=SQ[:], in_=X[xb][:], func=AF.Square), r=[bX[xb]], w=[bSQ])
        for dc in range(8):
            k.op("pe", lambda e: e.matmul(PSS[:], lhsT=ONES[:], rhs=SQ[:, dc, :], start=(dc == 0), stop=(dc == 7)),
                 r=[bONES, bSQ], w=[bPSS])
        k.op("act", lambda e: e.activation(out=RS[:], in_=PSS[:], func=AF.Sqrt, scale=1.0 / D_MODEL, bias=EPS),
             r=[bPSS], w=[bRS])
        k.op("dve", lambda e: e.reciprocal(out=RS[:], in_=RS[:]), r=[bRS], w=[bRS])
        for dc in range(8):
            k.op("dve", lambda e: e.scalar_tensor_tensor(
                out=HT[xb][:, dc, :], in0=X[xb][:, dc, :], scalar=G[:, dc:dc + 1], in1=RS[:],
                op0=ALU.mult, op1=ALU.mult), r=[bX[xb], bG, bRS], w=[bHT[xb]])
        n0 = len(k.stores)
        for dc in range(8):
            k.store("pool", h_dst(dc, t0), HT[xb][:, dc, :], r=[bHT[xb]])
        if tile_done is not None:
            tile_done(it, k.stores[n0:])


def emit_proj(k, nc, T, C, ht_src, w_in, fm_specs, fm_route, tok_cols, tok_route):
    NT = T // 512
    W = k.sb("W", [128, 8, C], BF16)
    HT = [k.sb("HT%d" % i, [128, 8, 512], BF16) for i in range(2)]
    NOB = 4
    OB = [k.sb("OB%d" % i, [128, 512], BF16) for i in range(NOB)]
    PS = [k.ps("PS%d" % i, [128, 512], F32) for i in range(6)]
    bW = k.bufs("W", 8); bHT = k.bufs("HT", 2); bOB = k.bufs("OB", NOB); bPS = k.bufs("PS", 6)
    wv = w_in.rearrange("(dc p) c -> p dc c", p=128)
    for dc in range(8):
        for c0 in range(0, C, 1024):
            c1 = min(C, c0 + 1024)
            k.dma("pool", W[:, dc, c0:c1], wv[:, dc, c0:c1], w=[bW[dc]])
    pi = 0
    oi = 0
    for it in range(NT):
        t0 = it * 512
        hb = it % 2
        k.dma("sp", HT[hb][:], ht_src(t0), w=[bHT[hb]])
        for (s0, s1, scale, func) in fm_specs:
            for c0 in range(s0, s1, 128):
                c1 = min(s1, c0 + 128)
                m = c1 - c0
                p = pi % 6
                pi += 1
                for dc in range(8):
                    k.op("pe", lambda e: e.matmul(PS[p][:m, :], lhsT=W[:, dc, c0:c1], rhs=HT[hb][:, dc, :],
                                                   start=(dc == 0), stop=(dc == 7)),
                         r=[bW[dc], bHT[hb]], w=[bPS[p]])
                o = oi % NOB
                oi += 1
                k.op("act", lambda e: e.activation(out=OB[o][:m, :], in_=PS[p][:m, :], func=func, scale=scale),
                     r=[bPS[p]], w=[bOB[o]])
                for (ro, nr, dst) in fm_route(c0, c1, t0):
                    k.store("pool", dst, OB[o][ro:ro + nr, :], r=[bOB[o]])
        for (c0, c1) in tok_cols:
            m = c1 - c0
            for tb in range(4):
                p = pi % 6
                pi += 1
                for dc in range(8):
                    k.op("pe", lambda e: e.matmul(PS[p][:, :m], lhsT=HT[hb][:, dc, tb * 128:(tb + 1) * 128],
                                                   rhs=W[:, dc, c0:c1], start=(dc == 0), stop=(dc == 7)),
                         r=[bW[dc], bHT[hb]], w=[bPS[p]])
                o = oi % NOB
                oi += 1
                k.op("dve", lambda e: e.tensor_copy(out=OB[o][:, :m], in_=PS[p][:, :m]), r=[bPS[p]], w=[bOB[o]])
                for (co, ncl, dst) in tok_route(c0, c1, t0, tb):
                    k.store("pool", dst, OB[o][:, co:co + ncl], r=[bOB[o]])


def build_k1(T, C, fm_specs, tok_cols):
    nc = new_nc()
    CV = sum(c1 - c0 for c0, c1 in tok_cols)
    xT = nc.dram_tensor("xT", [D_MODEL, T], F32, kind="ExternalInput").ap()
    g_in = nc.dram_tensor("g", [128, 8], F32, kind="ExternalInput").ap()
    w_in = nc.dram_tensor("w", [D_MODEL, C], F32, kind="ExternalInput").ap()
    projT = nc.dram_tensor("projT", [C, T], BF16, kind="ExternalOutput").ap()
    vtok = nc.dram_tensor("vtok", [T, max(CV, 1)], BF16, kind="ExternalOutput").ap()
    voff = {}
    vo = 0
    for (c0, c1) in tok_cols:
        voff[c0] = vo
        vo += c1 - c0
    k = Ctx(nc)
    k.begin_phase("")
    emit_k1(k, nc, T, C, xT.rearrange("(dc p) t -> p dc t", p=128), g_in, w_in, fm_specs,
            lambda c0, c1, t0: [(0, c1 - c0, projT[c0:c1, t0:t0 + 512])],
            tok_cols,
            lambda c0, c1, t0, tb: [(0, c1 - c0, vtok[t0 + tb * 128:t0 + (tb + 1) * 128,
                                                     voff[c0]:voff[c0] + c1 - c0])])
    k.finish()
    return nc


def g_layout(g):
    return np.ascontiguousarray(np.asarray(g, np.float32).reshape(8, 128).T)


def run_k1(xT_shards, g, w, fm_specs, tok_cols):
    T = xT_shards[0].shape[1]
    C = w.shape[1]
    nc = build_k1(T, C, fm_specs, tok_cols)
    gl = g_layout(g)
    w = np.ascontiguousarray(w, dtype=np.float32)
    in_maps = [{"xT": np.ascontiguousarray(s), "g": gl, "w": w} for s in xT_shards]
    res = run_bass_kernel_spmd(nc, in_maps, core_ids=list(range(NCORES)))
    return res.results


BIG = 29952.0


def alibi_slopes(n):
    return np.exp2(-8.0 * np.arange(1, n + 1, dtype=np.float64) / n)


def split3(v):
    v = np.asarray(v, np.float64)
    a = v.astype(NPBF)
    r = v - a.astype(np.float64)
    b = r.astype(NPBF)
    r = r - b.astype(np.float64)
    c = r.astype(NPBF)
    return np.stack([a, b, c], 0)


def q_aug_rows(m, S):
    tl = np.arange(S) % 512
    return split3(-m * tl)


def bt_table(m):
    sl = np.arange(128)[:, None]
    delta = np.arange(-3, 125)[None, :]
    return (m * sl - m * 128.0 * delta).astype(np.float32)


def btc_table(m):
    jl = np.arange(128)[:, None]
    dd = np.arange(-28, 32)[None, :]
    return (16.0 * m * jl - m * (512.0 * dd - 31.0)).astype(np.float32)


def dm_table():
    sl = np.arange(128)[:, None, None]
    dd = np.arange(4)[None, :, None]
    tl = np.arange(512)[None, None, :]
    return np.where(128 * dd + sl > tl, -1.0, 0.0).astype(NPBF)


def wm_table():
    sl = np.arange(128)[:, None, None]
    dd = np.arange(4)[None, :, None]
    tl = np.arange(512)[None, None, :]
    return np.where(tl - 128 * dd - sl >= 0, -1.0, 0.0).astype(NPBF)


def cm_table():
    jl = np.arange(128)[:, None, None]
    dd = np.arange(5)[None, :, None]
    tl = np.arange(512)[None, None, :]
    return np.where(16 * jl + 31 > 512 * dd + tl, -1.0, 0.0).astype(NPBF)


def bigi_table():
    return (np.eye(128) * BIG).astype(NPBF)


def bcast128(v):
    v = np.asarray(v, np.float32).reshape(1, -1)
    return np.ascontiguousarray(np.broadcast_to(v, (128, v.shape[1])))


def emit_sqmax(k, nc, fetch, ntiles, SQT, bSQT, ONESB, bONESB, psM, bpsM, MX, bMX, OUT, bOUT):
    nxt = fetch(0)
    for i in range(ntiles):
        rb, src = nxt
        if i + 1 < ntiles:
            nxt = fetch(i + 1)
        a = i % 2
        w_ = src.shape[-1]
        k.op("dve", lambda e: e.tensor_tensor(out=SQT[a][0:64, :w_], in0=src, in1=src, op=ALU.mult),
             r=rb, w=[bSQT[a]])
        k.op("pe", lambda e: e.matmul(psM[a][:, :w_], lhsT=ONESB[0:64, :], rhs=SQT[a][0:64, :w_],
                                       start=True, stop=True), r=[bSQT[a], bONESB], w=[bpsM[a]])
        k.op("dve", lambda e: e.reduce_max(out=MX[:, i:i + 1], in_=psM[a][:, :w_], axis=AX.X),
             r=[bpsM[a]], w=[bMX])
    k.op("dve", lambda e: e.reduce_max(out=OUT, in_=MX[:, 0:ntiles], axis=AX.X),
         r=[bMX], w=[bOUT])


class IOK2bStandalone:
    def __init__(self, nc, S, NH):
        d = lambda n, sh, dt: nc.dram_tensor(n, sh, dt, kind="ExternalInput").ap()
        self.qa = d("qa", [NH * 2, 64, S], BF16)
        self.ka = d("ka", [NH * 2, 64, S], BF16)
        self.v_in = d("v", [S, NH * 128], BF16)
        self.qaug = d("qaug", [NH, 3, 512], BF16)
        self.bt_in = d("bt", [NH, 128, 128], F32)
        self.dm_in = d("dm", [128, 4, 512], BF16)
        self.bigi_in = d("bigi", [128, 128], BF16)
        self.lam_in = d("lam", [128, 4, 64], F32)
        self.sg_in = d("sg", [128, 1], F32)
        self.oT = nc.dram_tensor("oT", [NH * 128, S], BF16, kind="ExternalOutput").ap()

    def q(self, hh, c, t0, t1):
        return [(self.qa[hh * 2 + c, :, t0:t1], t0, t1)]

    def k(self, hh, c, t0, t1):
        return [(self.ka[hh * 2 + c, :, t0:t1], t0, t1)]

    def v(self, hh, t0, t1):
        return [(self.v_in[t0:t1, hh * 128:(hh + 1) * 128], t0, t1)]

    def out(self, hh, I):
        return self.oT[hh * 128:(hh + 1) * 128, I * 512:(I + 1) * 512]


def build_k2b(S, lambda_init, NH=2):
    nc = new_nc()
    io = IOK2bStandalone(nc, S, NH)
    k = Ctx(nc)
    k.begin_phase("")
    emit_k2b(k, nc, S, lambda_init, io, NH)
    k.finish()
    return nc


def emit_k2b(k, nc, S, lambda_init, io, NH=2):
    NQ = S // 512
    NB = S // 128
    bt_in, dm_in, bigi_in, lam_in, sg_in = io.bt_in, io.dm_in, io.bigi_in, io.lam_in, io.sg_in

    KA = [k.sb("KA%d" % c, [67, S], BF16) for c in range(2)]
    V = k.sb("V", [128, NB, 128], BF16)
    QT = [k.sb("QT%d" % i, [67, 512], BF16) for i in range(4)]
    BT = k.sb("BT", [128, 128], F32)
    BTC = [k.sb("BTC%d" % c, [128, 128], F32) for c in range(2)]
    DM = k.sb("DM", [128, 4, 512], BF16)
    BIGI = k.sb("BIGI", [128, 128], BF16)
    ONESB = k.sb("ONESB", [128, 128], BF16)
    ONESF = k.sb("ONESF", [128, 128], F32)
    LAM = k.sb("LAM", [128, 4, 64], F32)
    LT = k.sb("LT", [128, 2, 64], F32)
    LS = k.sb("LS", [128, 4], F32)
    SG = k.sb("SG", [128, 1], F32)
    SQT = [k.sb("SQT%d" % i, [128, 512], BF16) for i in range(2)]
    MX = k.sb("MX", [128, 64], F32)
    Q2 = k.sb("Q2", [128, 4], F32)
    NPT = 6
    PT = [k.sb("PT%d" % i, [128, 512], BF16) for i in range(NPT)]
    RL = k.sb("RL", [128, 512], F32)
    OC = [k.sb("OC%d" % c, [128, 512], F32) for c in range(2)]
    OD = k.sb("OD", [128, 512], F32)
    ACP = [k.sb("ACP%d" % c, [128, 512], F32) for c in range(2)]
    OSQ = k.sb("OSQ", [128, 512], F32)
    RS = k.sb("RS", [128, 512], F32)
    OUT = [k.sb("OUT%d" % i, [128, 512], BF16) for i in range(2)]
    psS = [k.ps("psS%d" % i, [128, 512], F32) for i in range(4)]
    psO = [k.ps("psO%d" % i, [128, 512], F32) for i in range(2)]
    psL = [k.ps("psL%d" % i, [128, 512], F32) for i in range(2)]
    psM = psS[0]

    bKA = k.bufs("KA", 2); bV = k.buf("V"); bQT = k.bufs("QT", 4); bBT = k.buf("BT")
    bBTC = k.bufs("BTC", 2); bDM = k.buf("DM"); bBIGI = k.buf("BIGI"); bONESB = k.buf("ONESB")
    bONESF = k.buf("ONESF"); bLAM = k.buf("LAM"); bLT = k.buf("LT"); bLS = k.buf("LS")
    bSG = k.buf("SG"); bSQT = k.bufs("SQT", 2); bMX = k.buf("MX"); bQ2 = k.buf("Q2")
    bPT = k.bufs("PT", NPT); bRL = k.buf("RL"); bOC = k.bufs("OC", 2); bOD = k.buf("OD")
    bOSQ = k.buf("OSQ"); bRS = k.buf("RS"); bOUT = k.bufs("OUT", 2); bACP = k.bufs("ACP", 2)
    bpsS = k.bufs("psS", 4); bpsO = k.bufs("psO", 2); bpsL = k.bufs("psL", 2); bpsM = bpsS[0]

    k.dma("sp", DM[:], dm_in, w=[bDM])
    k.dma("sp", BIGI[:], bigi_in, w=[bBIGI])
    k.dma("sp", LAM[:], lam_in, w=[bLAM])
    k.dma("sp", SG[:], sg_in, w=[bSG])
    k.op("dve", lambda e: e.memset(ONESB[:], 1.0), w=[bONESB])
    k.op("dve", lambda e: e.memset(ONESF[:], 1.0), w=[bONESF])
    for j in range(2):
        k.op("dve", lambda e: e.tensor_tensor(out=LT[:, j, :], in0=LAM[:, 2 * j, :],
                                               in1=LAM[:, 2 * j + 1, :], op=ALU.mult),
             r=[bLAM], w=[bLT])
        k.op("dve", lambda e: e.reduce_sum(out=LS[:, j:j + 1], in_=LT[:, j, :], axis=AX.X),
             r=[bLT], w=[bLS])
    k.op("act", lambda e: e.activation(out=LS[:, 0:2], in_=LS[:, 0:2], func=AF.Exp),
         r=[bLS], w=[bLS])
    k.op("dve", lambda e: e.tensor_tensor(out=LS[:, 2:3], in0=LS[:, 1:2], in1=LS[:, 0:1],
                                           op=ALU.subtract), r=[bLS], w=[bLS])
    k.op("dve", lambda e: e.tensor_scalar(out=LS[:, 3:4], in0=LS[:, 2:3], scalar1=-lambda_init,
                                           scalar2=None, op0=ALU.add), r=[bLS], w=[bLS])
    k.op("dve", lambda e: e.tensor_scalar(out=SG[:], in0=SG[:], scalar1=1.0 - lambda_init,
                                           scalar2=None, op0=ALU.mult), r=[bSG], w=[bSG])

    qti = 0
    pti = 0
    psi = 0
    oi = 0
    deferred = []
    for hh in range(NH):
        for c in range(2):
            for s0 in range(0, S, 4096):
                s1 = min(S, s0 + 4096)
                for (ap, lo, hi) in io.k(hh, c, s0, s1):
                    k.dma("sp", KA[c][0:64, lo:hi], ap, w=[bKA[c]])
            k.op("pool", lambda e: e.memset(KA[c][64:67, :], 1.0), w=[bKA[c]])
        for b0 in range(0, NB, 32):
            b1 = min(NB, b0 + 32)
            for (ap, lo, hi) in io.v(hh, b0 * 128, b1 * 128):
                k.dma("sp", V[:, lo // 128:hi // 128, :], ap.rearrange("(nb p) c -> p nb c", p=128), w=[bV])
        k.dma("sp", BT[:], bt_in[hh], w=[bBT])
        for qi_ in range(4):
            k.dma("sp", QT[qi_][64:67, :], io.qaug[hh], w=[bQT[qi_]])
        for c in range(2):
            def fetch_q(i, c=c):
                nonlocal qti
                qb = qti % 4
                qti += 1
                for (ap, lo, hi) in io.q(hh, c, i * 512, (i + 1) * 512):
                    k.dma("sp", QT[qb][0:64, :], ap, w=[bQT[qb]])
                return [bQT[qb]], QT[qb][0:64, :]
            emit_sqmax(k, nc, fetch_q, NQ, SQT, bSQT, ONESB, bONESB, psS[0:2], bpsS[0:2], MX, bMX,
                       Q2[:, 0:1], bQ2)
            emit_sqmax(k, nc, lambda i, c=c: ([bKA[c]], KA[c][0:64, i * 512:(i + 1) * 512]), NQ,
                       SQT, bSQT, ONESB, bONESB, psS[0:2], bpsS[0:2], MX, bMX, Q2[:, 1:2], bQ2)
            k.op("dve", lambda e: e.tensor_tensor(out=Q2[:, 2:3], in0=Q2[:, 0:1], in1=Q2[:, 1:2],
                                                   op=ALU.mult), r=[bQ2], w=[bQ2])
            k.op("act", lambda e: e.activation(out=Q2[:, 3:4], in_=Q2[:, 2:3], func=AF.Sqrt),
                 r=[bQ2], w=[bQ2])
            k.op("dve", lambda e: e.tensor_scalar(out=BTC[c][:], in0=BT[:], scalar1=Q2[:, 3:4],
                                                   scalar2=None, op0=ALU.subtract),
                 r=[bBT, bQ2], w=[bBTC[c]])
        for I in range(NQ):
            nkb = 4 * I + 4
            qbs = []
            for c in range(2):
                qb = qti % 4
                qti += 1
                for (ap, lo, hi) in io.q(hh, c, I * 512, (I + 1) * 512):
                    k.dma("sp", QT[qb][0:64, :], ap, w=[bQT[qb]])
                qbs.append(qb)
            staged = []

            def stage_a(jb):
                nonlocal psi, pti
                pts = []
                diag = jb >= 4 * I
                idx = 4 * I - jb + 3
                for c in range(2):
                    ps = psi % 4
                    psi += 1
                    k.op("pe", lambda e: e.matmul(psS[ps][:], lhsT=KA[c][:, jb * 128:(jb + 1) * 128],
                                                   rhs=QT[qbs[c]][:], start=True, stop=not diag),
                         r=[bKA[c], bQT[qbs[c]]], w=[bpsS[ps]])
                    if diag:
                        dd = jb - 4 * I
                        k.op("pe", lambda e: e.matmul(psS[ps][:], lhsT=BIGI[:], rhs=DM[:, dd, :],
                                                       start=False, stop=True),
                             r=[bBIGI, bDM], w=[bpsS[ps]])
                    pt = pti % NPT
                    pti += 1
                    k.op("act", lambda e: e.activation(out=PT[pt][:], in_=psS[ps][:], func=AF.Exp,
                                                        bias=BTC[c][:, idx:idx + 1], scale=1.0),
                         r=[bpsS[ps], bBTC[c]], w=[bPT[pt]])
                    pts.append(pt)
                return pts

            def stage_b(jb, pts):
                for c in range(2):
                    k.op("pe", lambda e: e.matmul(psO[c][:], lhsT=V[:, jb, :], rhs=PT[pts[c]][:],
                                                   start=(jb == 0), stop=(jb == nkb - 1)),
                         r=[bV, bPT[pts[c]]], w=[bpsO[c]])
                for c in range(2):
                    if jb % 3 == 2:
                        if jb == 2:
                            k.op("pool", lambda e: e.tensor_copy(out=ACP[c][:], in_=PT[pts[c]][:]),
                                 r=[bPT[pts[c]]], w=[bACP[c]])
                        else:
                            k.op("pool", lambda e: e.tensor_tensor(out=ACP[c][:], in0=ACP[c][:],
                                                                    in1=PT[pts[c]][:], op=ALU.add),
                                 r=[bACP[c], bPT[pts[c]]], w=[bACP[c]])
                    elif jb == 0:
                        k.op("dve", lambda e: e.tensor_copy(out=psL[c][:], in_=PT[pts[c]][:]),
                             r=[bPT[pts[c]]], w=[bpsL[c]])
                    else:
                        k.op("dve", lambda e: e.tensor_tensor(out=psL[c][:], in0=psL[c][:], in1=PT[pts[c]][:],
                                                               op=ALU.add),
                             r=[bpsL[c], bPT[pts[c]]], w=[bpsL[c]])

            for jb in range(nkb):
                staged.append((jb, stage_a(jb)))
                if jb == 1:
                    while deferred:
                        deferred.pop(0)()
                if len(staged) > 1:
                    stage_b(*staged.pop(0))
            while staged:
                stage_b(*staged.pop(0))
            def tile_epilogue(hh=hh, I=I):
                nonlocal psi, oi
                for c in range(2):
                    k.op("dve", lambda e: e.tensor_tensor(out=OSQ[:], in0=psL[c][:], in1=ACP[c][:], op=ALU.add),
                         r=[bpsL[c], bACP[c]], w=[bOSQ])
                    ps = psi % 4
                    psi += 1
                    k.op("pe", lambda e: e.matmul(psS[ps][:], lhsT=ONESF[:], rhs=OSQ[:], start=True, stop=True),
                         r=[bONESF, bOSQ], w=[bpsS[ps]])
                    k.op("dve", lambda e: e.tensor_scalar(out=RL[:], in0=psS[ps][:], scalar1=1e-30,
                                                           scalar2=None, op0=ALU.max),
                         r=[bpsS[ps]], w=[bRL])
                    k.op("dve", lambda e: e.reciprocal(out=RL[:], in_=RL[:]), r=[bRL], w=[bRL])
                    k.op("dve", lambda e: e.tensor_tensor(out=OC[c][:], in0=psO[c][:], in1=RL[:], op=ALU.mult),
                         r=[bpsO[c], bRL], w=[bOC[c]])
                k.op("dve", lambda e: e.scalar_tensor_tensor(out=OD[:], in0=OC[1][:], scalar=LS[:, 3:4],
                                                              in1=OC[0][:], op0=ALU.mult, op1=ALU.add),
                     r=[bOC[0], bOC[1], bLS], w=[bOD])
                k.op("act", lambda e: e.activation(out=OSQ[:], in_=OD[:], func=AF.Square),
                     r=[bOD], w=[bOSQ])
                k.op("pe", lambda e: e.matmul(psM[:], lhsT=ONESF[:], rhs=OSQ[:], start=True, stop=True),
                     r=[bONESF, bOSQ], w=[bpsM])
                k.op("act", lambda e: e.activation(out=RS[:], in_=psM[:], func=AF.Sqrt,
                                                    scale=1.0 / 128.0, bias=EPS), r=[bpsM], w=[bRS])
                k.op("dve", lambda e: e.reciprocal(out=RS[:], in_=RS[:]), r=[bRS], w=[bRS])
                ob = oi % 2
                oi += 1
                k.op("dve", lambda e: e.scalar_tensor_tensor(out=OUT[ob][:], in0=OD[:], scalar=SG[:, 0:1],
                                                              in1=RS[:], op0=ALU.mult, op1=ALU.mult),
                     r=[bOD, bSG, bRS], w=[bOUT[ob]])
                n0 = len(k.stores)
                k.store("sp", io.out(hh, I), OUT[ob][:], r=[bOUT[ob]])
                if getattr(io, "out_done", None) is not None:
                    io.out_done(I, hh, k.stores[n0:])
            deferred.append(tile_epilogue)
    while deferred:
        deferred.pop(0)()


D_FF = 2816


class IOK3Standalone:
    def __init__(self, nc, T):
        d = lambda n, sh, dt: nc.dram_tensor(n, sh, dt, kind="ExternalInput").ap()
        NCH = 2 * D_FF // 128
        self.xT = d("xT", [D_MODEL, 2 + T], F32)
        self.aT = d("aT", [D_MODEL, 2 + T], BF16)
        self.wo_in = d("wo", [D_MODEL, D_MODEL], F32)
        self.wu_in = d("wu", [D_MODEL, 2 * D_FF], F32)
        self.wd_in = d("wd", [D_FF, D_MODEL], F32)
        self.cw_in = d("cw", [128, NCH, 3], F32)
        self.cb_in = d("cb", [128, NCH], F32)
        self.g_in = d("g", [128, 8], F32)
        self.gf_in = d("gf", [128, 8], F32)
        self.yT = nc.dram_tensor("yT", [D_MODEL, T], F32, kind="ExternalOutput").ap()
        self.halo_scale = None
        self.tail_dst = None

    def x_src(self, col0, n):
        return self.xT.rearrange("(dc p) t -> p dc t", p=128)[:, :, col0:col0 + n]

    def a_src(self, col0, n):
        return [(0, 1, self.aT.rearrange("(dc p) t -> p dc t", p=128)[:, :, col0:col0 + n])]

    def y_dst(self, oc, tcol, n):
        return self.yT[oc * 128:(oc + 1) * 128, tcol:tcol + n]

    def make_scratch(self, nc, T):
        self.X1D = nc.dram_tensor("X1D", [D_MODEL, T], F32).ap()
        self.AFFD = nc.dram_tensor("AFFD", [D_FF, T], BF16).ap()

    def x1_dst(self, dc, tcol, n):
        return self.X1D[dc * 128:(dc + 1) * 128, tcol:tcol + n]

    def x1_src(self, tcol, n):
        return self.X1D.rearrange("(dc p) t -> p dc t", p=128)[:, :, tcol:tcol + n]

    def aff_dst(self, gch, tcol, n):
        return self.AFFD[gch * 128:(gch + 1) * 128, tcol:tcol + n]

    def aff_src(self, tcol, n):
        return self.AFFD.rearrange("(kc p) t -> p kc t", p=128)[:, :, tcol:tcol + n]


def build_k3_split(T, final_norm):
    nc = new_nc()
    io = IOK3Standalone(nc, T)
    io.make_scratch(nc, T)
    k = Ctx(nc)
    k.begin_phase("a_")
    emit_k3(k, nc, T, final_norm, io, 512, "a")
    k.end_phase()
    k.begin_phase("b_")
    emit_k3(k, nc, T, final_norm, io, 512, "b")
    k.finish()
    return nc


def build_k3(T, final_norm, N=256):
    nc = new_nc()
    io = IOK3Standalone(nc, T)
    k = Ctx(nc)
    k.begin_phase("")
    emit_k3(k, nc, T, final_norm, io, N)
    k.finish()
    return nc


def emit_k3(k, nc, T, final_norm, io, N=256, part="ab"):
    A_, B_ = ("a" in part), ("b" in part)
    NT = T // N
    NCH = 2 * D_FF // 128
    NG = NCH // 2
    wo_in, wu_in, wd_in, cw_in, cb_in, g_in, gf_in = (io.wo_in, io.wu_in, io.wd_in, io.cw_in, io.cb_in,
                                                      io.g_in, io.gf_in)

    WO = k.sb("WO", [128, 8, D_MODEL], BF16) if A_ else None
    WU = k.sb("WU", [128, 8, 2 * D_FF], BF16) if A_ else None
    WD = k.sb("WD", [128, NG, D_MODEL], BF16) if B_ else None
    CW = k.sb("CW", [128, NCH, 3], F32)
    CB = k.sb("CB", [128, NCH], F32)
    G = k.sb("G", [128, 8], F32)
    GF = k.sb("GF", [128, 8], F32)
    ONES = k.sb("ONES", [128, 128], F32)
    HALO = k.sb("HALO", [128, NCH, 2], F32) if A_ else None
    NX1 = 2 if part == "b" else 1
    X1s = [k.sb("X1_%d" % i, [128, 8, N], F32) for i in range(NX1)]
    X1 = X1s[0]
    AT = k.sb("AT", [128, 8, N], BF16) if A_ else None
    SQ = [k.sb("SQ%d" % i, [128, N], F32) for i in range(2)]
    RS = k.sb("RS", [128, N], F32)
    HT = k.sb("HT", [128, 8, N], BF16) if A_ else None
    NAF = {"ab": 1, "a": 1, "b": 2}[part]
    AFFs = [k.sb("AFF%d" % i, [128, NG if B_ else 2, N], BF16) for i in range(NAF)]
    AFF = AFFs[0]
    NU, NY, NSG = 3, 4, 2
    U = [k.sb("U%d" % i, [128, 2 + N], F32) for i in range(NU)] if A_ else None
    Y = [k.sb("Y%d" % i, [128, N], F32) for i in range(NY)] if A_ else None
    SGT = [k.sb("SGT%d" % i, [128, N], F32) for i in range(NSG)] if A_ else None
    OUT = [k.sb("OUT%d" % i, [128, N], F32) for i in range(2)] if B_ else None
    mid = B_ and getattr(io, "h_dst", None) is not None
    assert not (mid and final_norm)
    if mid:
        OUTH = [k.sb("OUTH%d" % i, [128, N], BF16) for i in range(2)]
        bOUTH = k.bufs("OUTH", 2)
    PS = [k.ps("PS%d" % i, [128, 512], F32) for i in range(6)]
    PSS = k.ps("PSS", [128, 512], F32)

    bWO = k.buf("WO"); bWU = k.bufs("WU", 8); bWD = k.buf("WD"); bCW = k.buf("CW"); bCB = k.buf("CB")
    bG = k.buf("G"); bGF = k.buf("GF"); bONES = k.buf("ONES"); bHALO = k.bufs("HALO", NCH)
    bX1s = [k.bufs("X1", 8) for _ in range(NX1)]; bX1 = bX1s[0]; bAT = k.buf("AT"); bSQ = k.bufs("SQ", 2); bRS = k.buf("RS"); bHT = k.buf("HT")
    bAFFs = [k.bufs("AFF", NG) for _ in range(NAF)]; bAFF = bAFFs[0]; bU = k.bufs("U", NU); bY = k.bufs("Y", NY); bSGT = k.bufs("SGT", NSG)
    bOUT = k.bufs("OUT", 2); bPS = k.bufs("PS", 6); bPSS = k.buf("PSS")

    if A_:
        k.dma("sp", CW[:], cw_in, w=[bCW])
        k.dma("sp", CB[:], cb_in, w=[bCB])
        k.dma("sp", G[:], g_in, w=[bG])
    k.dma("sp", GF[:], gf_in, w=[bGF])
    k.op("dve", lambda e: e.memset(ONES[:], 1.0), w=[bONES])
    pre = getattr(io, "wb", None)
    if pre is not None:
        wov = pre[0].rearrange("(dc p) c -> p dc c", p=128)
        wuv = pre[1].rearrange("(dc p) c -> p dc c", p=128)
        wdv = pre[2].rearrange("(kc p) c -> p kc c", p=128)
        for dc in range(8 if A_ else 0):
            k.dma("sp", WU[:, dc, :], wuv[:, dc, :], w=[bWU[dc]])
        if A_:
            k.dma("sp", WO[:], wov, w=[bWO])
        for kc in range(0, NG if B_ else 0, 2):
            k.dma("sp", WD[:, kc:kc + 2, :], wdv[:, kc:kc + 2, :], w=[bWD])
    else:
        wov = wo_in.rearrange("(dc p) c -> p dc c", p=128)
        for dc in range(8 if A_ else 0):
            k.dma("pool", WO[:, dc, :], wov[:, dc, :], w=[bWO])
        wuv = wu_in.rearrange("(dc p) c -> p dc c", p=128)
        for dc in range(8 if A_ else 0):
            for c0 in range(0, 2 * D_FF, 1024):
                c1 = min(2 * D_FF, c0 + 1024)
                k.dma("pool", WU[:, dc, c0:c1], wuv[:, dc, c0:c1], w=[bWU[dc]])
        wdv = wd_in.rearrange("(kc p) c -> p kc c", p=128)
        for kc in range(NG if B_ else 0):
            k.dma("pool", WD[:, kc, :], wdv[:, kc, :], w=[bWD])

    st = {"pi": 0, "ui": 0, "yi": 0, "oi": 0, "sq": 0}
    if A_ and io.halo_scale is not None:
        M0 = k.sb("M0", [128, 1], F32)
        bM0 = k.buf("M0")
        k.dma("sp", M0[:], io.halo_scale, w=[bM0])

    def rms(n, gtile, inv_d):
        for dc in range(8):
            s = st["sq"] % 2
            st["sq"] += 1
            k.op("act", lambda e: e.activation(out=SQ[s][:, :n], in_=X1[:, dc, :n], func=AF.Square),
                 r=[bX1[dc]], w=[bSQ[s]])
            k.op("pe", lambda e: e.matmul(PSS[:, :n], lhsT=ONES[:], rhs=SQ[s][:, :n],
                                           start=(dc == 0), stop=(dc == 7)),
                 r=[bONES, bSQ[s]], w=[bPSS])
        k.op("act", lambda e: e.activation(out=RS[:, :n], in_=PSS[:, :n], func=AF.Sqrt,
                                            scale=inv_d, bias=EPS), r=[bPSS], w=[bRS])
        k.op("dve", lambda e: e.reciprocal(out=RS[:, :n], in_=RS[:, :n]), r=[bRS], w=[bRS])

    def tile(col0, n, halo_only, it=0):
        nonlocal X1, bX1, AFF, bAFF
        X1, bX1 = X1s[it % NX1], bX1s[it % NX1]
        AFF, bAFF = AFFs[it % NAF], bAFFs[it % NAF]
        if part == "b":
            afv = io.aff_src(col0 - 2, n)
            for h0 in (0, NG // 2):
                k.dma("sp", AFF[:, h0:h0 + NG // 2, :n], afv[:, h0:h0 + NG // 2, :], w=bAFF[h0:h0 + NG // 2])
            k.dma("sp", X1[:, :, :n], io.x1_src(col0 - 2, n), w=bX1)
        else:
            tile_a(col0, n, halo_only)
        if halo_only or part == "a":
            return
        tile_b(col0, n)

    def tile_a(col0, n, halo_only):
        k.dma("sp", X1[:, :, :n], io.x_src(col0, n), w=bX1)
        for (dc0, dstep, ap_) in io.a_src(col0, n):
            ndc = ap_.shape[1]
            k.dma("sp", AT[:, dc0:dc0 + (ndc - 1) * dstep + 1:dstep, :n], ap_, w=[bAT])
        if halo_only and io.halo_scale is not None:
            k.op("dve", lambda e: e.tensor_scalar(out=X1[:, :, :n], in0=X1[:, :, :n], scalar1=M0[:, 0:1],
                                                   scalar2=None, op0=ALU.mult), r=bX1 + [bM0], w=bX1)
            k.op("dve", lambda e: e.tensor_scalar(out=AT[:, :, :n], in0=AT[:, :, :n], scalar1=M0[:, 0:1],
                                                   scalar2=None, op0=ALU.mult), r=[bAT, bM0], w=[bAT])
        for oc in range(8):
            p = st["pi"] % 6
            st["pi"] += 1
            for dc in range(8):
                k.op("pe", lambda e: e.matmul(PS[p][:, :n], lhsT=WO[:, dc, oc * 128:(oc + 1) * 128],
                                               rhs=AT[:, dc, :n], start=(dc == 0), stop=(dc == 7)),
                     r=[bWO, bAT], w=[bPS[p]])
            k.op("dve", lambda e: e.tensor_tensor(out=X1[:, oc, :n], in0=X1[:, oc, :n], in1=PS[p][:, :n],
                                                   op=ALU.add), r=[bX1[oc], bPS[p]], w=[bX1[oc]])
        if part == "a" and not halo_only:
            for dc in range(8):
                k.store("pool", io.x1_dst(dc, col0 - 2, n), X1[:, dc, :n], r=[bX1[dc]])
        rms(n, None, 1.0 / D_MODEL)
        for dc in range(8):
            k.op("dve", lambda e: e.scalar_tensor_tensor(
                out=HT[:, dc, :n], in0=X1[:, dc, :n], scalar=G[:, dc:dc + 1], in1=RS[:, :n],
                op0=ALU.mult, op1=ALU.mult), r=[bX1[dc], bG, bRS], w=[bHT])
        for cc in range(NCH):
            c = (cc // 2) + (NG if cc % 2 else 0)
            p = st["pi"] % 6
            st["pi"] += 1
            for dc in range(8):
                k.op("pe", lambda e: e.matmul(PS[p][:, :n], lhsT=WU[:, dc, c * 128:(c + 1) * 128],
                                               rhs=HT[:, dc, :n], start=(dc == 0), stop=(dc == 7)),
                     r=[bWU[dc], bHT], w=[bPS[p]])
            if halo_only:
                k.op("act", lambda e: e.copy(out=HALO[:, c, :], in_=PS[p][:, :n]),
                     r=[bPS[p]], w=[bHALO[c]])
                continue
            u = st["ui"] % NU
            st["ui"] += 1
            k.op("pool", lambda e: e.tensor_copy(out=U[u][:, 0:2], in_=HALO[:, c, :]),
                 r=[bHALO[c]], w=[bU[u]])
            k.op("act", lambda e: e.copy(out=U[u][:, 2:2 + n], in_=PS[p][:, :n]),
                 r=[bPS[p]], w=[bU[u]])
            k.op("pool", lambda e: e.tensor_copy(out=HALO[:, c, :], in_=U[u][:, n:n + 2]),
                 r=[bU[u]], w=[bHALO[c]])
            y = st["yi"] % NY
            st["yi"] += 1
            k.op("pool", lambda e: e.tensor_scalar(out=Y[y][:, :n], in0=U[u][:, 2:2 + n],
                                                    scalar1=CW[:, c, 2:3], scalar2=CB[:, c:c + 1],
                                                    op0=ALU.mult, op1=ALU.add),
                 r=[bU[u], bCW, bCB], w=[bY[y]])
            k.op("dve", lambda e: e.scalar_tensor_tensor(out=Y[y][:, :n], in0=U[u][:, 1:1 + n],
                                                          scalar=CW[:, c, 1:2], in1=Y[y][:, :n],
                                                          op0=ALU.mult, op1=ALU.add),
                 r=[bU[u], bCW, bY[y]], w=[bY[y]])
            k.op("dve", lambda e: e.scalar_tensor_tensor(out=Y[y][:, :n], in0=U[u][:, 0:n],
                                                          scalar=CW[:, c, 0:1], in1=Y[y][:, :n],
                                                          op0=ALU.mult, op1=ALU.add),
                 r=[bU[u], bCW, bY[y]], w=[bY[y]])
            sg = (cc // 2) % NSG
            if cc % 2 == 0:
                k.op("act", lambda e: e.activation(out=SGT[sg][:, :n], in_=Y[y][:, :n], func=AF.Silu),
                     r=[bY[y]], w=[bSGT[sg]])
            else:
                gch = cc // 2
                asl = gch if part == "ab" else gch % 2
                k.op("pool", lambda e: e.tensor_tensor(out=AFF[:, asl, :n], in0=SGT[sg][:, :n],
                                                        in1=Y[y][:, :n], op=ALU.mult),
                     r=[bSGT[sg], bY[y]], w=[bAFF[asl]])
                if part == "a":
                    k.store("pool", io.aff_dst(gch, col0 - 2, n), AFF[:, asl, :n], r=[bAFF[asl]])

    def tile_b(col0, n):
        for oc in range(8):
            p = st["pi"] % 6
            st["pi"] += 1
            for kc in range(NG):
                k.op("pe", lambda e: e.matmul(PS[p][:, :n], lhsT=WD[:, kc, oc * 128:(oc + 1) * 128],
                                               rhs=AFF[:, kc, :n], start=(kc == 0), stop=(kc == NG - 1)),
                     r=[bWD, bAFF[kc]], w=[bPS[p]])
            if final_norm or mid:
                k.op("dve", lambda e: e.tensor_tensor(out=X1[:, oc, :n], in0=X1[:, oc, :n],
                                                       in1=PS[p][:, :n], op=ALU.add),
                     r=[bX1[oc], bPS[p]], w=[bX1[oc]])
            else:
                o = st["oi"] % 2
                st["oi"] += 1
                k.op("dve", lambda e: e.tensor_tensor(out=OUT[o][:, :n], in0=X1[:, oc, :n],
                                                       in1=PS[p][:, :n], op=ALU.add),
                     r=[bX1[oc], bPS[p]], w=[bOUT[o]])
                k.store("pool", io.y_dst(oc, col0 - 2, n), OUT[o][:, :n], r=[bOUT[o]])
                if io.tail_dst is not None and col0 - 2 + n == T:
                    k.store("pool", io.tail_dst(oc), OUT[o][:, n - 2:n], r=[bOUT[o]])
        if mid:
            rms(n, None, 1.0 / D_MODEL)
            for oc in range(8):
                k.store("pool", io.y_dst(oc, col0 - 2, n), X1[:, oc, :n], r=[bX1[oc]])
                if col0 - 2 + n == T:
                    k.store("pool", io.tail_dst(oc), X1[:, oc, n - 2:n], r=[bX1[oc]])
                o = st["oi"] % 2
                st["oi"] += 1
                k.op("dve", lambda e: e.scalar_tensor_tensor(
                    out=OUTH[o][:, :n], in0=X1[:, oc, :n], scalar=GF[:, oc:oc + 1], in1=RS[:, :n],
                    op0=ALU.mult, op1=ALU.mult), r=[bX1[oc], bGF, bRS], w=[bOUTH[o]])
                k.store("pool", io.h_dst(oc, col0 - 2, n), OUTH[o][:, :n], r=[bOUTH[o]])
        if final_norm:
            rms(n, None, 1.0 / D_MODEL)
            for oc in range(8):
                o = st["oi"] % 2
                st["oi"] += 1
                k.op("dve", lambda e: e.scalar_tensor_tensor(
                    out=OUT[o][:, :n], in0=X1[:, oc, :n], scalar=GF[:, oc:oc + 1], in1=RS[:, :n],
                    op0=ALU.mult, op1=ALU.mult), r=[bX1[oc], bGF, bRS], w=[bOUT[o]])
                k.store("pool", io.y_dst(oc, col0 - 2, n), OUT[o][:, :n], r=[bOUT[o]])

    if A_:
        tile(0, 2, True)
    for it in range(NT):
        n0 = len(k.stores)
        tile(2 + it * N, N, False, it)
        if B_ and getattr(io, "tile_done", None) is not None:
            io.tile_done(it, k.stores[n0:], N)


def conv_layouts(conv_w, conv_b):
    cw = np.ascontiguousarray(np.asarray(conv_w, np.float32).reshape(3, 44, 128).transpose(2, 1, 0))
    cb = np.ascontiguousarray(np.asarray(conv_b, np.float32).reshape(44, 128).T)
    return cw, cb


def esel_table():
    n = np.arange(128)[:, None, None]
    jj = np.arange(64)[None, :, None]
    s = np.arange(128)[None, None, :]
    return np.where(n == 2 * jj + (s >= 64), BIG, 0.0).astype(NPBF)


def itab_table(m):
    tl = np.arange(128)[:, None]
    jp = np.arange(1024)[None, :] - 1016
    d = tl - 16 * jp - 31
    return np.where(d >= 0, -m * d, -1e30).astype(np.float32)


def ab_tables():
    tl = np.arange(128)[:, None]
    npr = np.arange(256)[None, :] - 254
    cc = (tl >= 64).astype(np.int64)
    V = npr <= cc
    Fn = V & (npr >= cc - 1)
    A = V.astype(np.float32)
    B = (V.astype(np.float32) - 1.0) + 1e6 * Fn.astype(np.float32)
    return A, B.astype(np.float32)


def selg_table():
    t = np.zeros((12, 12, 64), np.float32)
    for r in range(12):
        t[r, r, :] = 1.0
    return t.astype(NPBF)


K2A_STATIC = [("pek", [64, 32], F32), ("pev", [64, 32], F32), ("w1k", [2048, 256], F32),
              ("w2k", [256, 64], F32), ("w1v", [2048, 256], F32), ("w2v", [256, 64], F32),
              ("bt", [128, 4, 128], F32), ("btc", [128, 4, 60], F32), ("dm", [128, 4, 512], BF16),
              ("wm", [128, 4, 512], BF16), ("cm", [128, 5, 512], BF16), ("bigi", [128, 128], BF16),
              ("idb", [128, 128], F32), ("esel", [128, 64, 128], BF16), ("itab", [128, 4, 1024], F32),
              ("atab", [128, 256], F32), ("btab", [128, 256], F32), ("selg", [12, 12, 64], BF16),
              ("qaug", [4, 3, 512], BF16)]


class IOK2aStandalone:
    def __init__(self, nc, S):
        d = lambda n, sh, dt: nc.dram_tensor(n, sh, dt, kind="ExternalInput").ap()
        self.t = {"q%d" % p: None for p in range(4)}
        qa = d("qa", [4, 64, S], BF16)
        for p in range(4):
            self.t["q%d" % p] = qa[p]
        self.t["kc"] = d("kca", [64, S], BF16)
        self.t["vc"] = d("vca", [64, S], BF16)
        self.t["ks"] = d("ksa", [64, S], BF16)
        self.t["kw"] = d("kwa", [64, S], BF16)
        self.t["gt"] = d("gt", [12, S], BF16)
        self.t["vs"] = d("vs", [S, 64], BF16)
        self.t["vw"] = d("vw", [S, 64], BF16)
        self.st = {n: d(n, sh, dt) for (n, sh, dt) in K2A_STATIC}
        self.oT = nc.dram_tensor("oT", [256, S], BF16, kind="ExternalOutput").ap()

    def fm(self, name, t0, t1):
        return [(self.t[name][:, t0:t1], t0, t1)]

    def tok(self, name, t0, t1):
        return [(self.t[name][t0:t1, :], t0, t1)]

    def out(self, p, I):
        return self.oT[p * 64:(p + 1) * 64, I * 512:(I + 1) * 512]


def build_k2a(S):
    nc = new_nc()
    io = IOK2aStandalone(nc, S)
    k = Ctx(nc)
    k.begin_phase("")
    emit_k2a(k, nc, S, io)
    k.finish()
    return nc


def emit_k2a(k, nc, S, io):
    NQ = S // 512
    NB = S // 128
    NCMP = S // 16 - 1
    CW_ = S // 16
    NCB = (CW_ + 127) // 128
    stc = io.st
    pek_in, pev_in, w1k_in, w2k_in, w1v_in, w2v_in = (stc["pek"], stc["pev"], stc["w1k"], stc["w2k"],
                                                      stc["w1v"], stc["w2v"])
    bt_in, btc_in, dm_in, wm_in, cm_in, bigi_in, idb_in = (stc["bt"], stc["btc"], stc["dm"], stc["wm"],
                                                           stc["cm"], stc["bigi"], stc["idb"])
    esel_in, itab_in, atab_in, btab_in, selg_in, qaug_in = (stc["esel"], stc["itab"], stc["atab"],
                                                            stc["btab"], stc["selg"], stc["qaug"])
    A = k.sb
    P = k.ps

    def ld_fm(dst_fn, name, t0, t1, w):
        for (ap, lo, hi) in io.fm(name, t0, t1):
            k.dma("sp", dst_fn(lo - t0, hi - t0), ap, w=w)

    def ld_tok(dst_fn, name, t0, t1, w):
        for (ap, lo, hi) in io.tok(name, t0, t1):
            k.dma("sp", dst_fn((lo - t0) // 128, (hi - t0) // 128),
                  ap.rearrange("(nb p) d -> p nb d", p=128), w=w)

    KS = A("KS", [67, S], BF16); bKS = k.buf("KS")
    VS = A("VS", [128, NB, 65], BF16); bVS = k.buf("VS")
    KCMP = A("KCMP", [67, NCB * 128], BF16); bKCMP = k.buf("KCMP")
    VC = A("VC", [128, NCB, 65], BF16); bVC = k.buf("VC")
    BT = A("BT", [128, 4, 128], F32); bBT = k.buf("BT")
    BTCs = A("BTCs", [128, 4, 128], F32); bBTCs = k.buf("BTCs")
    BTCw = A("BTCw", [128, 4, 8], F32); bBTCw = k.buf("BTCw")
    BTC0 = A("BTC0", [128, 4, 60], F32); bBTC0 = k.buf("BTC0")
    BTCc = A("BTCc", [128, 4, 60], F32); bBTCc = k.buf("BTCc")
    DM = A("DM", [128, 4, 512], BF16); bDM = k.buf("DM")
    WM = A("WM", [128, 4, 512], BF16); bWM = k.buf("WM")
    CM = A("CM", [128, 5, 512], BF16); bCM = k.buf("CM")
    BIGI = A("BIGI", [128, 128], BF16); bBIGI = k.buf("BIGI")
    IDB = A("IDB", [128, 128], F32); bIDB = k.buf("IDB")
    ESEL = A("ESEL", [128, 64, 128], BF16); bESEL = k.buf("ESEL")
    ITAB = A("ITAB", [128, 4, 1024], F32); bITAB = k.buf("ITAB")
    ATAB = A("ATAB", [128, 256], F32); bATAB = k.buf("ATAB")
    BTAB = A("BTAB", [128, 256], F32); bBTAB = k.buf("BTAB")
    SELG = A("SELG", [12, 12, 64], BF16); bSELG = k.buf("SELG")
    ONESB = A("ONESB", [128, 128], BF16); bONESB = k.buf("ONESB")
    ONESF = A("ONESF", [128, 64], F32); bONESF = k.buf("ONESF")
    QT = [A("QT%d" % i, [67, 4, 512], BF16) for i in range(2)]; bQT = k.bufs("QT", 2)
    KW = [A("KW%d" % i, [67, 1024], BF16) for i in range(2)]; bKW = k.bufs("KW", 2)
    VW = [A("VW%d" % i, [128, 8, 65], BF16) for i in range(2)]; bVW = k.bufs("VW", 2)
    GT = [A("GT%d" % i, [12, 512], BF16) for i in range(2)]; bGT = k.bufs("GT", 2)
    SQT = [A("SQT%d" % i, [128, 512], BF16) for i in range(2)]; bSQT = k.bufs("SQT", 2)
    MX = A("MX", [128, 64], F32); bMX = k.buf("MX")
    ST = A("ST", [128, 16], F32); bST = k.buf("ST")
    SX = A("SX", [128, 1024], F32); bSX = k.buf("SX")
    EX = A("EX", [128, 1024], F32); bEX = k.buf("EX")
    PG = A("PG", [128, 1028], F32); bPG = k.buf("PG")
    SC = A("SC", [128, 8], F32); bSC = k.buf("SC")
    IMP = A("IMP", [128, 256], F32); bIMP = k.buf("IMP")
    I2 = A("I2", [128, 256], F32); bI2 = k.buf("I2")
    I3 = A("I3", [128, 256], F32); bI3 = k.buf("I3")
    M8 = A("M8", [128, 16], F32); bM8 = k.buf("M8")
    MQ = A("MQ", [128, 256], F32); bMQ = k.buf("MQ")
    MT = [A("MT%d" % i, [128, 2, 512], BF16) for i in range(2)]; bMT = k.bufs("MT", 2)
    NPT = 6
    PT = [A("PT%d" % i, [128, 512], BF16) for i in range(NPT)]; bPT = k.bufs("PT", NPT)
    RL = A("RL", [128, 512], F32); bRL = k.buf("RL")
    OBF = A("OBF", [64, 512], F32); bOBF = k.buf("OBF")
    TT = A("TT", [64, 512], F32); bTT = k.buf("TT")
    ACC = [A("ACC%d" % i, [64, 512], F32) for i in range(2)]; bACC = k.bufs("ACC", 2)
    OUTB = [A("OUTB%d" % i, [64, 512], BF16) for i in range(2)]; bOUTB = k.bufs("OUTB", 2)
    psS = [P("psS%d" % i, [128, 512], F32) for i in range(4)]; bpsS = k.bufs("psS", 4)
    psO = [P("psO%d" % i, [128, 512], F32) for i in range(2)]; bpsO = k.bufs("psO", 2)
    psA = P("psA", [128, 512], F32); bpsA = k.buf("psA")
    psX = [P("psX0", [128, 512], F32), psS[0]]; bpsX = [k.buf("psX0"), bpsS[0]]

    for (dst, src, b) in [(BT, bt_in, bBT), (BTC0, btc_in, bBTC0), (DM, dm_in, bDM), (WM, wm_in, bWM),
                          (CM, cm_in, bCM), (BIGI, bigi_in, bBIGI), (IDB, idb_in, bIDB),
                          (ITAB, itab_in, bITAB), (ATAB, atab_in, bATAB), (BTAB, btab_in, bBTAB),
                          (SELG, selg_in, bSELG)]:
        k.dma("sp", dst[:], src, w=[b])
    for j0 in range(0, 64, 16):
        k.dma("sp", ESEL[:, j0:j0 + 16, :], esel_in[:, j0:j0 + 16, :], w=[bESEL])
    k.op("dve", lambda e: e.memset(ONESB[:], 1.0), w=[bONESB])
    k.op("dve", lambda e: e.memset(ONESF[:], 1.0), w=[bONESF])
    k.op("dve", lambda e: e.memset(PG[:], 0.0), w=[bPG])
    k.op("dve", lambda e: e.memset(I2[:], -1.0), w=[bI2])
    k.op("dve", lambda e: e.memset(RL[:], 1.0), w=[bRL])
    k.op("pool", lambda e: e.memset(KCMP[:], 0.0), w=[bKCMP])
    k.op("pool", lambda e: e.memset(KCMP[64:67, :], 1.0), w=[bKCMP])
    k.op("pool", lambda e: e.memset(VC[:], 0.0), w=[bVC])
    for s0 in range(0, S, 4096):
        s1 = min(S, s0 + 4096)
        ld_fm(lambda a, b, s0=s0: KS[0:64, s0 + a:s0 + b], "ks", s0, s1, [bKS])
    k.op("pool", lambda e: e.memset(KS[64:67, :], 1.0), w=[bKS])
    for b0 in range(0, NB, 8):
        ld_tok(lambda a, b, b0=b0: VS[:, b0 + a:b0 + b, 0:64], "vs", b0 * 128, (b0 + 8) * 128, [bVS])
    k.op("pool", lambda e: e.memset(VS[:, :, 64:65], 1.0), w=[bVS])
    for i in range(2):
        k.op("pool", lambda e: e.memset(VW[i][:, :, 64:65], 1.0), w=[bVW[i]])
        k.op("pool", lambda e: e.memset(KW[i][64:67, :], 1.0), w=[bKW[i]])
        k.dma("sp", QT[i][64:67, :, :], qaug_in.rearrange("h r t -> r h t"), w=[bQT[i]])

    with nc.sbuf_tensor("KCH", [64, 8208], BF16) as KCH, \
            nc.sbuf_tensor("W1", [64, 32, 256], BF16) as W1, \
            nc.sbuf_tensor("W2", [128, 2, 64], BF16) as W2, \
            nc.sbuf_tensor("PEF", [64, 32], F32) as PEF, \
            nc.sbuf_tensor("PEB", [64, 32], BF16) as PEB, \
            nc.sbuf_tensor("BH", [128, 2], F32) as BH, \
            nc.sbuf_tensor("HX", [128, 512], F32) as HX, \
            nc.sbuf_tensor("H2", [128, 512], F32) as H2, \
            nc.sbuf_tensor("HID", [128, 2, 512], BF16) as HID:
        bKCH = k.buf("KCH"); bW1 = k.buf("W1"); bW2 = k.buf("W2"); bPEF = k.buf("PEF"); bPEB = k.buf("PEB")
        bBH = k.buf("BH"); bHX = k.buf("HX"); bH2 = k.buf("H2"); bHID = k.bufs("HID", 2)
        for which, (src, pe_in, w1_in, w2_in) in enumerate([("kc", pek_in, w1k_in, w2k_in),
                                                           ("vc", pev_in, w1v_in, w2v_in)]):
            w1v_ = w1_in.rearrange("(p d) h -> d p h", d=64)
            for p0 in range(0, 32, 8):
                k.dma("pool", W1[:, p0:p0 + 8, :], w1v_[:, p0:p0 + 8, :], w=[bW1])
            k.dma("pool", W2[:], w2_in.rearrange("(c p) d -> p c d", p=128), w=[bW2])
            k.dma("sp", PEF[:], pe_in, w=[bPEF])
            k.op("dve", lambda e: e.tensor_copy(out=PEB[:], in_=PEF[:]), r=[bPEF], w=[bPEB])
            for hc in range(2):
                for pos in range(32):
                    k.op("pe", lambda e: e.matmul(psX[0][:, hc:hc + 1], lhsT=W1[:, pos, hc * 128:(hc + 1) * 128],
                                                   rhs=PEB[:, pos:pos + 1], start=(pos == 0), stop=(pos == 31)),
                         r=[bW1, bPEB], w=[bpsX[0]])
            k.op("dve", lambda e: e.tensor_copy(out=BH[:], in_=psX[0][:, 0:2]), r=[bpsX[0]], w=[bBH])
            for j0 in range(0, NCMP, 512):
                n = min(512, NCMP - j0)
                t0 = 16 * j0
                t1 = min(S, t0 + 16 * n + 16)
                ld_fm(lambda a, b: KCH[:, a:b], src, t0, t1, [bKCH])
                for hc in range(2):
                    px = psX[1]
                    for pos in range(32):
                        k.op("pe", lambda e: e.matmul(px[:, :n], lhsT=W1[:, pos, hc * 128:(hc + 1) * 128],
                                                       rhs=KCH[:, pos:pos + 16 * (n - 1) + 1:16],
                                                       start=(pos == 0), stop=(pos == 31)),
                             r=[bW1, bKCH], w=[bpsX[1]])
                    k.op("act", lambda e: e.activation(out=HX[:, :n], in_=px[:, :n], func=AF.Identity,
                                                        bias=BH[:, hc:hc + 1], scale=1.0),
                         r=[bpsX[1], bBH], w=[bHX])
                    k.op("dve", lambda e: e.tensor_tensor(out=H2[:, :n], in0=HX[:, :n], in1=HX[:, :n],
                                                           op=ALU.mult), r=[bHX], w=[bH2])
                    k.op("dve", lambda e: e.tensor_scalar(out=H2[:, :n], in0=H2[:, :n], scalar1=0.044715,
                                                           scalar2=1.0, op0=ALU.mult, op1=ALU.add),
                         r=[bH2], w=[bH2])
                    k.op("dve", lambda e: e.tensor_tensor(out=H2[:, :n], in0=H2[:, :n], in1=HX[:, :n],
                                                           op=ALU.mult), r=[bH2, bHX], w=[bH2])
                    k.op("act", lambda e: e.activation(out=H2[:, :n], in_=H2[:, :n], func=AF.Tanh,
                                                        scale=0.7978845608028654), r=[bH2], w=[bH2])
                    k.op("dve", lambda e: e.tensor_scalar(out=H2[:, :n], in0=H2[:, :n], scalar1=0.5,
                                                           scalar2=0.5, op0=ALU.mult, op1=ALU.add),
                         r=[bH2], w=[bH2])
                    k.op("dve", lambda e: e.tensor_tensor(out=HID[:, hc, :n], in0=H2[:, :n], in1=HX[:, :n],
                                                           op=ALU.mult), r=[bH2, bHX], w=[bHID[hc]])
                if which == 0:
                    for hc in range(2):
                        k.op("pe", lambda e: e.matmul(psX[0][0:64, :n], lhsT=W2[:, hc, :], rhs=HID[:, hc, :n],
                                                       start=(hc == 0), stop=(hc == 1)),
                             r=[bW2, bHID[hc]], w=[bpsX[0]])
                    k.op("act", lambda e: e.copy(out=KCMP[0:64, j0:j0 + n], in_=psX[0][0:64, :n]),
                         r=[bpsX[0]], w=[bKCMP])
                else:
                    for jb in range((n + 127) // 128):
                        m = min(128, n - jb * 128)
                        for hc in range(2):
                            k.op("pe", lambda e: e.matmul(psX[0][0:m, 0:64], lhsT=HID[:, hc, jb * 128:jb * 128 + m],
                                                           rhs=W2[:, hc, :], start=(hc == 0), stop=(hc == 1)),
                                 r=[bW2, bHID[hc]], w=[bpsX[0]])
                        gb = j0 // 128 + jb
                        k.op("act", lambda e: e.copy(out=VC[0:m, gb, 0:64], in_=psX[0][0:m, 0:64]),
                             r=[bpsX[0]], w=[bVC])
                        k.op("pool", lambda e: e.memset(VC[0:m, gb, 64:65], 1.0), w=[bVC])

    st = {"q": 0, "kw": 0, "ps": 0, "pt": 0, "out": 0}
    NT5 = S // 512

    def load_q(I):
        qb = st["q"] % 2
        st["q"] += 1
        for p_ in range(4):
            ld_fm(lambda a, b, p_=p_: QT[qb][0:64, p_, a:b], "q%d" % p_, I * 512, (I + 1) * 512, [bQT[qb]])
        return qb

    ncw = (NCB * 128 + 511) // 512
    emit_sqmax(k, nc, lambda i: ([bKCMP], KCMP[0:64, i * 512:min(NCB * 128, (i + 1) * 512)]), ncw,
               SQT, bSQT, ONESB, bONESB, psS[1:3], bpsS[1:3], MX, bMX, ST[:, 4:5], bST)
    emit_sqmax(k, nc, lambda i: ([bKS], KS[0:64, i * 512:(i + 1) * 512]), NT5,
               SQT, bSQT, ONESB, bONESB, psS[1:3], bpsS[1:3], MX, bMX, ST[:, 5:6], bST)

    def fetch_kw(i):
        wb = st["kw"] % 2
        st["kw"] += 1
        ld_fm(lambda a, b: KW[wb][0:64, a:b], "kw", i * 512, (i + 1) * 512, [bKW[wb]])
        return [bKW[wb]], KW[wb][0:64, 0:512]
    emit_sqmax(k, nc, fetch_kw, NT5, SQT, bSQT, ONESB, bONESB, psS[1:3], bpsS[1:3], MX, bMX, ST[:, 6:7], bST)
    MXQ = A("MXQ", [128, 4, 32], F32); bMXQ = k.buf("MXQ")
    it_ = 0
    qb_nx = load_q(0)
    for i in range(NT5):
        qb_ = qb_nx
        if i + 1 < NT5:
            qb_nx = load_q(i + 1)
        for p in range(4):
            a = it_ % 2
            it_ += 1
            k.op("dve", lambda e: e.tensor_tensor(out=SQT[a][0:64, :], in0=QT[qb_][0:64, p, :],
                                                   in1=QT[qb_][0:64, p, :], op=ALU.mult),
                 r=[bQT[qb_]], w=[bSQT[a]])
            k.op("pe", lambda e: e.matmul(psS[1 + a][:], lhsT=ONESB[0:64, :], rhs=SQT[a][0:64, :],
                                           start=True, stop=True), r=[bSQT[a], bONESB], w=[bpsS[1 + a]])
            k.op("dve", lambda e: e.reduce_max(out=MXQ[:, p, i:i + 1], in_=psS[1 + a][:], axis=AX.X),
                 r=[bpsS[1 + a]], w=[bMXQ])
    for p in range(4):
        k.op("dve", lambda e: e.reduce_max(out=ST[:, p:p + 1], in_=MXQ[:, p, 0:NT5], axis=AX.X),
             r=[bMXQ], w=[bST])
    for p in range(4):
        k.op("dve", lambda e: e.tensor_scalar(out=ST[:, 8:11], in0=ST[:, 4:7], scalar1=ST[:, p:p + 1],
                                               scalar2=None, op0=ALU.mult), r=[bST], w=[bST])
        k.op("act", lambda e: e.activation(out=ST[:, 8:11], in_=ST[:, 8:11], func=AF.Sqrt), r=[bST], w=[bST])
        k.op("dve", lambda e: e.tensor_scalar(out=BTCc[:, p, :], in0=BTC0[:, p, :], scalar1=ST[:, 8:9],
                                               scalar2=None, op0=ALU.subtract), r=[bST, bBTC0], w=[bBTCc])
        k.op("dve", lambda e: e.tensor_scalar(out=BTCs[:, p, :], in0=BT[:, p, :], scalar1=ST[:, 9:10],
                                               scalar2=None, op0=ALU.subtract), r=[bST, bBT], w=[bBTCs])
        k.op("dve", lambda e: e.tensor_scalar(out=BTCw[:, p, :], in0=BT[:, p, 0:8], scalar1=ST[:, 10:11],
                                               scalar2=None, op0=ALU.subtract), r=[bST, bBT], w=[bBTCw])


    def importance_block(I, qb, mb, qi):
        for u in importance_units(I, qb, mb, qi):
            u()

    def importance_units(I, qb, mb, qi):
        return [lambda p=p: importance_head(I, qb, qi, p) for p in range(4)] + [lambda: importance_topk(I, mb, qi)]

    def importance_head(I, qb, qi, p):
        i = 4 * I + qi
        ncols = min(8 * (i + 1), NCB * 128)
        nb = 2 * (i + 1)
        if True:
            io_ = 1016 - 8 * i
            for c0 in range(0, ncols, 512):
                c1 = min(ncols, c0 + 512)
                k.op("pe", lambda e: e.matmul(psA[:, 0:c1 - c0], lhsT=QT[qb][0:64, p, qi * 128:(qi + 1) * 128],
                                               rhs=KCMP[0:64, c0:c1], start=True, stop=True),
                     r=[bQT[qb], bKCMP], w=[bpsA])
                k.op("dve", lambda e: e.tensor_tensor(out=SX[:, c0:c1], in0=psA[:, 0:c1 - c0],
                                                       in1=ITAB[:, p, io_ + c0:io_ + c1], op=ALU.add),
                     r=[bpsA, bITAB], w=[bSX])
            k.op("dve", lambda e: e.reduce_max(out=SC[:, 0:1], in_=SX[:, :ncols], axis=AX.X),
                 r=[bSX], w=[bSC])
            k.op("dve", lambda e: e.tensor_scalar(out=SC[:, 1:2], in0=SC[:, 0:1], scalar1=-1e20,
                                                   scalar2=-1.0, op0=ALU.max, op1=ALU.mult),
                 r=[bSC], w=[bSC])
            k.op("act", lambda e: e.activation(out=EX[:, :ncols], in_=SX[:, :ncols], func=AF.Exp,
                                                bias=SC[:, 1:2], scale=1.0, accum_out=SC[:, 2:3]),
                 r=[bSX, bSC], w=[bEX, bSC])
            k.op("dve", lambda e: e.tensor_scalar(out=SC[:, 3:4], in0=SC[:, 2:3], scalar1=1e-30,
                                                   scalar2=None, op0=ALU.max), r=[bSC], w=[bSC])
            k.op("dve", lambda e: e.reciprocal(out=SC[:, 3:4], in_=SC[:, 3:4]), r=[bSC], w=[bSC])
            if p == 0:
                k.op("dve", lambda e: e.tensor_scalar(out=PG[:, 1:1 + ncols], in0=EX[:, :ncols],
                                                       scalar1=SC[:, 3:4], scalar2=None, op0=ALU.mult),
                     r=[bEX, bSC], w=[bPG])
            else:
                k.op("dve", lambda e: e.scalar_tensor_tensor(out=PG[:, 1:1 + ncols], in0=EX[:, :ncols],
                                                              scalar=SC[:, 3:4], in1=PG[:, 1:1 + ncols],
                                                              op0=ALU.mult, op1=ALU.add),
                     r=[bEX, bSC, bPG], w=[bPG])

    def importance_topk(I, mb, qi):
        i = 4 * I + qi
        nb = 2 * (i + 1)
        k.op("dve", lambda e: e.reduce_sum(out=IMP[:, :nb],
                                            in_=PG[:, 0:4 * nb].rearrange("p (n r) -> p n r", r=4),
                                            axis=AX.X), r=[bPG], w=[bIMP])
        k.op("dve", lambda e: e.tensor_tensor(out=IMP[:, :nb], in0=IMP[:, :nb],
                                               in1=PG[:, 4:4 * nb + 1:4], op=ALU.add),
             r=[bPG, bIMP], w=[bIMP])
        k.op("dve", lambda e: e.tensor_tensor(out=I2[:, :nb], in0=IMP[:, :nb], in1=ATAB[:, 256 - nb:256],
                                               op=ALU.mult), r=[bIMP, bATAB], w=[bI2])
        k.op("dve", lambda e: e.tensor_tensor(out=I2[:, :nb], in0=I2[:, :nb], in1=BTAB[:, 256 - nb:256],
                                               op=ALU.add), r=[bI2, bBTAB], w=[bI2])
        k.op("dve", lambda e: e.memset(I2[:, 0:1], 1e6), w=[bI2])
        k.op("dve", lambda e: e.max(out=M8[:, 0:8], in_=I2[:]), r=[bI2], w=[bM8])
        k.op("dve", lambda e: e.match_replace(out=I3[:], in_to_replace=M8[:, 0:8], in_values=I2[:],
                                               imm_value=-2.0), r=[bI2, bM8], w=[bI3])
        k.op("dve", lambda e: e.max(out=M8[:, 8:16], in_=I3[:]), r=[bI3], w=[bM8])
        k.op("dve", lambda e: e.tensor_scalar(out=MQ[:], in0=I2[:], scalar1=M8[:, 15:16], scalar2=1.0,
                                               op0=ALU.is_ge, op1=ALU.subtract), r=[bI2, bM8], w=[bMQ])
        nch = 2 if nb > 128 else 1
        for ch in range(nch):
            k.op("pe", lambda e: e.transpose(out=psX[0][:, 0:128], in_=MQ[:, ch * 128:(ch + 1) * 128],
                                              identity=IDB[:]), r=[bMQ, bIDB], w=[bpsX[0]])
            k.op("act", lambda e: e.copy(out=MT[mb][:, ch, qi * 128:(qi + 1) * 128], in_=psX[0][:, 0:128]),
                 r=[bpsX[0]], w=[bMT[mb]])

    def branch(I, qb, heads, br, steps, bias_fn, first, hook=None):
        nst = len(steps)
        staged = []

        def stage_a(n_):
            lk, kb, extra, bidx, vl, vb = steps[n_]
            pts = []
            for hi_, p in enumerate(heads):
                ps = st["ps"] % 4
                st["ps"] += 1
                pts.append((ps, None))
            for hi_, p in enumerate(heads):
                ps = pts[hi_][0]
                k.op("pe", lambda e: e.matmul(psS[ps][:], lhsT=lk, rhs=QT[qb][:, p, :], start=True,
                                               stop=(len(extra) == 0)), r=kb + [bQT[qb]], w=[bpsS[ps]])
            for xi, (xl, xr, xb) in enumerate(extra):
                for hi_, p in enumerate(heads):
                    ps = pts[hi_][0]
                    k.op("pe", lambda e: e.matmul(psS[ps][:], lhsT=xl, rhs=xr, start=False,
                                                   stop=(xi == len(extra) - 1)), r=xb, w=[bpsS[ps]])
            out = []
            for hi_, p in enumerate(heads):
                ps = pts[hi_][0]
                pt = st["pt"] % NPT
                st["pt"] += 1
                bias, bbuf = bias_fn(p, bidx)
                k.op("act", lambda e: e.activation(out=PT[pt][:], in_=psS[ps][:], func=AF.Exp, bias=bias,
                                                    scale=1.0), r=[bpsS[ps], bbuf], w=[bPT[pt]])
                out.append(pt)
            return out

        def stage_b(n_, pts):
            lk, kb, extra, bidx, vl, vb = steps[n_]
            for hi_, p in enumerate(heads):
                k.op("pe", lambda e: e.matmul(psO[hi_][0:65, :], lhsT=vl, rhs=PT[pts[hi_]][:], start=(n_ == 0),
                                               stop=(n_ == nst - 1)), r=[vb, bPT[pts[hi_]]], w=[bpsO[hi_]])

        for n_ in range(nst):
            staged.append((n_, stage_a(n_)))
            if n_ == min(1, nst - 1):
                while deferred:
                    deferred.pop(0)()
            if hook is not None:
                hook(n_, nst)
            if len(staged) > 1:
                stage_b(*staged.pop(0))
        while staged:
            stage_b(*staged.pop(0))
        deferred.append(lambda: epilogue(qb, heads, br, first))

    def epilogue(qb, heads, br, first):
        for hi_, p in enumerate(heads):
            po = psO[hi_]
            k.op("dve", lambda e: e.tensor_scalar(out=RL[64:65, :], in0=po[64:65, :], scalar1=1e-30, scalar2=None,
                                                   op0=ALU.max), r=[bpsO[hi_]], w=[bRL])
            k.op("dve", lambda e: e.reciprocal(out=RL[64:65, :], in_=RL[64:65, :]), r=[bRL], w=[bRL])
            k.op("act", lambda e: e.copy(out=OBF[:], in_=po[0:64, :]), r=[bpsO[hi_]], w=[bOBF])
            k.op("pe", lambda e: e.matmul(psX[0][0:64, :], lhsT=ONESF[64:65, :], rhs=RL[64:65, :], start=True,
                                           stop=True), r=[bONESF, bRL], w=[bpsX[0]])
            k.op("dve", lambda e: e.tensor_tensor(out=TT[:], in0=OBF[:], in1=psX[0][0:64, :], op=ALU.mult),
                 r=[bOBF, bpsX[0]], w=[bTT])
            gr = p * 3 + br
            k.op("pe", lambda e: e.matmul(psX[0][0:64, :], lhsT=SELG[:, gr, :], rhs=GT[qb][:, :], start=True,
                                           stop=True), r=[bSELG, bGT[qb]], w=[bpsX[0]])
            if first:
                k.op("dve", lambda e: e.tensor_tensor(out=ACC[hi_][:], in0=TT[:], in1=psX[0][0:64, :], op=ALU.mult),
                     r=[bTT, bpsX[0]], w=[bACC[hi_]])
            else:
                k.op("dve", lambda e: e.tensor_tensor(out=TT[:], in0=TT[:], in1=psX[0][0:64, :], op=ALU.mult),
                     r=[bTT, bpsX[0]], w=[bTT])
                k.op("dve", lambda e: e.tensor_tensor(out=ACC[hi_][:], in0=ACC[hi_][:], in1=TT[:], op=ALU.add),
                     r=[bTT, bACC[hi_]], w=[bACC[hi_]])

    def load_tile(I):
        qb = load_q(I)
        ld_fm(lambda a, b: GT[qb][:, a:b], "gt", I * 512, (I + 1) * 512, [bGT[qb]])
        wb = I % 2
        jlo = max(0, 4 * I - 4)
        lo = jlo - (4 * I - 4)
        ld_fm(lambda a, b: KW[wb][0:64, lo * 128 + a:lo * 128 + b], "kw", jlo * 128, (4 * I + 4) * 128,
              [bKW[wb]])
        ld_tok(lambda a, b: VW[wb][:, lo + a:lo + b, 0:64], "vw", jlo * 128, (4 * I + 4) * 128, [bVW[wb]])
        return qb

    if getattr(io, "after_prologue", None) is not None:
        io.after_prologue()
    deferred = []
    qb_next = load_tile(0)
    for qi in range(4):
        importance_block(0, qb_next, 0, qi)
    for I in range(NQ):
        qb = qb_next
        mb = I % 2
        wb = I % 2
        jlo = max(0, 4 * I - 4)
        pend = []
        while deferred:
            deferred.pop(0)()
        if I + 1 < NQ:
            qb_next = load_tile(I + 1)
            for qi in range(4):
                pend += importance_units(I + 1, qb_next, (I + 1) % 2, qi)
        gap = max(1, (2 * (4 * I + 4)) // 22)
        half = [10]

        def hook(n_, nst):
            if pend and half[0] > 0 and n_ >= 1 and (n_ - 1) % gap == 0:
                pend.pop(0)()
                half[0] -= 1

        for hp in range(2):
            heads = (2 * hp, 2 * hp + 1)
            steps = []
            for jb in range(NCB):
                dd = I - 4 * jb
                if dd < 0:
                    continue
                extra = []
                if dd <= 4:
                    extra.append((BIGI[:], CM[:, dd, :], [bBIGI, bCM]))
                steps.append((KCMP[:, jb * 128:(jb + 1) * 128], [bKCMP], extra, dd + 28, VC[:, jb, :], bVC))
            branch(I, qb, heads, 0, steps, lambda p, ix: (BTCc[:, p, ix:ix + 1], bBTCc), True)
            steps = []
            for jb in range(4 * I + 4):
                extra = [(ESEL[:, jb % 64, :], MT[mb][:, jb // 64, :], [bESEL, bMT[mb]])]
                if jb >= 4 * I:
                    extra.append((BIGI[:], DM[:, jb - 4 * I, :], [bBIGI, bDM]))
                steps.append((KS[:, jb * 128:(jb + 1) * 128], [bKS], extra, 4 * I - jb + 3, VS[:, jb, :], bVS))
            half[0] = 10
            branch(I, qb, heads, 1, steps, lambda p, ix: (BTCs[:, p, ix:ix + 1], bBTCs), False, hook)
            steps = []
            for jb in range(jlo, 4 * I + 4):
                lw = jb - (4 * I - 4)
                if lw < 4:
                    extra = [(BIGI[:], WM[:, lw, :], [bBIGI, bWM])]
                else:
                    extra = [(BIGI[:], DM[:, lw - 4, :], [bBIGI, bDM])]
                steps.append((KW[wb][:, lw * 128:(lw + 1) * 128], [bKW[wb]], extra, 4 * I - jb + 3,
                              VW[wb][:, lw, :], bVW[wb]))
            branch(I, qb, heads, 2, steps, lambda p, ix: (BTCw[:, p, ix:ix + 1], bBTCw), False)
            def emit_out(I=I, hp=hp, heads=heads):
                n0 = len(k.stores)
                for hi_, p in enumerate(heads):
                    ob = st["out"] % 2
                    st["out"] += 1
                    k.op("act", lambda e: e.copy(out=OUTB[ob][:], in_=ACC[hi_][:]), r=[bACC[hi_]], w=[bOUTB[ob]])
                    k.store("sp", io.out(p, I), OUTB[ob][:], r=[bOUTB[ob]])
                if getattr(io, "out_done", None) is not None:
                    io.out_done(I, hp, k.stores[n0:])
            deferred.append(emit_out)
        while pend:
            pend.pop(0)()
    while deferred:
        deferred.pop(0)()


def k2a_consts(S):
    sl16 = alibi_slopes(16)
    A_, B_ = ab_tables()
    c = {"dm": dm_table(), "wm": wm_table(), "cm": cm_table(), "bigi": bigi_table(),
         "idb": np.eye(128).astype(np.float32), "esel": esel_table(), "atab": A_, "btab": B_,
         "selg": selg_table()}
    per_g = []
    for g in range(4):
        ms = sl16[g * 4:(g + 1) * 4]
        per_g.append({
            "bt": np.ascontiguousarray(np.stack([bt_table(m) for m in ms], 1)),
            "btc": np.ascontiguousarray(np.stack([btc_table(m) for m in ms], 1)),
            "itab": np.ascontiguousarray(np.stack([itab_table(m) for m in ms], 1)),
            "qaug": np.stack([q_aug_rows(m, 512) for m in ms], 0),
        })
    return c, per_g


def prep_k2a(pT, vt, g, consts, per_g, wts, S):
    d = dict(consts)
    d.update(per_g[g])
    d.update(wts)
    d["qa"] = np.ascontiguousarray(pT[g * 256:(g + 1) * 256].reshape(4, 64, S))
    d["kca"] = np.ascontiguousarray(pT[1024 + g * 64:1024 + (g + 1) * 64])
    d["vca"] = np.ascontiguousarray(pT[1280 + g * 64:1280 + (g + 1) * 64])
    d["ksa"] = np.ascontiguousarray(pT[1536 + g * 64:1536 + (g + 1) * 64])
    d["kwa"] = np.ascontiguousarray(pT[2048 + g * 64:2048 + (g + 1) * 64])
    d["vs"] = np.ascontiguousarray(vt[:, g * 64:(g + 1) * 64])
    d["vw"] = np.ascontiguousarray(vt[:, 256 + g * 64:256 + (g + 1) * 64])
    d["gt"] = np.ascontiguousarray(pT[2560 + g * 12:2560 + (g + 1) * 12])
    return d


def k2a_weights(pek, w1k, w2k, pev, w1v, w2v):
    f = lambda a: np.ascontiguousarray(np.asarray(a, np.float32))
    return {"pek": f(np.asarray(pek).T), "pev": f(np.asarray(pev).T), "w1k": f(w1k), "w2k": f(w2k),
            "w1v": f(w1v), "w2v": f(w2v)}


TPC = BATCH * SEQ // NCORES
CPB = NCORES // BATCH


def _run(nc, in_maps):
    res = run_bass_kernel_spmd(nc, in_maps, core_ids=list(range(NCORES)))
    return res.results


def _with_halo(full_T, c):
    b, j = divmod(c, CPB)
    a = full_T[b]
    out = np.zeros((a.shape[0], 2 + TPC), a.dtype)
    lo = j * TPC
    if j > 0:
        out[:, 0:2] = a[:, lo - 2:lo]
    out[:, 2:] = a[:, lo:lo + TPC]
    return out


def kernel_unfused(x, norm_mix_g, norm_ffn_g, final_norm_g,
           nsa_w_in, nsa_cmp_k_pe, nsa_cmp_k_w1, nsa_cmp_k_w2,
           nsa_cmp_v_pe, nsa_cmp_v_w1, nsa_cmp_v_w2, nsa_w_out,
           diff_w_in, diff_lam_q1, diff_lam_k1, diff_lam_q2, diff_lam_k2,
           diff_subln_g, diff_w_out,
           ffn_w_up, ffn_conv_w, ffn_conv_b, ffn_w_down):
    f32 = lambda a: np.ascontiguousarray(np.asarray(a, dtype=np.float32))
    x = f32(x)
    S = SEQ
    xT = [np.ascontiguousarray(x[b].T) for b in range(BATCH)]

    def tok_shards(full_T):
        return [np.ascontiguousarray(full_T[c // CPB][:, (c % CPB) * TPC:(c % CPB + 1) * TPC])
                for c in range(NCORES)]

    fm = [(0, 1024, 0.125, AF.Copy), (1024, 2560, 1.0, AF.Copy), (2560, 2608, 1.0, AF.Sigmoid)]
    tok = [(1792, 2048), (2304, 2560)]
    nc1 = build_k1(TPC, 2608, fm, tok)
    gl = g_layout(norm_mix_g[0])
    w = f32(nsa_w_in[0])
    r1 = _run(nc1, [{"xT": s, "g": gl, "w": w} for s in tok_shards(xT)])
    pT = [np.concatenate([r1[b * CPB + j]["projT"] for j in range(CPB)], axis=1) for b in range(BATCH)]
    vt = [np.concatenate([r1[b * CPB + j]["vtok"] for j in range(CPB)], axis=0) for b in range(BATCH)]
    del r1
    consts, per_g = k2a_consts(S)
    wts = k2a_weights(nsa_cmp_k_pe[0], nsa_cmp_k_w1[0], nsa_cmp_k_w2[0],
                      nsa_cmp_v_pe[0], nsa_cmp_v_w1[0], nsa_cmp_v_w2[0])
    nc2 = build_k2a(S)
    r2 = _run(nc2, [prep_k2a(pT[c // CPB], vt[c // CPB], c % CPB, consts, per_g, wts, S)
                    for c in range(NCORES)])
    aT = [np.concatenate([r2[b * CPB + g]["oT"] for g in range(CPB)], axis=0) for b in range(BATCH)]
    del r2, pT, vt
    cw, cb = conv_layouts(ffn_conv_w[0], ffn_conv_b[0])
    nc3 = build_k3(TPC, False)
    ins = [{"xT": _with_halo(xT, c), "aT": _with_halo(aT, c), "wo": f32(nsa_w_out[0]),
            "wu": f32(ffn_w_up[0]), "wd": f32(ffn_w_down[0]), "cw": cw, "cb": cb,
            "g": g_layout(norm_ffn_g[0]), "gf": g_layout(final_norm_g)} for c in range(NCORES)]
    r3 = _run(nc3, ins)
    xT = [np.concatenate([r3[b * CPB + j]["yT"] for j in range(CPB)], axis=1) for b in range(BATCH)]
    del r3, ins, aT

    lambda_init = 0.8 - 0.6 * float(np.exp(-0.3 * 1))
    fm = [(0, 1024, 0.125, AF.Copy), (1024, 2048, 1.0, AF.Copy)]
    tok = [(2048, 2560), (2560, 3072)]
    nc4 = build_k1(TPC, 3072, fm, tok)
    gl = g_layout(norm_mix_g[1])
    w = f32(diff_w_in[0])
    r4 = _run(nc4, [{"xT": s, "g": gl, "w": w} for s in tok_shards(xT)])
    pT = [np.concatenate([r4[b * CPB + j]["projT"] for j in range(CPB)], axis=1) for b in range(BATCH)]
    vt = [np.concatenate([r4[b * CPB + j]["vtok"] for j in range(CPB)], axis=0) for b in range(BATCH)]
    del r4
    sl8 = alibi_slopes(8)
    dmt, bigit = dm_table(), bigi_table()
    lam = np.stack([f32(diff_lam_q1[0]), f32(diff_lam_k1[0]), f32(diff_lam_q2[0]), f32(diff_lam_k2[0])], 0)
    lam = np.ascontiguousarray(np.broadcast_to(lam[None], (128, 4, 64)))
    sg = f32(diff_subln_g[0]).reshape(128, 1)
    ins = []
    for c in range(NCORES):
        b, hp = divmod(c, CPB)
        qa = np.ascontiguousarray(pT[b][hp * 256:(hp + 1) * 256].reshape(4, 64, S))
        ka = np.ascontiguousarray(pT[b][1024 + hp * 256:1024 + (hp + 1) * 256].reshape(4, 64, S))
        bt = np.stack([bt_table(sl8[hp * 2 + hh]) for hh in range(2)], 0)
        qaug = np.stack([q_aug_rows(sl8[hp * 2 + hh], 512) for hh in range(2)], 0)
        ins.append({"qa": qa, "ka": ka, "qaug": qaug,
                    "v": np.ascontiguousarray(vt[b][:, hp * 256:(hp + 1) * 256]),
                    "bt": bt, "dm": dmt, "bigi": bigit, "lam": lam, "sg": sg})
    nc5 = build_k2b(S, lambda_init)
    r5 = _run(nc5, ins)
    aT = [np.concatenate([r5[b * CPB + hp]["oT"] for hp in range(CPB)], axis=0) for b in range(BATCH)]
    del r5, ins, pT, vt
    cw, cb = conv_layouts(ffn_conv_w[1], ffn_conv_b[1])
    nc6 = build_k3(TPC, True)
    ins = [{"xT": _with_halo(xT, c), "aT": _with_halo(aT, c), "wo": f32(diff_w_out[0]),
            "wu": f32(ffn_w_up[1]), "wd": f32(ffn_w_down[1]), "cw": cw, "cb": cb,
            "g": g_layout(norm_ffn_g[1]), "gf": g_layout(final_norm_g)} for c in range(NCORES)]
    r6 = _run(nc6, ins)
    out = np.empty((BATCH, SEQ, D_MODEL), np.float32)
    for c in range(NCORES):
        b, j = divmod(c, CPB)
        out[b, j * TPC:(j + 1) * TPC, :] = r6[c]["yT"].T
    return out

TPC = BATCH * SEQ // NCORES
CPB = NCORES // BATCH
GROUPS = [[0, 1, 2, 3], [4, 5, 6, 7]]
RB1 = 640
RB4 = 512


def _chunks(t0, t1):
    j = t0 // TPC
    while j * TPC < t1:
        lo, hi = max(t0, j * TPC), min(t1, (j + 1) * TPC)
        yield j, lo, hi
        j += 1


class IOK2aFused:
    ROW = {"q0": 0, "q1": 64, "q2": 128, "q3": 192, "kc": 256, "vc": 320, "ks": 384, "kw": 448, "gt": 512}

    def __init__(self, L1F, L1T, SND2, st, k=None, G2=None):
        self.WF, self.WT, self.SND2, self.st = L1F, L1T, SND2, st
        self.k, self.G2, self.pend = k, G2, {}

    def fm(self, name, t0, t1):
        row0 = self.ROW[name]
        nr = 12 if name == "gt" else 64
        rr = lambda j: ((row0 // 128) * 4 + j) * 128 + row0 % 128
        return [(self.WF[rr(j):rr(j) + nr, lo - j * TPC:hi - j * TPC], lo, hi)
                for (j, lo, hi) in _chunks(t0, t1)]

    def tok(self, name, t0, t1):
        c0 = 0 if name == "vs" else 64
        return [(self.WT[lo:hi, c0:c0 + 64], lo, hi) for (j, lo, hi) in _chunks(t0, t1)]

    def out(self, p, I):
        j, i8 = divmod(I, 8)
        return self.SND2[j * 256 + p * 64:j * 256 + (p + 1) * 64, i8 * 512:(i8 + 1) * 512]

    def out_done(self, I, hp, toks):
        self.pend.setdefault(hp, []).extend(toks)
        if I % 8 == 7:
            i = (I // 8) * 2 + hp
            self.k.collective_async(self.SND2[i * 128:(i + 1) * 128, :], self.G2[i * 512:(i + 1) * 512, :],
                                    GROUPS, self.pend.pop(hp))


class IOK2bFused:
    def __init__(self, L4F, L4T, SND5, d, k=None, G5=None):
        self.WF, self.WT, self.SND5 = L4F, L4T, SND5
        self.ctx, self.G5, self.pend = k, G5, {}
        self.qaug, self.bt_in, self.dm_in, self.bigi_in, self.lam_in, self.sg_in = (
            d["b_qaug"], d["b_bt"], d["a_dm"], d["a_bigi"], d["b_lam"], d["b_sg"])

    def _fm(self, row0, t0, t1):
        rr = lambda j: ((row0 // 128) * 4 + j) * 128 + row0 % 128
        return [(self.WF[rr(j):rr(j) + 64, lo - j * TPC:hi - j * TPC], lo, hi)
                for (j, lo, hi) in _chunks(t0, t1)]

    def q(self, hh, c, t0, t1):
        return self._fm((hh * 2 + c) * 64, t0, t1)

    def k(self, hh, c, t0, t1):
        return self._fm(256 + (hh * 2 + c) * 64, t0, t1)

    def v(self, hh, t0, t1):
        out = []
        for (j, lo, hi) in _chunks(t0, t1):
            a = lo
            while a < hi:
                tl = a - j * TPC
                b = min(hi, j * TPC + (tl // 2048 + 1) * 2048)
                row = ((tl // 2048) * 4 + j) * 2048 + tl % 2048
                out.append((self.WT[row:row + (b - a), hh * 128:(hh + 1) * 128], a, b))
                a = b
        return out

    def out(self, hh, I):
        j, i8 = divmod(I, 8)
        return self.SND5[j * 256 + hh * 128:j * 256 + (hh + 1) * 128, i8 * 512:(i8 + 1) * 512]

    def out_done(self, I, hh, toks):
        self.pend.setdefault(hh, []).extend(toks)
        if I % 8 == 7:
            i = (I // 8) * 2 + hh
            self.ctx.collective_async(self.SND5[i * 128:(i + 1) * 128, :], self.G5[i * 512:(i + 1) * 512, :],
                                    GROUPS, self.pend.pop(hh))


class IOK3Fused:
    def __init__(self, layer, d, xh0, X2, LA, LHA, LH, SND3, yT, SNDH=None, k=None, GH=None):
        self.layer, self.xh0, self.X2, self.SND3, self.yT = layer, xh0, X2, SND3, yT
        self.WA = LA.rearrange("(h g p) t -> h p g t", h=2, g=4, p=128)
        self.WHA = LHA.rearrange("(h g p) t -> h p g t", h=2, g=4, p=128)
        self.WH = LH.rearrange("(dc p) t -> p dc t", p=128)
        sfx = str(layer)
        self.wo_in, self.wu_in, self.wd_in = d["wo" + sfx], d["wu" + sfx], d["wd" + sfx]
        self.cw_in, self.cb_in, self.g_in, self.gf_in = d["cw" + sfx], d["cb" + sfx], d["g_ffn" + sfx], d["g_fin"]
        self.halo_scale = d["m0"]
        self.tail_dst = (lambda oc: SND3[oc * 128:(oc + 1) * 128, 0:2]) if layer == 0 else None
        if layer == 0:
            self.gf_in = d["g_mix1"]
            self.h_dst = lambda oc, tcol, n: SNDH[(tcol // 512) * 1024 + oc * 128:(tcol // 512) * 1024 + (oc + 1) * 128,
                                                  tcol % 512:tcol % 512 + n]
            self.k, self.GH, self.SNDH, self.pend = k, GH, SNDH, []

    def tile_done(self, it, toks, N=256):
        if self.layer != 0:
            return
        self.pend.extend(toks)
        per = 512 // N
        if it % per == per - 1:
            c = it // per
            self.k.collective_async(self.SNDH[c * 1024:(c + 1) * 1024, :], self.GH[c * 4096:(c + 1) * 4096, :],
                                    GROUPS, self.pend)
            self.pend = []

    def x_src(self, col0, n):
        if self.layer == 0:
            return self.xh0.rearrange("(dc p) t -> p dc t", p=128)[:, :, col0:col0 + n]
        if col0 == 0:
            assert n == 2
            return self.WH
        return self.X2.rearrange("(dc p) t -> p dc t", p=128)[:, :, col0 - 2:col0 - 2 + n]

    def a_src(self, col0, n):
        if col0 == 0:
            assert n == 2
            return [(h, 2, self.WHA[h]) for h in range(2)]
        return [(h, 2, self.WA[h][:, :, col0 - 2:col0 - 2 + n]) for h in range(2)]

    def y_dst(self, oc, tcol, n):
        dst = self.X2 if self.layer == 0 else self.yT
        return dst[oc * 128:(oc + 1) * 128, tcol:tcol + n]

    def x1_dst(self, dc, tcol, n):
        return self.X1D[dc * 128:(dc + 1) * 128, tcol:tcol + n]

    def x1_src(self, tcol, n):
        return self.X1D.rearrange("(dc p) t -> p dc t", p=128)[:, :, tcol:tcol + n]

    def aff_dst(self, gch, tcol, n):
        return self.AFFD[gch * 128:(gch + 1) * 128, tcol:tcol + n]

    def aff_src(self, tcol, n):
        return self.AFFD.rearrange("(kc p) t -> p kc t", p=128)[:, :, tcol:tcol + n]


FUSED_INPUTS = [("xh0", [D_MODEL, 2 + TPC], F32), ("w_in0g", [D_MODEL, 652], F32), ("w_in1g", [D_MODEL, 768], F32),
                ("g_mix0", [128, 8], F32), ("g_mix1", [128, 8], F32), ("g_ffn0", [128, 8], F32),
                ("g_ffn1", [128, 8], F32), ("g_fin", [128, 8], F32), ("m0", [128, 1], F32),
                ("b_qaug", [2, 3, 512], BF16), ("b_bt", [2, 128, 128], F32), ("b_lam", [128, 4, 64], F32),
                ("b_sg", [128, 1], F32)]
for _l in range(2):
    FUSED_INPUTS += [("wo%d" % _l, [D_MODEL, D_MODEL], F32), ("wu%d" % _l, [D_MODEL, 2 * D_FF], F32),
                     ("wd%d" % _l, [D_FF, D_MODEL], F32), ("cw%d" % _l, [128, 44, 3], F32),
                     ("cb%d" % _l, [128, 44], F32)]
FUSED_INPUTS += [("a_" + n, sh, dt) for (n, sh, dt) in K2A_STATIC]


def build_fused(lambda_init, upto=None):
    nc = new_nc()
    S, T = SEQ, TPC
    d = {n: nc.dram_tensor(n, sh, dt, kind="ExternalInput").ap() for (n, sh, dt) in FUSED_INPUTS}
    yT = nc.dram_tensor("yT", [D_MODEL, T], F32, kind="ExternalOutput").ap()
    sc = lambda n, sh, dt: nc.dram_tensor(n, sh, dt).ap()
    SND1F = sc("SND1F", [4 * RB1, T], BF16); G1F = sc("G1F", [16 * RB1, T], BF16)
    SND1T = sc("SND1T", [4 * T, 128], BF16); G1T = sc("G1T", [16 * T, 128], BF16)
    SND2 = sc("SND2", [4 * 256, T], BF16); G2 = sc("G2", [16 * 256, T], BF16)
    X2 = sc("X2", [D_MODEL, T], F32)
    SND3 = sc("SND3", [D_MODEL, 2], F32); G3 = sc("G3", [4 * D_MODEL, 2], F32)
    SND4F = sc("SND4F", [4 * RB4, T], BF16); G4F = sc("G4F", [16 * RB4, T], BF16)
    SND4T = sc("SND4T", [4 * T, 256], BF16); G4T = sc("G4T", [16 * T, 256], BF16)
    SND5 = sc("SND5", [4 * 256, T], BF16); G5 = sc("G5", [16 * 256, T], BF16)
    L1F = sc("L1F", [4 * RB1, T], BF16); L1T = sc("L1T", [4 * T, 128], BF16)
    L4F = sc("L4F", [4 * RB4, T], BF16); L4T = sc("L4T", [4 * T, 256], BF16)
    LA2 = sc("LA2", [1024, T], BF16); LA5 = sc("LA5", [1024, T], BF16)
    LHA2 = sc("LHA2", [1024, 2], BF16); LHA5 = sc("LHA5", [1024, 2], BF16)
    LH = sc("LH", [D_MODEL, 2], F32)
    k = Ctx(nc)
    r = nc.sync.partition_id() % 4

    dbg = nc.dram_tensor("dbg", [2560, 4096], BF16, kind="ExternalOutput").ap() if upto else None

    def stop_here(tag, src_ap):
        if upto != tag:
            return False
        k.barrier()
        db = Buf("dbg")
        k.dma("sp", dbg[0:src_ap.shape[0], 0:src_ap.shape[1]], src_ap, w=[db], own=db)
        k.stores.append(db.w)
        k.finish()
        return True

    def gather_rows(SND, G, rows):
        n = SND.shape[0] // rows
        k.all_gather_chunks([(SND[i * rows:(i + 1) * rows, :], G[i * 4 * rows:(i + 1) * 4 * rows, :])
                             for i in range(n)], GROUPS)

    def extract_window(G, L):
        n = L.shape[0] * L.shape[1]
        a = n // 16384
        assert a * 16384 == n and G.shape[0] * G.shape[1] == 4 * n
        gf = G.rearrange("r t -> (r t)").rearrange("(q a l) -> q a l", q=4, a=a, l=16384)
        lf = L.rearrange("r t -> (r t)").rearrange("(a l) -> a l", a=a, l=16384)
        bX = Buf("extract")
        k.dma("sp", lf, gf[bass.ds(r, 1)].rearrange("o a l -> (o a) l"), w=[bX], own=bX)

    def extract_att(G, L, LHA_):
        extract_window(G, L)
        gq = G.rearrange("(q x) t -> q x t", q=4)
        bX2 = Buf("extract")
        with nc.allow_non_contiguous_dma(reason="2-column conv halo"):
            k.dma("sp", LHA_, gq[bass.ds((r + 3) % 4, 1), :, T - 2:T].rearrange("o x t -> (o x) t"), w=[bX2],
                  own=bX2)

    SNDH = sc("SNDH", [8 * D_MODEL, 512], BF16)
    GH = sc("GH", [32 * D_MODEL, 512], BF16)
    GHv = GH.rearrange("(it j dc p) t -> p it j dc t", it=8, j=4, dc=8, p=128)

    def ht_src(t0):
        j, tl = divmod(t0, T)
        return GHv[:, tl // 512, j, :, :]

    k.begin_phase("A_")
    emit_norm(k, nc, T, d["xh0"].rearrange("(dc p) t -> p dc t", p=128)[:, :, 2:2 + T], d["g_mix0"],
              lambda dc, t0: SNDH[(t0 // 512) * 1024 + dc * 128:(t0 // 512) * 1024 + (dc + 1) * 128, 0:512],
              lambda it, toks: k.collective_async(SNDH[it * 1024:(it + 1) * 1024, :],
                                                  GH[it * 4096:(it + 1) * 4096, :], GROUPS, toks))
    k.end_phase()

    k.begin_phase("P_")

    def fm_route_a(c0, c1, t0):
        j, tl = divmod(t0, T)
        row = ((c0 // 128) * 4 + j) * 128 + c0 % 128
        return [(0, c1 - c0, L1F[row:row + c1 - c0, tl:tl + 512])]

    def tok_route_a(c0, c1, t0, tb):
        return [(0, 128, L1T[t0 + tb * 128:t0 + (tb + 1) * 128, 0:128])]

    emit_proj(k, nc, S, 652, ht_src, d["w_in0g"],
              [(0, 256, 0.125, AF.Copy), (256, 512, 1.0, AF.Copy), (512, 524, 1.0, AF.Sigmoid)], fm_route_a,
              [(524, 652)], tok_route_a)
    k.end_phase()

    WB = [(sc("WOB%d" % l, [D_MODEL, D_MODEL], BF16), sc("WUB%d" % l, [D_MODEL, 2 * D_FF], BF16),
           sc("WDB%d" % l, [D_FF, D_MODEL], BF16)) for l in range(2)]

    def precast_weights():
        for l in range(2):
            for (src, dst) in ((d["wo%d" % l], WB[l][0]), (d["wu%d" % l], WB[l][1]), (d["wd%d" % l], WB[l][2])):
                R_, C_ = src.shape
                bw = Buf("wcast")
                for r0 in range(0, R_, 128):
                    for c0 in range(0, C_, 1024):
                        c1 = min(C_, c0 + 1024)
                        k.dma("pool", dst[r0:r0 + 128, c0:c1], src[r0:r0 + 128, c0:c1], w=[bw], own=bw)

    k.begin_phase("B_")
    io_b = IOK2aFused(L1F, L1T, SND2, {n: d["a_" + n] for (n, _, _) in K2A_STATIC}, k, G2)
    io_b.after_prologue = precast_weights
    emit_k2a(k, nc, S, io_b)
    k.end_phase()
    extract_att(G2, LA2, LHA2)
    k.barrier()
    if stop_here("B2", LA2) or stop_here("B2H", LHA2):
        return nc

    k.begin_phase("C_")
    X1D = sc("X1D", [D_MODEL, T], F32)
    AFFD = sc("AFFD", [D_FF, T], BF16)
    io_c = IOK3Fused(0, d, d["xh0"], X2, LA2, LHA2, LH, SND3, yT, SNDH, k, GH)
    io_c.X1D, io_c.AFFD = X1D, AFFD
    io_c.wb = WB[0]
    emit_k3(k, nc, T, False, io_c, 512, "a")
    k.end_phase()
    k.begin_phase("Cb_")
    emit_k3(k, nc, T, False, io_c, 512, "b")
    k.end_phase()
    if upto == "C":
        k.barrier()
        k.finish()
        return nc
    k.all_gather_chunks([(SND3, G3)], GROUPS)
    bXh = Buf("extract")
    k.dma("sp", LH, G3[bass.ds(((r + 3) % 4) * D_MODEL, D_MODEL), :], w=[bXh], own=bXh)

    k.begin_phase("D_")

    def fm_route_d(c0, c1, t0):
        j, tl = divmod(t0, T)
        row = ((c0 // 128) * 4 + j) * 128
        return [(0, 128, L4F[row:row + 128, tl:tl + 512])]

    def tok_route_d(c0, c1, t0, tb):
        j, tl = divmod(t0 + tb * 128, T)
        row = ((tl // 2048) * 4 + j) * 2048 + tl % 2048
        return [(0, 256, L4T[row:row + 128, 0:256])]

    emit_proj(k, nc, S, 768, ht_src, d["w_in1g"],
              [(0, 256, 0.125, AF.Copy), (256, 512, 1.0, AF.Copy)], fm_route_d, [(512, 768)], tok_route_d)
    k.end_phase()

    k.begin_phase("E_")
    emit_k2b(k, nc, S, lambda_init, IOK2bFused(L4F, L4T, SND5, d, k, G5))
    k.end_phase()
    if upto == "E":
        k.barrier()
        k.finish()
        return nc
    extract_att(G5, LA5, LHA5)
    k.barrier()

    k.begin_phase("F_")
    io_f = IOK3Fused(1, d, d["xh0"], X2, LA5, LHA5, LH, SND3, yT)
    io_f.X1D, io_f.AFFD = X1D, AFFD
    io_f.wb = WB[1]
    emit_k3(k, nc, T, True, io_f, 512, "a")
    k.end_phase()
    k.begin_phase("Fb_")
    emit_k3(k, nc, T, True, io_f, 512, "b")
    k.barrier()
    k.finish()
    print("[fused] semaphores used:", k.nsem)
    return nc


def fused_inputs(x, norm_mix_g, norm_ffn_g, final_norm_g,
                 nsa_w_in, nsa_cmp_k_pe, nsa_cmp_k_w1, nsa_cmp_k_w2,
                 nsa_cmp_v_pe, nsa_cmp_v_w1, nsa_cmp_v_w2, nsa_w_out,
                 diff_w_in, diff_lam_q1, diff_lam_k1, diff_lam_q2, diff_lam_k2,
                 diff_subln_g, diff_w_out,
                 ffn_w_up, ffn_conv_w, ffn_conv_b, ffn_w_down):
    f32 = lambda a: np.ascontiguousarray(np.asarray(a, dtype=np.float32))
    x = f32(x)
    xT = [np.ascontiguousarray(x[b].T) for b in range(BATCH)]
    consts, per_g = k2a_consts(SEQ)
    wts = k2a_weights(nsa_cmp_k_pe[0], nsa_cmp_k_w1[0], nsa_cmp_k_w2[0],
                      nsa_cmp_v_pe[0], nsa_cmp_v_w1[0], nsa_cmp_v_w2[0])
    sl8 = alibi_slopes(8)
    lam = np.stack([f32(diff_lam_q1[0]), f32(diff_lam_k1[0]), f32(diff_lam_q2[0]), f32(diff_lam_k2[0])], 0)
    lam = np.ascontiguousarray(np.broadcast_to(lam[None], (128, 4, 64)))
    common = {"g_mix0": g_layout(norm_mix_g[0]), "g_mix1": g_layout(norm_mix_g[1]),
              "g_ffn0": g_layout(norm_ffn_g[0]), "g_ffn1": g_layout(norm_ffn_g[1]),
              "g_fin": g_layout(final_norm_g), "b_lam": lam, "b_sg": f32(diff_subln_g[0]).reshape(128, 1),
              "wo0": f32(nsa_w_out[0]), "wo1": f32(diff_w_out[0])}
    for l in range(2):
        cw, cb = conv_layouts(ffn_conv_w[l], ffn_conv_b[l])
        common.update({"wu%d" % l: f32(ffn_w_up[l]), "wd%d" % l: f32(ffn_w_down[l]), "cw%d" % l: cw,
                       "cb%d" % l: cb})
    for n_, v_ in list(consts.items()) + list(wts.items()):
        common["a_" + n_] = v_
    in_maps = []
    for c in range(NCORES):
        r = c % CPB
        m = dict(common)
        for n_, v_ in per_g[r].items():
            m["a_" + n_] = v_
        m["xh0"] = _with_halo(xT, c)
        w0, w1 = f32(nsa_w_in[0]), f32(diff_w_in[0])
        cols0 = (list(range(r * 256, (r + 1) * 256)) + [o_ + r * 64 + i for o_ in (1024, 1280, 1536, 2048)
                                                         for i in range(64)]
                 + list(range(2560 + r * 12, 2560 + (r + 1) * 12))
                 + [o_ + r * 64 + i for o_ in (1792, 2304) for i in range(64)])
        m["w_in0g"] = np.ascontiguousarray(w0[:, cols0])
        cols1 = [o_ + r * 256 + i for o_ in (0, 1024, 2048) for i in range(256)]
        m["w_in1g"] = np.ascontiguousarray(w1[:, cols1])
        m["m0"] = np.full((128, 1), 1.0 if r > 0 else 0.0, np.float32)
        m["b_qaug"] = np.stack([q_aug_rows(sl8[r * 2 + hh], 512) for hh in range(2)], 0)
        m["b_bt"] = np.stack([bt_table(sl8[r * 2 + hh]) for hh in range(2)], 0)
        in_maps.append(m)
    return in_maps


def kernel(**inputs):
    lambda_init = 0.8 - 0.6 * float(np.exp(-0.3 * 1))
    nc = build_fused(lambda_init)
    in_maps = fused_inputs(**inputs)
    res = run_bass_kernel_spmd(nc, in_maps, core_ids=list(range(NCORES))).results
    out = np.empty((BATCH, SEQ, D_MODEL), np.float32)
    for c in range(NCORES):
        b, j = divmod(c, CPB)
        out[b, j * TPC:(j + 1) * TPC, :] = res[c]["yT"].T
    return out
```

```python
import numpy as np
import ml_dtypes
import concourse.bass as bass
import concourse.mybir as mybir
from concourse.bass_utils import run_bass_kernel_spmd

F32 = mybir.dt.float32
BF16 = mybir.dt.bfloat16
AF = mybir.ActivationFunctionType
ALU = mybir.AluOpType
AX = mybir.AxisListType
NPBF = ml_dtypes.bfloat16

NCORES = 8
D_MODEL = 1024
BATCH = 2
SEQ = 16384
EPS = 1e-6


class Buf:
    __slots__ = ("name", "w", "r", "sem", "cnt")

    def __init__(self, name):
        self.name = name
        self.w = None
        self.r = {}
        self.sem = None
        self.cnt = 0


class Ctx:
    SEM_ROLL = 30000

    def __init__(self, nc, same_engine_sync=True):
        self.nc = nc
        self.same = same_engine_sync
        self.nsem = 0
        self.E = {}
        for n, e in [("pe", nc.tensor), ("act", nc.scalar), ("dve", nc.vector),
                     ("pool", nc.gpsimd), ("sp", nc.sync)]:
            self.E[n] = {"eng": e, "sem": self._newsem("e_" + n), "cnt": 0, "waited": {}}
        self.stores = []
        self.es = None
        self.pfx = ""
        self.phase_bufs = []
        self.sempool = []
        self.ccbuf = None
        self.dummy = self.nc.alloc_sbuf_tensor("bar_dummy", [128, 8], F32)
        self.bdummy = Buf("bar_dummy")

    def _newsem(self, name):
        self.nsem += 1
        s = self.nc.alloc_semaphore("%s_%d" % (name, self.nsem))
        return (s, self.nsem)

    def buf(self, name):
        return Buf(name)

    def begin_phase(self, pfx):
        from contextlib import ExitStack
        self.es = ExitStack()
        self.pfx = pfx

    def sb(self, name, shape, dt):
        return self.es.enter_context(self.nc.sbuf_tensor(self.pfx + name, shape, dt))

    def ps(self, name, shape, dt=F32):
        return self.es.enter_context(self.nc.psum_tensor(self.pfx + name, shape, dt))

    def _own_sem(self, own):
        if own.sem is None or own.cnt >= self.SEM_ROLL:
            if self.sempool and own.sem is None:
                sem, cnt = self.sempool.pop()
                own.sem = sem
                own.cnt = cnt
            else:
                own.sem = self._newsem("d")
                own.cnt = 0
            self.phase_bufs.append(own)

    def barrier(self):
        toks = []
        for n, E in self.E.items():
            if E["cnt"] > 0:
                toks.append((E["sem"], E["cnt"], n))
        for b in self.phase_bufs:
            toks.append((b.sem, b.cnt, "dma"))
        if self.ccbuf is not None and self.ccbuf.cnt > 0:
            toks.append((self.ccbuf.sem, self.ccbuf.cnt, "dma"))
        self._wait("pool", toks)
        self.op("pool", lambda e: e.memset(self.dummy[:], 0.0), w=[self.bdummy])
        for n in self.E:
            if n != "pool":
                self._wait(n, [self.bdummy.w])

    def end_phase(self):
        self.barrier()
        self.es.close()
        self.es = None
        seen = set()
        for b in self.phase_bufs:
            if b.sem[1] not in seen and b.cnt < self.SEM_ROLL // 2:
                seen.add(b.sem[1])
                self.sempool.append((b.sem, b.cnt))
        self.phase_bufs = []

    def collective_async(self, src_ap, dst_ap, groups, deps):
        if self.ccbuf is None:
            self.ccbuf = Buf("cc_async")
            self.ccbuf.sem = self._newsem("cca")
        self._wait("pool", deps)
        inst = self.nc.gpsimd.collective_compute("AllGather", ALU.bypass, replica_groups=groups,
                                                 ins=[src_ap.opt()], outs=[dst_ap.opt()])
        inst.then_inc(self.ccbuf.sem[0], 1)
        self.ccbuf.cnt += 1

    def all_gather_chunks(self, pairs, groups):
        self.barrier()
        cb = Buf("cc")
        cb.sem = self._newsem("cc")
        for (src_ap, dst_ap) in pairs:
            inst = self.nc.gpsimd.collective_compute("AllGather", ALU.bypass, replica_groups=groups,
                                                     ins=[src_ap.opt()], outs=[dst_ap.opt()])
            inst.then_inc(cb.sem[0], 1)
            cb.cnt += 1
        self.phase_bufs.append(cb)
        self.barrier()
        self.phase_bufs.remove(cb)

    def all_gather(self, src_ap, dst_ap, groups):
        self.barrier()
        cb = Buf("cc")
        cb.sem = self._newsem("cc")
        inst = self.nc.gpsimd.collective_compute("AllGather", ALU.bypass, replica_groups=groups,
                                                 ins=[src_ap.opt()], outs=[dst_ap.opt()])
        inst.then_inc(cb.sem[0], 1)
        cb.cnt = 1
        self.phase_bufs.append(cb)
        self.barrier()
        self.phase_bufs.remove(cb)

    def bufs(self, name, n):
        return [Buf("%s%d" % (name, i)) for i in range(n)]

    def _wait(self, en, toks):
        E = self.E[en]
        need = {}
        for t in toks:
            if t is None:
                continue
            (sem, key), val, src = t
            if src == en and (en == "pe" or not self.same):
                continue
            if need.get(key, (None, 0))[1] < val:
                need[key] = (sem, val)
        for key, (sem, val) in need.items():
            if E["waited"].get(key, 0) < val:
                E["eng"].wait_ge(sem, val)
                E["waited"][key] = val

    def _collect(self, r, w):
        toks = []
        for b in r:
            toks.append(b.w)
        for b in w:
            toks.append(b.w)
            toks.extend(b.r.values())
        return toks

    def op(self, en, fn, r=(), w=()):
        E = self.E[en]
        if E["cnt"] >= self.SEM_ROLL:
            E["sem"] = self._newsem("e_" + en)
            E["cnt"] = 0
        self._wait(en, self._collect(r, w))
        inst = fn(E["eng"])
        E["cnt"] += 1
        inst.then_inc(E["sem"][0], 1)
        tok = (E["sem"], E["cnt"], en)
        for b in r:
            b.r[en] = tok
        for b in w:
            b.w = tok
            b.r = {}
        return inst

    def dma(self, qn, out, in_, r=(), w=(), own=None, **kw):
        E = self.E[qn]
        if own is None:
            own = w[0] if len(w) else r[0]
        self._own_sem(own)
        self._wait(qn, self._collect(r, w))
        inst = E["eng"].dma_start(out=out, in_=in_, **kw)
        own.cnt += 16
        inst.then_inc(own.sem[0], 16)
        tok = (own.sem, own.cnt, "dma")
        for b in r:
            b.r["dma_%d" % own.sem[1]] = tok
        for b in w:
            b.w = tok
            b.r = {}
        return tok

    def store(self, qn, out, in_, r, **kw):
        tok = self.dma(qn, out, in_, r=r, w=(), **kw)
        self.stores.append(tok)

    def finish(self):
        self._wait("sp", self.stores)
        self.stores = []
        if self.es is not None:
            self.es.close()
            self.es = None


def new_nc():
    return bass.Bass("TRN2", target_bir_lowering=False)


def emit_k1(k, nc, T, C, xv, g_in, w_in, fm_specs, fm_route, tok_cols, tok_route):
    NT = T // 512
    W = k.sb("W", [128, 8, C], BF16)
    G = k.sb("G", [128, 8], F32)
    ONES = k.sb("ONES", [128, 128], F32)
    X = [k.sb("X%d" % i, [128, 8, 512], F32) for i in range(2)]
    SQ = k.sb("SQ", [128, 8, 512], F32)
    RS = k.sb("RS", [128, 512], F32)
    HT = k.sb("HT", [128, 8, 512], BF16)
    NOB = 4
    OB = [k.sb("OB%d" % i, [128, 512], BF16) for i in range(NOB)]
    PS = [k.ps("PS%d" % i, [128, 512], F32) for i in range(6)]
    PSS = k.ps("PSS", [128, 512], F32)

    bW = k.bufs("W", 8)
    bG = k.buf("G")
    bONES = k.buf("ONES")
    bX = k.bufs("X", 2)
    bSQ = k.buf("SQ")
    bRS = k.buf("RS")
    bHT = k.buf("HT")
    bOB = k.bufs("OB", NOB)
    bPS = k.bufs("PS", 6)
    bPSS = k.buf("PSS")

    wv = w_in.rearrange("(dc p) c -> p dc c", p=128)
    for dc in range(8):
        for c0 in range(0, C, 1024):
            c1 = min(C, c0 + 1024)
            k.dma("pool", W[:, dc, c0:c1], wv[:, dc, c0:c1], w=[bW[dc]])
    k.dma("sp", G[:], g_in, w=[bG])
    k.op("dve", lambda e: e.memset(ONES[:], 1.0), w=[bONES])

    pi = 0
    oi = 0
    for it in range(NT):
        t0 = it * 512
        xb = it % 2
        k.dma("sp", X[xb][:], xv[:, :, t0:t0 + 512], w=[bX[xb]])
        k.op("act", lambda e: e.activation(out=SQ[:], in_=X[xb][:], func=AF.Square),
             r=[bX[xb]], w=[bSQ])
        for dc in range(8):
            k.op("pe", lambda e: e.matmul(PSS[:], lhsT=ONES[:], rhs=SQ[:, dc, :],
                                           start=(dc == 0), stop=(dc == 7)),
                 r=[bONES, bSQ], w=[bPSS])
        k.op("act", lambda e: e.activation(out=RS[:], in_=PSS[:], func=AF.Sqrt,
                                            scale=1.0 / D_MODEL, bias=EPS),
             r=[bPSS], w=[bRS])
        k.op("dve", lambda e: e.reciprocal(out=RS[:], in_=RS[:]), r=[bRS], w=[bRS])
        for dc in range(8):
            k.op("dve", lambda e: e.scalar_tensor_tensor(
                out=HT[:, dc, :], in0=X[xb][:, dc, :], scalar=G[:, dc:dc + 1], in1=RS[:],
                op0=ALU.mult, op1=ALU.mult), r=[bX[xb], bG, bRS], w=[bHT])
        for (s0, s1, scale, func) in fm_specs:
            for c0 in range(s0, s1, 128):
                c1 = min(s1, c0 + 128)
                m = c1 - c0
                p = pi % 6
                pi += 1
                for dc in range(8):
                    k.op("pe", lambda e: e.matmul(PS[p][:m, :], lhsT=W[:, dc, c0:c1],
                                                   rhs=HT[:, dc, :], start=(dc == 0),
                                                   stop=(dc == 7)),
                         r=[bW[dc], bHT], w=[bPS[p]])
                o = oi % NOB
                oi += 1
                k.op("act", lambda e: e.activation(out=OB[o][:m, :], in_=PS[p][:m, :],
                                                    func=func, scale=scale),
                     r=[bPS[p]], w=[bOB[o]])
                for (ro, nr, dst) in fm_route(c0, c1, t0):
                    k.store("sp", dst, OB[o][ro:ro + nr, :], r=[bOB[o]])
        for (c0, c1) in tok_cols:
            m = c1 - c0
            for tb in range(4):
                p = pi % 6
                pi += 1
                for dc in range(8):
                    k.op("pe", lambda e: e.matmul(PS[p][:, :m],
                                                   lhsT=HT[:, dc, tb * 128:(tb + 1) * 128],
                                                   rhs=W[:, dc, c0:c1], start=(dc == 0),
                                                   stop=(dc == 7)),
                         r=[bW[dc], bHT], w=[bPS[p]])
                o = oi % NOB
                oi += 1
                k.op("dve", lambda e: e.tensor_copy(out=OB[o][:, :m], in_=PS[p][:, :m]),
                     r=[bPS[p]], w=[bOB[o]])
                for (co, ncl, dst) in tok_route(c0, c1, t0, tb):
                    k.store("sp", dst, OB[o][:, co:co + ncl], r=[bOB[o]])


def emit_norm(k, nc, T, xv, g_in, h_dst, tile_done=None):
    NT = T // 512
    G = k.sb("G", [128, 8], F32)
    ONES = k.sb("ONES", [128, 128], F32)
    X = [k.sb("X%d" % i, [128, 8, 512], F32) for i in range(2)]
    SQ = k.sb("SQ", [128, 8, 512], F32)
    RS = k.sb("RS", [128, 512], F32)
    HT = [k.sb("HT%d" % i, [128, 8, 512], BF16) for i in range(2)]
    PSS = k.ps("PSS", [128, 512], F32)
    bG = k.buf("G"); bONES = k.buf("ONES"); bX = k.bufs("X", 2); bSQ = k.buf("SQ"); bRS = k.buf("RS")
    bHT = k.bufs("HT", 2); bPSS = k.buf("PSS")
    k.dma("sp", G[:], g_in, w=[bG])
    k.op("dve", lambda e: e.memset(ONES[:], 1.0), w=[bONES])
    for it in range(NT):
        t0 = it * 512
        xb = it % 2
        k.dma("sp", X[xb][:], xv[:, :, t0:t0 + 512], w=[bX[xb]])
        k.op("act", lambda e: e.activation(out=SQ[:], in_=X[xb][:], func=AF.Square), r=[bX[xb]], w=[bSQ])
        for dc in range(8):
            k.op("pe", lambda e: e.matmul(PSS[:], lhsT=ONES[:], rhs=SQ[:, dc, :], start=(dc == 0), stop=(dc == 7)),
                 r=[bONES, bSQ], w=[bPSS])
        k.op("act", lambda e: e.activation(out=RS[:], in_=PSS[:], func=AF.Sqrt, scale=1.0 / D_MODEL, bias=EPS),
             r=[bPSS], w=[bRS])
        k.op("dve", lambda e: e.reciprocal(out=RS[:], in_=RS[:]), r=[bRS], w=[bRS])
        for dc in range(8):
            k.op("dve", lambda e: e.scalar_tensor_tensor(
                out=HT[xb][:, dc, :], in0=X[xb][:, dc, :], scalar=G[:, dc:dc + 1], in1=RS[:],
                op0=ALU.mult, op1=ALU.mult), r=[bX[xb], bG, bRS], w=[bHT[xb]])
        n0 = len(k.stores)
        for dc in range(8):
            k.store("pool", h_dst(dc, t0), HT[xb][:, dc, :], r=[bHT[xb]])
        if tile_done is not None:
            tile_done(it, k.stores[n0:])


def emit_proj(k, nc, T, C, ht_src, w_in, fm_specs, fm_route, tok_cols, tok_route):
    NT = T // 512
    W = k.sb("W", [128, 8, C], BF16)
    HT = [k.sb("HT%d" % i, [128, 8, 512], BF16) for i in range(2)]
    NOB = 4
    OB = [k.sb("OB%d" % i, [128, 512], BF16) for i in range(NOB)]
    PS = [k.ps("PS%d" % i, [128, 512], F32) for i in range(6)]
    bW = k.bufs("W", 8); bHT = k.bufs("HT", 2); bOB = k.bufs("OB", NOB); bPS = k.bufs("PS", 6)
    wv = w_in.rearrange("(dc p) c -> p dc c", p=128)
    for dc in range(8):
        for c0 in range(0, C, 1024):
            c1 = min(C, c0 + 1024)
            k.dma("pool", W[:, dc, c0:c1], wv[:, dc, c0:c1], w=[bW[dc]])
    pi = 0
    oi = 0
    for it in range(NT):
        t0 = it * 512
        hb = it % 2
        k.dma("sp", HT[hb][:], ht_src(t0), w=[bHT[hb]])
        for (s0, s1, scale, func) in fm_specs:
            for c0 in range(s0, s1, 128):
                c1 = min(s1, c0 + 128)
                m = c1 - c0
                p = pi % 6
                pi += 1
                for dc in range(8):
                    k.op("pe", lambda e: e.matmul(PS[p][:m, :], lhsT=W[:, dc, c0:c1], rhs=HT[hb][:, dc, :],
                                                   start=(dc == 0), stop=(dc == 7)),
                         r=[bW[dc], bHT[hb]], w=[bPS[p]])
                o = oi % NOB
                oi += 1
                k.op("act", lambda e: e.activation(out=OB[o][:m, :], in_=PS[p][:m, :], func=func, scale=scale),
                     r=[bPS[p]], w=[bOB[o]])
                for (ro, nr, dst) in fm_route(c0, c1, t0):
                    k.store("pool", dst, OB[o][ro:ro + nr, :], r=[bOB[o]])
        for (c0, c1) in tok_cols:
            m = c1 - c0
            for tb in range(4):
                p = pi % 6
                pi += 1
                for dc in range(8):
                    k.op("pe", lambda e: e.matmul(PS[p][:, :m], lhsT=HT[hb][:, dc, tb * 128:(tb + 1) * 128],
                                                   rhs=W[:, dc, c0:c1], start=(dc == 0), stop=(dc == 7)),
                         r=[bW[dc], bHT[hb]], w=[bPS[p]])
                o = oi % NOB
                oi += 1
                k.op("dve", lambda e: e.tensor_copy(out=OB[o][:, :m], in_=PS[p][:, :m]), r=[bPS[p]], w=[bOB[o]])
                for (co, ncl, dst) in tok_route(c0, c1, t0, tb):
                    k.store("pool", dst, OB[o][:, co:co + ncl], r=[bOB[o]])


def build_k1(T, C, fm_specs, tok_cols):
    nc = new_nc()
    CV = sum(c1 - c0 for c0, c1 in tok_cols)
    xT = nc.dram_tensor("xT", [D_MODEL, T], F32, kind="ExternalInput").ap()
    g_in = nc.dram_tensor("g", [128, 8], F32, kind="ExternalInput").ap()
    w_in = nc.dram_tensor("w", [D_MODEL, C], F32, kind="ExternalInput").ap()
    projT = nc.dram_tensor("projT", [C, T], BF16, kind="ExternalOutput").ap()
    vtok = nc.dram_tensor("vtok", [T, max(CV, 1)], BF16, kind="ExternalOutput").ap()
    voff = {}
    vo = 0
    for (c0, c1) in tok_cols:
        voff[c0] = vo
        vo += c1 - c0
    k = Ctx(nc)
    k.begin_phase("")
    emit_k1(k, nc, T, C, xT.rearrange("(dc p) t -> p dc t", p=128), g_in, w_in, fm_specs,
            lambda c0, c1, t0: [(0, c1 - c0, projT[c0:c1, t0:t0 + 512])],
            tok_cols,
            lambda c0, c1, t0, tb: [(0, c1 - c0, vtok[t0 + tb * 128:t0 + (tb + 1) * 128,
                                                     voff[c0]:voff[c0] + c1 - c0])])
    k.finish()
    return nc


def g_layout(g):
    return np.ascontiguousarray(np.asarray(g, np.float32).reshape(8, 128).T)


def run_k1(xT_shards, g, w, fm_specs, tok_cols):
    T = xT_shards[0].shape[1]
    C = w.shape[1]
    nc = build_k1(T, C, fm_specs, tok_cols)
    gl = g_layout(g)
    w = np.ascontiguousarray(w, dtype=np.float32)
    in_maps = [{"xT": np.ascontiguousarray(s), "g": gl, "w": w} for s in xT_shards]
    res = run_bass_kernel_spmd(nc, in_maps, core_ids=list(range(NCORES)))
    return res.results


BIG = 29952.0


def alibi_slopes(n):
    return np.exp2(-8.0 * np.arange(1, n + 1, dtype=np.float64) / n)


def split3(v):
    v = np.asarray(v, np.float64)
    a = v.astype(NPBF)
    r = v - a.astype(np.float64)
    b = r.astype(NPBF)
    r = r - b.astype(np.float64)
    c = r.astype(NPBF)
    return np.stack([a, b, c], 0)


def q_aug_rows(m, S):
    tl = np.arange(S) % 512
    return split3(-m * tl)


def bt_table(m):
    sl = np.arange(128)[:, None]
    delta = np.arange(-3, 125)[None, :]
    return (m * sl - m * 128.0 * delta).astype(np.float32)


def btc_table(m):
    jl = np.arange(128)[:, None]
    dd = np.arange(-28, 32)[None, :]
    return (16.0 * m * jl - m * (512.0 * dd - 31.0)).astype(np.float32)


def dm_table():
    sl = np.arange(128)[:, None, None]
    dd = np.arange(4)[None, :, None]
    tl = np.arange(512)[None, None, :]
    return np.where(128 * dd + sl > tl, -1.0, 0.0).astype(NPBF)


def wm_table():
    sl = np.arange(128)[:, None, None]
    dd = np.arange(4)[None, :, None]
    tl = np.arange(512)[None, None, :]
    return np.where(tl - 128 * dd - sl >= 0, -1.0, 0.0).astype(NPBF)


def cm_table():
    jl = np.arange(128)[:, None, None]
    dd = np.arange(5)[None, :, None]
    tl = np.arange(512)[None, None, :]
    return np.where(16 * jl + 31 > 512 * dd + tl, -1.0, 0.0).astype(NPBF)


def bigi_table():
    return (np.eye(128) * BIG).astype(NPBF)


def bcast128(v):
    v = np.asarray(v, np.float32).reshape(1, -1)
    return np.ascontiguousarray(np.broadcast_to(v, (128, v.shape[1])))


def emit_sqmax(k, nc, fetch, ntiles, SQT, bSQT, ONESB, bONESB, psM, bpsM, MX, bMX, OUT, bOUT):
    nxt = fetch(0)
    for i in range(ntiles):
        rb, src = nxt
        if i + 1 < ntiles:
            nxt = fetch(i + 1)
        a = i % 2
        w_ = src.shape[-1]
        k.op("dve", lambda e: e.tensor_tensor(out=SQT[a][0:64, :w_], in0=src, in1=src, op=ALU.mult),
             r=rb, w=[bSQT[a]])
        k.op("pe", lambda e: e.matmul(psM[a][:, :w_], lhsT=ONESB[0:64, :], rhs=SQT[a][0:64, :w_],
                                       start=True, stop=True), r=[bSQT[a], bONESB], w=[bpsM[a]])
        k.op("dve", lambda e: e.reduce_max(out=MX[:, i:i + 1], in_=psM[a][:, :w_], axis=AX.X),
             r=[bpsM[a]], w=[bMX])
    k.op("dve", lambda e: e.reduce_max(out=OUT, in_=MX[:, 0:ntiles], axis=AX.X),
         r=[bMX], w=[bOUT])


class IOK2bStandalone:
    def __init__(self, nc, S, NH):
        d = lambda n, sh, dt: nc.dram_tensor(n, sh, dt, kind="ExternalInput").ap()
        self.qa = d("qa", [NH * 2, 64, S], BF16)
        self.ka = d("ka", [NH * 2, 64, S], BF16)
        self.v_in = d("v", [S, NH * 128], BF16)
        self.qaug = d("qaug", [NH, 3, 512], BF16)
        self.bt_in = d("bt", [NH, 128, 128], F32)
        self.dm_in = d("dm", [128, 4, 512], BF16)
        self.bigi_in = d("bigi", [128, 128], BF16)
        self.lam_in = d("lam", [128, 4, 64], F32)
        self.sg_in = d("sg", [128, 1], F32)
        self.oT = nc.dram_tensor("oT", [NH * 128, S], BF16, kind="ExternalOutput").ap()

    def q(self, hh, c, t0, t1):
        return [(self.qa[hh * 2 + c, :, t0:t1], t0, t1)]

    def k(self, hh, c, t0, t1):
        return [(self.ka[hh * 2 + c, :, t0:t1], t0, t1)]

    def v(self, hh, t0, t1):
        return [(self.v_in[t0:t1, hh * 128:(hh + 1) * 128], t0, t1)]

    def out(self, hh, I):
        return self.oT[hh * 128:(hh + 1) * 128, I * 512:(I + 1) * 512]


def build_k2b(S, lambda_init, NH=2):
    nc = new_nc()
    io = IOK2bStandalone(nc, S, NH)
    k = Ctx(nc)
    k.begin_phase("")
    emit_k2b(k, nc, S, lambda_init, io, NH)
    k.finish()
    return nc


def emit_k2b(k, nc, S, lambda_init, io, NH=2):
    NQ = S // 512
    NB = S // 128
    bt_in, dm_in, bigi_in, lam_in, sg_in = io.bt_in, io.dm_in, io.bigi_in, io.lam_in, io.sg_in

    KA = [k.sb("KA%d" % c, [67, S], BF16) for c in range(2)]
    V = k.sb("V", [128, NB, 128], BF16)
    QT = [k.sb("QT%d" % i, [67, 512], BF16) for i in range(4)]
    BT = k.sb("BT", [128, 128], F32)
    BTC = [k.sb("BTC%d" % c, [128, 128], F32) for c in range(2)]
    DM = k.sb("DM", [128, 4, 512], BF16)
    BIGI = k.sb("BIGI", [128, 128], BF16)
    ONESB = k.sb("ONESB", [128, 128], BF16)
    ONESF = k.sb("ONESF", [128, 128], F32)
    LAM = k.sb("LAM", [128, 4, 64], F32)
    LT = k.sb("LT", [128, 2, 64], F32)
    LS = k.sb("LS", [128, 4], F32)
    SG = k.sb("SG", [128, 1], F32)
    SQT = [k.sb("SQT%d" % i, [128, 512], BF16) for i in range(2)]
    MX = k.sb("MX", [128, 64], F32)
    Q2 = k.sb("Q2", [128, 4], F32)
    NPT = 6
    PT = [k.sb("PT%d" % i, [128, 512], BF16) for i in range(NPT)]
    RL = k.sb("RL", [128, 512], F32)
    OC = [k.sb("OC%d" % c, [128, 512], F32) for c in range(2)]
    OD = k.sb("OD", [128, 512], F32)
    ACP = [k.sb("ACP%d" % c, [128, 512], F32) for c in range(2)]
    OSQ = k.sb("OSQ", [128, 512], F32)
    RS = k.sb("RS", [128, 512], F32)
    OUT = [k.sb("OUT%d" % i, [128, 512], BF16) for i in range(2)]
    psS = [k.ps("psS%d" % i, [128, 512], F32) for i in range(4)]
    psO = [k.ps("psO%d" % i, [128, 512], F32) for i in range(2)]
    psL = [k.ps("psL%d" % i, [128, 512], F32) for i in range(2)]
    psM = psS[0]

    bKA = k.bufs("KA", 2); bV = k.buf("V"); bQT = k.bufs("QT", 4); bBT = k.buf("BT")
    bBTC = k.bufs("BTC", 2); bDM = k.buf("DM"); bBIGI = k.buf("BIGI"); bONESB = k.buf("ONESB")
    bONESF = k.buf("ONESF"); bLAM = k.buf("LAM"); bLT = k.buf("LT"); bLS = k.buf("LS")
    bSG = k.buf("SG"); bSQT = k.bufs("SQT", 2); bMX = k.buf("MX"); bQ2 = k.buf("Q2")
    bPT = k.bufs("PT", NPT); bRL = k.buf("RL"); bOC = k.bufs("OC", 2); bOD = k.buf("OD")
    bOSQ = k.buf("OSQ"); bRS = k.buf("RS"); bOUT = k.bufs("OUT", 2); bACP = k.bufs("ACP", 2)
    bpsS = k.bufs("psS", 4); bpsO = k.bufs("psO", 2); bpsL = k.bufs("psL", 2); bpsM = bpsS[0]

    k.dma("sp", DM[:], dm_in, w=[bDM])
    k.dma("sp", BIGI[:], bigi_in, w=[bBIGI])
    k.dma("sp", LAM[:], lam_in, w=[bLAM])
    k.dma("sp", SG[:], sg_in, w=[bSG])
    k.op("dve", lambda e: e.memset(ONESB[:], 1.0), w=[bONESB])
    k.op("dve", lambda e: e.memset(ONESF[:], 1.0), w=[bONESF])
    for j in range(2):
        k.op("dve", lambda e: e.tensor_tensor(out=LT[:, j, :], in0=LAM[:, 2 * j, :],
                                               in1=LAM[:, 2 * j + 1, :], op=ALU.mult),
             r=[bLAM], w=[bLT])
        k.op("dve", lambda e: e.reduce_sum(out=LS[:, j:j + 1], in_=LT[:, j, :], axis=AX.X),
             r=[bLT], w=[bLS])
    k.op("act", lambda e: e.activation(out=LS[:, 0:2], in_=LS[:, 0:2], func=AF.Exp),
         r=[bLS], w=[bLS])
    k.op("dve", lambda e: e.tensor_tensor(out=LS[:, 2:3], in0=LS[:, 1:2], in1=LS[:, 0:1],
                                           op=ALU.subtract), r=[bLS], w=[bLS])
    k.op("dve", lambda e: e.tensor_scalar(out=LS[:, 3:4], in0=LS[:, 2:3], scalar1=-lambda_init,
                                           scalar2=None, op0=ALU.add), r=[bLS], w=[bLS])
    k.op("dve", lambda e: e.tensor_scalar(out=SG[:], in0=SG[:], scalar1=1.0 - lambda_init,
                                           scalar2=None, op0=ALU.mult), r=[bSG], w=[bSG])

    qti = 0
    pti = 0
    psi = 0
    oi = 0
    deferred = []
    for hh in range(NH):
        for c in range(2):
            for s0 in range(0, S, 4096):
                s1 = min(S, s0 + 4096)
                for (ap, lo, hi) in io.k(hh, c, s0, s1):
                    k.dma("act", KA[c][0:64, lo:hi], ap, w=[bKA[c]])
            k.op("pool", lambda e: e.memset(KA[c][64:67, :], 1.0), w=[bKA[c]])
        for b0 in range(0, NB, 32):
            b1 = min(NB, b0 + 32)
            for (ap, lo, hi) in io.v(hh, b0 * 128, b1 * 128):
                k.dma("act", V[:, lo // 128:hi // 128, :], ap.rearrange("(nb p) c -> p nb c", p=128), w=[bV])
        k.dma("sp", BT[:], bt_in[hh], w=[bBT])
        for qi_ in range(4):
            k.dma("sp", QT[qi_][64:67, :], io.qaug[hh], w=[bQT[qi_]])
        for c in range(2):
            def fetch_q(i, c=c):
                nonlocal qti
                qb = qti % 4
                qti += 1
                for (ap, lo, hi) in io.q(hh, c, i * 512, (i + 1) * 512):
                    k.dma("sp", QT[qb][0:64, :], ap, w=[bQT[qb]])
                return [bQT[qb]], QT[qb][0:64, :]
            emit_sqmax(k, nc, fetch_q, NQ, SQT, bSQT, ONESB, bONESB, psS[0:2], bpsS[0:2], MX, bMX,
                       Q2[:, 0:1], bQ2)
            emit_sqmax(k, nc, lambda i, c=c: ([bKA[c]], KA[c][0:64, i * 512:(i + 1) * 512]), NQ,
                       SQT, bSQT, ONESB, bONESB, psS[0:2], bpsS[0:2], MX, bMX, Q2[:, 1:2], bQ2)
            k.op("dve", lambda e: e.tensor_tensor(out=Q2[:, 2:3], in0=Q2[:, 0:1], in1=Q2[:, 1:2],
                                                   op=ALU.mult), r=[bQ2], w=[bQ2])
            k.op("act", lambda e: e.activation(out=Q2[:, 3:4], in_=Q2[:, 2:3], func=AF.Sqrt),
                 r=[bQ2], w=[bQ2])
            k.op("dve", lambda e: e.tensor_scalar(out=BTC[c][:], in0=BT[:], scalar1=Q2[:, 3:4],
                                                   scalar2=None, op0=ALU.subtract),
                 r=[bBT, bQ2], w=[bBTC[c]])
        for I in range(NQ):
            nkb = 4 * I + 4
            qbs = []
            for c in range(2):
                qb = qti % 4
                qti += 1
                for (ap, lo, hi) in io.q(hh, c, I * 512, (I + 1) * 512):
                    k.dma("sp", QT[qb][0:64, :], ap, w=[bQT[qb]])
                qbs.append(qb)
            staged = []

            def stage_a(jb):
                nonlocal psi, pti
                pts = []
                diag = jb >= 4 * I
                idx = 4 * I - jb + 3
                for c in range(2):
                    ps = psi % 4
                    psi += 1
                    k.op("pe", lambda e: e.matmul(psS[ps][:], lhsT=KA[c][:, jb * 128:(jb + 1) * 128],
                                                   rhs=QT[qbs[c]][:], start=True, stop=not diag),
                         r=[bKA[c], bQT[qbs[c]]], w=[bpsS[ps]])
                    if diag:
                        dd = jb - 4 * I
                        k.op("pe", lambda e: e.matmul(psS[ps][:], lhsT=BIGI[:], rhs=DM[:, dd, :],
                                                       start=False, stop=True),
                             r=[bBIGI, bDM], w=[bpsS[ps]])
                    pt = pti % NPT
                    pti += 1
                    k.op("act", lambda e: e.activation(out=PT[pt][:], in_=psS[ps][:], func=AF.Exp,
                                                        bias=BTC[c][:, idx:idx + 1], scale=1.0),
                         r=[bpsS[ps], bBTC[c]], w=[bPT[pt]])
                    pts.append(pt)
                return pts

            def stage_b(jb, pts):
                for c in range(2):
                    k.op("pe", lambda e: e.matmul(psO[c][:], lhsT=V[:, jb, :], rhs=PT[pts[c]][:],
                                                   start=(jb == 0), stop=(jb == nkb - 1)),
                         r=[bV, bPT[pts[c]]], w=[bpsO[c]])
                for c in range(2):
                    if jb % 3 == 2:
                        if jb == 2:
                            k.op("pool", lambda e: e.tensor_copy(out=ACP[c][:], in_=PT[pts[c]][:]),
                                 r=[bPT[pts[c]]], w=[bACP[c]])
                        else:
                            k.op("pool", lambda e: e.tensor_tensor(out=ACP[c][:], in0=ACP[c][:],
                                                                    in1=PT[pts[c]][:], op=ALU.add),
                                 r=[bACP[c], bPT[pts[c]]], w=[bACP[c]])
                    elif jb == 0:
                        k.op("dve", lambda e: e.tensor_copy(out=psL[c][:], in_=PT[pts[c]][:]),
                             r=[bPT[pts[c]]], w=[bpsL[c]])
                    else:
                        k.op("dve", lambda e: e.tensor_tensor(out=psL[c][:], in0=psL[c][:], in1=PT[pts[c]][:],
                                                               op=ALU.add),
                             r=[bpsL[c], bPT[pts[c]]], w=[bpsL[c]])

            for jb in range(nkb):
                staged.append((jb, stage_a(jb)))
                if jb == 1:
                    while deferred:
                        deferred.pop(0)()
                if len(staged) > 1:
                    stage_b(*staged.pop(0))
            while staged:
                stage_b(*staged.pop(0))
            def tile_epilogue(hh=hh, I=I):
                nonlocal psi, oi
                for c in range(2):
                    k.op("dve", lambda e: e.tensor_tensor(out=OSQ[:], in0=psL[c][:], in1=ACP[c][:], op=ALU.add),
                         r=[bpsL[c], bACP[c]], w=[bOSQ])
                    ps = psi % 4
                    psi += 1
                    k.op("pe", lambda e: e.matmul(psS[ps][:], lhsT=ONESF[:], rhs=OSQ[:], start=True, stop=True),
                         r=[bONESF, bOSQ], w=[bpsS[ps]])
                    k.op("dve", lambda e: e.tensor_scalar(out=RL[:], in0=psS[ps][:], scalar1=1e-30,
                                                           scalar2=None, op0=ALU.max),
                         r=[bpsS[ps]], w=[bRL])
                    k.op("dve", lambda e: e.reciprocal(out=RL[:], in_=RL[:]), r=[bRL], w=[bRL])
                    k.op("dve", lambda e: e.tensor_tensor(out=OC[c][:], in0=psO[c][:], in1=RL[:], op=ALU.mult),
                         r=[bpsO[c], bRL], w=[bOC[c]])
                k.op("dve", lambda e: e.scalar_tensor_tensor(out=OD[:], in0=OC[1][:], scalar=LS[:, 3:4],
                                                              in1=OC[0][:], op0=ALU.mult, op1=ALU.add),
                     r=[bOC[0], bOC[1], bLS], w=[bOD])
                k.op("act", lambda e: e.activation(out=OSQ[:], in_=OD[:], func=AF.Square),
                     r=[bOD], w=[bOSQ])
                k.op("pe", lambda e: e.matmul(psM[:], lhsT=ONESF[:], rhs=OSQ[:], start=True, stop=True),
                     r=[bONESF, bOSQ], w=[bpsM])
                k.op("act", lambda e: e.activation(out=RS[:], in_=psM[:], func=AF.Sqrt,
                                                    scale=1.0 / 128.0, bias=EPS), r=[bpsM], w=[bRS])
                k.op("dve", lambda e: e.reciprocal(out=RS[:], in_=RS[:]), r=[bRS], w=[bRS])
                ob = oi % 2
                oi += 1
                k.op("dve", lambda e: e.scalar_tensor_tensor(out=OUT[ob][:], in0=OD[:], scalar=SG[:, 0:1],
                                                              in1=RS[:], op0=ALU.mult, op1=ALU.mult),
                     r=[bOD, bSG, bRS], w=[bOUT[ob]])
                n0 = len(k.stores)
                k.store("sp", io.out(hh, I), OUT[ob][:], r=[bOUT[ob]])
                if getattr(io, "out_done", None) is not None:
                    io.out_done(I, hh, k.stores[n0:])
            deferred.append(tile_epilogue)
    while deferred:
        deferred.pop(0)()


D_FF = 2816


class IOK3Standalone:
    def __init__(self, nc, T):
        d = lambda n, sh, dt: nc.dram_tensor(n, sh, dt, kind="ExternalInput").ap()
        NCH = 2 * D_FF // 128
        self.xT = d("xT", [D_MODEL, 2 + T], F32)
        self.aT = d("aT", [D_MODEL, 2 + T], BF16)
        self.wo_in = d("wo", [D_MODEL, D_MODEL], F32)
        self.wu_in = d("wu", [D_MODEL, 2 * D_FF], F32)
        self.wd_in = d("wd", [D_FF, D_MODEL], F32)
        self.cw_in = d("cw", [128, NCH, 3], F32)
        self.cb_in = d("cb", [128, NCH], F32)
        self.g_in = d("g", [128, 8], F32)
        self.gf_in = d("gf", [128, 8], F32)
        self.yT = nc.dram_tensor("yT", [D_MODEL, T], F32, kind="ExternalOutput").ap()
        self.halo_scale = None
        self.tail_dst = None

    def x_src(self, col0, n):
        return self.xT.rearrange("(dc p) t -> p dc t", p=128)[:, :, col0:col0 + n]

    def a_src(self, col0, n):
        return [(0, 1, self.aT.rearrange("(dc p) t -> p dc t", p=128)[:, :, col0:col0 + n])]

    def y_dst(self, oc, tcol, n):
        return self.yT[oc * 128:(oc + 1) * 128, tcol:tcol + n]

    def make_scratch(self, nc, T):
        self.X1D = nc.dram_tensor("X1D", [D_MODEL, T], F32).ap()
        self.AFFD = nc.dram_tensor("AFFD", [D_FF, T], BF16).ap()

    def x1_dst(self, dc, tcol, n):
        return self.X1D[dc * 128:(dc + 1) * 128, tcol:tcol + n]

    def x1_src(self, tcol, n):
        return self.X1D.rearrange("(dc p) t -> p dc t", p=128)[:, :, tcol:tcol + n]

    def aff_dst(self, gch, tcol, n):
        return self.AFFD[gch * 128:(gch + 1) * 128, tcol:tcol + n]

    def aff_src(self, tcol, n):
        return self.AFFD.rearrange("(kc p) t -> p kc t", p=128)[:, :, tcol:tcol + n]


def build_k3_split(T, final_norm):
    nc = new_nc()
    io = IOK3Standalone(nc, T)
    io.make_scratch(nc, T)
    k = Ctx(nc)
    k.begin_phase("a_")
    emit_k3(k, nc, T, final_norm, io, 512, "a")
    k.end_phase()
    k.begin_phase("b_")
    emit_k3(k, nc, T, final_norm, io, 512, "b")
    k.finish()
    return nc


def build_k3(T, final_norm, N=256):
    nc = new_nc()
    io = IOK3Standalone(nc, T)
    k = Ctx(nc)
    k.begin_phase("")
    emit_k3(k, nc, T, final_norm, io, N)
    k.finish()
    return nc


def emit_k3(k, nc, T, final_norm, io, N=256, part="ab"):
    A_, B_ = ("a" in part), ("b" in part)
    NT = T // N
    NCH = 2 * D_FF // 128
    NG = NCH // 2
    wo_in, wu_in, wd_in, cw_in, cb_in, g_in, gf_in = (io.wo_in, io.wu_in, io.wd_in, io.cw_in, io.cb_in,
                                                      io.g_in, io.gf_in)

    WO = k.sb("WO", [128, 8, D_MODEL], BF16) if A_ else None
    WU = k.sb("WU", [128, 8, 2 * D_FF], BF16) if A_ else None
    WD = k.sb("WD", [128, NG, D_MODEL], BF16) if B_ else None
    CW = k.sb("CW", [128, NCH, 3], F32)
    CB = k.sb("CB", [128, NCH], F32)
    G = k.sb("G", [128, 8], F32)
    GF = k.sb("GF", [128, 8], F32)
    ONES = k.sb("ONES", [128, 128], F32)
    HALO = k.sb("HALO", [128, NCH, 2], F32) if A_ else None
    NX1 = 2 if part == "b" else 1
    X1s = [k.sb("X1_%d" % i, [128, 8, N], F32) for i in range(NX1)]
    X1 = X1s[0]
    AT = k.sb("AT", [128, 8, N], BF16) if A_ else None
    SQ = [k.sb("SQ%d" % i, [128, N], F32) for i in range(2)]
    RS = k.sb("RS", [128, N], F32)
    HT = k.sb("HT", [128, 8, N], BF16) if A_ else None
    NAF = {"ab": 1, "a": 1, "b": 2}[part]
    AFFs = [k.sb("AFF%d" % i, [128, NG if B_ else 2, N], BF16) for i in range(NAF)]
    AFF = AFFs[0]
    NU, NY, NSG = 3, 4, 2
    U = [k.sb("U%d" % i, [128, 2 + N], F32) for i in range(NU)] if A_ else None
    Y = [k.sb("Y%d" % i, [128, N], F32) for i in range(NY)] if A_ else None
    SGT = [k.sb("SGT%d" % i, [128, N], F32) for i in range(NSG)] if A_ else None
    OUT = [k.sb("OUT%d" % i, [128, N], F32) for i in range(2)] if B_ else None
    mid = B_ and getattr(io, "h_dst", None) is not None
    assert not (mid and final_norm)
    if mid:
        OUTH = [k.sb("OUTH%d" % i, [128, N], BF16) for i in range(2)]
        bOUTH = k.bufs("OUTH", 2)
    PS = [k.ps("PS%d" % i, [128, 512], F32) for i in range(6)]
    PSS = k.ps("PSS", [128, 512], F32)

    bWO = k.buf("WO"); bWU = k.bufs("WU", 8); bWD = k.buf("WD"); bCW = k.buf("CW"); bCB = k.buf("CB")
    bG = k.buf("G"); bGF = k.buf("GF"); bONES = k.buf("ONES"); bHALO = k.bufs("HALO", NCH)
    bX1s = [k.bufs("X1", 8) for _ in range(NX1)]; bX1 = bX1s[0]; bAT = k.buf("AT"); bSQ = k.bufs("SQ", 2); bRS = k.buf("RS"); bHT = k.buf("HT")
    bAFFs = [k.bufs("AFF", NG) for _ in range(NAF)]; bAFF = bAFFs[0]; bU = k.bufs("U", NU); bY = k.bufs("Y", NY); bSGT = k.bufs("SGT", NSG)
    bOUT = k.bufs("OUT", 2); bPS = k.bufs("PS", 6); bPSS = k.buf("PSS")

    if A_:
        k.dma("sp", CW[:], cw_in, w=[bCW])
        k.dma("sp", CB[:], cb_in, w=[bCB])
        k.dma("sp", G[:], g_in, w=[bG])
    k.dma("sp", GF[:], gf_in, w=[bGF])
    k.op("dve", lambda e: e.memset(ONES[:], 1.0), w=[bONES])
    pre = getattr(io, "wb", None)
    if pre is not None:
        wov = pre[0].rearrange("(dc p) c -> p dc c", p=128)
        wuv = pre[1].rearrange("(dc p) c -> p dc c", p=128)
        wdv = pre[2].rearrange("(kc p) c -> p kc c", p=128)
        for dc in range(8 if A_ else 0):
            k.dma("sp", WU[:, dc, :], wuv[:, dc, :], w=[bWU[dc]])
        if A_:
            k.dma("sp", WO[:], wov, w=[bWO])
        for kc in range(0, NG if B_ else 0, 2):
            k.dma("sp", WD[:, kc:kc + 2, :], wdv[:, kc:kc + 2, :], w=[bWD])
    else:
        wov = wo_in.rearrange("(dc p) c -> p dc c", p=128)
        for dc in range(8 if A_ else 0):
            k.dma("pool", WO[:, dc, :], wov[:, dc, :], w=[bWO])
        wuv = wu_in.rearrange("(dc p) c -> p dc c", p=128)
        for dc in range(8 if A_ else 0):
            for c0 in range(0, 2 * D_FF, 1024):
                c1 = min(2 * D_FF, c0 + 1024)
                k.dma("pool", WU[:, dc, c0:c1], wuv[:, dc, c0:c1], w=[bWU[dc]])
        wdv = wd_in.rearrange("(kc p) c -> p kc c", p=128)
        for kc in range(NG if B_ else 0):
            k.dma("pool", WD[:, kc, :], wdv[:, kc, :], w=[bWD])

    st = {"pi": 0, "ui": 0, "yi": 0, "oi": 0, "sq": 0}
    if A_ and io.halo_scale is not None:
        M0 = k.sb("M0", [128, 1], F32)
        bM0 = k.buf("M0")
        k.dma("sp", M0[:], io.halo_scale, w=[bM0])

    def rms(n, gtile, inv_d):
        for dc in range(8):
            s = st["sq"] % 2
            st["sq"] += 1
            k.op("act", lambda e: e.activation(out=SQ[s][:, :n], in_=X1[:, dc, :n], func=AF.Square),
                 r=[bX1[dc]], w=[bSQ[s]])
            k.op("pe", lambda e: e.matmul(PSS[:, :n], lhsT=ONES[:], rhs=SQ[s][:, :n],
                                           start=(dc == 0), stop=(dc == 7)),
                 r=[bONES, bSQ[s]], w=[bPSS])
        k.op("act", lambda e: e.activation(out=RS[:, :n], in_=PSS[:, :n], func=AF.Sqrt,
                                            scale=inv_d, bias=EPS), r=[bPSS], w=[bRS])
        k.op("dve", lambda e: e.reciprocal(out=RS[:, :n], in_=RS[:, :n]), r=[bRS], w=[bRS])

    def tile(col0, n, halo_only, it=0):
        nonlocal X1, bX1, AFF, bAFF
        X1, bX1 = X1s[it % NX1], bX1s[it % NX1]
        AFF, bAFF = AFFs[it % NAF], bAFFs[it % NAF]
        if part == "b":
            afv = io.aff_src(col0 - 2, n)
            for h0 in (0, NG // 2):
                k.dma("act", AFF[:, h0:h0 + NG // 2, :n], afv[:, h0:h0 + NG // 2, :], w=bAFF[h0:h0 + NG // 2])
            k.dma("sp", X1[:, :, :n], io.x1_src(col0 - 2, n), w=bX1)
        else:
            tile_a(col0, n, halo_only)
        if halo_only or part == "a":
            return
        tile_b(col0, n)

    def tile_a(col0, n, halo_only):
        k.dma("sp", X1[:, :, :n], io.x_src(col0, n), w=bX1)
        for (dc0, dstep, ap_) in io.a_src(col0, n):
            ndc = ap_.shape[1]
            k.dma("sp", AT[:, dc0:dc0 + (ndc - 1) * dstep + 1:dstep, :n], ap_, w=[bAT])
        if halo_only and io.halo_scale is not None:
            k.op("dve", lambda e: e.tensor_scalar(out=X1[:, :, :n], in0=X1[:, :, :n], scalar1=M0[:, 0:1],
                                                   scalar2=None, op0=ALU.mult), r=bX1 + [bM0], w=bX1)
            k.op("dve", lambda e: e.tensor_scalar(out=AT[:, :, :n], in0=AT[:, :, :n], scalar1=M0[:, 0:1],
                                                   scalar2=None, op0=ALU.mult), r=[bAT, bM0], w=[bAT])
        for oc in range(8):
            p = st["pi"] % 6
            st["pi"] += 1
            for dc in range(8):
                k.op("pe", lambda e: e.matmul(PS[p][:, :n], lhsT=WO[:, dc, oc * 128:(oc + 1) * 128],
                                               rhs=AT[:, dc, :n], start=(dc == 0), stop=(dc == 7)),
                     r=[bWO, bAT], w=[bPS[p]])
            k.op("dve", lambda e: e.tensor_tensor(out=X1[:, oc, :n], in0=X1[:, oc, :n], in1=PS[p][:, :n],
                                                   op=ALU.add), r=[bX1[oc], bPS[p]], w=[bX1[oc]])
        if part == "a" and not halo_only:
            for dc in range(8):
                k.store("pool", io.x1_dst(dc, col0 - 2, n), X1[:, dc, :n], r=[bX1[dc]])
        rms(n, None, 1.0 / D_MODEL)
        for dc in range(8):
            k.op("dve", lambda e: e.scalar_tensor_tensor(
                out=HT[:, dc, :n], in0=X1[:, dc, :n], scalar=G[:, dc:dc + 1], in1=RS[:, :n],
                op0=ALU.mult, op1=ALU.mult), r=[bX1[dc], bG, bRS], w=[bHT])
        for cc in range(NCH):
            c = (cc // 2) + (NG if cc % 2 else 0)
            p = st["pi"] % 6
            st["pi"] += 1
            for dc in range(8):
                k.op("pe", lambda e: e.matmul(PS[p][:, :n], lhsT=WU[:, dc, c * 128:(c + 1) * 128],
                                               rhs=HT[:, dc, :n], start=(dc == 0), stop=(dc == 7)),
                     r=[bWU[dc], bHT], w=[bPS[p]])
            if halo_only:
                k.op("act", lambda e: e.copy(out=HALO[:, c, :], in_=PS[p][:, :n]),
                     r=[bPS[p]], w=[bHALO[c]])
                continue
            u = st["ui"] % NU
            st["ui"] += 1
            k.op("pool", lambda e: e.tensor_copy(out=U[u][:, 0:2], in_=HALO[:, c, :]),
                 r=[bHALO[c]], w=[bU[u]])
            k.op("act", lambda e: e.copy(out=U[u][:, 2:2 + n], in_=PS[p][:, :n]),
                 r=[bPS[p]], w=[bU[u]])
            k.op("pool", lambda e: e.tensor_copy(out=HALO[:, c, :], in_=U[u][:, n:n + 2]),
                 r=[bU[u]], w=[bHALO[c]])
            y = st["yi"] % NY
            st["yi"] += 1
            k.op("dve", lambda e: e.tensor_scalar(out=Y[y][:, :n], in0=U[u][:, 2:2 + n],
                                                   scalar1=CW[:, c, 2:3], scalar2=CB[:, c:c + 1],
                                                   op0=ALU.mult, op1=ALU.add),
                 r=[bU[u], bCW, bCB], w=[bY[y]])
            k.op("dve", lambda e: e.scalar_tensor_tensor(out=Y[y][:, :n], in0=U[u][:, 1:1 + n],
                                                          scalar=CW[:, c, 1:2], in1=Y[y][:, :n],
                                                          op0=ALU.mult, op1=ALU.add),
                 r=[bU[u], bCW, bY[y]], w=[bY[y]])
            k.op("dve", lambda e: e.scalar_tensor_tensor(out=Y[y][:, :n], in0=U[u][:, 0:n],
                                                          scalar=CW[:, c, 0:1], in1=Y[y][:, :n],
                                                          op0=ALU.mult, op1=ALU.add),
                 r=[bU[u], bCW, bY[y]], w=[bY[y]])
            sg = (cc // 2) % NSG
            if cc % 2 == 0:
                k.op("act", lambda e: e.activation(out=SGT[sg][:, :n], in_=Y[y][:, :n], func=AF.Silu),
                     r=[bY[y]], w=[bSGT[sg]])
            else:
                gch = cc // 2
                asl = gch if part == "ab" else gch % 2
                k.op("pool", lambda e: e.tensor_tensor(out=AFF[:, asl, :n], in0=SGT[sg][:, :n],
                                                        in1=Y[y][:, :n], op=ALU.mult),
                     r=[bSGT[sg], bY[y]], w=[bAFF[asl]])
                if part == "a":
                    k.store("pool", io.aff_dst(gch, col0 - 2, n), AFF[:, asl, :n], r=[bAFF[asl]])

    def tile_b(col0, n):
        for oc in range(8):
            p = st["pi"] % 6
            st["pi"] += 1
            for kc in range(NG):
                k.op("pe", lambda e: e.matmul(PS[p][:, :n], lhsT=WD[:, kc, oc * 128:(oc + 1) * 128],
                                               rhs=AFF[:, kc, :n], start=(kc == 0), stop=(kc == NG - 1)),
                     r=[bWD, bAFF[kc]], w=[bPS[p]])
            if final_norm or mid:
                k.op("dve", lambda e: e.tensor_tensor(out=X1[:, oc, :n], in0=X1[:, oc, :n],
                                                       in1=PS[p][:, :n], op=ALU.add),
                     r=[bX1[oc], bPS[p]], w=[bX1[oc]])
            else:
                o = st["oi"] % 2
                st["oi"] += 1
                k.op("dve", lambda e: e.tensor_tensor(out=OUT[o][:, :n], in0=X1[:, oc, :n],
                                                       in1=PS[p][:, :n], op=ALU.add),
                     r=[bX1[oc], bPS[p]], w=[bOUT[o]])
                k.store("pool", io.y_dst(oc, col0 - 2, n), OUT[o][:, :n], r=[bOUT[o]])
                if io.tail_dst is not None and col0 - 2 + n == T:
                    k.store("pool", io.tail_dst(oc), OUT[o][:, n - 2:n], r=[bOUT[o]])
        if mid:
            rms(n, None, 1.0 / D_MODEL)
            for oc in range(8):
                k.store("pool", io.y_dst(oc, col0 - 2, n), X1[:, oc, :n], r=[bX1[oc]])
                if col0 - 2 + n == T:
                    k.store("pool", io.tail_dst(oc), X1[:, oc, n - 2:n], r=[bX1[oc]])
                o = st["oi"] % 2
                st["oi"] += 1
                k.op("dve", lambda e: e.scalar_tensor_tensor(
                    out=OUTH[o][:, :n], in0=X1[:, oc, :n], scalar=GF[:, oc:oc + 1], in1=RS[:, :n],
                    op0=ALU.mult, op1=ALU.mult), r=[bX1[oc], bGF, bRS], w=[bOUTH[o]])
                k.store("pool", io.h_dst(oc, col0 - 2, n), OUTH[o][:, :n], r=[bOUTH[o]])
        if final_norm:
            rms(n, None, 1.0 / D_MODEL)
            for oc in range(8):
                o = st["oi"] % 2
                st["oi"] += 1
                k.op("dve", lambda e: e.scalar_tensor_tensor(
                    out=OUT[o][:, :n], in0=X1[:, oc, :n], scalar=GF[:, oc:oc + 1], in1=RS[:, :n],
                    op0=ALU.mult, op1=ALU.mult), r=[bX1[oc], bGF, bRS], w=[bOUT[o]])
                k.store("pool", io.y_dst(oc, col0 - 2, n), OUT[o][:, :n], r=[bOUT[o]])

    if A_:
        tile(0, 2, True)
    for it in range(NT):
        n0 = len(k.stores)
        tile(2 + it * N, N, False, it)
        if B_ and getattr(io, "tile_done", None) is not None:
            io.tile_done(it, k.stores[n0:], N)


def conv_layouts(conv_w, conv_b):
    cw = np.ascontiguousarray(np.asarray(conv_w, np.float32).reshape(3, 44, 128).transpose(2, 1, 0))
    cb = np.ascontiguousarray(np.asarray(conv_b, np.float32).reshape(44, 128).T)
    return cw, cb


def esel_table():
    n = np.arange(128)[:, None, None]
    jj = np.arange(64)[None, :, None]
    s = np.arange(128)[None, None, :]
    return np.where(n == 2 * jj + (s >= 64), BIG, 0.0).astype(NPBF)


def itab_table(m):
    tl = np.arange(128)[:, None]
    jp = np.arange(1024)[None, :] - 1016
    d = tl - 16 * jp - 31
    return np.where(d >= 0, -m * d, -1e30).astype(np.float32)


def ab_tables():
    tl = np.arange(128)[:, None]
    npr = np.arange(256)[None, :] - 254
    cc = (tl >= 64).astype(np.int64)
    V = npr <= cc
    Fn = V & (npr >= cc - 1)
    A = V.astype(np.float32)
    B = (V.astype(np.float32) - 1.0) + 1e6 * Fn.astype(np.float32)
    return A, B.astype(np.float32)


def selg_table():
    t = np.zeros((12, 12, 64), np.float32)
    for r in range(12):
        t[r, r, :] = 1.0
    return t.astype(NPBF)


K2A_STATIC = [("pek", [64, 32], F32), ("pev", [64, 32], F32), ("w1k", [2048, 256], F32),
              ("w2k", [256, 64], F32), ("w1v", [2048, 256], F32), ("w2v", [256, 64], F32),
              ("bt", [128, 4, 128], F32), ("btc", [128, 4, 60], F32), ("dm", [128, 4, 512], BF16),
              ("wm", [128, 4, 512], BF16), ("cm", [128, 5, 512], BF16), ("bigi", [128, 128], BF16),
              ("idb", [128, 128], F32), ("esel", [128, 64, 128], BF16), ("itab", [128, 4, 1024], F32),
              ("atab", [128, 256], F32), ("btab", [128, 256], F32), ("selg", [12, 12, 64], BF16),
              ("qaug", [4, 3, 512], BF16)]


class IOK2aStandalone:
    def __init__(self, nc, S):
        d = lambda n, sh, dt: nc.dram_tensor(n, sh, dt, kind="ExternalInput").ap()
        self.t = {"q%d" % p: None for p in range(4)}
        qa = d("qa", [4, 64, S], BF16)
        for p in range(4):
            self.t["q%d" % p] = qa[p]
        self.t["kc"] = d("kca", [64, S], BF16)
        self.t["vc"] = d("vca", [64, S], BF16)
        self.t["ks"] = d("ksa", [64, S], BF16)
        self.t["kw"] = d("kwa", [64, S], BF16)
        self.t["gt"] = d("gt", [12, S], BF16)
        self.t["vs"] = d("vs", [S, 64], BF16)
        self.t["vw"] = d("vw", [S, 64], BF16)
        self.st = {n: d(n, sh, dt) for (n, sh, dt) in K2A_STATIC}
        self.oT = nc.dram_tensor("oT", [256, S], BF16, kind="ExternalOutput").ap()

    def fm(self, name, t0, t1):
        return [(self.t[name][:, t0:t1], t0, t1)]

    def tok(self, name, t0, t1):
        return [(self.t[name][t0:t1, :], t0, t1)]

    def out(self, p, I):
        return self.oT[p * 64:(p + 1) * 64, I * 512:(I + 1) * 512]


def build_k2a(S):
    nc = new_nc()
    io = IOK2aStandalone(nc, S)
    k = Ctx(nc)
    k.begin_phase("")
    emit_k2a(k, nc, S, io)
    k.finish()
    return nc


def emit_k2a(k, nc, S, io):
    NQ = S // 512
    NB = S // 128
    NCMP = S // 16 - 1
    CW_ = S // 16
    NCB = (CW_ + 127) // 128
    stc = io.st
    pek_in, pev_in, w1k_in, w2k_in, w1v_in, w2v_in = (stc["pek"], stc["pev"], stc["w1k"], stc["w2k"],
                                                      stc["w1v"], stc["w2v"])
    bt_in, btc_in, dm_in, wm_in, cm_in, bigi_in, idb_in = (stc["bt"], stc["btc"], stc["dm"], stc["wm"],
                                                           stc["cm"], stc["bigi"], stc["idb"])
    esel_in, itab_in, atab_in, btab_in, selg_in, qaug_in = (stc["esel"], stc["itab"], stc["atab"],
                                                            stc["btab"], stc["selg"], stc["qaug"])
    A = k.sb
    P = k.ps

    def ld_fm(dst_fn, name, t0, t1, w, q="sp"):
        for (ap, lo, hi) in io.fm(name, t0, t1):
            k.dma(q, dst_fn(lo - t0, hi - t0), ap, w=w)

    def ld_tok(dst_fn, name, t0, t1, w, q="sp"):
        for (ap, lo, hi) in io.tok(name, t0, t1):
            k.dma(q, dst_fn((lo - t0) // 128, (hi - t0) // 128),
                  ap.rearrange("(nb p) d -> p nb d", p=128), w=w)

    KS = A("KS", [67, S], BF16); bKS = k.buf("KS")
    VS = A("VS", [128, NB, 65], BF16); bVS = k.buf("VS")
    KCMP = A("KCMP", [67, NCB * 128], BF16); bKCMP = k.buf("KCMP")
    VC = A("VC", [128, NCB, 65], BF16); bVC = k.buf("VC")
    BT = A("BT", [128, 4, 128], F32); bBT = k.buf("BT")
    BTCs = A("BTCs", [128, 4, 128], F32); bBTCs = k.buf("BTCs")
    BTCw = A("BTCw", [128, 4, 8], F32); bBTCw = k.buf("BTCw")
    BTC0 = A("BTC0", [128, 4, 60], F32); bBTC0 = k.buf("BTC0")
    BTCc = A("BTCc", [128, 4, 60], F32); bBTCc = k.buf("BTCc")
    DM = A("DM", [128, 4, 512], BF16); bDM = k.buf("DM")
    WM = A("WM", [128, 4, 512], BF16); bWM = k.buf("WM")
    CM = A("CM", [128, 5, 512], BF16); bCM = k.buf("CM")
    BIGI = A("BIGI", [128, 128], BF16); bBIGI = k.buf("BIGI")
    IDB = A("IDB", [128, 128], F32); bIDB = k.buf("IDB")
    ESEL = A("ESEL", [128, 64, 128], BF16); bESEL = k.buf("ESEL")
    ITAB = A("ITAB", [128, 4, 1024], F32); bITAB = k.buf("ITAB")
    ATAB = A("ATAB", [128, 256], F32); bATAB = k.buf("ATAB")
    BTAB = A("BTAB", [128, 256], F32); bBTAB = k.buf("BTAB")
    SELG = A("SELG", [12, 12, 64], BF16); bSELG = k.buf("SELG")
    ONESB = A("ONESB", [128, 128], BF16); bONESB = k.buf("ONESB")
    ONESF = A("ONESF", [128, 64], F32); bONESF = k.buf("ONESF")
    QT = [A("QT%d" % i, [67, 4, 512], BF16) for i in range(2)]; bQT = k.bufs("QT", 2)
    KW = [A("KW%d" % i, [67, 1024], BF16) for i in range(2)]; bKW = k.bufs("KW", 2)
    VW = [A("VW%d" % i, [128, 8, 65], BF16) for i in range(2)]; bVW = k.bufs("VW", 2)
    GT = [A("GT%d" % i, [12, 512], BF16) for i in range(2)]; bGT = k.bufs("GT", 2)
    SQT = [A("SQT%d" % i, [128, 512], BF16) for i in range(2)]; bSQT = k.bufs("SQT", 2)
    MX = A("MX", [128, 64], F32); bMX = k.buf("MX")
    ST = A("ST", [128, 16], F32); bST = k.buf("ST")
    SX = A("SX", [128, 1024], F32); bSX = k.buf("SX")
    EX = A("EX", [128, 1024], F32); bEX = k.buf("EX")
    PG = A("PG", [128, 1028], F32); bPG = k.buf("PG")
    SC = A("SC", [128, 8], F32); bSC = k.buf("SC")
    IMP = A("IMP", [128, 256], F32); bIMP = k.buf("IMP")
    I2 = A("I2", [128, 256], F32); bI2 = k.buf("I2")
    I3 = A("I3", [128, 256], F32); bI3 = k.buf("I3")
    M8 = A("M8", [128, 16], F32); bM8 = k.buf("M8")
    MQ = A("MQ", [128, 256], F32); bMQ = k.buf("MQ")
    MT = [A("MT%d" % i, [128, 2, 512], BF16) for i in range(2)]; bMT = k.bufs("MT", 2)
    NPT = 6
    PT = [A("PT%d" % i, [128, 512], BF16) for i in range(NPT)]; bPT = k.bufs("PT", NPT)
    RL = A("RL", [128, 512], F32); bRL = k.buf("RL")
    OBF = A("OBF", [64, 512], F32); bOBF = k.buf("OBF")
    TT = A("TT", [64, 512], F32); bTT = k.buf("TT")
    ACC = [A("ACC%d" % i, [64, 512], F32) for i in range(2)]; bACC = k.bufs("ACC", 2)
    OUTB = [A("OUTB%d" % i, [64, 512], BF16) for i in range(2)]; bOUTB = k.bufs("OUTB", 2)
    psS = [P("psS%d" % i, [128, 512], F32) for i in range(4)]; bpsS = k.bufs("psS", 4)
    psO = [P("psO%d" % i, [128, 512], F32) for i in range(2)]; bpsO = k.bufs("psO", 2)
    psA = P("psA", [128, 512], F32); bpsA = k.buf("psA")
    psX = [P("psX0", [128, 512], F32), psS[0]]; bpsX = [k.buf("psX0"), bpsS[0]]

    for (dst, src, b) in [(BT, bt_in, bBT), (BTC0, btc_in, bBTC0), (DM, dm_in, bDM), (WM, wm_in, bWM),
                          (CM, cm_in, bCM), (BIGI, bigi_in, bBIGI), (IDB, idb_in, bIDB),
                          (ITAB, itab_in, bITAB), (ATAB, atab_in, bATAB), (BTAB, btab_in, bBTAB),
                          (SELG, selg_in, bSELG)]:
        k.dma("sp", dst[:], src, w=[b])
    for j0 in range(0, 64, 16):
        k.dma("sp", ESEL[:, j0:j0 + 16, :], esel_in[:, j0:j0 + 16, :], w=[bESEL])
    k.op("dve", lambda e: e.memset(ONESB[:], 1.0), w=[bONESB])
    k.op("dve", lambda e: e.memset(ONESF[:], 1.0), w=[bONESF])
    k.op("dve", lambda e: e.memset(PG[:], 0.0), w=[bPG])
    k.op("dve", lambda e: e.memset(I2[:], -1.0), w=[bI2])
    k.op("dve", lambda e: e.memset(RL[:], 1.0), w=[bRL])
    k.op("pool", lambda e: e.memset(KCMP[:], 0.0), w=[bKCMP])
    k.op("pool", lambda e: e.memset(KCMP[64:67, :], 1.0), w=[bKCMP])
    k.op("pool", lambda e: e.memset(VC[:], 0.0), w=[bVC])
    for s0 in range(0, S, 4096):
        s1 = min(S, s0 + 4096)
        ld_fm(lambda a, b, s0=s0: KS[0:64, s0 + a:s0 + b], "ks", s0, s1, [bKS], "act")
    k.op("pool", lambda e: e.memset(KS[64:67, :], 1.0), w=[bKS])
    for b0 in range(0, NB, 8):
        ld_tok(lambda a, b, b0=b0: VS[:, b0 + a:b0 + b, 0:64], "vs", b0 * 128, (b0 + 8) * 128, [bVS], "act")
    k.op("pool", lambda e: e.memset(VS[:, :, 64:65], 1.0), w=[bVS])
    for i in range(2):
        k.op("pool", lambda e: e.memset(VW[i][:, :, 64:65], 1.0), w=[bVW[i]])
        k.op("pool", lambda e: e.memset(KW[i][64:67, :], 1.0), w=[bKW[i]])
        k.dma("sp", QT[i][64:67, :, :], qaug_in.rearrange("h r t -> r h t"), w=[bQT[i]])

    with nc.sbuf_tensor("KCH", [64, 8208], BF16) as KCH, \
            nc.sbuf_tensor("W1", [64, 32, 256], BF16) as W1, \
            nc.sbuf_tensor("W2", [128, 2, 64], BF16) as W2, \
            nc.sbuf_tensor("PEF", [64, 32], F32) as PEF, \
            nc.sbuf_tensor("PEB", [64, 32], BF16) as PEB, \
            nc.sbuf_tensor("BH", [128, 2], F32) as BH, \
            nc.sbuf_tensor("HX", [128, 512], F32) as HX, \
            nc.sbuf_tensor("H2", [128, 512], F32) as H2, \
            nc.sbuf_tensor("HID", [128, 2, 512], BF16) as HID:
        bKCH = k.buf("KCH"); bW1 = k.buf("W1"); bW2 = k.buf("W2"); bPEF = k.buf("PEF"); bPEB = k.buf("PEB")
        bBH = k.buf("BH"); bHX = k.buf("HX"); bH2 = k.buf("H2"); bHID = k.bufs("HID", 2)
        for which, (src, pe_in, w1_in, w2_in) in enumerate([("kc", pek_in, w1k_in, w2k_in),
                                                           ("vc", pev_in, w1v_in, w2v_in)]):
            w1v_ = w1_in.rearrange("(p d) h -> d p h", d=64)
            for p0 in range(0, 32, 8):
                k.dma("pool", W1[:, p0:p0 + 8, :], w1v_[:, p0:p0 + 8, :], w=[bW1])
            k.dma("pool", W2[:], w2_in.rearrange("(c p) d -> p c d", p=128), w=[bW2])
            k.dma("sp", PEF[:], pe_in, w=[bPEF])
            k.op("dve", lambda e: e.tensor_copy(out=PEB[:], in_=PEF[:]), r=[bPEF], w=[bPEB])
            for hc in range(2):
                for pos in range(32):
                    k.op("pe", lambda e: e.matmul(psX[0][:, hc:hc + 1], lhsT=W1[:, pos, hc * 128:(hc + 1) * 128],
                                                   rhs=PEB[:, pos:pos + 1], start=(pos == 0), stop=(pos == 31)),
                         r=[bW1, bPEB], w=[bpsX[0]])
            k.op("dve", lambda e: e.tensor_copy(out=BH[:], in_=psX[0][:, 0:2]), r=[bpsX[0]], w=[bBH])
            for j0 in range(0, NCMP, 512):
                n = min(512, NCMP - j0)
                t0 = 16 * j0
                t1 = min(S, t0 + 16 * n + 16)
                ld_fm(lambda a, b: KCH[:, a:b], src, t0, t1, [bKCH])
                for hc in range(2):
                    px = psX[1]
                    for pos in range(32):
                        k.op("pe", lambda e: e.matmul(px[:, :n], lhsT=W1[:, pos, hc * 128:(hc + 1) * 128],
                                                       rhs=KCH[:, pos:pos + 16 * (n - 1) + 1:16],
                                                       start=(pos == 0), stop=(pos == 31)),
                             r=[bW1, bKCH], w=[bpsX[1]])
                    k.op("act", lambda e: e.activation(out=HX[:, :n], in_=px[:, :n], func=AF.Identity,
                                                        bias=BH[:, hc:hc + 1], scale=1.0),
                         r=[bpsX[1], bBH], w=[bHX])
                    k.op("dve", lambda e: e.tensor_tensor(out=H2[:, :n], in0=HX[:, :n], in1=HX[:, :n],
                                                           op=ALU.mult), r=[bHX], w=[bH2])
                    k.op("dve", lambda e: e.tensor_scalar(out=H2[:, :n], in0=H2[:, :n], scalar1=0.044715,
                                                           scalar2=1.0, op0=ALU.mult, op1=ALU.add),
                         r=[bH2], w=[bH2])
                    k.op("dve", lambda e: e.tensor_tensor(out=H2[:, :n], in0=H2[:, :n], in1=HX[:, :n],
                                                           op=ALU.mult), r=[bH2, bHX], w=[bH2])
                    k.op("act", lambda e: e.activation(out=H2[:, :n], in_=H2[:, :n], func=AF.Tanh,
                                                        scale=0.7978845608028654), r=[bH2], w=[bH2])
                    k.op("dve", lambda e: e.tensor_scalar(out=H2[:, :n], in0=H2[:, :n], scalar1=0.5,
                                                           scalar2=0.5, op0=ALU.mult, op1=ALU.add),
                         r=[bH2], w=[bH2])
                    k.op("dve", lambda e: e.tensor_tensor(out=HID[:, hc, :n], in0=H2[:, :n], in1=HX[:, :n],
                                                           op=ALU.mult), r=[bH2, bHX], w=[bHID[hc]])
                if which == 0:
                    for hc in range(2):
                        k.op("pe", lambda e: e.matmul(psX[0][0:64, :n], lhsT=W2[:, hc, :], rhs=HID[:, hc, :n],
                                                       start=(hc == 0), stop=(hc == 1)),
                             r=[bW2, bHID[hc]], w=[bpsX[0]])
                    k.op("act", lambda e: e.copy(out=KCMP[0:64, j0:j0 + n], in_=psX[0][0:64, :n]),
                         r=[bpsX[0]], w=[bKCMP])
                else:
                    for jb in range((n + 127) // 128):
                        m = min(128, n - jb * 128)
                        for hc in range(2):
                            k.op("pe", lambda e: e.matmul(psX[0][0:m, 0:64], lhsT=HID[:, hc, jb * 128:jb * 128 + m],
                                                           rhs=W2[:, hc, :], start=(hc == 0), stop=(hc == 1)),
                                 r=[bW2, bHID[hc]], w=[bpsX[0]])
                        gb = j0 // 128 + jb
                        k.op("act", lambda e: e.copy(out=VC[0:m, gb, 0:64], in_=psX[0][0:m, 0:64]),
                             r=[bpsX[0]], w=[bVC])
                        k.op("pool", lambda e: e.memset(VC[0:m, gb, 64:65], 1.0), w=[bVC])

    st = {"q": 0, "kw": 0, "ps": 0, "pt": 0, "out": 0}
    NT5 = S // 512

    def load_q(I):
        qb = st["q"] % 2
        st["q"] += 1
        for p_ in range(4):
            ld_fm(lambda a, b, p_=p_: QT[qb][0:64, p_, a:b], "q%d" % p_, I * 512, (I + 1) * 512, [bQT[qb]])
        return qb

    ncw = (NCB * 128 + 511) // 512
    emit_sqmax(k, nc, lambda i: ([bKCMP], KCMP[0:64, i * 512:min(NCB * 128, (i + 1) * 512)]), ncw,
               SQT, bSQT, ONESB, bONESB, psS[1:3], bpsS[1:3], MX, bMX, ST[:, 4:5], bST)
    emit_sqmax(k, nc, lambda i: ([bKS], KS[0:64, i * 512:(i + 1) * 512]), NT5,
               SQT, bSQT, ONESB, bONESB, psS[1:3], bpsS[1:3], MX, bMX, ST[:, 5:6], bST)

    def fetch_kw(i):
        wb = st["kw"] % 2
        st["kw"] += 1
        ld_fm(lambda a, b: KW[wb][0:64, a:b], "kw", i * 512, (i + 1) * 512, [bKW[wb]])
        return [bKW[wb]], KW[wb][0:64, 0:512]
    emit_sqmax(k, nc, fetch_kw, NT5, SQT, bSQT, ONESB, bONESB, psS[1:3], bpsS[1:3], MX, bMX, ST[:, 6:7], bST)
    MXQ = A("MXQ", [128, 4, 32], F32); bMXQ = k.buf("MXQ")
    it_ = 0
    qb_nx = load_q(0)
    for i in range(NT5):
        qb_ = qb_nx
        if i + 1 < NT5:
            qb_nx = load_q(i + 1)
        for p in range(4):
            a = it_ % 2
            it_ += 1
            k.op("dve", lambda e: e.tensor_tensor(out=SQT[a][0:64, :], in0=QT[qb_][0:64, p, :],
                                                   in1=QT[qb_][0:64, p, :], op=ALU.mult),
                 r=[bQT[qb_]], w=[bSQT[a]])
            k.op("pe", lambda e: e.matmul(psS[1 + a][:], lhsT=ONESB[0:64, :], rhs=SQT[a][0:64, :],
                                           start=True, stop=True), r=[bSQT[a], bONESB], w=[bpsS[1 + a]])
            k.op("dve", lambda e: e.reduce_max(out=MXQ[:, p, i:i + 1], in_=psS[1 + a][:], axis=AX.X),
                 r=[bpsS[1 + a]], w=[bMXQ])
    for p in range(4):
        k.op("dve", lambda e: e.reduce_max(out=ST[:, p:p + 1], in_=MXQ[:, p, 0:NT5], axis=AX.X),
             r=[bMXQ], w=[bST])
    for p in range(4):
        k.op("dve", lambda e: e.tensor_scalar(out=ST[:, 8:11], in0=ST[:, 4:7], scalar1=ST[:, p:p + 1],
                                               scalar2=None, op0=ALU.mult), r=[bST], w=[bST])
        k.op("act", lambda e: e.activation(out=ST[:, 8:11], in_=ST[:, 8:11], func=AF.Sqrt), r=[bST], w=[bST])
        k.op("dve", lambda e: e.tensor_scalar(out=BTCc[:, p, :], in0=BTC0[:, p, :], scalar1=ST[:, 8:9],
                                               scalar2=None, op0=ALU.subtract), r=[bST, bBTC0], w=[bBTCc])
        k.op("dve", lambda e: e.tensor_scalar(out=BTCs[:, p, :], in0=BT[:, p, :], scalar1=ST[:, 9:10],
                                               scalar2=None, op0=ALU.subtract), r=[bST, bBT], w=[bBTCs])
        k.op("dve", lambda e: e.tensor_scalar(out=BTCw[:, p, :], in0=BT[:, p, 0:8], scalar1=ST[:, 10:11],
                                               scalar2=None, op0=ALU.subtract), r=[bST, bBT], w=[bBTCw])


    def importance_block(I, qb, mb, qi):
        for u in importance_units(I, qb, mb, qi):
            u()

    def importance_units(I, qb, mb, qi):
        return [lambda p=p: importance_head(I, qb, qi, p) for p in range(4)] + [lambda: importance_topk(I, mb, qi)]

    def importance_head(I, qb, qi, p):
        i = 4 * I + qi
        ncols = min(8 * (i + 1), NCB * 128)
        nb = 2 * (i + 1)
        if True:
            io_ = 1016 - 8 * i
            for c0 in range(0, ncols, 512):
                c1 = min(ncols, c0 + 512)
                k.op("pe", lambda e: e.matmul(psA[:, 0:c1 - c0], lhsT=QT[qb][0:64, p, qi * 128:(qi + 1) * 128],
                                               rhs=KCMP[0:64, c0:c1], start=True, stop=True),
                     r=[bQT[qb], bKCMP], w=[bpsA])
                k.op("dve", lambda e: e.tensor_tensor(out=SX[:, c0:c1], in0=psA[:, 0:c1 - c0],
                                                       in1=ITAB[:, p, io_ + c0:io_ + c1], op=ALU.add),
                     r=[bpsA, bITAB], w=[bSX])
            k.op("dve", lambda e: e.reduce_max(out=SC[:, 0:1], in_=SX[:, :ncols], axis=AX.X),
                 r=[bSX], w=[bSC])
            k.op("dve", lambda e: e.tensor_scalar(out=SC[:, 1:2], in0=SC[:, 0:1], scalar1=-1e20,
                                                   scalar2=-1.0, op0=ALU.max, op1=ALU.mult),
                 r=[bSC], w=[bSC])
            k.op("act", lambda e: e.activation(out=EX[:, :ncols], in_=SX[:, :ncols], func=AF.Exp,
                                                bias=SC[:, 1:2], scale=1.0, accum_out=SC[:, 2:3]),
                 r=[bSX, bSC], w=[bEX, bSC])
            k.op("dve", lambda e: e.tensor_scalar(out=SC[:, 3:4], in0=SC[:, 2:3], scalar1=1e-30,
                                                   scalar2=None, op0=ALU.max), r=[bSC], w=[bSC])
            k.op("dve", lambda e: e.reciprocal(out=SC[:, 3:4], in_=SC[:, 3:4]), r=[bSC], w=[bSC])
            if p == 0:
                k.op("dve", lambda e: e.tensor_scalar(out=PG[:, 1:1 + ncols], in0=EX[:, :ncols],
                                                       scalar1=SC[:, 3:4], scalar2=None, op0=ALU.mult),
                     r=[bEX, bSC], w=[bPG])
            else:
                k.op("dve", lambda e: e.scalar_tensor_tensor(out=PG[:, 1:1 + ncols], in0=EX[:, :ncols],
                                                              scalar=SC[:, 3:4], in1=PG[:, 1:1 + ncols],
                                                              op0=ALU.mult, op1=ALU.add),
                     r=[bEX, bSC, bPG], w=[bPG])

    def importance_topk(I, mb, qi):
        i = 4 * I + qi
        nb = 2 * (i + 1)
        k.op("dve", lambda e: e.reduce_sum(out=IMP[:, :nb],
                                            in_=PG[:, 0:4 * nb].rearrange("p (n r) -> p n r", r=4),
                                            axis=AX.X), r=[bPG], w=[bIMP])
        k.op("dve", lambda e: e.tensor_tensor(out=IMP[:, :nb], in0=IMP[:, :nb],
                                               in1=PG[:, 4:4 * nb + 1:4], op=ALU.add),
             r=[bPG, bIMP], w=[bIMP])
        k.op("dve", lambda e: e.tensor_tensor(out=I2[:, :nb], in0=IMP[:, :nb], in1=ATAB[:, 256 - nb:256],
                                               op=ALU.mult), r=[bIMP, bATAB], w=[bI2])
        k.op("dve", lambda e: e.tensor_tensor(out=I2[:, :nb], in0=I2[:, :nb], in1=BTAB[:, 256 - nb:256],
                                               op=ALU.add), r=[bI2, bBTAB], w=[bI2])
        k.op("dve", lambda e: e.memset(I2[:, 0:1], 1e6), w=[bI2])
        k.op("dve", lambda e: e.max(out=M8[:, 0:8], in_=I2[:]), r=[bI2], w=[bM8])
        k.op("dve", lambda e: e.match_replace(out=I3[:], in_to_replace=M8[:, 0:8], in_values=I2[:],
                                               imm_value=-2.0), r=[bI2, bM8], w=[bI3])
        k.op("dve", lambda e: e.max(out=M8[:, 8:16], in_=I3[:]), r=[bI3], w=[bM8])
        k.op("dve", lambda e: e.tensor_scalar(out=MQ[:], in0=I2[:], scalar1=M8[:, 15:16], scalar2=1.0,
                                               op0=ALU.is_ge, op1=ALU.subtract), r=[bI2, bM8], w=[bMQ])
        nch = 2 if nb > 128 else 1
        for ch in range(nch):
            k.op("pe", lambda e: e.transpose(out=psX[0][:, 0:128], in_=MQ[:, ch * 128:(ch + 1) * 128],
                                              identity=IDB[:]), r=[bMQ, bIDB], w=[bpsX[0]])
            k.op("act", lambda e: e.copy(out=MT[mb][:, ch, qi * 128:(qi + 1) * 128], in_=psX[0][:, 0:128]),
                 r=[bpsX[0]], w=[bMT[mb]])

    def branch(I, qb, heads, br, steps, bias_fn, first, hook=None):
        nst = len(steps)
        staged = []

        def stage_a(n_):
            lk, kb, extra, bidx, vl, vb = steps[n_]
            pts = []
            for hi_, p in enumerate(heads):
                ps = st["ps"] % 4
                st["ps"] += 1
                pts.append((ps, None))
            for hi_, p in enumerate(heads):
                ps = pts[hi_][0]
                k.op("pe", lambda e: e.matmul(psS[ps][:], lhsT=lk, rhs=QT[qb][:, p, :], start=True,
                                               stop=(len(extra) == 0)), r=kb + [bQT[qb]], w=[bpsS[ps]])
            for xi, (xl, xr, xb) in enumerate(extra):
                for hi_, p in enumerate(heads):
                    ps = pts[hi_][0]
                    k.op("pe", lambda e: e.matmul(psS[ps][:], lhsT=xl, rhs=xr, start=False,
                                                   stop=(xi == len(extra) - 1)), r=xb, w=[bpsS[ps]])
            out = []
            for hi_, p in enumerate(heads):
                ps = pts[hi_][0]
                pt = st["pt"] % NPT
                st["pt"] += 1
                bias, bbuf = bias_fn(p, bidx)
                k.op("act", lambda e: e.activation(out=PT[pt][:], in_=psS[ps][:], func=AF.Exp, bias=bias,
                                                    scale=1.0), r=[bpsS[ps], bbuf], w=[bPT[pt]])
                out.append(pt)
            return out

        def stage_b(n_, pts):
            lk, kb, extra, bidx, vl, vb = steps[n_]
            for hi_, p in enumerate(heads):
                k.op("pe", lambda e: e.matmul(psO[hi_][0:65, :], lhsT=vl, rhs=PT[pts[hi_]][:], start=(n_ == 0),
                                               stop=(n_ == nst - 1)), r=[vb, bPT[pts[hi_]]], w=[bpsO[hi_]])

        for n_ in range(nst):
            staged.append((n_, stage_a(n_)))
            if n_ == min(1, nst - 1):
                while deferred:
                    deferred.pop(0)()
            if hook is not None:
                hook(n_, nst)
            if len(staged) > 1:
                stage_b(*staged.pop(0))
        while staged:
            stage_b(*staged.pop(0))
        deferred.append(lambda: epilogue(qb, heads, br, first))

    def epilogue(qb, heads, br, first):
        for hi_, p in enumerate(heads):
            po = psO[hi_]
            k.op("dve", lambda e: e.tensor_scalar(out=RL[64:65, :], in0=po[64:65, :], scalar1=1e-30, scalar2=None,
                                                   op0=ALU.max), r=[bpsO[hi_]], w=[bRL])
            k.op("dve", lambda e: e.reciprocal(out=RL[64:65, :], in_=RL[64:65, :]), r=[bRL], w=[bRL])
            k.op("act", lambda e: e.copy(out=OBF[:], in_=po[0:64, :]), r=[bpsO[hi_]], w=[bOBF])
            k.op("pe", lambda e: e.matmul(psX[0][0:64, :], lhsT=ONESF[64:65, :], rhs=RL[64:65, :], start=True,
                                           stop=True), r=[bONESF, bRL], w=[bpsX[0]])
            k.op("dve", lambda e: e.tensor_tensor(out=TT[:], in0=OBF[:], in1=psX[0][0:64, :], op=ALU.mult),
                 r=[bOBF, bpsX[0]], w=[bTT])
            gr = p * 3 + br
            k.op("pe", lambda e: e.matmul(psX[0][0:64, :], lhsT=SELG[:, gr, :], rhs=GT[qb][:, :], start=True,
                                           stop=True), r=[bSELG, bGT[qb]], w=[bpsX[0]])
            if first:
                k.op("dve", lambda e: e.tensor_tensor(out=ACC[hi_][:], in0=TT[:], in1=psX[0][0:64, :], op=ALU.mult),
                     r=[bTT, bpsX[0]], w=[bACC[hi_]])
            else:
                k.op("dve", lambda e: e.tensor_tensor(out=TT[:], in0=TT[:], in1=psX[0][0:64, :], op=ALU.mult),
                     r=[bTT, bpsX[0]], w=[bTT])
                k.op("dve", lambda e: e.tensor_tensor(out=ACC[hi_][:], in0=ACC[hi_][:], in1=TT[:], op=ALU.add),
                     r=[bTT, bACC[hi_]], w=[bACC[hi_]])

    def load_tile(I):
        qb = load_q(I)
        ld_fm(lambda a, b: GT[qb][:, a:b], "gt", I * 512, (I + 1) * 512, [bGT[qb]])
        wb = I % 2
        jlo = max(0, 4 * I - 4)
        lo = jlo - (4 * I - 4)
        ld_fm(lambda a, b: KW[wb][0:64, lo * 128 + a:lo * 128 + b], "kw", jlo * 128, (4 * I + 4) * 128,
              [bKW[wb]])
        ld_tok(lambda a, b: VW[wb][:, lo + a:lo + b, 0:64], "vw", jlo * 128, (4 * I + 4) * 128, [bVW[wb]])
        return qb

    if getattr(io, "after_prologue", None) is not None:
        io.after_prologue()
    deferred = []
    qb_next = load_tile(0)
    for qi in range(4):
        importance_block(0, qb_next, 0, qi)
    for I in range(NQ):
        qb = qb_next
        mb = I % 2
        wb = I % 2
        jlo = max(0, 4 * I - 4)
        pend = []
        while deferred:
            deferred.pop(0)()
        if I + 1 < NQ:
            qb_next = load_tile(I + 1)
            for qi in range(4):
                pend += importance_units(I + 1, qb_next, (I + 1) % 2, qi)
        gap = max(1, (2 * (4 * I + 4)) // 22)
        half = [10]

        def hook(n_, nst):
            if pend and half[0] > 0 and n_ >= 1 and (n_ - 1) % gap == 0:
                pend.pop(0)()
                half[0] -= 1

        for hp in range(2):
            heads = (2 * hp, 2 * hp + 1)
            steps = []
            for jb in range(NCB):
                dd = I - 4 * jb
                if dd < 0:
                    continue
                extra = []
                if dd <= 4:
                    extra.append((BIGI[:], CM[:, dd, :], [bBIGI, bCM]))
                steps.append((KCMP[:, jb * 128:(jb + 1) * 128], [bKCMP], extra, dd + 28, VC[:, jb, :], bVC))
            branch(I, qb, heads, 0, steps, lambda p, ix: (BTCc[:, p, ix:ix + 1], bBTCc), True)
            steps = []
            for jb in range(4 * I + 4):
                extra = [(ESEL[:, jb % 64, :], MT[mb][:, jb // 64, :], [bESEL, bMT[mb]])]
                if jb >= 4 * I:
                    extra.append((BIGI[:], DM[:, jb - 4 * I, :], [bBIGI, bDM]))
                steps.append((KS[:, jb * 128:(jb + 1) * 128], [bKS], extra, 4 * I - jb + 3, VS[:, jb, :], bVS))
            half[0] = 10
            branch(I, qb, heads, 1, steps, lambda p, ix: (BTCs[:, p, ix:ix + 1], bBTCs), False, hook)
            steps = []
            for jb in range(jlo, 4 * I + 4):
                lw = jb - (4 * I - 4)
                if lw < 4:
                    extra = [(BIGI[:], WM[:, lw, :], [bBIGI, bWM])]
                else:
                    extra = [(BIGI[:], DM[:, lw - 4, :], [bBIGI, bDM])]
                steps.append((KW[wb][:, lw * 128:(lw + 1) * 128], [bKW[wb]], extra, 4 * I - jb + 3,
                              VW[wb][:, lw, :], bVW[wb]))
            branch(I, qb, heads, 2, steps, lambda p, ix: (BTCw[:, p, ix:ix + 1], bBTCw), False)
            def emit_out(I=I, hp=hp, heads=heads):
                n0 = len(k.stores)
                for hi_, p in enumerate(heads):
                    ob = st["out"] % 2
                    st["out"] += 1
                    k.op("act", lambda e: e.copy(out=OUTB[ob][:], in_=ACC[hi_][:]), r=[bACC[hi_]], w=[bOUTB[ob]])
                    k.store("sp", io.out(p, I), OUTB[ob][:], r=[bOUTB[ob]])
                if getattr(io, "out_done", None) is not None:
                    io.out_done(I, hp, k.stores[n0:])
            deferred.append(emit_out)
        while pend:
            pend.pop(0)()
    while deferred:
        deferred.pop(0)()


def k2a_consts(S):
    sl16 = alibi_slopes(16)
    A_, B_ = ab_tables()
    c = {"dm": dm_table(), "wm": wm_table(), "cm": cm_table(), "bigi": bigi_table(),
         "idb": np.eye(128).astype(np.float32), "esel": esel_table(), "atab": A_, "btab": B_,
         "selg": selg_table()}
    per_g = []
    for g in range(4):
        ms = sl16[g * 4:(g + 1) * 4]
        per_g.append({
            "bt": np.ascontiguousarray(np.stack([bt_table(m) for m in ms], 1)),
            "btc": np.ascontiguousarray(np.stack([btc_table(m) for m in ms], 1)),
            "itab": np.ascontiguousarray(np.stack([itab_table(m) for m in ms], 1)),
            "qaug": np.stack([q_aug_rows(m, 512) for m in ms], 0),
        })
    return c, per_g


def prep_k2a(pT, vt, g, consts, per_g, wts, S):
    d = dict(consts)
    d.update(per_g[g])
    d.update(wts)
    d["qa"] = np.ascontiguousarray(pT[g * 256:(g + 1) * 256].reshape(4, 64, S))
    d["kca"] = np.ascontiguousarray(pT[1024 + g * 64:1024 + (g + 1) * 64])
    d["vca"] = np.ascontiguousarray(pT[1280 + g * 64:1280 + (g + 1) * 64])
    d["ksa"] = np.ascontiguousarray(pT[1536 + g * 64:1536 + (g + 1) * 64])
    d["kwa"] = np.ascontiguousarray(pT[2048 + g * 64:2048 + (g + 1) * 64])
    d["vs"] = np.ascontiguousarray(vt[:, g * 64:(g + 1) * 64])
    d["vw"] = np.ascontiguousarray(vt[:, 256 + g * 64:256 + (g + 1) * 64])
    d["gt"] = np.ascontiguousarray(pT[2560 + g * 12:2560 + (g + 1) * 12])
    return d


def k2a_weights(pek, w1k, w2k, pev, w1v, w2v):
    f = lambda a: np.ascontiguousarray(np.asarray(a, np.float32))
    return {"pek": f(np.asarray(pek).T), "pev": f(np.asarray(pev).T), "w1k": f(w1k), "w2k": f(w2k),
            "w1v": f(w1v), "w2v": f(w2v)}


TPC = BATCH * SEQ // NCORES
CPB = NCORES // BATCH


def _run(nc, in_maps):
    res = run_bass_kernel_spmd(nc, in_maps, core_ids=list(range(NCORES)))
    return res.results


def _with_halo(full_T, c):
    b, j = divmod(c, CPB)
    a = full_T[b]
    out = np.zeros((a.shape[0], 2 + TPC), a.dtype)
    lo = j * TPC
    if j > 0:
        out[:, 0:2] = a[:, lo - 2:lo]
    out[:, 2:] = a[:, lo:lo + TPC]
    return out


def kernel_unfused(x, norm_mix_g, norm_ffn_g, final_norm_g,
           nsa_w_in, nsa_cmp_k_pe, nsa_cmp_k_w1, nsa_cmp_k_w2,
           nsa_cmp_v_pe, nsa_cmp_v_w1, nsa_cmp_v_w2, nsa_w_out,
           diff_w_in, diff_lam_q1, diff_lam_k1, diff_lam_q2, diff_lam_k2,
           diff_subln_g, diff_w_out,
           ffn_w_up, ffn_conv_w, ffn_conv_b, ffn_w_down):
    f32 = lambda a: np.ascontiguousarray(np.asarray(a, dtype=np.float32))
    x = f32(x)
    S = SEQ
    xT = [np.ascontiguousarray(x[b].T) for b in range(BATCH)]

    def tok_shards(full_T):
        return [np.ascontiguousarray(full_T[c // CPB][:, (c % CPB) * TPC:(c % CPB + 1) * TPC])
                for c in range(NCORES)]

    fm = [(0, 1024, 0.125, AF.Copy), (1024, 2560, 1.0, AF.Copy), (2560, 2608, 1.0, AF.Sigmoid)]
    tok = [(1792, 2048), (2304, 2560)]
    nc1 = build_k1(TPC, 2608, fm, tok)
    gl = g_layout(norm_mix_g[0])
    w = f32(nsa_w_in[0])
    r1 = _run(nc1, [{"xT": s, "g": gl, "w": w} for s in tok_shards(xT)])
    pT = [np.concatenate([r1[b * CPB + j]["projT"] for j in range(CPB)], axis=1) for b in range(BATCH)]
    vt = [np.concatenate([r1[b * CPB + j]["vtok"] for j in range(CPB)], axis=0) for b in range(BATCH)]
    del r1
    consts, per_g = k2a_consts(S)
    wts = k2a_weights(nsa_cmp_k_pe[0], nsa_cmp_k_w1[0], nsa_cmp_k_w2[0],
                      nsa_cmp_v_pe[0], nsa_cmp_v_w1[0], nsa_cmp_v_w2[0])
    nc2 = build_k2a(S)
    r2 = _run(nc2, [prep_k2a(pT[c // CPB], vt[c // CPB], c % CPB, consts, per_g, wts, S)
                    for c in range(NCORES)])
    aT = [np.concatenate([r2[b * CPB + g]["oT"] for g in range(CPB)], axis=0) for b in range(BATCH)]
    del r2, pT, vt
    cw, cb = conv_layouts(ffn_conv_w[0], ffn_conv_b[0])
    nc3 = build_k3(TPC, False)
    ins = [{"xT": _with_halo(xT, c), "aT": _with_halo(aT, c), "wo": f32(nsa_w_out[0]),
            "wu": f32(ffn_w_up[0]), "wd": f32(ffn_w_down[0]), "cw": cw, "cb": cb,
            "g": g_layout(norm_ffn_g[0]), "gf": g_layout(final_norm_g)} for c in range(NCORES)]
    r3 = _run(nc3, ins)
    xT = [np.concatenate([r3[b * CPB + j]["yT"] for j in range(CPB)], axis=1) for b in range(BATCH)]
    del r3, ins, aT

    lambda_init = 0.8 - 0.6 * float(np.exp(-0.3 * 1))
    fm = [(0, 1024, 0.125, AF.Copy), (1024, 2048, 1.0, AF.Copy)]
    tok = [(2048, 2560), (2560, 3072)]
    nc4 = build_k1(TPC, 3072, fm, tok)
    gl = g_layout(norm_mix_g[1])
    w = f32(diff_w_in[0])
    r4 = _run(nc4, [{"xT": s, "g": gl, "w": w} for s in tok_shards(xT)])
    pT = [np.concatenate([r4[b * CPB + j]["projT"] for j in range(CPB)], axis=1) for b in range(BATCH)]
    vt = [np.concatenate([r4[b * CPB + j]["vtok"] for j in range(CPB)], axis=0) for b in range(BATCH)]
    del r4
    sl8 = alibi_slopes(8)
    dmt, bigit = dm_table(), bigi_table()
    lam = np.stack([f32(diff_lam_q1[0]), f32(diff_lam_k1[0]), f32(diff_lam_q2[0]), f32(diff_lam_k2[0])], 0)
    lam = np.ascontiguousarray(np.broadcast_to(lam[None], (128, 4, 64)))
    sg = f32(diff_subln_g[0]).reshape(128, 1)
    ins = []
    for c in range(NCORES):
        b, hp = divmod(c, CPB)
        qa = np.ascontiguousarray(pT[b][hp * 256:(hp + 1) * 256].reshape(4, 64, S))
        ka = np.ascontiguousarray(pT[b][1024 + hp * 256:1024 + (hp + 1) * 256].reshape(4, 64, S))
        bt = np.stack([bt_table(sl8[hp * 2 + hh]) for hh in range(2)], 0)
        qaug = np.stack([q_aug_rows(sl8[hp * 2 + hh], 512) for hh in range(2)], 0)
        ins.append({"qa": qa, "ka": ka, "qaug": qaug,
                    "v": np.ascontiguousarray(vt[b][:, hp * 256:(hp + 1) * 256]),
                    "bt": bt, "dm": dmt, "bigi": bigit, "lam": lam, "sg": sg})
    nc5 = build_k2b(S, lambda_init)
    r5 = _run(nc5, ins)
    aT = [np.concatenate([r5[b * CPB + hp]["oT"] for hp in range(CPB)], axis=0) for b in range(BATCH)]
    del r5, ins, pT, vt
    cw, cb = conv_layouts(ffn_conv_w[1], ffn_conv_b[1])
    nc6 = build_k3(TPC, True)
    ins = [{"xT": _with_halo(xT, c), "aT": _with_halo(aT, c), "wo": f32(diff_w_out[0]),
            "wu": f32(ffn_w_up[1]), "wd": f32(ffn_w_down[1]), "cw": cw, "cb": cb,
            "g": g_layout(norm_ffn_g[1]), "gf": g_layout(final_norm_g)} for c in range(NCORES)]
    r6 = _run(nc6, ins)
    out = np.empty((BATCH, SEQ, D_MODEL), np.float32)
    for c in range(NCORES):
        b, j = divmod(c, CPB)
        out[b, j * TPC:(j + 1) * TPC, :] = r6[c]["yT"].T
    return out

TPC = BATCH * SEQ // NCORES
CPB = NCORES // BATCH
GROUPS = [[0, 1, 2, 3], [4, 5, 6, 7]]
RB1 = 640
RB4 = 512


def _chunks(t0, t1):
    j = t0 // TPC
    while j * TPC < t1:
        lo, hi = max(t0, j * TPC), min(t1, (j + 1) * TPC)
        yield j, lo, hi
        j += 1


class IOK2aFused:
    ROW = {"q0": 0, "q1": 64, "q2": 128, "q3": 192, "kc": 256, "vc": 320, "ks": 384, "kw": 448, "gt": 512}

    def __init__(self, L1F, L1T, SND2, st, k=None, G2=None):
        self.WF, self.WT, self.SND2, self.st = L1F, L1T, SND2, st
        self.k, self.G2, self.pend = k, G2, {}

    def fm(self, name, t0, t1):
        row0 = self.ROW[name]
        nr = 12 if name == "gt" else 64
        rr = lambda j: ((row0 // 128) * 4 + j) * 128 + row0 % 128
        return [(self.WF[rr(j):rr(j) + nr, lo - j * TPC:hi - j * TPC], lo, hi)
                for (j, lo, hi) in _chunks(t0, t1)]

    def tok(self, name, t0, t1):
        c0 = 0 if name == "vs" else 64
        return [(self.WT[lo:hi, c0:c0 + 64], lo, hi) for (j, lo, hi) in _chunks(t0, t1)]

    def out(self, p, I):
        j, i8 = divmod(I, 8)
        return self.SND2[j * 256 + p * 64:j * 256 + (p + 1) * 64, i8 * 512:(i8 + 1) * 512]

    def out_done(self, I, hp, toks):
        self.pend.setdefault(hp, []).extend(toks)
        if I % 8 == 7:
            i = (I // 8) * 2 + hp
            self.k.collective_async(self.SND2[i * 128:(i + 1) * 128, :], self.G2[i * 512:(i + 1) * 512, :],
                                    GROUPS, self.pend.pop(hp))


class IOK2bFused:
    def __init__(self, L4F, L4T, SND5, d, k=None, G5=None):
        self.WF, self.WT, self.SND5 = L4F, L4T, SND5
        self.ctx, self.G5, self.pend = k, G5, {}
        self.qaug, self.bt_in, self.dm_in, self.bigi_in, self.lam_in, self.sg_in = (
            d["b_qaug"], d["b_bt"], d["a_dm"], d["a_bigi"], d["b_lam"], d["b_sg"])

    def _fm(self, row0, t0, t1):
        rr = lambda j: ((row0 // 128) * 4 + j) * 128 + row0 % 128
        return [(self.WF[rr(j):rr(j) + 64, lo - j * TPC:hi - j * TPC], lo, hi)
                for (j, lo, hi) in _chunks(t0, t1)]

    def q(self, hh, c, t0, t1):
        return self._fm((hh * 2 + c) * 64, t0, t1)

    def k(self, hh, c, t0, t1):
        return self._fm(256 + (hh * 2 + c) * 64, t0, t1)

    def v(self, hh, t0, t1):
        out = []
        for (j, lo, hi) in _chunks(t0, t1):
            a = lo
            while a < hi:
                tl = a - j * TPC
                b = min(hi, j * TPC + (tl // 2048 + 1) * 2048)
                row = ((tl // 2048) * 4 + j) * 2048 + tl % 2048
                out.append((self.WT[row:row + (b - a), hh * 128:(hh + 1) * 128], a, b))
                a = b
        return out

    def out(self, hh, I):
        j, i8 = divmod(I, 8)
        return self.SND5[j * 256 + hh * 128:j * 256 + (hh + 1) * 128, i8 * 512:(i8 + 1) * 512]

    def out_done(self, I, hh, toks):
        self.pend.setdefault(hh, []).extend(toks)
        if I % 8 == 7:
            i = (I // 8) * 2 + hh
            self.ctx.collective_async(self.SND5[i * 128:(i + 1) * 128, :], self.G5[i * 512:(i + 1) * 512, :],
                                    GROUPS, self.pend.pop(hh))


class IOK3Fused:
    def __init__(self, layer, d, xh0, X2, LA, LHA, LH, SND3, yT, SNDH=None, k=None, GH=None):
        self.layer, self.xh0, self.X2, self.SND3, self.yT = layer, xh0, X2, SND3, yT
        self.WA = LA.rearrange("(h g p) t -> h p g t", h=2, g=4, p=128)
        self.WHA = LHA.rearrange("(h g p) t -> h p g t", h=2, g=4, p=128)
        self.WH = LH.rearrange("(dc p) t -> p dc t", p=128)
        sfx = str(layer)
        self.wo_in, self.wu_in, self.wd_in = d["wo" + sfx], d["wu" + sfx], d["wd" + sfx]
        self.cw_in, self.cb_in, self.g_in, self.gf_in = d["cw" + sfx], d["cb" + sfx], d["g_ffn" + sfx], d["g_fin"]
        self.halo_scale = d["m0"]
        self.tail_dst = (lambda oc: SND3[oc * 128:(oc + 1) * 128, 0:2]) if layer == 0 else None
        if layer == 0:
            self.gf_in = d["g_mix1"]
            self.h_dst = lambda oc, tcol, n: SNDH[(tcol // 512) * 1024 + oc * 128:(tcol // 512) * 1024 + (oc + 1) * 128,
                                                  tcol % 512:tcol % 512 + n]
            self.k, self.GH, self.SNDH, self.pend = k, GH, SNDH, []

    def tile_done(self, it, toks, N=256):
        if self.layer != 0:
            return
        self.pend.extend(toks)
        per = 512 // N
        if it % per == per - 1:
            c = it // per
            self.k.collective_async(self.SNDH[c * 1024:(c + 1) * 1024, :], self.GH[c * 4096:(c + 1) * 4096, :],
                                    GROUPS, self.pend)
            self.pend = []

    def x_src(self, col0, n):
        if self.layer == 0:
            return self.xh0.rearrange("(dc p) t -> p dc t", p=128)[:, :, col0:col0 + n]
        if col0 == 0:
            assert n == 2
            return self.WH
        return self.X2.rearrange("(dc p) t -> p dc t", p=128)[:, :, col0 - 2:col0 - 2 + n]

    def a_src(self, col0, n):
        if col0 == 0:
            assert n == 2
            return [(h, 2, self.WHA[h]) for h in range(2)]
        return [(h, 2, self.WA[h][:, :, col0 - 2:col0 - 2 + n]) for h in range(2)]

    def y_dst(self, oc, tcol, n):
        dst = self.X2 if self.layer == 0 else self.yT
        return dst[oc * 128:(oc + 1) * 128, tcol:tcol + n]

    def x1_dst(self, dc, tcol, n):
        return self.X1D[dc * 128:(dc + 1) * 128, tcol:tcol + n]

    def x1_src(self, tcol, n):
        return self.X1D.rearrange("(dc p) t -> p dc t", p=128)[:, :, tcol:tcol + n]

    def aff_dst(self, gch, tcol, n):
        return self.AFFD[gch * 128:(gch + 1) * 128, tcol:tcol + n]

    def aff_src(self, tcol, n):
        return self.AFFD.rearrange("(kc p) t -> p kc t", p=128)[:, :, tcol:tcol + n]


FUSED_INPUTS = [("xh0", [D_MODEL, 2 + TPC], F32), ("w_in0g", [D_MODEL, 652], F32), ("w_in1g", [D_MODEL, 768], F32),
                ("g_mix0", [128, 8], F32), ("g_mix1", [128, 8], F32), ("g_ffn0", [128, 8], F32),
                ("g_ffn1", [128, 8], F32), ("g_fin", [128, 8], F32), ("m0", [128, 1], F32),
                ("b_qaug", [2, 3, 512], BF16), ("b_bt", [2, 128, 128], F32), ("b_lam", [128, 4, 64], F32),
                ("b_sg", [128, 1], F32)]
for _l in range(2):
    FUSED_INPUTS += [("wo%d" % _l, [D_MODEL, D_MODEL], F32), ("wu%d" % _l, [D_MODEL, 2 * D_FF], F32),
                     ("wd%d" % _l, [D_FF, D_MODEL], F32), ("cw%d" % _l, [128, 44, 3], F32),
                     ("cb%d" % _l, [128, 44], F32)]
FUSED_INPUTS += [("a_" + n, sh, dt) for (n, sh, dt) in K2A_STATIC]


def build_fused(lambda_init, upto=None):
    nc = new_nc()
    S, T = SEQ, TPC
    d = {n: nc.dram_tensor(n, sh, dt, kind="ExternalInput").ap() for (n, sh, dt) in FUSED_INPUTS}
    yT = nc.dram_tensor("yT", [D_MODEL, T], F32, kind="ExternalOutput").ap()
    sc = lambda n, sh, dt: nc.dram_tensor(n, sh, dt).ap()
    SND1F = sc("SND1F", [4 * RB1, T], BF16); G1F = sc("G1F", [16 * RB1, T], BF16)
    SND1T = sc("SND1T", [4 * T, 128], BF16); G1T = sc("G1T", [16 * T, 128], BF16)
    SND2 = sc("SND2", [4 * 256, T], BF16); G2 = sc("G2", [16 * 256, T], BF16)
    X2 = sc("X2", [D_MODEL, T], F32)
    SND3 = sc("SND3", [D_MODEL, 2], F32); G3 = sc("G3", [4 * D_MODEL, 2], F32)
    SND4F = sc("SND4F", [4 * RB4, T], BF16); G4F = sc("G4F", [16 * RB4, T], BF16)
    SND4T = sc("SND4T", [4 * T, 256], BF16); G4T = sc("G4T", [16 * T, 256], BF16)
    SND5 = sc("SND5", [4 * 256, T], BF16); G5 = sc("G5", [16 * 256, T], BF16)
    L1F = sc("L1F", [4 * RB1, T], BF16); L1T = sc("L1T", [4 * T, 128], BF16)
    L4F = sc("L4F", [4 * RB4, T], BF16); L4T = sc("L4T", [4 * T, 256], BF16)
    LA2 = sc("LA2", [1024, T], BF16); LA5 = sc("LA5", [1024, T], BF16)
    LHA2 = sc("LHA2", [1024, 2], BF16); LHA5 = sc("LHA5", [1024, 2], BF16)
    LH = sc("LH", [D_MODEL, 2], F32)
    k = Ctx(nc)
    r = nc.sync.partition_id() % 4

    dbg = nc.dram_tensor("dbg", [2560, 4096], BF16, kind="ExternalOutput").ap() if upto else None

    def stop_here(tag, src_ap):
        if upto != tag:
            return False
        k.barrier()
        db = Buf("dbg")
        k.dma("sp", dbg[0:src_ap.shape[0], 0:src_ap.shape[1]], src_ap, w=[db], own=db)
        k.stores.append(db.w)
        k.finish()
        return True

    def gather_rows(SND, G, rows):
        n = SND.shape[0] // rows
        k.all_gather_chunks([(SND[i * rows:(i + 1) * rows, :], G[i * 4 * rows:(i + 1) * 4 * rows, :])
                             for i in range(n)], GROUPS)

    def extract_window(G, L):
        n = L.shape[0] * L.shape[1]
        a = n // 16384
        assert a * 16384 == n and G.shape[0] * G.shape[1] == 4 * n
        gf = G.rearrange("r t -> (r t)").rearrange("(q a l) -> q a l", q=4, a=a, l=16384)
        lf = L.rearrange("r t -> (r t)").rearrange("(a l) -> a l", a=a, l=16384)
        bX = Buf("extract")
        k.dma("sp", lf, gf[bass.ds(r, 1)].rearrange("o a l -> (o a) l"), w=[bX], own=bX)

    def extract_att(G, L, LHA_):
        extract_window(G, L)
        gq = G.rearrange("(q x) t -> q x t", q=4)
        bX2 = Buf("extract")
        with nc.allow_non_contiguous_dma(reason="2-column conv halo"):
            k.dma("sp", LHA_, gq[bass.ds((r + 3) % 4, 1), :, T - 2:T].rearrange("o x t -> (o x) t"), w=[bX2],
                  own=bX2)

    SNDH = sc("SNDH", [8 * D_MODEL, 512], BF16)
    GH = sc("GH", [32 * D_MODEL, 512], BF16)
    GHv = GH.rearrange("(it j dc p) t -> p it j dc t", it=8, j=4, dc=8, p=128)

    def ht_src(t0):
        j, tl = divmod(t0, T)
        return GHv[:, tl // 512, j, :, :]

    k.begin_phase("A_")
    emit_norm(k, nc, T, d["xh0"].rearrange("(dc p) t -> p dc t", p=128)[:, :, 2:2 + T], d["g_mix0"],
              lambda dc, t0: SNDH[(t0 // 512) * 1024 + dc * 128:(t0 // 512) * 1024 + (dc + 1) * 128, 0:512],
              lambda it, toks: k.collective_async(SNDH[it * 1024:(it + 1) * 1024, :],
                                                  GH[it * 4096:(it + 1) * 4096, :], GROUPS, toks))
    k.end_phase()

    k.begin_phase("P_")

    def fm_route_a(c0, c1, t0):
        j, tl = divmod(t0, T)
        row = ((c0 // 128) * 4 + j) * 128 + c0 % 128
        return [(0, c1 - c0, L1F[row:row + c1 - c0, tl:tl + 512])]

    def tok_route_a(c0, c1, t0, tb):
        return [(0, 128, L1T[t0 + tb * 128:t0 + (tb + 1) * 128, 0:128])]

    emit_proj(k, nc, S, 652, ht_src, d["w_in0g"],
              [(0, 256, 0.125, AF.Copy), (256, 512, 1.0, AF.Copy), (512, 524, 1.0, AF.Sigmoid)], fm_route_a,
              [(524, 652)], tok_route_a)
    k.end_phase()

    WB = [(sc("WOB%d" % l, [D_MODEL, D_MODEL], BF16), sc("WUB%d" % l, [D_MODEL, 2 * D_FF], BF16),
           sc("WDB%d" % l, [D_FF, D_MODEL], BF16)) for l in range(2)]

    def precast_weights():
        for l in range(2):
            for (src, dst) in ((d["wo%d" % l], WB[l][0]), (d["wu%d" % l], WB[l][1]), (d["wd%d" % l], WB[l][2])):
                R_, C_ = src.shape
                bw = Buf("wcast")
                for r0 in range(0, R_, 128):
                    for c0 in range(0, C_, 1024):
                        c1 = min(C_, c0 + 1024)
                        k.dma("pool", dst[r0:r0 + 128, c0:c1], src[r0:r0 + 128, c0:c1], w=[bw], own=bw)

    k.begin_phase("B_")
    io_b = IOK2aFused(L1F, L1T, SND2, {n: d["a_" + n] for (n, _, _) in K2A_STATIC}, k, G2)
    io_b.after_prologue = precast_weights
    emit_k2a(k, nc, S, io_b)
    k.end_phase()
    extract_att(G2, LA2, LHA2)
    k.barrier()
    if stop_here("B2", LA2) or stop_here("B2H", LHA2):
        return nc

    k.begin_phase("C_")
    X1D = sc("X1D", [D_MODEL, T], F32)
    AFFD = sc("AFFD", [D_FF, T], BF16)
    io_c = IOK3Fused(0, d, d["xh0"], X2, LA2, LHA2, LH, SND3, yT, SNDH, k, GH)
    io_c.X1D, io_c.AFFD = X1D, AFFD
    io_c.wb = WB[0]
    emit_k3(k, nc, T, False, io_c, 512, "a")
    k.end_phase()
    k.begin_phase("Cb_")
    emit_k3(k, nc, T, False, io_c, 512, "b")
    k.end_phase()
    if upto == "C":
        k.barrier()
        k.finish()
        return nc
    k.all_gather_chunks([(SND3, G3)], GROUPS)
    bXh = Buf("extract")
    k.dma("sp", LH, G3[bass.ds(((r + 3) % 4) * D_MODEL, D_MODEL), :], w=[bXh], own=bXh)

    k.begin_phase("D_")

    def fm_route_d(c0, c1, t0):
        j, tl = divmod(t0, T)
        row = ((c0 // 128) * 4 + j) * 128
        return [(0, 128, L4F[row:row + 128, tl:tl + 512])]

    def tok_route_d(c0, c1, t0, tb):
        j, tl = divmod(t0 + tb * 128, T)
        row = ((tl // 2048) * 4 + j) * 2048 + tl % 2048
        return [(0, 256, L4T[row:row + 128, 0:256])]

    emit_proj(k, nc, S, 768, ht_src, d["w_in1g"],
              [(0, 256, 0.125, AF.Copy), (256, 512, 1.0, AF.Copy)], fm_route_d, [(512, 768)], tok_route_d)
    k.end_phase()

    k.begin_phase("E_")
    emit_k2b(k, nc, S, lambda_init, IOK2bFused(L4F, L4T, SND5, d, k, G5))
    k.end_phase()
    if upto == "E":
        k.barrier()
        k.finish()
        return nc
    extract_att(G5, LA5, LHA5)
    k.barrier()

    k.begin_phase("F_")
    io_f = IOK3Fused(1, d, d["xh0"], X2, LA5, LHA5, LH, SND3, yT)
    io_f.X1D, io_f.AFFD = X1D, AFFD
    io_f.wb = WB[1]
    emit_k3(k, nc, T, True, io_f, 512, "a")
    k.end_phase()
    k.begin_phase("Fb_")
    emit_k3(k, nc, T, True, io_f, 512, "b")
    k.barrier()
    k.finish()
    print("[fused] semaphores used:", k.nsem)
    return nc


def fused_inputs(x, norm_mix_g, norm_ffn_g, final_norm_g,
                 nsa_w_in, nsa_cmp_k_pe, nsa_cmp_k_w1, nsa_cmp_k_w2,
                 nsa_cmp_v_pe, nsa_cmp_v_w1, nsa_cmp_v_w2, nsa_w_out,
                 diff_w_in, diff_lam_q1, diff_lam_k1, diff_lam_q2, diff_lam_k2,
                 diff_subln_g, diff_w_out,
                 ffn_w_up, ffn_conv_w, ffn_conv_b, ffn_w_down):
    f32 = lambda a: np.ascontiguousarray(np.asarray(a, dtype=np.float32))
    x = f32(x)
    xT = [np.ascontiguousarray(x[b].T) for b in range(BATCH)]
    consts, per_g = k2a_consts(SEQ)
    wts = k2a_weights(nsa_cmp_k_pe[0], nsa_cmp_k_w1[0], nsa_cmp_k_w2[0],
                      nsa_cmp_v_pe[0], nsa_cmp_v_w1[0], nsa_cmp_v_w2[0])
    sl8 = alibi_slopes(8)
    lam = np.stack([f32(diff_lam_q1[0]), f32(diff_lam_k1[0]), f32(diff_lam_q2[0]), f32(diff_lam_k2[0])], 0)
    lam = np.ascontiguousarray(np.broadcast_to(lam[None], (128, 4, 64)))
    common = {"g_mix0": g_layout(norm_mix_g[0]), "g_mix1": g_layout(norm_mix_g[1]),
              "g_ffn0": g_layout(norm_ffn_g[0]), "g_ffn1": g_layout(norm_ffn_g[1]),
              "g_fin": g_layout(final_norm_g), "b_lam": lam, "b_sg": f32(diff_subln_g[0]).reshape(128, 1),
              "wo0": f32(nsa_w_out[0]), "wo1": f32(diff_w_out[0])}
    for l in range(2):
        cw, cb = conv_layouts(ffn_conv_w[l], ffn_conv_b[l])
        common.update({"wu%d" % l: f32(ffn_w_up[l]), "wd%d" % l: f32(ffn_w_down[l]), "cw%d" % l: cw,
                       "cb%d" % l: cb})
    for n_, v_ in list(consts.items()) + list(wts.items()):
        common["a_" + n_] = v_
    in_maps = []
    for c in range(NCORES):
        r = c % CPB
        m = dict(common)
        for n_, v_ in per_g[r].items():
            m["a_" + n_] = v_
        m["xh0"] = _with_halo(xT, c)
        w0, w1 = f32(nsa_w_in[0]), f32(diff_w_in[0])
        cols0 = (list(range(r * 256, (r + 1) * 256)) + [o_ + r * 64 + i for o_ in (1024, 1280, 1536, 2048)
                                                         for i in range(64)]
                 + list(range(2560 + r * 12, 2560 + (r + 1) * 12))
                 + [o_ + r * 64 + i for o_ in (1792, 2304) for i in range(64)])
        m["w_in0g"] = np.ascontiguousarray(w0[:, cols0])
        cols1 = [o_ + r * 256 + i for o_ in (0, 1024, 2048) for i in range(256)]
        m["w_in1g"] = np.ascontiguousarray(w1[:, cols1])
        m["m0"] = np.full((128, 1), 1.0 if r > 0 else 0.0, np.float32)
        m["b_qaug"] = np.stack([q_aug_rows(sl8[r * 2 + hh], 512) for hh in range(2)], 0)
        m["b_bt"] = np.stack([bt_table(sl8[r * 2 + hh]) for hh in range(2)], 0)
        in_maps.append(m)
    return in_maps


def kernel(**inputs):
    lambda_init = 0.8 - 0.6 * float(np.exp(-0.3 * 1))
    nc = build_fused(lambda_init)
    in_maps = fused_inputs(**inputs)
    res = run_bass_kernel_spmd(nc, in_maps, core_ids=list(range(NCORES))).results
    out = np.empty((BATCH, SEQ, D_MODEL), np.float32)
    for c in range(NCORES):
        b, j = divmod(c, CPB)
        out[b, j * TPC:(j + 1) * TPC, :] = res[c]["yT"].T
    return out
```

```python
import numpy as np
import ml_dtypes
import concourse.bass as bass
import concourse.mybir as mybir
from concourse.bass_utils import run_bass_kernel_spmd

F32 = mybir.dt.float32
BF16 = mybir.dt.bfloat16
AF = mybir.ActivationFunctionType
ALU = mybir.AluOpType
AX = mybir.AxisListType
NPBF = ml_dtypes.bfloat16

NCORES = 8
D_MODEL = 1024
BATCH = 2
SEQ = 16384
EPS = 1e-6


class Buf:
    __slots__ = ("name", "w", "r", "sem", "cnt")

    def __init__(self, name):
        self.name = name
        self.w = None
        self.r = {}
        self.sem = None
        self.cnt = 0


class Ctx:
    SEM_ROLL = 30000

    def __init__(self, nc, same_engine_sync=True):
        self.nc = nc
        self.same = same_engine_sync
        self.nsem = 0
        self.E = {}
        for n, e in [("pe", nc.tensor), ("act", nc.scalar), ("dve", nc.vector),
                     ("pool", nc.gpsimd), ("sp", nc.sync)]:
            self.E[n] = {"eng": e, "sem": self._newsem("e_" + n), "cnt": 0, "waited": {}}
        self.stores = []
        self.es = None
        self.pfx = ""
        self.phase_bufs = []
        self.sempool = []
        self.ccbuf = None
        self.dummy = self.nc.alloc_sbuf_tensor("bar_dummy", [128, 8], F32)
        self.bdummy = Buf("bar_dummy")

    def _newsem(self, name):
        self.nsem += 1
        s = self.nc.alloc_semaphore("%s_%d" % (name, self.nsem))
        return (s, self.nsem)

    def buf(self, name):
        return Buf(name)

    def begin_phase(self, pfx):
        from contextlib import ExitStack
        self.es = ExitStack()
        self.pfx = pfx

    def sb(self, name, shape, dt):
        return self.es.enter_context(self.nc.sbuf_tensor(self.pfx + name, shape, dt))

    def ps(self, name, shape, dt=F32):
        return self.es.enter_context(self.nc.psum_tensor(self.pfx + name, shape, dt))

    def _own_sem(self, own):
        if own.sem is None or own.cnt >= self.SEM_ROLL:
            if self.sempool and own.sem is None:
                sem, cnt = self.sempool.pop()
                own.sem = sem
                own.cnt = cnt
            else:
                own.sem = self._newsem("d")
                own.cnt = 0
            self.phase_bufs.append(own)

    def barrier(self):
        toks = []
        for n, E in self.E.items():
            if E["cnt"] > 0:
                toks.append((E["sem"], E["cnt"], n))
        for b in self.phase_bufs:
            toks.append((b.sem, b.cnt, "dma"))
        if self.ccbuf is not None and self.ccbuf.cnt > 0:
            toks.append((self.ccbuf.sem, self.ccbuf.cnt, "dma"))
        self._wait("pool", toks)
        self.op("pool", lambda e: e.memset(self.dummy[:], 0.0), w=[self.bdummy])
        for n in self.E:
            if n != "pool":
                self._wait(n, [self.bdummy.w])

    def end_phase(self):
        self.barrier()
        self.es.close()
        self.es = None
        seen = set()
        for b in self.phase_bufs:
            if b.sem[1] not in seen and b.cnt < self.SEM_ROLL // 2:
                seen.add(b.sem[1])
                self.sempool.append((b.sem, b.cnt))
        self.phase_bufs = []

    def collective_async(self, src_ap, dst_ap, groups, deps):
        if self.ccbuf is None:
            self.ccbuf = Buf("cc_async")
            self.ccbuf.sem = self._newsem("cca")
        self._wait("pool", deps)
        inst = self.nc.gpsimd.collective_compute("AllGather", ALU.bypass, replica_groups=groups,
                                                 ins=[src_ap.opt()], outs=[dst_ap.opt()])
        inst.then_inc(self.ccbuf.sem[0], 1)
        self.ccbuf.cnt += 1

    def all_gather_chunks(self, pairs, groups):
        self.barrier()
        cb = Buf("cc")
        cb.sem = self._newsem("cc")
        for (src_ap, dst_ap) in pairs:
            inst = self.nc.gpsimd.collective_compute("AllGather", ALU.bypass, replica_groups=groups,
                                                     ins=[src_ap.opt()], outs=[dst_ap.opt()])
            inst.then_inc(cb.sem[0], 1)
            cb.cnt += 1
        self.phase_bufs.append(cb)
        self.barrier()
        self.phase_bufs.remove(cb)

    def all_gather(self, src_ap, dst_ap, groups):
        self.barrier()
        cb = Buf("cc")
        cb.sem = self._newsem("cc")
        inst = self.nc.gpsimd.collective_compute("AllGather", ALU.bypass, replica_groups=groups,
                                                 ins=[src_ap.opt()], outs=[dst_ap.opt()])
        inst.then_inc(cb.sem[0], 1)
        cb.cnt = 1
        self.phase_bufs.append(cb)
        self.barrier()
        self.phase_bufs.remove(cb)

    def bufs(self, name, n):
        return [Buf("%s%d" % (name, i)) for i in range(n)]

    def _wait(self, en, toks):
        E = self.E[en]
        need = {}
        for t in toks:
            if t is None:
                continue
            (sem, key), val, src = t
            if src == en and (en == "pe" or not self.same):
                continue
            if need.get(key, (None, 0))[1] < val:
                need[key] = (sem, val)
        for key, (sem, val) in need.items():
            if E["waited"].get(key, 0) < val:
                E["eng"].wait_ge(sem, val)
                E["waited"][key] = val

    def _collect(self, r, w):
        toks = []
        for b in r:
            toks.append(b.w)
        for b in w:
            toks.append(b.w)
            toks.extend(b.r.values())
        return toks

    def op(self, en, fn, r=(), w=(), inc=True):
        E = self.E[en]
        if E["cnt"] >= self.SEM_ROLL and not E.get("pending"):
            E["sem"] = self._newsem("e_" + en)
            E["cnt"] = 0
        self._wait(en, self._collect(r, w))
        inst = fn(E["eng"])
        if inc:
            E["cnt"] += 1
            inst.then_inc(E["sem"][0], 1)
            E["pending"] = False
            tok = (E["sem"], E["cnt"], en)
        else:
            E["pending"] = True
            tok = (E["sem"], E["cnt"] + 1, en)
        for b in r:
            b.r[en] = tok
        for b in w:
            b.w = tok
            b.r = {}
        return inst

    def dma(self, qn, out, in_, r=(), w=(), own=None, **kw):
        E = self.E[qn]
        if own is None:
            own = w[0] if len(w) else r[0]
        self._own_sem(own)
        self._wait(qn, self._collect(r, w))
        inst = E["eng"].dma_start(out=out, in_=in_, **kw)
        own.cnt += 16
        inst.then_inc(own.sem[0], 16)
        tok = (own.sem, own.cnt, "dma")
        for b in r:
            b.r["dma_%d" % own.sem[1]] = tok
        for b in w:
            b.w = tok
            b.r = {}
        return tok

    def store(self, qn, out, in_, r, **kw):
        tok = self.dma(qn, out, in_, r=r, w=(), **kw)
        self.stores.append(tok)

    def finish(self):
        self._wait("sp", self.stores)
        self.stores = []
        if self.es is not None:
            self.es.close()
            self.es = None


def new_nc():
    return bass.Bass("TRN2", target_bir_lowering=False)


def emit_k1(k, nc, T, C, xv, g_in, w_in, fm_specs, fm_route, tok_cols, tok_route):
    NT = T // 512
    W = k.sb("W", [128, 8, C], BF16)
    G = k.sb("G", [128, 8], F32)
    ONES = k.sb("ONES", [128, 128], F32)
    X = [k.sb("X%d" % i, [128, 8, 512], F32) for i in range(2)]
    SQ = k.sb("SQ", [128, 8, 512], F32)
    RS = k.sb("RS", [128, 512], F32)
    HT = k.sb("HT", [128, 8, 512], BF16)
    NOB = 4
    OB = [k.sb("OB%d" % i, [128, 512], BF16) for i in range(NOB)]
    PS = [k.ps("PS%d" % i, [128, 512], F32) for i in range(6)]
    PSS = k.ps("PSS", [128, 512], F32)

    bW = k.bufs("W", 8)
    bG = k.buf("G")
    bONES = k.buf("ONES")
    bX = k.bufs("X", 2)
    bSQ = k.buf("SQ")
    bRS = k.buf("RS")
    bHT = k.buf("HT")
    bOB = k.bufs("OB", NOB)
    bPS = k.bufs("PS", 6)
    bPSS = k.buf("PSS")

    wv = w_in.rearrange("(dc p) c -> p dc c", p=128)
    for dc in range(8):
        for c0 in range(0, C, 1024):
            c1 = min(C, c0 + 1024)
            k.dma("pool", W[:, dc, c0:c1], wv[:, dc, c0:c1], w=[bW[dc]])
    k.dma("sp", G[:], g_in, w=[bG])
    k.op("dve", lambda e: e.memset(ONES[:], 1.0), w=[bONES])

    pi = 0
    oi = 0
    for it in range(NT):
        t0 = it * 512
        xb = it % 2
        k.dma("sp", X[xb][:], xv[:, :, t0:t0 + 512], w=[bX[xb]])
        k.op("act", lambda e: e.activation(out=SQ[:], in_=X[xb][:], func=AF.Square),
             r=[bX[xb]], w=[bSQ])
        for dc in range(8):
            k.op("pe", lambda e: e.matmul(PSS[:], lhsT=ONES[:], rhs=SQ[:, dc, :],
                                           start=(dc == 0), stop=(dc == 7)),
                 r=[bONES, bSQ], w=[bPSS])
        k.op("act", lambda e: e.activation(out=RS[:], in_=PSS[:], func=AF.Sqrt,
                                            scale=1.0 / D_MODEL, bias=EPS),
             r=[bPSS], w=[bRS])
        k.op("dve", lambda e: e.reciprocal(out=RS[:], in_=RS[:]), r=[bRS], w=[bRS])
        for dc in range(8):
            k.op("dve", lambda e: e.scalar_tensor_tensor(
                out=HT[:, dc, :], in0=X[xb][:, dc, :], scalar=G[:, dc:dc + 1], in1=RS[:],
                op0=ALU.mult, op1=ALU.mult), r=[bX[xb], bG, bRS], w=[bHT])
        for (s0, s1, scale, func) in fm_specs:
            for c0 in range(s0, s1, 128):
                c1 = min(s1, c0 + 128)
                m = c1 - c0
                p = pi % 6
                pi += 1
                for dc in range(8):
                    k.op("pe", lambda e: e.matmul(PS[p][:m, :], lhsT=W[:, dc, c0:c1],
                                                   rhs=HT[:, dc, :], start=(dc == 0),
                                                   stop=(dc == 7)),
                         r=[bW[dc], bHT], w=[bPS[p]], inc=(dc == 7))
                o = oi % NOB
                oi += 1
                k.op("act", lambda e: e.activation(out=OB[o][:m, :], in_=PS[p][:m, :],
                                                    func=func, scale=scale),
                     r=[bPS[p]], w=[bOB[o]])
                for (ro, nr, dst) in fm_route(c0, c1, t0):
                    k.store("sp", dst, OB[o][ro:ro + nr, :], r=[bOB[o]])
        for (c0, c1) in tok_cols:
            m = c1 - c0
            for tb in range(4):
                p = pi % 6
                pi += 1
                for dc in range(8):
                    k.op("pe", lambda e: e.matmul(PS[p][:, :m],
                                                   lhsT=HT[:, dc, tb * 128:(tb + 1) * 128],
                                                   rhs=W[:, dc, c0:c1], start=(dc == 0),
                                                   stop=(dc == 7)),
                         r=[bW[dc], bHT], w=[bPS[p]], inc=(dc == 7))
                o = oi % NOB
                oi += 1
                k.op("dve", lambda e: e.tensor_copy(out=OB[o][:, :m], in_=PS[p][:, :m]),
                     r=[bPS[p]], w=[bOB[o]])
                for (co, ncl, dst) in tok_route(c0, c1, t0, tb):
                    k.store("sp", dst, OB[o][:, co:co + ncl], r=[bOB[o]])


def emit_norm(k, nc, T, xv, g_in, h_dst, tile_done=None):
    NT = T // 512
    G = k.sb("G", [128, 8], F32)
    ONES = k.sb("ONES", [128, 128], F32)
    X = [k.sb("X%d" % i, [128, 8, 512], F32) for i in range(2)]
    SQ = k.sb("SQ", [128, 8, 512], F32)
    RS = k.sb("RS", [128, 512], F32)
    HT = [k.sb("HT%d" % i, [128, 8, 512], BF16) for i in range(2)]
    PSS = k.ps("PSS", [128, 512], F32)
    bG = k.buf("G"); bONES = k.buf("ONES"); bX = k.bufs("X", 2); bSQ = k.buf("SQ"); bRS = k.buf("RS")
    bHT = k.bufs("HT", 2); bPSS = k.buf("PSS")
    k.dma("sp", G[:], g_in, w=[bG])
    k.op("dve", lambda e: e.memset(ONES[:], 1.0), w=[bONES])
    for it in range(NT):
        t0 = it * 512
        xb = it % 2
        k.dma("sp", X[xb][:], xv[:, :, t0:t0 + 512], w=[bX[xb]])
        k.op("act", lambda e: e.activation(out=SQ[:], in_=X[xb][:], func=AF.Square), r=[bX[xb]], w=[bSQ])
        for dc in range(8):
            k.op("pe", lambda e: e.matmul(PSS[:], lhsT=ONES[:], rhs=SQ[:, dc, :], start=(dc == 0), stop=(dc == 7)),
                 r=[bONES, bSQ], w=[bPSS])
        k.op("act", lambda e: e.activation(out=RS[:], in_=PSS[:], func=AF.Sqrt, scale=1.0 / D_MODEL, bias=EPS),
             r=[bPSS], w=[bRS])
        k.op("dve", lambda e: e.reciprocal(out=RS[:], in_=RS[:]), r=[bRS], w=[bRS])
        for dc in range(8):
            k.op("dve", lambda e: e.scalar_tensor_tensor(
                out=HT[xb][:, dc, :], in0=X[xb][:, dc, :], scalar=G[:, dc:dc + 1], in1=RS[:],
                op0=ALU.mult, op1=ALU.mult), r=[bX[xb], bG, bRS], w=[bHT[xb]])
        n0 = len(k.stores)
        for dc in range(8):
            k.store("pool", h_dst(dc, t0), HT[xb][:, dc, :], r=[bHT[xb]])
        if tile_done is not None:
            tile_done(it, k.stores[n0:])


def emit_proj(k, nc, T, C, ht_src, w_in, fm_specs, fm_route, tok_cols, tok_route):
    NT = T // 512
    W = k.sb("W", [128, 8, C], BF16)
    HT = [k.sb("HT%d" % i, [128, 8, 512], BF16) for i in range(2)]
    NOB = 4
    OB = [k.sb("OB%d" % i, [128, 512], BF16) for i in range(NOB)]
    PS = [k.ps("PS%d" % i, [128, 512], F32) for i in range(6)]
    bW = k.bufs("W", 8); bHT = k.bufs("HT", 2); bOB = k.bufs("OB", NOB); bPS = k.bufs("PS", 6)
    wv = w_in.rearrange("(dc p) c -> p dc c", p=128)
    for dc in range(8):
        for c0 in range(0, C, 1024):
            c1 = min(C, c0 + 1024)
            k.dma("pool", W[:, dc, c0:c1], wv[:, dc, c0:c1], w=[bW[dc]])
    pi = 0
    oi = 0
    for it in range(NT):
        t0 = it * 512
        hb = it % 2
        k.dma("sp", HT[hb][:], ht_src(t0), w=[bHT[hb]])
        for (s0, s1, scale, func) in fm_specs:
            for c0 in range(s0, s1, 128):
                c1 = min(s1, c0 + 128)
                m = c1 - c0
                p = pi % 6
                pi += 1
                for dc in range(8):
                    k.op("pe", lambda e: e.matmul(PS[p][:m, :], lhsT=W[:, dc, c0:c1], rhs=HT[hb][:, dc, :],
                                                   start=(dc == 0), stop=(dc == 7)),
                         r=[bW[dc], bHT[hb]], w=[bPS[p]], inc=(dc == 7))
                o = oi % NOB
                oi += 1
                k.op("act", lambda e: e.activation(out=OB[o][:m, :], in_=PS[p][:m, :], func=func, scale=scale),
                     r=[bPS[p]], w=[bOB[o]])
                for (ro, nr, dst) in fm_route(c0, c1, t0):
                    k.store("pool", dst, OB[o][ro:ro + nr, :], r=[bOB[o]])
        for (c0, c1) in tok_cols:
            m = c1 - c0
            for tb in range(4):
                p = pi % 6
                pi += 1
                for dc in range(8):
                    k.op("pe", lambda e: e.matmul(PS[p][:, :m], lhsT=HT[hb][:, dc, tb * 128:(tb + 1) * 128],
                                                   rhs=W[:, dc, c0:c1], start=(dc == 0), stop=(dc == 7)),
                         r=[bW[dc], bHT[hb]], w=[bPS[p]], inc=(dc == 7))
                o = oi % NOB
                oi += 1
                k.op("dve", lambda e: e.tensor_copy(out=OB[o][:, :m], in_=PS[p][:, :m]), r=[bPS[p]], w=[bOB[o]])
                for (co, ncl, dst) in tok_route(c0, c1, t0, tb):
                    k.store("pool", dst, OB[o][:, co:co + ncl], r=[bOB[o]])


def build_k1(T, C, fm_specs, tok_cols):
    nc = new_nc()
    CV = sum(c1 - c0 for c0, c1 in tok_cols)
    xT = nc.dram_tensor("xT", [D_MODEL, T], F32, kind="ExternalInput").ap()
    g_in = nc.dram_tensor("g", [128, 8], F32, kind="ExternalInput").ap()
    w_in = nc.dram_tensor("w", [D_MODEL, C], F32, kind="ExternalInput").ap()
    projT = nc.dram_tensor("projT", [C, T], BF16, kind="ExternalOutput").ap()
    vtok = nc.dram_tensor("vtok", [T, max(CV, 1)], BF16, kind="ExternalOutput").ap()
    voff = {}
    vo = 0
    for (c0, c1) in tok_cols:
        voff[c0] = vo
        vo += c1 - c0
    k = Ctx(nc)
    k.begin_phase("")
    emit_k1(k, nc, T, C, xT.rearrange("(dc p) t -> p dc t", p=128), g_in, w_in, fm_specs,
            lambda c0, c1, t0: [(0, c1 - c0, projT[c0:c1, t0:t0 + 512])],
            tok_cols,
            lambda c0, c1, t0, tb: [(0, c1 - c0, vtok[t0 + tb * 128:t0 + (tb + 1) * 128,
                                                     voff[c0]:voff[c0] + c1 - c0])])
    k.finish()
    return nc


def g_layout(g):
    return np.ascontiguousarray(np.asarray(g, np.float32).reshape(8, 128).T)


def run_k1(xT_shards, g, w, fm_specs, tok_cols):
    T = xT_shards[0].shape[1]
    C = w.shape[1]
    nc = build_k1(T, C, fm_specs, tok_cols)
    gl = g_layout(g)
    w = np.ascontiguousarray(w, dtype=np.float32)
    in_maps = [{"xT": np.ascontiguousarray(s), "g": gl, "w": w} for s in xT_shards]
    res = run_bass_kernel_spmd(nc, in_maps, core_ids=list(range(NCORES)))
    return res.results


BIG = 29952.0


def alibi_slopes(n):
    return np.exp2(-8.0 * np.arange(1, n + 1, dtype=np.float64) / n)


def split3(v):
    v = np.asarray(v, np.float64)
    a = v.astype(NPBF)
    r = v - a.astype(np.float64)
    b = r.astype(NPBF)
    r = r - b.astype(np.float64)
    c = r.astype(NPBF)
    return np.stack([a, b, c], 0)


def q_aug_rows(m, S):
    tl = np.arange(S) % 512
    return split3(-m * tl)


def bt_table(m):
    sl = np.arange(128)[:, None]
    delta = np.arange(-3, 125)[None, :]
    return (m * sl - m * 128.0 * delta).astype(np.float32)


def btc_table(m):
    jl = np.arange(128)[:, None]
    dd = np.arange(-28, 32)[None, :]
    return (16.0 * m * jl - m * (512.0 * dd - 31.0)).astype(np.float32)


def dm_table():
    sl = np.arange(128)[:, None, None]
    dd = np.arange(4)[None, :, None]
    tl = np.arange(512)[None, None, :]
    return np.where(128 * dd + sl > tl, -1.0, 0.0).astype(NPBF)


def wm_table():
    sl = np.arange(128)[:, None, None]
    dd = np.arange(4)[None, :, None]
    tl = np.arange(512)[None, None, :]
    return np.where(tl - 128 * dd - sl >= 0, -1.0, 0.0).astype(NPBF)


def cm_table():
    jl = np.arange(128)[:, None, None]
    dd = np.arange(5)[None, :, None]
    tl = np.arange(512)[None, None, :]
    return np.where(16 * jl + 31 > 512 * dd + tl, -1.0, 0.0).astype(NPBF)


def bigi_table():
    return (np.eye(128) * BIG).astype(NPBF)


def bcast128(v):
    v = np.asarray(v, np.float32).reshape(1, -1)
    return np.ascontiguousarray(np.broadcast_to(v, (128, v.shape[1])))


def emit_sqmax(k, nc, fetch, ntiles, SQT, bSQT, ONESB, bONESB, psM, bpsM, MX, bMX, OUT, bOUT):
    nxt = fetch(0)
    for i in range(ntiles):
        rb, src = nxt
        if i + 1 < ntiles:
            nxt = fetch(i + 1)
        a = i % 2
        w_ = src.shape[-1]
        k.op("dve", lambda e: e.tensor_tensor(out=SQT[a][0:64, :w_], in0=src, in1=src, op=ALU.mult),
             r=rb, w=[bSQT[a]])
        k.op("pe", lambda e: e.matmul(psM[a][:, :w_], lhsT=ONESB[0:64, :], rhs=SQT[a][0:64, :w_],
                                       start=True, stop=True), r=[bSQT[a], bONESB], w=[bpsM[a]])
        k.op("dve", lambda e: e.reduce_max(out=MX[:, i:i + 1], in_=psM[a][:, :w_], axis=AX.X),
             r=[bpsM[a]], w=[bMX])
    k.op("dve", lambda e: e.reduce_max(out=OUT, in_=MX[:, 0:ntiles], axis=AX.X),
         r=[bMX], w=[bOUT])


class IOK2bStandalone:
    def __init__(self, nc, S, NH):
        d = lambda n, sh, dt: nc.dram_tensor(n, sh, dt, kind="ExternalInput").ap()
        self.qa = d("qa", [NH * 2, 64, S], BF16)
        self.ka = d("ka", [NH * 2, 64, S], BF16)
        self.v_in = d("v", [S, NH * 128], BF16)
        self.qaug = d("qaug", [NH, 3, 512], BF16)
        self.bt_in = d("bt", [NH, 128, 128], F32)
        self.dm_in = d("dm", [128, 4, 512], BF16)
        self.bigi_in = d("bigi", [128, 128], BF16)
        self.lam_in = d("lam", [128, 4, 64], F32)
        self.sg_in = d("sg", [128, 1], F32)
        self.oT = nc.dram_tensor("oT", [NH * 128, S], BF16, kind="ExternalOutput").ap()

    def q(self, hh, c, t0, t1):
        return [(self.qa[hh * 2 + c, :, t0:t1], t0, t1)]

    def k(self, hh, c, t0, t1):
        return [(self.ka[hh * 2 + c, :, t0:t1], t0, t1)]

    def v(self, hh, t0, t1):
        return [(self.v_in[t0:t1, hh * 128:(hh + 1) * 128], t0, t1)]

    def out(self, hh, I):
        return self.oT[hh * 128:(hh + 1) * 128, I * 512:(I + 1) * 512]


def build_k2b(S, lambda_init, NH=2):
    nc = new_nc()
    io = IOK2bStandalone(nc, S, NH)
    k = Ctx(nc)
    k.begin_phase("")
    emit_k2b(k, nc, S, lambda_init, io, NH)
    k.finish()
    return nc


def emit_k2b(k, nc, S, lambda_init, io, NH=2):
    NQ = S // 512
    NB = S // 128
    bt_in, dm_in, bigi_in, lam_in, sg_in = io.bt_in, io.dm_in, io.bigi_in, io.lam_in, io.sg_in

    KA = [k.sb("KA%d" % c, [67, S], BF16) for c in range(2)]
    V = k.sb("V", [128, NB, 128], BF16)
    QT = [k.sb("QT%d" % i, [67, 512], BF16) for i in range(4)]
    BT = k.sb("BT", [128, 128], F32)
    BTC = [k.sb("BTC%d" % c, [128, 128], F32) for c in range(2)]
    DM = k.sb("DM", [128, 4, 512], BF16)
    BIGI = k.sb("BIGI", [128, 128], BF16)
    ONESB = k.sb("ONESB", [128, 128], BF16)
    ONESF = k.sb("ONESF", [128, 128], F32)
    LAM = k.sb("LAM", [128, 4, 64], F32)
    LT = k.sb("LT", [128, 2, 64], F32)
    LS = k.sb("LS", [128, 4], F32)
    SG = k.sb("SG", [128, 1], F32)
    SQT = [k.sb("SQT%d" % i, [128, 512], BF16) for i in range(2)]
    MX = k.sb("MX", [128, 64], F32)
    Q2 = k.sb("Q2", [128, 4], F32)
    NPT = 6
    PT = [k.sb("PT%d" % i, [128, 512], BF16) for i in range(NPT)]
    RL = k.sb("RL", [128, 512], F32)
    OC = [k.sb("OC%d" % c, [128, 512], F32) for c in range(2)]
    OD = k.sb("OD", [128, 512], F32)
    ACP = [k.sb("ACP%d" % c, [128, 512], F32) for c in range(2)]
    OSQ = k.sb("OSQ", [128, 512], F32)
    RS = k.sb("RS", [128, 512], F32)
    OUT = [k.sb("OUT%d" % i, [128, 512], BF16) for i in range(2)]
    psS = [k.ps("psS%d" % i, [128, 512], F32) for i in range(4)]
    psO = [k.ps("psO%d" % i, [128, 512], F32) for i in range(2)]
    psL = [k.ps("psL%d" % i, [128, 512], F32) for i in range(2)]
    psM = psS[0]

    bKA = k.bufs("KA", 2); bV = k.buf("V"); bQT = k.bufs("QT", 4); bBT = k.buf("BT")
    bBTC = k.bufs("BTC", 2); bDM = k.buf("DM"); bBIGI = k.buf("BIGI"); bONESB = k.buf("ONESB")
    bONESF = k.buf("ONESF"); bLAM = k.buf("LAM"); bLT = k.buf("LT"); bLS = k.buf("LS")
    bSG = k.buf("SG"); bSQT = k.bufs("SQT", 2); bMX = k.buf("MX"); bQ2 = k.buf("Q2")
    bPT = k.bufs("PT", NPT); bRL = k.buf("RL"); bOC = k.bufs("OC", 2); bOD = k.buf("OD")
    bOSQ = k.buf("OSQ"); bRS = k.buf("RS"); bOUT = k.bufs("OUT", 2); bACP = k.bufs("ACP", 2)
    bpsS = k.bufs("psS", 4); bpsO = k.bufs("psO", 2); bpsL = k.bufs("psL", 2); bpsM = bpsS[0]

    k.dma("sp", DM[:], dm_in, w=[bDM])
    k.dma("sp", BIGI[:], bigi_in, w=[bBIGI])
    k.dma("sp", LAM[:], lam_in, w=[bLAM])
    k.dma("sp", SG[:], sg_in, w=[bSG])
    k.op("dve", lambda e: e.memset(ONESB[:], 1.0), w=[bONESB])
    k.op("dve", lambda e: e.memset(ONESF[:], 1.0), w=[bONESF])
    for j in range(2):
        k.op("dve", lambda e: e.tensor_tensor(out=LT[:, j, :], in0=LAM[:, 2 * j, :],
                                               in1=LAM[:, 2 * j + 1, :], op=ALU.mult),
             r=[bLAM], w=[bLT])
        k.op("dve", lambda e: e.reduce_sum(out=LS[:, j:j + 1], in_=LT[:, j, :], axis=AX.X),
             r=[bLT], w=[bLS])
    k.op("act", lambda e: e.activation(out=LS[:, 0:2], in_=LS[:, 0:2], func=AF.Exp),
         r=[bLS], w=[bLS])
    k.op("dve", lambda e: e.tensor_tensor(out=LS[:, 2:3], in0=LS[:, 1:2], in1=LS[:, 0:1],
                                           op=ALU.subtract), r=[bLS], w=[bLS])
    k.op("dve", lambda e: e.tensor_scalar(out=LS[:, 3:4], in0=LS[:, 2:3], scalar1=-lambda_init,
                                           scalar2=None, op0=ALU.add), r=[bLS], w=[bLS])
    k.op("dve", lambda e: e.tensor_scalar(out=SG[:], in0=SG[:], scalar1=1.0 - lambda_init,
                                           scalar2=None, op0=ALU.mult), r=[bSG], w=[bSG])

    qti = 0
    pti = 0
    psi = 0
    oi = 0
    deferred = []
    for hh in range(NH):
        for c in range(2):
            for s0 in range(0, S, 4096):
                s1 = min(S, s0 + 4096)
                for (ap, lo, hi) in io.k(hh, c, s0, s1):
                    k.dma("act", KA[c][0:64, lo:hi], ap, w=[bKA[c]])
            k.op("pool", lambda e: e.memset(KA[c][64:67, :], 1.0), w=[bKA[c]])
        for b0 in range(0, NB, 32):
            b1 = min(NB, b0 + 32)
            for (ap, lo, hi) in io.v(hh, b0 * 128, b1 * 128):
                k.dma("act", V[:, lo // 128:hi // 128, :], ap.rearrange("(nb p) c -> p nb c", p=128), w=[bV])
        k.dma("sp", BT[:], bt_in[hh], w=[bBT])
        for qi_ in range(4):
            k.dma("sp", QT[qi_][64:67, :], io.qaug[hh], w=[bQT[qi_]])
        for c in range(2):
            def fetch_q(i, c=c):
                nonlocal qti
                qb = qti % 4
                qti += 1
                for (ap, lo, hi) in io.q(hh, c, i * 512, (i + 1) * 512):
                    k.dma("sp", QT[qb][0:64, :], ap, w=[bQT[qb]])
                return [bQT[qb]], QT[qb][0:64, :]
            emit_sqmax(k, nc, fetch_q, NQ, SQT, bSQT, ONESB, bONESB, psS[0:2], bpsS[0:2], MX, bMX,
                       Q2[:, 0:1], bQ2)
            emit_sqmax(k, nc, lambda i, c=c: ([bKA[c]], KA[c][0:64, i * 512:(i + 1) * 512]), NQ,
                       SQT, bSQT, ONESB, bONESB, psS[0:2], bpsS[0:2], MX, bMX, Q2[:, 1:2], bQ2)
            k.op("dve", lambda e: e.tensor_tensor(out=Q2[:, 2:3], in0=Q2[:, 0:1], in1=Q2[:, 1:2],
                                                   op=ALU.mult), r=[bQ2], w=[bQ2])
            k.op("act", lambda e: e.activation(out=Q2[:, 3:4], in_=Q2[:, 2:3], func=AF.Sqrt),
                 r=[bQ2], w=[bQ2])
            k.op("dve", lambda e: e.tensor_scalar(out=BTC[c][:], in0=BT[:], scalar1=Q2[:, 3:4],
                                                   scalar2=None, op0=ALU.subtract),
                 r=[bBT, bQ2], w=[bBTC[c]])
        for I in range(NQ):
            nkb = 4 * I + 4
            qbs = []
            for c in range(2):
                qb = qti % 4
                qti += 1
                for (ap, lo, hi) in io.q(hh, c, I * 512, (I + 1) * 512):
                    k.dma("sp", QT[qb][0:64, :], ap, w=[bQT[qb]])
                qbs.append(qb)
            staged = []

            def stage_a(jb):
                nonlocal psi, pti
                pts = []
                diag = jb >= 4 * I
                idx = 4 * I - jb + 3
                for c in range(2):
                    ps = psi % 4
                    psi += 1
                    k.op("pe", lambda e: e.matmul(psS[ps][:], lhsT=KA[c][:, jb * 128:(jb + 1) * 128],
                                                   rhs=QT[qbs[c]][:], start=True, stop=not diag),
                         r=[bKA[c], bQT[qbs[c]]], w=[bpsS[ps]])
                    if diag:
                        dd = jb - 4 * I
                        k.op("pe", lambda e: e.matmul(psS[ps][:], lhsT=BIGI[:], rhs=DM[:, dd, :],
                                                       start=False, stop=True),
                             r=[bBIGI, bDM], w=[bpsS[ps]])
                    pt = pti % NPT
                    pti += 1
                    k.op("act", lambda e: e.activation(out=PT[pt][:], in_=psS[ps][:], func=AF.Exp,
                                                        bias=BTC[c][:, idx:idx + 1], scale=1.0),
                         r=[bpsS[ps], bBTC[c]], w=[bPT[pt]])
                    pts.append(pt)
                return pts

            def stage_b(jb, pts):
                for c in range(2):
                    k.op("pe", lambda e: e.matmul(psO[c][:], lhsT=V[:, jb, :], rhs=PT[pts[c]][:],
                                                   start=(jb == 0), stop=(jb == nkb - 1)),
                         r=[bV, bPT[pts[c]]], w=[bpsO[c]])
                for c in range(2):
                    if jb % 3 == 2:
                        if jb == 2:
                            k.op("pool", lambda e: e.tensor_copy(out=ACP[c][:], in_=PT[pts[c]][:]),
                                 r=[bPT[pts[c]]], w=[bACP[c]])
                        else:
                            k.op("pool", lambda e: e.tensor_tensor(out=ACP[c][:], in0=ACP[c][:],
                                                                    in1=PT[pts[c]][:], op=ALU.add),
                                 r=[bACP[c], bPT[pts[c]]], w=[bACP[c]])
                    elif jb == 0:
                        k.op("dve", lambda e: e.tensor_copy(out=psL[c][:], in_=PT[pts[c]][:]),
                             r=[bPT[pts[c]]], w=[bpsL[c]])
                    else:
                        k.op("dve", lambda e: e.tensor_tensor(out=psL[c][:], in0=psL[c][:], in1=PT[pts[c]][:],
                                                               op=ALU.add),
                             r=[bpsL[c], bPT[pts[c]]], w=[bpsL[c]])

            for jb in range(nkb):
                staged.append((jb, stage_a(jb)))
                if jb == 1:
                    while deferred:
                        deferred.pop(0)()
                if len(staged) > 1:
                    stage_b(*staged.pop(0))
            while staged:
                stage_b(*staged.pop(0))
            def tile_epilogue(hh=hh, I=I):
                nonlocal psi, oi
                for c in range(2):
                    k.op("dve", lambda e: e.tensor_tensor(out=OSQ[:], in0=psL[c][:], in1=ACP[c][:], op=ALU.add),
                         r=[bpsL[c], bACP[c]], w=[bOSQ])
                    ps = psi % 4
                    psi += 1
                    k.op("pe", lambda e: e.matmul(psS[ps][:], lhsT=ONESF[:], rhs=OSQ[:], start=True, stop=True),
                         r=[bONESF, bOSQ], w=[bpsS[ps]])
                    k.op("dve", lambda e: e.tensor_scalar(out=RL[:], in0=psS[ps][:], scalar1=1e-30,
                                                           scalar2=None, op0=ALU.max),
                         r=[bpsS[ps]], w=[bRL])
                    k.op("dve", lambda e: e.reciprocal(out=RL[:], in_=RL[:]), r=[bRL], w=[bRL])
                    k.op("dve", lambda e: e.tensor_tensor(out=OC[c][:], in0=psO[c][:], in1=RL[:], op=ALU.mult),
                         r=[bpsO[c], bRL], w=[bOC[c]])
                k.op("dve", lambda e: e.scalar_tensor_tensor(out=OD[:], in0=OC[1][:], scalar=LS[:, 3:4],
                                                              in1=OC[0][:], op0=ALU.mult, op1=ALU.add),
                     r=[bOC[0], bOC[1], bLS], w=[bOD])
                k.op("act", lambda e: e.activation(out=OSQ[:], in_=OD[:], func=AF.Square),
                     r=[bOD], w=[bOSQ])
                k.op("pe", lambda e: e.matmul(psM[:], lhsT=ONESF[:], rhs=OSQ[:], start=True, stop=True),
                     r=[bONESF, bOSQ], w=[bpsM])
                k.op("act", lambda e: e.activation(out=RS[:], in_=psM[:], func=AF.Sqrt,
                                                    scale=1.0 / 128.0, bias=EPS), r=[bpsM], w=[bRS])
                k.op("dve", lambda e: e.reciprocal(out=RS[:], in_=RS[:]), r=[bRS], w=[bRS])
                ob = oi % 2
                oi += 1
                k.op("dve", lambda e: e.scalar_tensor_tensor(out=OUT[ob][:], in0=OD[:], scalar=SG[:, 0:1],
                                                              in1=RS[:], op0=ALU.mult, op1=ALU.mult),
                     r=[bOD, bSG, bRS], w=[bOUT[ob]])
                n0 = len(k.stores)
                k.store("sp", io.out(hh, I), OUT[ob][:], r=[bOUT[ob]])
                if getattr(io, "out_done", None) is not None:
                    io.out_done(I, hh, k.stores[n0:])
            deferred.append(tile_epilogue)
    while deferred:
        deferred.pop(0)()


D_FF = 2816


class IOK3Standalone:
    def __init__(self, nc, T):
        d = lambda n, sh, dt: nc.dram_tensor(n, sh, dt, kind="ExternalInput").ap()
        NCH = 2 * D_FF // 128
        self.xT = d("xT", [D_MODEL, 2 + T], F32)
        self.aT = d("aT", [D_MODEL, 2 + T], BF16)
        self.wo_in = d("wo", [D_MODEL, D_MODEL], F32)
        self.wu_in = d("wu", [D_MODEL, 2 * D_FF], F32)
        self.wd_in = d("wd", [D_FF, D_MODEL], F32)
        self.cw_in = d("cw", [128, NCH, 3], F32)
        self.cb_in = d("cb", [128, NCH], F32)
        self.g_in = d("g", [128, 8], F32)
        self.gf_in = d("gf", [128, 8], F32)
        self.yT = nc.dram_tensor("yT", [D_MODEL, T], F32, kind="ExternalOutput").ap()
        self.halo_scale = None
        self.tail_dst = None

    def x_src(self, col0, n):
        return self.xT.rearrange("(dc p) t -> p dc t", p=128)[:, :, col0:col0 + n]

    def a_src(self, col0, n):
        return [(0, 1, self.aT.rearrange("(dc p) t -> p dc t", p=128)[:, :, col0:col0 + n])]

    def y_dst(self, oc, tcol, n):
        return self.yT[oc * 128:(oc + 1) * 128, tcol:tcol + n]

    def make_scratch(self, nc, T):
        self.X1D = nc.dram_tensor("X1D", [D_MODEL, T], F32).ap()
        self.AFFD = nc.dram_tensor("AFFD", [D_FF, T], BF16).ap()

    def x1_dst(self, dc, tcol, n):
        return self.X1D[dc * 128:(dc + 1) * 128, tcol:tcol + n]

    def x1_src(self, tcol, n):
        return self.X1D.rearrange("(dc p) t -> p dc t", p=128)[:, :, tcol:tcol + n]

    def aff_dst(self, gch, tcol, n):
        return self.AFFD[gch * 128:(gch + 1) * 128, tcol:tcol + n]

    def aff_src(self, tcol, n):
        return self.AFFD.rearrange("(kc p) t -> p kc t", p=128)[:, :, tcol:tcol + n]


def build_k3_split(T, final_norm):
    nc = new_nc()
    io = IOK3Standalone(nc, T)
    io.make_scratch(nc, T)
    k = Ctx(nc)
    k.begin_phase("a_")
    emit_k3(k, nc, T, final_norm, io, 512, "a")
    k.end_phase()
    k.begin_phase("b_")
    emit_k3(k, nc, T, final_norm, io, 512, "b")
    k.finish()
    return nc


def build_k3(T, final_norm, N=256):
    nc = new_nc()
    io = IOK3Standalone(nc, T)
    k = Ctx(nc)
    k.begin_phase("")
    emit_k3(k, nc, T, final_norm, io, N)
    k.finish()
    return nc


def emit_k3(k, nc, T, final_norm, io, N=256, part="ab"):
    A_, B_ = ("a" in part), ("b" in part)
    NT = T // N
    NCH = 2 * D_FF // 128
    NG = NCH // 2
    wo_in, wu_in, wd_in, cw_in, cb_in, g_in, gf_in = (io.wo_in, io.wu_in, io.wd_in, io.cw_in, io.cb_in,
                                                      io.g_in, io.gf_in)

    WO = k.sb("WO", [128, 8, D_MODEL], BF16) if A_ else None
    WU = k.sb("WU", [128, 8, 2 * D_FF], BF16) if A_ else None
    WD = k.sb("WD", [128, NG, D_MODEL], BF16) if B_ else None
    CW = k.sb("CW", [128, NCH, 3], F32)
    CB = k.sb("CB", [128, NCH], F32)
    G = k.sb("G", [128, 8], F32)
    GF = k.sb("GF", [128, 8], F32)
    ONES = k.sb("ONES", [128, 128], F32)
    HALO = k.sb("HALO", [128, NCH, 2], F32) if A_ else None
    NX1 = 2 if part == "b" else 1
    X1s = [k.sb("X1_%d" % i, [128, 8, N], F32) for i in range(NX1)]
    X1 = X1s[0]
    AT = k.sb("AT", [128, 8, N], BF16) if A_ else None
    SQ = [k.sb("SQ%d" % i, [128, N], F32) for i in range(2)]
    RS = k.sb("RS", [128, N], F32)
    HT = k.sb("HT", [128, 8, N], BF16) if A_ else None
    NAF = {"ab": 1, "a": 1, "b": 2}[part]
    AFFs = [k.sb("AFF%d" % i, [128, NG if B_ else 2, N], BF16) for i in range(NAF)]
    AFF = AFFs[0]
    NU, NY, NSG = 3, 4, 2
    U = [k.sb("U%d" % i, [128, 2 + N], F32) for i in range(NU)] if A_ else None
    Y = [k.sb("Y%d" % i, [128, N], F32) for i in range(NY)] if A_ else None
    SGT = [k.sb("SGT%d" % i, [128, N], F32) for i in range(NSG)] if A_ else None
    OUT = [k.sb("OUT%d" % i, [128, N], F32) for i in range(2)] if B_ else None
    mid = B_ and getattr(io, "h_dst", None) is not None
    assert not (mid and final_norm)
    if mid:
        OUTH = [k.sb("OUTH%d" % i, [128, N], BF16) for i in range(2)]
        bOUTH = k.bufs("OUTH", 2)
    PS = [k.ps("PS%d" % i, [128, 512], F32) for i in range(6)]
    PSS = k.ps("PSS", [128, 512], F32)

    bWO = k.buf("WO"); bWU = k.bufs("WU", 8); bWD = k.buf("WD"); bCW = k.buf("CW"); bCB = k.buf("CB")
    bG = k.buf("G"); bGF = k.buf("GF"); bONES = k.buf("ONES"); bHALO = k.bufs("HALO", NCH)
    bX1s = [k.bufs("X1", 8) for _ in range(NX1)]; bX1 = bX1s[0]; bAT = k.buf("AT"); bSQ = k.bufs("SQ", 2); bRS = k.buf("RS"); bHT = k.buf("HT")
    bAFFs = [k.bufs("AFF", NG) for _ in range(NAF)]; bAFF = bAFFs[0]; bU = k.bufs("U", NU); bY = k.bufs("Y", NY); bSGT = k.bufs("SGT", NSG)
    bOUT = k.bufs("OUT", 2); bPS = k.bufs("PS", 6); bPSS = k.buf("PSS")

    if A_:
        k.dma("sp", CW[:], cw_in, w=[bCW])
        k.dma("sp", CB[:], cb_in, w=[bCB])
        k.dma("sp", G[:], g_in, w=[bG])
    k.dma("sp", GF[:], gf_in, w=[bGF])
    k.op("dve", lambda e: e.memset(ONES[:], 1.0), w=[bONES])
    pre = getattr(io, "wb", None)
    if pre is not None:
        wov = pre[0].rearrange("(dc p) c -> p dc c", p=128)
        wuv = pre[1].rearrange("(dc p) c -> p dc c", p=128)
        wdv = pre[2].rearrange("(kc p) c -> p kc c", p=128)
        for dc in range(8 if A_ else 0):
            k.dma("sp", WU[:, dc, :], wuv[:, dc, :], w=[bWU[dc]])
        if A_:
            k.dma("sp", WO[:], wov, w=[bWO])
        for kc in range(0, NG if B_ else 0, 2):
            k.dma("sp", WD[:, kc:kc + 2, :], wdv[:, kc:kc + 2, :], w=[bWD])
    else:
        wov = wo_in.rearrange("(dc p) c -> p dc c", p=128)
        for dc in range(8 if A_ else 0):
            k.dma("pool", WO[:, dc, :], wov[:, dc, :], w=[bWO])
        wuv = wu_in.rearrange("(dc p) c -> p dc c", p=128)
        for dc in range(8 if A_ else 0):
            for c0 in range(0, 2 * D_FF, 1024):
                c1 = min(2 * D_FF, c0 + 1024)
                k.dma("pool", WU[:, dc, c0:c1], wuv[:, dc, c0:c1], w=[bWU[dc]])
        wdv = wd_in.rearrange("(kc p) c -> p kc c", p=128)
        for kc in range(NG if B_ else 0):
            k.dma("pool", WD[:, kc, :], wdv[:, kc, :], w=[bWD])

    st = {"pi": 0, "ui": 0, "yi": 0, "oi": 0, "sq": 0}
    if A_ and io.halo_scale is not None:
        M0 = k.sb("M0", [128, 1], F32)
        bM0 = k.buf("M0")
        k.dma("sp", M0[:], io.halo_scale, w=[bM0])

    def rms(n, gtile, inv_d):
        for dc in range(8):
            s = st["sq"] % 2
            st["sq"] += 1
            k.op("act", lambda e: e.activation(out=SQ[s][:, :n], in_=X1[:, dc, :n], func=AF.Square),
                 r=[bX1[dc]], w=[bSQ[s]])
            k.op("pe", lambda e: e.matmul(PSS[:, :n], lhsT=ONES[:], rhs=SQ[s][:, :n],
                                           start=(dc == 0), stop=(dc == 7)),
                 r=[bONES, bSQ[s]], w=[bPSS])
        k.op("act", lambda e: e.activation(out=RS[:, :n], in_=PSS[:, :n], func=AF.Sqrt,
                                            scale=inv_d, bias=EPS), r=[bPSS], w=[bRS])
        k.op("dve", lambda e: e.reciprocal(out=RS[:, :n], in_=RS[:, :n]), r=[bRS], w=[bRS])

    def tile(col0, n, halo_only, it=0):
        nonlocal X1, bX1, AFF, bAFF
        X1, bX1 = X1s[it % NX1], bX1s[it % NX1]
        AFF, bAFF = AFFs[it % NAF], bAFFs[it % NAF]
        if part == "b":
            afv = io.aff_src(col0 - 2, n)
            for h0 in (0, NG // 2):
                k.dma("act", AFF[:, h0:h0 + NG // 2, :n], afv[:, h0:h0 + NG // 2, :], w=bAFF[h0:h0 + NG // 2])
            k.dma("sp", X1[:, :, :n], io.x1_src(col0 - 2, n), w=bX1)
        else:
            tile_a(col0, n, halo_only)
        if halo_only or part == "a":
            return
        tile_b(col0, n)

    def tile_a(col0, n, halo_only):
        k.dma("sp", X1[:, :, :n], io.x_src(col0, n), w=bX1)
        for (dc0, dstep, ap_) in io.a_src(col0, n):
            ndc = ap_.shape[1]
            k.dma("sp", AT[:, dc0:dc0 + (ndc - 1) * dstep + 1:dstep, :n], ap_, w=[bAT])
        if halo_only and io.halo_scale is not None:
            k.op("dve", lambda e: e.tensor_scalar(out=X1[:, :, :n], in0=X1[:, :, :n], scalar1=M0[:, 0:1],
                                                   scalar2=None, op0=ALU.mult), r=bX1 + [bM0], w=bX1)
            k.op("dve", lambda e: e.tensor_scalar(out=AT[:, :, :n], in0=AT[:, :, :n], scalar1=M0[:, 0:1],
                                                   scalar2=None, op0=ALU.mult), r=[bAT, bM0], w=[bAT])
        for oc in range(8):
            p = st["pi"] % 6
            st["pi"] += 1
            for dc in range(8):
                k.op("pe", lambda e: e.matmul(PS[p][:, :n], lhsT=WO[:, dc, oc * 128:(oc + 1) * 128],
                                               rhs=AT[:, dc, :n], start=(dc == 0), stop=(dc == 7)),
                     r=[bWO, bAT], w=[bPS[p]], inc=(dc == 7))
            k.op("dve", lambda e: e.tensor_tensor(out=X1[:, oc, :n], in0=X1[:, oc, :n], in1=PS[p][:, :n],
                                                   op=ALU.add), r=[bX1[oc], bPS[p]], w=[bX1[oc]])
        if part == "a" and not halo_only:
            for dc in range(8):
                k.store("pool", io.x1_dst(dc, col0 - 2, n), X1[:, dc, :n], r=[bX1[dc]])
        rms(n, None, 1.0 / D_MODEL)
        for dc in range(8):
            k.op("dve", lambda e: e.scalar_tensor_tensor(
                out=HT[:, dc, :n], in0=X1[:, dc, :n], scalar=G[:, dc:dc + 1], in1=RS[:, :n],
                op0=ALU.mult, op1=ALU.mult), r=[bX1[dc], bG, bRS], w=[bHT])
        for cc in range(NCH):
            c = (cc // 2) + (NG if cc % 2 else 0)
            p = st["pi"] % 6
            st["pi"] += 1
            for dc in range(8):
                k.op("pe", lambda e: e.matmul(PS[p][:, :n], lhsT=WU[:, dc, c * 128:(c + 1) * 128],
                                               rhs=HT[:, dc, :n], start=(dc == 0), stop=(dc == 7)),
                     r=[bWU[dc], bHT], w=[bPS[p]], inc=(dc == 7))
            if halo_only:
                k.op("act", lambda e: e.copy(out=HALO[:, c, :], in_=PS[p][:, :n]),
                     r=[bPS[p]], w=[bHALO[c]])
                continue
            u = st["ui"] % NU
            st["ui"] += 1
            k.op("pool", lambda e: e.tensor_copy(out=U[u][:, 0:2], in_=HALO[:, c, :]),
                 r=[bHALO[c]], w=[bU[u]])
            k.op("act", lambda e: e.copy(out=U[u][:, 2:2 + n], in_=PS[p][:, :n]),
                 r=[bPS[p]], w=[bU[u]])
            k.op("pool", lambda e: e.tensor_copy(out=HALO[:, c, :], in_=U[u][:, n:n + 2]),
                 r=[bU[u]], w=[bHALO[c]])
            y = st["yi"] % NY
            st["yi"] += 1
            k.op("dve", lambda e: e.tensor_scalar(out=Y[y][:, :n], in0=U[u][:, 2:2 + n],
                                                   scalar1=CW[:, c, 2:3], scalar2=CB[:, c:c + 1],
                                                   op0=ALU.mult, op1=ALU.add),
                 r=[bU[u], bCW, bCB], w=[bY[y]])
            k.op("dve", lambda e: e.scalar_tensor_tensor(out=Y[y][:, :n], in0=U[u][:, 1:1 + n],
                                                          scalar=CW[:, c, 1:2], in1=Y[y][:, :n],
                                                          op0=ALU.mult, op1=ALU.add),
                 r=[bU[u], bCW, bY[y]], w=[bY[y]])
            k.op("dve", lambda e: e.scalar_tensor_tensor(out=Y[y][:, :n], in0=U[u][:, 0:n],
                                                          scalar=CW[:, c, 0:1], in1=Y[y][:, :n],
                                                          op0=ALU.mult, op1=ALU.add),
                 r=[bU[u], bCW, bY[y]], w=[bY[y]])
            sg = (cc // 2) % NSG
            if cc % 2 == 0:
                k.op("act", lambda e: e.activation(out=SGT[sg][:, :n], in_=Y[y][:, :n], func=AF.Silu),
                     r=[bY[y]], w=[bSGT[sg]])
            else:
                gch = cc // 2
                asl = gch if part == "ab" else gch % 2
                k.op("pool", lambda e: e.tensor_tensor(out=AFF[:, asl, :n], in0=SGT[sg][:, :n],
                                                        in1=Y[y][:, :n], op=ALU.mult),
                     r=[bSGT[sg], bY[y]], w=[bAFF[asl]])
                if part == "a":
                    k.store("pool", io.aff_dst(gch, col0 - 2, n), AFF[:, asl, :n], r=[bAFF[asl]])

    def tile_b(col0, n):
        for oc in range(8):
            p = st["pi"] % 6
            st["pi"] += 1
            for kc in range(NG):
                k.op("pe", lambda e: e.matmul(PS[p][:, :n], lhsT=WD[:, kc, oc * 128:(oc + 1) * 128],
                                               rhs=AFF[:, kc, :n], start=(kc == 0), stop=(kc == NG - 1)),
                     r=[bWD, bAFF[kc]], w=[bPS[p]], inc=(kc == NG - 1))
            if final_norm or mid:
                k.op("dve", lambda e: e.tensor_tensor(out=X1[:, oc, :n], in0=X1[:, oc, :n],
                                                       in1=PS[p][:, :n], op=ALU.add),
                     r=[bX1[oc], bPS[p]], w=[bX1[oc]])
            else:
                o = st["oi"] % 2
                st["oi"] += 1
                k.op("dve", lambda e: e.tensor_tensor(out=OUT[o][:, :n], in0=X1[:, oc, :n],
                                                       in1=PS[p][:, :n], op=ALU.add),
                     r=[bX1[oc], bPS[p]], w=[bOUT[o]])
                k.store("pool", io.y_dst(oc, col0 - 2, n), OUT[o][:, :n], r=[bOUT[o]])
                if io.tail_dst is not None and col0 - 2 + n == T:
                    k.store("pool", io.tail_dst(oc), OUT[o][:, n - 2:n], r=[bOUT[o]])
        if mid:
            rms(n, None, 1.0 / D_MODEL)
            for oc in range(8):
                k.store("pool", io.y_dst(oc, col0 - 2, n), X1[:, oc, :n], r=[bX1[oc]])
                if col0 - 2 + n == T:
                    k.store("pool", io.tail_dst(oc), X1[:, oc, n - 2:n], r=[bX1[oc]])
                o = st["oi"] % 2
                st["oi"] += 1
                k.op("dve", lambda e: e.scalar_tensor_tensor(
                    out=OUTH[o][:, :n], in0=X1[:, oc, :n], scalar=GF[:, oc:oc + 1], in1=RS[:, :n],
                    op0=ALU.mult, op1=ALU.mult), r=[bX1[oc], bGF, bRS], w=[bOUTH[o]])
                k.store("pool", io.h_dst(oc, col0 - 2, n), OUTH[o][:, :n], r=[bOUTH[o]])
        if final_norm:
            rms(n, None, 1.0 / D_MODEL)
            for oc in range(8):
                o = st["oi"] % 2
                st["oi"] += 1
                k.op("dve", lambda e: e.scalar_tensor_tensor(
                    out=OUT[o][:, :n], in0=X1[:, oc, :n], scalar=GF[:, oc:oc + 1], in1=RS[:, :n],
                    op0=ALU.mult, op1=ALU.mult), r=[bX1[oc], bGF, bRS], w=[bOUT[o]])
                k.store("pool", io.y_dst(oc, col0 - 2, n), OUT[o][:, :n], r=[bOUT[o]])

    if A_:
        tile(0, 2, True)
    for it in range(NT):
        n0 = len(k.stores)
        tile(2 + it * N, N, False, it)
        if B_ and getattr(io, "tile_done", None) is not None:
            io.tile_done(it, k.stores[n0:], N)


def conv_layouts(conv_w, conv_b):
    cw = np.ascontiguousarray(np.asarray(conv_w, np.float32).reshape(3, 44, 128).transpose(2, 1, 0))
    cb = np.ascontiguousarray(np.asarray(conv_b, np.float32).reshape(44, 128).T)
    return cw, cb


def esel_table():
    n = np.arange(128)[:, None, None]
    jj = np.arange(64)[None, :, None]
    s = np.arange(128)[None, None, :]
    return np.where(n == 2 * jj + (s >= 64), BIG, 0.0).astype(NPBF)


def itab_table(m):
    tl = np.arange(128)[:, None]
    jp = np.arange(1024)[None, :] - 1016
    d = tl - 16 * jp - 31
    return np.where(d >= 0, -m * d, -1e30).astype(np.float32)


def ab_tables():
    tl = np.arange(128)[:, None]
    npr = np.arange(256)[None, :] - 254
    cc = (tl >= 64).astype(np.int64)
    V = npr <= cc
    Fn = V & (npr >= cc - 1)
    A = V.astype(np.float32)
    B = (V.astype(np.float32) - 1.0) + 1e6 * Fn.astype(np.float32)
    return A, B.astype(np.float32)


def selg_table():
    t = np.zeros((12, 12, 64), np.float32)
    for r in range(12):
        t[r, r, :] = 1.0
    return t.astype(NPBF)


K2A_STATIC = [("pek", [64, 32], F32), ("pev", [64, 32], F32), ("w1k", [2048, 256], F32),
              ("w2k", [256, 64], F32), ("w1v", [2048, 256], F32), ("w2v", [256, 64], F32),
              ("bt", [128, 4, 128], F32), ("btc", [128, 4, 60], F32), ("dm", [128, 4, 512], BF16),
              ("wm", [128, 4, 512], BF16), ("cm", [128, 5, 512], BF16), ("bigi", [128, 128], BF16),
              ("idb", [128, 128], F32), ("esel", [128, 64, 128], BF16), ("itab", [128, 4, 1024], F32),
              ("atab", [128, 256], F32), ("btab", [128, 256], F32), ("selg", [12, 12, 64], BF16),
              ("qaug", [4, 3, 512], BF16)]


class IOK2aStandalone:
    def __init__(self, nc, S):
        d = lambda n, sh, dt: nc.dram_tensor(n, sh, dt, kind="ExternalInput").ap()
        self.t = {"q%d" % p: None for p in range(4)}
        qa = d("qa", [4, 64, S], BF16)
        for p in range(4):
            self.t["q%d" % p] = qa[p]
        self.t["kc"] = d("kca", [64, S], BF16)
        self.t["vc"] = d("vca", [64, S], BF16)
        self.t["ks"] = d("ksa", [64, S], BF16)
        self.t["kw"] = d("kwa", [64, S], BF16)
        self.t["gt"] = d("gt", [12, S], BF16)
        self.t["vs"] = d("vs", [S, 64], BF16)
        self.t["vw"] = d("vw", [S, 64], BF16)
        self.st = {n: d(n, sh, dt) for (n, sh, dt) in K2A_STATIC}
        self.oT = nc.dram_tensor("oT", [256, S], BF16, kind="ExternalOutput").ap()

    def fm(self, name, t0, t1):
        return [(self.t[name][:, t0:t1], t0, t1)]

    def tok(self, name, t0, t1):
        return [(self.t[name][t0:t1, :], t0, t1)]

    def out(self, p, I):
        return self.oT[p * 64:(p + 1) * 64, I * 512:(I + 1) * 512]


def build_k2a(S):
    nc = new_nc()
    io = IOK2aStandalone(nc, S)
    k = Ctx(nc)
    k.begin_phase("")
    emit_k2a(k, nc, S, io)
    k.finish()
    return nc


def emit_k2a(k, nc, S, io):
    NQ = S // 512
    NB = S // 128
    NCMP = S // 16 - 1
    CW_ = S // 16
    NCB = (CW_ + 127) // 128
    stc = io.st
    pek_in, pev_in, w1k_in, w2k_in, w1v_in, w2v_in = (stc["pek"], stc["pev"], stc["w1k"], stc["w2k"],
                                                      stc["w1v"], stc["w2v"])
    bt_in, btc_in, dm_in, wm_in, cm_in, bigi_in, idb_in = (stc["bt"], stc["btc"], stc["dm"], stc["wm"],
                                                           stc["cm"], stc["bigi"], stc["idb"])
    esel_in, itab_in, atab_in, btab_in, selg_in, qaug_in = (stc["esel"], stc["itab"], stc["atab"],
                                                            stc["btab"], stc["selg"], stc["qaug"])
    A = k.sb
    P = k.ps

    def ld_fm(dst_fn, name, t0, t1, w, q="sp"):
        for (ap, lo, hi) in io.fm(name, t0, t1):
            k.dma(q, dst_fn(lo - t0, hi - t0), ap, w=w)

    def ld_tok(dst_fn, name, t0, t1, w, q="sp"):
        for (ap, lo, hi) in io.tok(name, t0, t1):
            k.dma(q, dst_fn((lo - t0) // 128, (hi - t0) // 128),
                  ap.rearrange("(nb p) d -> p nb d", p=128), w=w)

    KS = A("KS", [67, S], BF16); bKS = k.buf("KS")
    VS = A("VS", [128, NB, 65], BF16); bVS = k.buf("VS")
    KCMP = A("KCMP", [67, NCB * 128], BF16); bKCMP = k.buf("KCMP")
    VC = A("VC", [128, NCB, 65], BF16); bVC = k.buf("VC")
    BT = A("BT", [128, 4, 128], F32); bBT = k.buf("BT")
    BTCs = A("BTCs", [128, 4, 128], F32); bBTCs = k.buf("BTCs")
    BTCw = A("BTCw", [128, 4, 8], F32); bBTCw = k.buf("BTCw")
    BTC0 = A("BTC0", [128, 4, 60], F32); bBTC0 = k.buf("BTC0")
    BTCc = A("BTCc", [128, 4, 60], F32); bBTCc = k.buf("BTCc")
    DM = A("DM", [128, 4, 512], BF16); bDM = k.buf("DM")
    WM = A("WM", [128, 4, 512], BF16); bWM = k.buf("WM")
    CM = A("CM", [128, 5, 512], BF16); bCM = k.buf("CM")
    BIGI = A("BIGI", [128, 128], BF16); bBIGI = k.buf("BIGI")
    IDB = A("IDB", [128, 128], F32); bIDB = k.buf("IDB")
    ESEL = A("ESEL", [128, 64, 128], BF16); bESEL = k.buf("ESEL")
    ITAB = A("ITAB", [128, 4, 1024], F32); bITAB = k.buf("ITAB")
    ATAB = A("ATAB", [128, 256], F32); bATAB = k.buf("ATAB")
    BTAB = A("BTAB", [128, 256], F32); bBTAB = k.buf("BTAB")
    SELG = A("SELG", [12, 12, 64], BF16); bSELG = k.buf("SELG")
    ONESB = A("ONESB", [128, 128], BF16); bONESB = k.buf("ONESB")
    ONESF = A("ONESF", [128, 64], F32); bONESF = k.buf("ONESF")
    QT = [A("QT%d" % i, [67, 4, 512], BF16) for i in range(2)]; bQT = k.bufs("QT", 2)
    KW = [A("KW%d" % i, [67, 1024], BF16) for i in range(2)]; bKW = k.bufs("KW", 2)
    VW = [A("VW%d" % i, [128, 8, 65], BF16) for i in range(2)]; bVW = k.bufs("VW", 2)
    GT = [A("GT%d" % i, [12, 512], BF16) for i in range(2)]; bGT = k.bufs("GT", 2)
    SQT = [A("SQT%d" % i, [128, 512], BF16) for i in range(2)]; bSQT = k.bufs("SQT", 2)
    MX = A("MX", [128, 64], F32); bMX = k.buf("MX")
    ST = A("ST", [128, 16], F32); bST = k.buf("ST")
    SX = A("SX", [128, 1024], F32); bSX = k.buf("SX")
    EX = A("EX", [128, 1024], F32); bEX = k.buf("EX")
    PG = A("PG", [128, 1028], F32); bPG = k.buf("PG")
    SC = A("SC", [128, 8], F32); bSC = k.buf("SC")
    IMP = A("IMP", [128, 256], F32); bIMP = k.buf("IMP")
    I2 = A("I2", [128, 256], F32); bI2 = k.buf("I2")
    I3 = A("I3", [128, 256], F32); bI3 = k.buf("I3")
    M8 = A("M8", [128, 16], F32); bM8 = k.buf("M8")
    MQ = A("MQ", [128, 256], F32); bMQ = k.buf("MQ")
    MT = [A("MT%d" % i, [128, 2, 512], BF16) for i in range(2)]; bMT = k.bufs("MT", 2)
    NPT = 6
    PT = [A("PT%d" % i, [128, 512], BF16) for i in range(NPT)]; bPT = k.bufs("PT", NPT)
    RL = A("RL", [128, 512], F32); bRL = k.buf("RL")
    OBF = A("OBF", [64, 512], F32); bOBF = k.buf("OBF")
    TT = A("TT", [64, 512], F32); bTT = k.buf("TT")
    ACC = [A("ACC%d" % i, [64, 512], F32) for i in range(2)]; bACC = k.bufs("ACC", 2)
    OUTB = [A("OUTB%d" % i, [64, 512], BF16) for i in range(2)]; bOUTB = k.bufs("OUTB", 2)
    psS = [P("psS%d" % i, [128, 512], F32) for i in range(4)]; bpsS = k.bufs("psS", 4)
    psO = [P("psO%d" % i, [128, 512], F32) for i in range(2)]; bpsO = k.bufs("psO", 2)
    psA = P("psA", [128, 512], F32); bpsA = k.buf("psA")
    psX = [P("psX0", [128, 512], F32), psS[0]]; bpsX = [k.buf("psX0"), bpsS[0]]

    for (dst, src, b) in [(BT, bt_in, bBT), (BTC0, btc_in, bBTC0), (DM, dm_in, bDM), (WM, wm_in, bWM),
                          (CM, cm_in, bCM), (BIGI, bigi_in, bBIGI), (IDB, idb_in, bIDB),
                          (ITAB, itab_in, bITAB), (ATAB, atab_in, bATAB), (BTAB, btab_in, bBTAB),
                          (SELG, selg_in, bSELG)]:
        k.dma("sp", dst[:], src, w=[b])
    for j0 in range(0, 64, 16):
        k.dma("sp", ESEL[:, j0:j0 + 16, :], esel_in[:, j0:j0 + 16, :], w=[bESEL])
    k.op("dve", lambda e: e.memset(ONESB[:], 1.0), w=[bONESB])
    k.op("dve", lambda e: e.memset(ONESF[:], 1.0), w=[bONESF])
    k.op("dve", lambda e: e.memset(PG[:], 0.0), w=[bPG])
    k.op("dve", lambda e: e.memset(I2[:], -1.0), w=[bI2])
    k.op("dve", lambda e: e.memset(RL[:], 1.0), w=[bRL])
    k.op("pool", lambda e: e.memset(KCMP[:], 0.0), w=[bKCMP])
    k.op("pool", lambda e: e.memset(KCMP[64:67, :], 1.0), w=[bKCMP])
    k.op("pool", lambda e: e.memset(VC[:], 0.0), w=[bVC])
    for s0 in range(0, S, 4096):
        s1 = min(S, s0 + 4096)
        ld_fm(lambda a, b, s0=s0: KS[0:64, s0 + a:s0 + b], "ks", s0, s1, [bKS], "act")
    k.op("pool", lambda e: e.memset(KS[64:67, :], 1.0), w=[bKS])
    for b0 in range(0, NB, 8):
        ld_tok(lambda a, b, b0=b0: VS[:, b0 + a:b0 + b, 0:64], "vs", b0 * 128, (b0 + 8) * 128, [bVS], "act")
    k.op("pool", lambda e: e.memset(VS[:, :, 64:65], 1.0), w=[bVS])
    for i in range(2):
        k.op("pool", lambda e: e.memset(VW[i][:, :, 64:65], 1.0), w=[bVW[i]])
        k.op("pool", lambda e: e.memset(KW[i][64:67, :], 1.0), w=[bKW[i]])
        k.dma("sp", QT[i][64:67, :, :], qaug_in.rearrange("h r t -> r h t"), w=[bQT[i]])

    with nc.sbuf_tensor("KCH", [64, 8208], BF16) as KCH, \
            nc.sbuf_tensor("W1", [64, 32, 256], BF16) as W1, \
            nc.sbuf_tensor("W2", [128, 2, 64], BF16) as W2, \
            nc.sbuf_tensor("PEF", [64, 32], F32) as PEF, \
            nc.sbuf_tensor("PEB", [64, 32], BF16) as PEB, \
            nc.sbuf_tensor("BH", [128, 2], F32) as BH, \
            nc.sbuf_tensor("HX", [128, 512], F32) as HX, \
            nc.sbuf_tensor("H2", [128, 512], F32) as H2, \
            nc.sbuf_tensor("HID", [128, 2, 512], BF16) as HID:
        bKCH = k.buf("KCH"); bW1 = k.buf("W1"); bW2 = k.buf("W2"); bPEF = k.buf("PEF"); bPEB = k.buf("PEB")
        bBH = k.buf("BH"); bHX = k.buf("HX"); bH2 = k.buf("H2"); bHID = k.bufs("HID", 2)
        for which, (src, pe_in, w1_in, w2_in) in enumerate([("kc", pek_in, w1k_in, w2k_in),
                                                           ("vc", pev_in, w1v_in, w2v_in)]):
            w1v_ = w1_in.rearrange("(p d) h -> d p h", d=64)
            for p0 in range(0, 32, 8):
                k.dma("pool", W1[:, p0:p0 + 8, :], w1v_[:, p0:p0 + 8, :], w=[bW1])
            k.dma("pool", W2[:], w2_in.rearrange("(c p) d -> p c d", p=128), w=[bW2])
            k.dma("sp", PEF[:], pe_in, w=[bPEF])
            k.op("dve", lambda e: e.tensor_copy(out=PEB[:], in_=PEF[:]), r=[bPEF], w=[bPEB])
            for hc in range(2):
                for pos in range(32):
                    k.op("pe", lambda e: e.matmul(psX[0][:, hc:hc + 1], lhsT=W1[:, pos, hc * 128:(hc + 1) * 128],
                                                   rhs=PEB[:, pos:pos + 1], start=(pos == 0), stop=(pos == 31)),
                         r=[bW1, bPEB], w=[bpsX[0]])
            k.op("dve", lambda e: e.tensor_copy(out=BH[:], in_=psX[0][:, 0:2]), r=[bpsX[0]], w=[bBH])
            for j0 in range(0, NCMP, 512):
                n = min(512, NCMP - j0)
                t0 = 16 * j0
                t1 = min(S, t0 + 16 * n + 16)
                ld_fm(lambda a, b: KCH[:, a:b], src, t0, t1, [bKCH])
                for hc in range(2):
                    px = psX[1]
                    for pos in range(32):
                        k.op("pe", lambda e: e.matmul(px[:, :n], lhsT=W1[:, pos, hc * 128:(hc + 1) * 128],
                                                       rhs=KCH[:, pos:pos + 16 * (n - 1) + 1:16],
                                                       start=(pos == 0), stop=(pos == 31)),
                             r=[bW1, bKCH], w=[bpsX[1]])
                    k.op("act", lambda e: e.activation(out=HX[:, :n], in_=px[:, :n], func=AF.Identity,
                                                        bias=BH[:, hc:hc + 1], scale=1.0),
                         r=[bpsX[1], bBH], w=[bHX])
                    k.op("dve", lambda e: e.tensor_tensor(out=H2[:, :n], in0=HX[:, :n], in1=HX[:, :n],
                                                           op=ALU.mult), r=[bHX], w=[bH2])
                    k.op("dve", lambda e: e.tensor_scalar(out=H2[:, :n], in0=H2[:, :n], scalar1=0.044715,
                                                           scalar2=1.0, op0=ALU.mult, op1=ALU.add),
                         r=[bH2], w=[bH2])
                    k.op("dve", lambda e: e.tensor_tensor(out=H2[:, :n], in0=H2[:, :n], in1=HX[:, :n],
                                                           op=ALU.mult), r=[bH2, bHX], w=[bH2])
                    k.op("act", lambda e: e.activation(out=H2[:, :n], in_=H2[:, :n], func=AF.Tanh,
                                                        scale=0.7978845608028654), r=[bH2], w=[bH2])
                    k.op("dve", lambda e: e.tensor_scalar(out=H2[:, :n], in0=H2[:, :n], scalar1=0.5,
                                                           scalar2=0.5, op0=ALU.mult, op1=ALU.add),
                         r=[bH2], w=[bH2])
                    k.op("dve", lambda e: e.tensor_tensor(out=HID[:, hc, :n], in0=H2[:, :n], in1=HX[:, :n],
                                                           op=ALU.mult), r=[bH2, bHX], w=[bHID[hc]])
                if which == 0:
                    for hc in range(2):
                        k.op("pe", lambda e: e.matmul(psX[0][0:64, :n], lhsT=W2[:, hc, :], rhs=HID[:, hc, :n],
                                                       start=(hc == 0), stop=(hc == 1)),
                             r=[bW2, bHID[hc]], w=[bpsX[0]])
                    k.op("act", lambda e: e.copy(out=KCMP[0:64, j0:j0 + n], in_=psX[0][0:64, :n]),
                         r=[bpsX[0]], w=[bKCMP])
                else:
                    for jb in range((n + 127) // 128):
                        m = min(128, n - jb * 128)
                        for hc in range(2):
                            k.op("pe", lambda e: e.matmul(psX[0][0:m, 0:64], lhsT=HID[:, hc, jb * 128:jb * 128 + m],
                                                           rhs=W2[:, hc, :], start=(hc == 0), stop=(hc == 1)),
                                 r=[bW2, bHID[hc]], w=[bpsX[0]])
                        gb = j0 // 128 + jb
                        k.op("act", lambda e: e.copy(out=VC[0:m, gb, 0:64], in_=psX[0][0:m, 0:64]),
                             r=[bpsX[0]], w=[bVC])
                        k.op("pool", lambda e: e.memset(VC[0:m, gb, 64:65], 1.0), w=[bVC])

    st = {"q": 0, "kw": 0, "ps": 0, "pt": 0, "out": 0}
    NT5 = S // 512

    def load_q(I):
        qb = st["q"] % 2
        st["q"] += 1
        for p_ in range(4):
            ld_fm(lambda a, b, p_=p_: QT[qb][0:64, p_, a:b], "q%d" % p_, I * 512, (I + 1) * 512, [bQT[qb]])
        return qb

    ncw = (NCB * 128 + 511) // 512
    emit_sqmax(k, nc, lambda i: ([bKCMP], KCMP[0:64, i * 512:min(NCB * 128, (i + 1) * 512)]), ncw,
               SQT, bSQT, ONESB, bONESB, psS[1:3], bpsS[1:3], MX, bMX, ST[:, 4:5], bST)
    emit_sqmax(k, nc, lambda i: ([bKS], KS[0:64, i * 512:(i + 1) * 512]), NT5,
               SQT, bSQT, ONESB, bONESB, psS[1:3], bpsS[1:3], MX, bMX, ST[:, 5:6], bST)

    def fetch_kw(i):
        wb = st["kw"] % 2
        st["kw"] += 1
        ld_fm(lambda a, b: KW[wb][0:64, a:b], "kw", i * 512, (i + 1) * 512, [bKW[wb]])
        return [bKW[wb]], KW[wb][0:64, 0:512]
    emit_sqmax(k, nc, fetch_kw, NT5, SQT, bSQT, ONESB, bONESB, psS[1:3], bpsS[1:3], MX, bMX, ST[:, 6:7], bST)
    MXQ = A("MXQ", [128, 4, 32], F32); bMXQ = k.buf("MXQ")
    it_ = 0
    qb_nx = load_q(0)
    for i in range(NT5):
        qb_ = qb_nx
        if i + 1 < NT5:
            qb_nx = load_q(i + 1)
        for p in range(4):
            a = it_ % 2
            it_ += 1
            k.op("dve", lambda e: e.tensor_tensor(out=SQT[a][0:64, :], in0=QT[qb_][0:64, p, :],
                                                   in1=QT[qb_][0:64, p, :], op=ALU.mult),
                 r=[bQT[qb_]], w=[bSQT[a]])
            k.op("pe", lambda e: e.matmul(psS[1 + a][:], lhsT=ONESB[0:64, :], rhs=SQT[a][0:64, :],
                                           start=True, stop=True), r=[bSQT[a], bONESB], w=[bpsS[1 + a]])
            k.op("dve", lambda e: e.reduce_max(out=MXQ[:, p, i:i + 1], in_=psS[1 + a][:], axis=AX.X),
                 r=[bpsS[1 + a]], w=[bMXQ])
    for p in range(4):
        k.op("dve", lambda e: e.reduce_max(out=ST[:, p:p + 1], in_=MXQ[:, p, 0:NT5], axis=AX.X),
             r=[bMXQ], w=[bST])
    for p in range(4):
        k.op("dve", lambda e: e.tensor_scalar(out=ST[:, 8:11], in0=ST[:, 4:7], scalar1=ST[:, p:p + 1],
                                               scalar2=None, op0=ALU.mult), r=[bST], w=[bST])
        k.op("act", lambda e: e.activation(out=ST[:, 8:11], in_=ST[:, 8:11], func=AF.Sqrt), r=[bST], w=[bST])
        k.op("dve", lambda e: e.tensor_scalar(out=BTCc[:, p, :], in0=BTC0[:, p, :], scalar1=ST[:, 8:9],
                                               scalar2=None, op0=ALU.subtract), r=[bST, bBTC0], w=[bBTCc])
        k.op("dve", lambda e: e.tensor_scalar(out=BTCs[:, p, :], in0=BT[:, p, :], scalar1=ST[:, 9:10],
                                               scalar2=None, op0=ALU.subtract), r=[bST, bBT], w=[bBTCs])
        k.op("dve", lambda e: e.tensor_scalar(out=BTCw[:, p, :], in0=BT[:, p, 0:8], scalar1=ST[:, 10:11],
                                               scalar2=None, op0=ALU.subtract), r=[bST, bBT], w=[bBTCw])


    def importance_block(I, qb, mb, qi):
        for u in importance_units(I, qb, mb, qi):
            u()

    def importance_units(I, qb, mb, qi):
        return [lambda p=p: importance_head(I, qb, qi, p) for p in range(4)] + [lambda: importance_topk(I, mb, qi)]

    def importance_head(I, qb, qi, p):
        i = 4 * I + qi
        ncols = min(8 * (i + 1), NCB * 128)
        nb = 2 * (i + 1)
        if True:
            io_ = 1016 - 8 * i
            for c0 in range(0, ncols, 512):
                c1 = min(ncols, c0 + 512)
                k.op("pe", lambda e: e.matmul(psA[:, 0:c1 - c0], lhsT=QT[qb][0:64, p, qi * 128:(qi + 1) * 128],
                                               rhs=KCMP[0:64, c0:c1], start=True, stop=True),
                     r=[bQT[qb], bKCMP], w=[bpsA])
                k.op("dve", lambda e: e.tensor_tensor(out=SX[:, c0:c1], in0=psA[:, 0:c1 - c0],
                                                       in1=ITAB[:, p, io_ + c0:io_ + c1], op=ALU.add),
                     r=[bpsA, bITAB], w=[bSX])
            k.op("dve", lambda e: e.reduce_max(out=SC[:, 0:1], in_=SX[:, :ncols], axis=AX.X),
                 r=[bSX], w=[bSC])
            k.op("dve", lambda e: e.tensor_scalar(out=SC[:, 1:2], in0=SC[:, 0:1], scalar1=-1e20,
                                                   scalar2=-1.0, op0=ALU.max, op1=ALU.mult),
                 r=[bSC], w=[bSC])
            k.op("act", lambda e: e.activation(out=EX[:, :ncols], in_=SX[:, :ncols], func=AF.Exp,
                                                bias=SC[:, 1:2], scale=1.0, accum_out=SC[:, 2:3]),
                 r=[bSX, bSC], w=[bEX, bSC])
            k.op("dve", lambda e: e.tensor_scalar(out=SC[:, 3:4], in0=SC[:, 2:3], scalar1=1e-30,
                                                   scalar2=None, op0=ALU.max), r=[bSC], w=[bSC])
            k.op("dve", lambda e: e.reciprocal(out=SC[:, 3:4], in_=SC[:, 3:4]), r=[bSC], w=[bSC])
            if p == 0:
                k.op("dve", lambda e: e.tensor_scalar(out=PG[:, 1:1 + ncols], in0=EX[:, :ncols],
                                                       scalar1=SC[:, 3:4], scalar2=None, op0=ALU.mult),
                     r=[bEX, bSC], w=[bPG])
            else:
                k.op("dve", lambda e: e.scalar_tensor_tensor(out=PG[:, 1:1 + ncols], in0=EX[:, :ncols],
                                                              scalar=SC[:, 3:4], in1=PG[:, 1:1 + ncols],
                                                              op0=ALU.mult, op1=ALU.add),
                     r=[bEX, bSC, bPG], w=[bPG])

    def importance_topk(I, mb, qi):
        i = 4 * I + qi
        nb = 2 * (i + 1)
        k.op("dve", lambda e: e.reduce_sum(out=IMP[:, :nb],
                                            in_=PG[:, 0:4 * nb].rearrange("p (n r) -> p n r", r=4),
                                            axis=AX.X), r=[bPG], w=[bIMP])
        k.op("dve", lambda e: e.tensor_tensor(out=IMP[:, :nb], in0=IMP[:, :nb],
                                               in1=PG[:, 4:4 * nb + 1:4], op=ALU.add),
             r=[bPG, bIMP], w=[bIMP])
        k.op("dve", lambda e: e.tensor_tensor(out=I2[:, :nb], in0=IMP[:, :nb], in1=ATAB[:, 256 - nb:256],
                                               op=ALU.mult), r=[bIMP, bATAB], w=[bI2])
        k.op("dve", lambda e: e.tensor_tensor(out=I2[:, :nb], in0=I2[:, :nb], in1=BTAB[:, 256 - nb:256],
                                               op=ALU.add), r=[bI2, bBTAB], w=[bI2])
        k.op("dve", lambda e: e.memset(I2[:, 0:1], 1e6), w=[bI2])
        k.op("dve", lambda e: e.max(out=M8[:, 0:8], in_=I2[:]), r=[bI2], w=[bM8])
        k.op("dve", lambda e: e.match_replace(out=I3[:], in_to_replace=M8[:, 0:8], in_values=I2[:],
                                               imm_value=-2.0), r=[bI2, bM8], w=[bI3])
        k.op("dve", lambda e: e.max(out=M8[:, 8:16], in_=I3[:]), r=[bI3], w=[bM8])
        k.op("dve", lambda e: e.tensor_scalar(out=MQ[:], in0=I2[:], scalar1=M8[:, 15:16], scalar2=1.0,
                                               op0=ALU.is_ge, op1=ALU.subtract), r=[bI2, bM8], w=[bMQ])
        nch = 2 if nb > 128 else 1
        for ch in range(nch):
            k.op("pe", lambda e: e.transpose(out=psX[0][:, 0:128], in_=MQ[:, ch * 128:(ch + 1) * 128],
                                              identity=IDB[:]), r=[bMQ, bIDB], w=[bpsX[0]])
            k.op("act", lambda e: e.copy(out=MT[mb][:, ch, qi * 128:(qi + 1) * 128], in_=psX[0][:, 0:128]),
                 r=[bpsX[0]], w=[bMT[mb]])

    def branch(I, qb, heads, br, steps, bias_fn, first, hook=None):
        nst = len(steps)
        staged = []

        def stage_a(n_):
            lk, kb, extra, bidx, vl, vb = steps[n_]
            pts = []
            for hi_, p in enumerate(heads):
                ps = st["ps"] % 4
                st["ps"] += 1
                pts.append((ps, None))
            for hi_, p in enumerate(heads):
                ps = pts[hi_][0]
                k.op("pe", lambda e: e.matmul(psS[ps][:], lhsT=lk, rhs=QT[qb][:, p, :], start=True,
                                               stop=(len(extra) == 0)), r=kb + [bQT[qb]], w=[bpsS[ps]],
                     inc=(len(extra) == 0))
            for xi, (xl, xr, xb) in enumerate(extra):
                for hi_, p in enumerate(heads):
                    ps = pts[hi_][0]
                    k.op("pe", lambda e: e.matmul(psS[ps][:], lhsT=xl, rhs=xr, start=False,
                                                   stop=(xi == len(extra) - 1)), r=xb, w=[bpsS[ps]],
                         inc=(xi == len(extra) - 1))
            out = []
            for hi_, p in enumerate(heads):
                ps = pts[hi_][0]
                pt = st["pt"] % NPT
                st["pt"] += 1
                bias, bbuf = bias_fn(p, bidx)
                k.op("act", lambda e: e.activation(out=PT[pt][:], in_=psS[ps][:], func=AF.Exp, bias=bias,
                                                    scale=1.0), r=[bpsS[ps], bbuf], w=[bPT[pt]])
                out.append(pt)
            return out

        def stage_b(n_, pts):
            lk, kb, extra, bidx, vl, vb = steps[n_]
            for hi_, p in enumerate(heads):
                k.op("pe", lambda e: e.matmul(psO[hi_][0:65, :], lhsT=vl, rhs=PT[pts[hi_]][:], start=(n_ == 0),
                                               stop=(n_ == nst - 1)), r=[vb, bPT[pts[hi_]]], w=[bpsO[hi_]])

        for n_ in range(nst):
            staged.append((n_, stage_a(n_)))
            if n_ == min(1, nst - 1):
                while deferred:
                    deferred.pop(0)()
            if hook is not None:
                hook(n_, nst)
            if len(staged) > 1:
                stage_b(*staged.pop(0))
        while staged:
            stage_b(*staged.pop(0))
        deferred.append(lambda: epilogue(qb, heads, br, first))

    def epilogue(qb, heads, br, first):
        for hi_, p in enumerate(heads):
            po = psO[hi_]
            k.op("dve", lambda e: e.tensor_scalar(out=RL[64:65, :], in0=po[64:65, :], scalar1=1e-30, scalar2=None,
                                                   op0=ALU.max), r=[bpsO[hi_]], w=[bRL])
            k.op("dve", lambda e: e.reciprocal(out=RL[64:65, :], in_=RL[64:65, :]), r=[bRL], w=[bRL])
            k.op("act", lambda e: e.copy(out=OBF[:], in_=po[0:64, :]), r=[bpsO[hi_]], w=[bOBF])
            k.op("pe", lambda e: e.matmul(psX[0][0:64, :], lhsT=ONESF[64:65, :], rhs=RL[64:65, :], start=True,
                                           stop=True), r=[bONESF, bRL], w=[bpsX[0]])
            k.op("dve", lambda e: e.tensor_tensor(out=TT[:], in0=OBF[:], in1=psX[0][0:64, :], op=ALU.mult),
                 r=[bOBF, bpsX[0]], w=[bTT])
            gr = p * 3 + br
            k.op("pe", lambda e: e.matmul(psX[0][0:64, :], lhsT=SELG[:, gr, :], rhs=GT[qb][:, :], start=True,
                                           stop=True), r=[bSELG, bGT[qb]], w=[bpsX[0]])
            if first:
                k.op("dve", lambda e: e.tensor_tensor(out=ACC[hi_][:], in0=TT[:], in1=psX[0][0:64, :], op=ALU.mult),
                     r=[bTT, bpsX[0]], w=[bACC[hi_]])
            else:
                k.op("dve", lambda e: e.tensor_tensor(out=TT[:], in0=TT[:], in1=psX[0][0:64, :], op=ALU.mult),
                     r=[bTT, bpsX[0]], w=[bTT])
                k.op("dve", lambda e: e.tensor_tensor(out=ACC[hi_][:], in0=ACC[hi_][:], in1=TT[:], op=ALU.add),
                     r=[bTT, bACC[hi_]], w=[bACC[hi_]])

    def load_tile(I):
        qb = load_q(I)
        ld_fm(lambda a, b: GT[qb][:, a:b], "gt", I * 512, (I + 1) * 512, [bGT[qb]])
        wb = I % 2
        jlo = max(0, 4 * I - 4)
        lo = jlo - (4 * I - 4)
        ld_fm(lambda a, b: KW[wb][0:64, lo * 128 + a:lo * 128 + b], "kw", jlo * 128, (4 * I + 4) * 128,
              [bKW[wb]])
        ld_tok(lambda a, b: VW[wb][:, lo + a:lo + b, 0:64], "vw", jlo * 128, (4 * I + 4) * 128, [bVW[wb]])
        return qb

    if getattr(io, "after_prologue", None) is not None:
        io.after_prologue()
    deferred = []
    qb_next = load_tile(0)
    for qi in range(4):
        importance_block(0, qb_next, 0, qi)
    for I in range(NQ):
        qb = qb_next
        mb = I % 2
        wb = I % 2
        jlo = max(0, 4 * I - 4)
        pend = []
        while deferred:
            deferred.pop(0)()
        if I + 1 < NQ:
            qb_next = load_tile(I + 1)
            for qi in range(4):
                pend += importance_units(I + 1, qb_next, (I + 1) % 2, qi)
        gap = max(1, (2 * (4 * I + 4)) // 22)
        half = [10]

        def hook(n_, nst):
            if pend and half[0] > 0 and n_ >= 1 and (n_ - 1) % gap == 0:
                pend.pop(0)()
                half[0] -= 1

        for hp in range(2):
            heads = (2 * hp, 2 * hp + 1)
            steps = []
            for jb in range(NCB):
                dd = I - 4 * jb
                if dd < 0:
                    continue
                extra = []
                if dd <= 4:
                    extra.append((BIGI[:], CM[:, dd, :], [bBIGI, bCM]))
                steps.append((KCMP[:, jb * 128:(jb + 1) * 128], [bKCMP], extra, dd + 28, VC[:, jb, :], bVC))
            branch(I, qb, heads, 0, steps, lambda p, ix: (BTCc[:, p, ix:ix + 1], bBTCc), True)
            steps = []
            for jb in range(4 * I + 4):
                extra = [(ESEL[:, jb % 64, :], MT[mb][:, jb // 64, :], [bESEL, bMT[mb]])]
                if jb >= 4 * I:
                    extra.append((BIGI[:], DM[:, jb - 4 * I, :], [bBIGI, bDM]))
                steps.append((KS[:, jb * 128:(jb + 1) * 128], [bKS], extra, 4 * I - jb + 3, VS[:, jb, :], bVS))
            half[0] = 10
            branch(I, qb, heads, 1, steps, lambda p, ix: (BTCs[:, p, ix:ix + 1], bBTCs), False, hook)
            steps = []
            for jb in range(jlo, 4 * I + 4):
                lw = jb - (4 * I - 4)
                if lw < 4:
                    extra = [(BIGI[:], WM[:, lw, :], [bBIGI, bWM])]
                else:
                    extra = [(BIGI[:], DM[:, lw - 4, :], [bBIGI, bDM])]
                steps.append((KW[wb][:, lw * 128:(lw + 1) * 128], [bKW[wb]], extra, 4 * I - jb + 3,
                              VW[wb][:, lw, :], bVW[wb]))
            branch(I, qb, heads, 2, steps, lambda p, ix: (BTCw[:, p, ix:ix + 1], bBTCw), False)
            def emit_out(I=I, hp=hp, heads=heads):
                n0 = len(k.stores)
                for hi_, p in enumerate(heads):
                    ob = st["out"] % 2
                    st["out"] += 1
                    k.op("act", lambda e: e.copy(out=OUTB[ob][:], in_=ACC[hi_][:]), r=[bACC[hi_]], w=[bOUTB[ob]])
                    k.store("sp", io.out(p, I), OUTB[ob][:], r=[bOUTB[ob]])
                if getattr(io, "out_done", None) is not None:
                    io.out_done(I, hp, k.stores[n0:])
            deferred.append(emit_out)
        while pend:
            pend.pop(0)()
    while deferred:
        deferred.pop(0)()


def k2a_consts(S):
    sl16 = alibi_slopes(16)
    A_, B_ = ab_tables()
    c = {"dm": dm_table(), "wm": wm_table(), "cm": cm_table(), "bigi": bigi_table(),
         "idb": np.eye(128).astype(np.float32), "esel": esel_table(), "atab": A_, "btab": B_,
         "selg": selg_table()}
    per_g = []
    for g in range(4):
        ms = sl16[g * 4:(g + 1) * 4]
        per_g.append({
            "bt": np.ascontiguousarray(np.stack([bt_table(m) for m in ms], 1)),
            "btc": np.ascontiguousarray(np.stack([btc_table(m) for m in ms], 1)),
            "itab": np.ascontiguousarray(np.stack([itab_table(m) for m in ms], 1)),
            "qaug": np.stack([q_aug_rows(m, 512) for m in ms], 0),
        })
    return c, per_g


def prep_k2a(pT, vt, g, consts, per_g, wts, S):
    d = dict(consts)
    d.update(per_g[g])
    d.update(wts)
    d["qa"] = np.ascontiguousarray(pT[g * 256:(g + 1) * 256].reshape(4, 64, S))
    d["kca"] = np.ascontiguousarray(pT[1024 + g * 64:1024 + (g + 1) * 64])
    d["vca"] = np.ascontiguousarray(pT[1280 + g * 64:1280 + (g + 1) * 64])
    d["ksa"] = np.ascontiguousarray(pT[1536 + g * 64:1536 + (g + 1) * 64])
    d["kwa"] = np.ascontiguousarray(pT[2048 + g * 64:2048 + (g + 1) * 64])
    d["vs"] = np.ascontiguousarray(vt[:, g * 64:(g + 1) * 64])
    d["vw"] = np.ascontiguousarray(vt[:, 256 + g * 64:256 + (g + 1) * 64])
    d["gt"] = np.ascontiguousarray(pT[2560 + g * 12:2560 + (g + 1) * 12])
    return d


def k2a_weights(pek, w1k, w2k, pev, w1v, w2v):
    f = lambda a: np.ascontiguousarray(np.asarray(a, np.float32))
    return {"pek": f(np.asarray(pek).T), "pev": f(np.asarray(pev).T), "w1k": f(w1k), "w2k": f(w2k),
            "w1v": f(w1v), "w2v": f(w2v)}


TPC = BATCH * SEQ // NCORES
CPB = NCORES // BATCH


def _run(nc, in_maps):
    res = run_bass_kernel_spmd(nc, in_maps, core_ids=list(range(NCORES)))
    return res.results


def _with_halo(full_T, c):
    b, j = divmod(c, CPB)
    a = full_T[b]
    out = np.zeros((a.shape[0], 2 + TPC), a.dtype)
    lo = j * TPC
    if j > 0:
        out[:, 0:2] = a[:, lo - 2:lo]
    out[:, 2:] = a[:, lo:lo + TPC]
    return out


def kernel_unfused(x, norm_mix_g, norm_ffn_g, final_norm_g,
           nsa_w_in, nsa_cmp_k_pe, nsa_cmp_k_w1, nsa_cmp_k_w2,
           nsa_cmp_v_pe, nsa_cmp_v_w1, nsa_cmp_v_w2, nsa_w_out,
           diff_w_in, diff_lam_q1, diff_lam_k1, diff_lam_q2, diff_lam_k2,
           diff_subln_g, diff_w_out,
           ffn_w_up, ffn_conv_w, ffn_conv_b, ffn_w_down):
    f32 = lambda a: np.ascontiguousarray(np.asarray(a, dtype=np.float32))
    x = f32(x)
    S = SEQ
    xT = [np.ascontiguousarray(x[b].T) for b in range(BATCH)]

    def tok_shards(full_T):
        return [np.ascontiguousarray(full_T[c // CPB][:, (c % CPB) * TPC:(c % CPB + 1) * TPC])
                for c in range(NCORES)]

    fm = [(0, 1024, 0.125, AF.Copy), (1024, 2560, 1.0, AF.Copy), (2560, 2608, 1.0, AF.Sigmoid)]
    tok = [(1792, 2048), (2304, 2560)]
    nc1 = build_k1(TPC, 2608, fm, tok)
    gl = g_layout(norm_mix_g[0])
    w = f32(nsa_w_in[0])
    r1 = _run(nc1, [{"xT": s, "g": gl, "w": w} for s in tok_shards(xT)])
    pT = [np.concatenate([r1[b * CPB + j]["projT"] for j in range(CPB)], axis=1) for b in range(BATCH)]
    vt = [np.concatenate([r1[b * CPB + j]["vtok"] for j in range(CPB)], axis=0) for b in range(BATCH)]
    del r1
    consts, per_g = k2a_consts(S)
    wts = k2a_weights(nsa_cmp_k_pe[0], nsa_cmp_k_w1[0], nsa_cmp_k_w2[0],
                      nsa_cmp_v_pe[0], nsa_cmp_v_w1[0], nsa_cmp_v_w2[0])
    nc2 = build_k2a(S)
    r2 = _run(nc2, [prep_k2a(pT[c // CPB], vt[c // CPB], c % CPB, consts, per_g, wts, S)
                    for c in range(NCORES)])
    aT = [np.concatenate([r2[b * CPB + g]["oT"] for g in range(CPB)], axis=0) for b in range(BATCH)]
    del r2, pT, vt
    cw, cb = conv_layouts(ffn_conv_w[0], ffn_conv_b[0])
    nc3 = build_k3(TPC, False)
    ins = [{"xT": _with_halo(xT, c), "aT": _with_halo(aT, c), "wo": f32(nsa_w_out[0]),
            "wu": f32(ffn_w_up[0]), "wd": f32(ffn_w_down[0]), "cw": cw, "cb": cb,
            "g": g_layout(norm_ffn_g[0]), "gf": g_layout(final_norm_g)} for c in range(NCORES)]
    r3 = _run(nc3, ins)
    xT = [np.concatenate([r3[b * CPB + j]["yT"] for j in range(CPB)], axis=1) for b in range(BATCH)]
    del r3, ins, aT

    lambda_init = 0.8 - 0.6 * float(np.exp(-0.3 * 1))
    fm = [(0, 1024, 0.125, AF.Copy), (1024, 2048, 1.0, AF.Copy)]
    tok = [(2048, 2560), (2560, 3072)]
    nc4 = build_k1(TPC, 3072, fm, tok)
    gl = g_layout(norm_mix_g[1])
    w = f32(diff_w_in[0])
    r4 = _run(nc4, [{"xT": s, "g": gl, "w": w} for s in tok_shards(xT)])
    pT = [np.concatenate([r4[b * CPB + j]["projT"] for j in range(CPB)], axis=1) for b in range(BATCH)]
    vt = [np.concatenate([r4[b * CPB + j]["vtok"] for j in range(CPB)], axis=0) for b in range(BATCH)]
    del r4
    sl8 = alibi_slopes(8)
    dmt, bigit = dm_table(), bigi_table()
    lam = np.stack([f32(diff_lam_q1[0]), f32(diff_lam_k1[0]), f32(diff_lam_q2[0]), f32(diff_lam_k2[0])], 0)
    lam = np.ascontiguousarray(np.broadcast_to(lam[None], (128, 4, 64)))
    sg = f32(diff_subln_g[0]).reshape(128, 1)
    ins = []
    for c in range(NCORES):
        b, hp = divmod(c, CPB)
        qa = np.ascontiguousarray(pT[b][hp * 256:(hp + 1) * 256].reshape(4, 64, S))
        ka = np.ascontiguousarray(pT[b][1024 + hp * 256:1024 + (hp + 1) * 256].reshape(4, 64, S))
        bt = np.stack([bt_table(sl8[hp * 2 + hh]) for hh in range(2)], 0)
        qaug = np.stack([q_aug_rows(sl8[hp * 2 + hh], 512) for hh in range(2)], 0)
        ins.append({"qa": qa, "ka": ka, "qaug": qaug,
                    "v": np.ascontiguousarray(vt[b][:, hp * 256:(hp + 1) * 256]),
                    "bt": bt, "dm": dmt, "bigi": bigit, "lam": lam, "sg": sg})
    nc5 = build_k2b(S, lambda_init)
    r5 = _run(nc5, ins)
    aT = [np.concatenate([r5[b * CPB + hp]["oT"] for hp in range(CPB)], axis=0) for b in range(BATCH)]
    del r5, ins, pT, vt
    cw, cb = conv_layouts(ffn_conv_w[1], ffn_conv_b[1])
    nc6 = build_k3(TPC, True)
    ins = [{"xT": _with_halo(xT, c), "aT": _with_halo(aT, c), "wo": f32(diff_w_out[0]),
            "wu": f32(ffn_w_up[1]), "wd": f32(ffn_w_down[1]), "cw": cw, "cb": cb,
            "g": g_layout(norm_ffn_g[1]), "gf": g_layout(final_norm_g)} for c in range(NCORES)]
    r6 = _run(nc6, ins)
    out = np.empty((BATCH, SEQ, D_MODEL), np.float32)
    for c in range(NCORES):
        b, j = divmod(c, CPB)
        out[b, j * TPC:(j + 1) * TPC, :] = r6[c]["yT"].T
    return out

TPC = BATCH * SEQ // NCORES
CPB = NCORES // BATCH
GROUPS = [[0, 1, 2, 3], [4, 5, 6, 7]]
RB1 = 640
RB4 = 512


def _chunks(t0, t1):
    j = t0 // TPC
    while j * TPC < t1:
        lo, hi = max(t0, j * TPC), min(t1, (j + 1) * TPC)
        yield j, lo, hi
        j += 1


class IOK2aFused:
    ROW = {"q0": 0, "q1": 64, "q2": 128, "q3": 192, "kc": 256, "vc": 320, "ks": 384, "kw": 448, "gt": 512}

    def __init__(self, L1F, L1T, SND2, st, k=None, G2=None):
        self.WF, self.WT, self.SND2, self.st = L1F, L1T, SND2, st
        self.k, self.G2, self.pend = k, G2, {}

    def fm(self, name, t0, t1):
        row0 = self.ROW[name]
        nr = 12 if name == "gt" else 64
        rr = lambda j: ((row0 // 128) * 4 + j) * 128 + row0 % 128
        return [(self.WF[rr(j):rr(j) + nr, lo - j * TPC:hi - j * TPC], lo, hi)
                for (j, lo, hi) in _chunks(t0, t1)]

    def tok(self, name, t0, t1):
        c0 = 0 if name == "vs" else 64
        return [(self.WT[lo:hi, c0:c0 + 64], lo, hi) for (j, lo, hi) in _chunks(t0, t1)]

    def out(self, p, I):
        j, i8 = divmod(I, 8)
        return self.SND2[j * 256 + p * 64:j * 256 + (p + 1) * 64, i8 * 512:(i8 + 1) * 512]

    def out_done(self, I, hp, toks):
        self.pend.setdefault(hp, []).extend(toks)
        if I % 8 == 7:
            i = (I // 8) * 2 + hp
            self.k.collective_async(self.SND2[i * 128:(i + 1) * 128, :], self.G2[i * 512:(i + 1) * 512, :],
                                    GROUPS, self.pend.pop(hp))


class IOK2bFused:
    def __init__(self, L4F, L4T, SND5, d, k=None, G5=None):
        self.WF, self.WT, self.SND5 = L4F, L4T, SND5
        self.ctx, self.G5, self.pend = k, G5, {}
        self.qaug, self.bt_in, self.dm_in, self.bigi_in, self.lam_in, self.sg_in = (
            d["b_qaug"], d["b_bt"], d["a_dm"], d["a_bigi"], d["b_lam"], d["b_sg"])

    def _fm(self, row0, t0, t1):
        rr = lambda j: ((row0 // 128) * 4 + j) * 128 + row0 % 128
        return [(self.WF[rr(j):rr(j) + 64, lo - j * TPC:hi - j * TPC], lo, hi)
                for (j, lo, hi) in _chunks(t0, t1)]

    def q(self, hh, c, t0, t1):
        return self._fm((hh * 2 + c) * 64, t0, t1)

    def k(self, hh, c, t0, t1):
        return self._fm(256 + (hh * 2 + c) * 64, t0, t1)

    def v(self, hh, t0, t1):
        out = []
        for (j, lo, hi) in _chunks(t0, t1):
            a = lo
            while a < hi:
                tl = a - j * TPC
                b = min(hi, j * TPC + (tl // 2048 + 1) * 2048)
                row = ((tl // 2048) * 4 + j) * 2048 + tl % 2048
                out.append((self.WT[row:row + (b - a), hh * 128:(hh + 1) * 128], a, b))
                a = b
        return out

    def out(self, hh, I):
        j, i8 = divmod(I, 8)
        return self.SND5[j * 256 + hh * 128:j * 256 + (hh + 1) * 128, i8 * 512:(i8 + 1) * 512]

    def out_done(self, I, hh, toks):
        self.pend.setdefault(hh, []).extend(toks)
        if I % 8 == 7:
            i = (I // 8) * 2 + hh
            self.ctx.collective_async(self.SND5[i * 128:(i + 1) * 128, :], self.G5[i * 512:(i + 1) * 512, :],
                                    GROUPS, self.pend.pop(hh))


class IOK3Fused:
    def __init__(self, layer, d, xh0, X2, LA, LHA, LH, SND3, yT, SNDH=None, k=None, GH=None):
        self.layer, self.xh0, self.X2, self.SND3, self.yT = layer, xh0, X2, SND3, yT
        self.WA = LA.rearrange("(h g p) t -> h p g t", h=2, g=4, p=128)
        self.WHA = LHA.rearrange("(h g p) t -> h p g t", h=2, g=4, p=128)
        self.WH = LH.rearrange("(dc p) t -> p dc t", p=128)
        sfx = str(layer)
        self.wo_in, self.wu_in, self.wd_in = d["wo" + sfx], d["wu" + sfx], d["wd" + sfx]
        self.cw_in, self.cb_in, self.g_in, self.gf_in = d["cw" + sfx], d["cb" + sfx], d["g_ffn" + sfx], d["g_fin"]
        self.halo_scale = d["m0"]
        self.tail_dst = (lambda oc: SND3[oc * 128:(oc + 1) * 128, 0:2]) if layer == 0 else None
        if layer == 0:
            self.gf_in = d["g_mix1"]
            self.h_dst = lambda oc, tcol, n: SNDH[(tcol // 512) * 1024 + oc * 128:(tcol // 512) * 1024 + (oc + 1) * 128,
                                                  tcol % 512:tcol % 512 + n]
            self.k, self.GH, self.SNDH, self.pend = k, GH, SNDH, []

    def tile_done(self, it, toks, N=256):
        if self.layer != 0:
            return
        self.pend.extend(toks)
        per = 512 // N
        if it % per == per - 1:
            c = it // per
            self.k.collective_async(self.SNDH[c * 1024:(c + 1) * 1024, :], self.GH[c * 4096:(c + 1) * 4096, :],
                                    GROUPS, self.pend)
            self.pend = []

    def x_src(self, col0, n):
        if self.layer == 0:
            return self.xh0.rearrange("(dc p) t -> p dc t", p=128)[:, :, col0:col0 + n]
        if col0 == 0:
            assert n == 2
            return self.WH
        return self.X2.rearrange("(dc p) t -> p dc t", p=128)[:, :, col0 - 2:col0 - 2 + n]

    def a_src(self, col0, n):
        if col0 == 0:
            assert n == 2
            return [(h, 2, self.WHA[h]) for h in range(2)]
        return [(h, 2, self.WA[h][:, :, col0 - 2:col0 - 2 + n]) for h in range(2)]

    def y_dst(self, oc, tcol, n):
        dst = self.X2 if self.layer == 0 else self.yT
        return dst[oc * 128:(oc + 1) * 128, tcol:tcol + n]

    def x1_dst(self, dc, tcol, n):
        return self.X1D[dc * 128:(dc + 1) * 128, tcol:tcol + n]

    def x1_src(self, tcol, n):
        return self.X1D.rearrange("(dc p) t -> p dc t", p=128)[:, :, tcol:tcol + n]

    def aff_dst(self, gch, tcol, n):
        return self.AFFD[gch * 128:(gch + 1) * 128, tcol:tcol + n]

    def aff_src(self, tcol, n):
        return self.AFFD.rearrange("(kc p) t -> p kc t", p=128)[:, :, tcol:tcol + n]


FUSED_INPUTS = [("xh0", [D_MODEL, 2 + TPC], F32), ("w_in0g", [D_MODEL, 652], F32), ("w_in1g", [D_MODEL, 768], F32),
                ("g_mix0", [128, 8], F32), ("g_mix1", [128, 8], F32), ("g_ffn0", [128, 8], F32),
                ("g_ffn1", [128, 8], F32), ("g_fin", [128, 8], F32), ("m0", [128, 1], F32),
                ("b_qaug", [2, 3, 512], BF16), ("b_bt", [2, 128, 128], F32), ("b_lam", [128, 4, 64], F32),
                ("b_sg", [128, 1], F32)]
for _l in range(2):
    FUSED_INPUTS += [("wo%d" % _l, [D_MODEL, D_MODEL], F32), ("wu%d" % _l, [D_MODEL, 2 * D_FF], F32),
                     ("wd%d" % _l, [D_FF, D_MODEL], F32), ("cw%d" % _l, [128, 44, 3], F32),
                     ("cb%d" % _l, [128, 44], F32)]
FUSED_INPUTS += [("a_" + n, sh, dt) for (n, sh, dt) in K2A_STATIC]


def build_fused(lambda_init, upto=None):
    nc = new_nc()
    S, T = SEQ, TPC
    d = {n: nc.dram_tensor(n, sh, dt, kind="ExternalInput").ap() for (n, sh, dt) in FUSED_INPUTS}
    yT = nc.dram_tensor("yT", [D_MODEL, T], F32, kind="ExternalOutput").ap()
    sc = lambda n, sh, dt: nc.dram_tensor(n, sh, dt).ap()
    SND1F = sc("SND1F", [4 * RB1, T], BF16); G1F = sc("G1F", [16 * RB1, T], BF16)
    SND1T = sc("SND1T", [4 * T, 128], BF16); G1T = sc("G1T", [16 * T, 128], BF16)
    SND2 = sc("SND2", [4 * 256, T], BF16); G2 = sc("G2", [16 * 256, T], BF16)
    X2 = sc("X2", [D_MODEL, T], F32)
    SND3 = sc("SND3", [D_MODEL, 2], F32); G3 = sc("G3", [4 * D_MODEL, 2], F32)
    SND4F = sc("SND4F", [4 * RB4, T], BF16); G4F = sc("G4F", [16 * RB4, T], BF16)
    SND4T = sc("SND4T", [4 * T, 256], BF16); G4T = sc("G4T", [16 * T, 256], BF16)
    SND5 = sc("SND5", [4 * 256, T], BF16); G5 = sc("G5", [16 * 256, T], BF16)
    L1F = sc("L1F", [4 * RB1, T], BF16); L1T = sc("L1T", [4 * T, 128], BF16)
    L4F = sc("L4F", [4 * RB4, T], BF16); L4T = sc("L4T", [4 * T, 256], BF16)
    LA2 = sc("LA2", [1024, T], BF16); LA5 = sc("LA5", [1024, T], BF16)
    LHA2 = sc("LHA2", [1024, 2], BF16); LHA5 = sc("LHA5", [1024, 2], BF16)
    LH = sc("LH", [D_MODEL, 2], F32)
    k = Ctx(nc)
    r = nc.sync.partition_id() % 4

    dbg = nc.dram_tensor("dbg", [2560, 4096], BF16, kind="ExternalOutput").ap() if upto else None

    def stop_here(tag, src_ap):
        if upto != tag:
            return False
        k.barrier()
        db = Buf("dbg")
        k.dma("sp", dbg[0:src_ap.shape[0], 0:src_ap.shape[1]], src_ap, w=[db], own=db)
        k.stores.append(db.w)
        k.finish()
        return True

    def gather_rows(SND, G, rows):
        n = SND.shape[0] // rows
        k.all_gather_chunks([(SND[i * rows:(i + 1) * rows, :], G[i * 4 * rows:(i + 1) * 4 * rows, :])
                             for i in range(n)], GROUPS)

    def extract_window(G, L):
        n = L.shape[0] * L.shape[1]
        a = n // 16384
        assert a * 16384 == n and G.shape[0] * G.shape[1] == 4 * n
        gf = G.rearrange("r t -> (r t)").rearrange("(q a l) -> q a l", q=4, a=a, l=16384)
        lf = L.rearrange("r t -> (r t)").rearrange("(a l) -> a l", a=a, l=16384)
        bX = Buf("extract")
        k.dma("sp", lf, gf[bass.ds(r, 1)].rearrange("o a l -> (o a) l"), w=[bX], own=bX)

    def extract_att(G, L, LHA_):
        extract_window(G, L)
        gq = G.rearrange("(q x) t -> q x t", q=4)
        bX2 = Buf("extract")
        with nc.allow_non_contiguous_dma(reason="2-column conv halo"):
            k.dma("sp", LHA_, gq[bass.ds((r + 3) % 4, 1), :, T - 2:T].rearrange("o x t -> (o x) t"), w=[bX2],
                  own=bX2)

    SNDH = sc("SNDH", [8 * D_MODEL, 512], BF16)
    GH = sc("GH", [32 * D_MODEL, 512], BF16)
    GHv = GH.rearrange("(it j dc p) t -> p it j dc t", it=8, j=4, dc=8, p=128)

    def ht_src(t0):
        j, tl = divmod(t0, T)
        return GHv[:, tl // 512, j, :, :]

    k.begin_phase("A_")
    emit_norm(k, nc, T, d["xh0"].rearrange("(dc p) t -> p dc t", p=128)[:, :, 2:2 + T], d["g_mix0"],
              lambda dc, t0: SNDH[(t0 // 512) * 1024 + dc * 128:(t0 // 512) * 1024 + (dc + 1) * 128, 0:512],
              lambda it, toks: k.collective_async(SNDH[it * 1024:(it + 1) * 1024, :],
                                                  GH[it * 4096:(it + 1) * 4096, :], GROUPS, toks))
    k.end_phase()

    k.begin_phase("P_")

    def fm_route_a(c0, c1, t0):
        j, tl = divmod(t0, T)
        row = ((c0 // 128) * 4 + j) * 128 + c0 % 128
        return [(0, c1 - c0, L1F[row:row + c1 - c0, tl:tl + 512])]

    def tok_route_a(c0, c1, t0, tb):
        return [(0, 128, L1T[t0 + tb * 128:t0 + (tb + 1) * 128, 0:128])]

    emit_proj(k, nc, S, 652, ht_src, d["w_in0g"],
              [(0, 256, 0.125, AF.Copy), (256, 512, 1.0, AF.Copy), (512, 524, 1.0, AF.Sigmoid)], fm_route_a,
              [(524, 652)], tok_route_a)
    k.end_phase()

    WB = [(sc("WOB%d" % l, [D_MODEL, D_MODEL], BF16), sc("WUB%d" % l, [D_MODEL, 2 * D_FF], BF16),
           sc("WDB%d" % l, [D_FF, D_MODEL], BF16)) for l in range(2)]

    def precast_weights():
        for l in range(2):
            for (src, dst) in ((d["wo%d" % l], WB[l][0]), (d["wu%d" % l], WB[l][1]), (d["wd%d" % l], WB[l][2])):
                R_, C_ = src.shape
                bw = Buf("wcast")
                for r0 in range(0, R_, 128):
                    for c0 in range(0, C_, 1024):
                        c1 = min(C_, c0 + 1024)
                        k.dma("pool", dst[r0:r0 + 128, c0:c1], src[r0:r0 + 128, c0:c1], w=[bw], own=bw)

    k.begin_phase("B_")
    io_b = IOK2aFused(L1F, L1T, SND2, {n: d["a_" + n] for (n, _, _) in K2A_STATIC}, k, G2)
    io_b.after_prologue = precast_weights
    emit_k2a(k, nc, S, io_b)
    k.end_phase()
    extract_att(G2, LA2, LHA2)
    k.barrier()
    if stop_here("B2", LA2) or stop_here("B2H", LHA2):
        return nc

    k.begin_phase("C_")
    X1D = sc("X1D", [D_MODEL, T], F32)
    AFFD = sc("AFFD", [D_FF, T], BF16)
    io_c = IOK3Fused(0, d, d["xh0"], X2, LA2, LHA2, LH, SND3, yT, SNDH, k, GH)
    io_c.X1D, io_c.AFFD = X1D, AFFD
    io_c.wb = WB[0]
    emit_k3(k, nc, T, False, io_c, 512, "a")
    k.end_phase()
    k.begin_phase("Cb_")
    emit_k3(k, nc, T, False, io_c, 512, "b")
    k.end_phase()
    if upto == "C":
        k.barrier()
        k.finish()
        return nc
    k.all_gather_chunks([(SND3, G3)], GROUPS)
    bXh = Buf("extract")
    k.dma("sp", LH, G3[bass.ds(((r + 3) % 4) * D_MODEL, D_MODEL), :], w=[bXh], own=bXh)

    k.begin_phase("D_")

    def fm_route_d(c0, c1, t0):
        j, tl = divmod(t0, T)
        row = ((c0 // 128) * 4 + j) * 128
        return [(0, 128, L4F[row:row + 128, tl:tl + 512])]

    def tok_route_d(c0, c1, t0, tb):
        j, tl = divmod(t0 + tb * 128, T)
        row = ((tl // 2048) * 4 + j) * 2048 + tl % 2048
        return [(0, 256, L4T[row:row + 128, 0:256])]

    emit_proj(k, nc, S, 768, ht_src, d["w_in1g"],
              [(0, 256, 0.125, AF.Copy), (256, 512, 1.0, AF.Copy)], fm_route_d, [(512, 768)], tok_route_d)
    k.end_phase()

    k.begin_phase("E_")
    emit_k2b(k, nc, S, lambda_init, IOK2bFused(L4F, L4T, SND5, d, k, G5))
    k.end_phase()
    if upto == "E":
        k.barrier()
        k.finish()
        return nc
    extract_att(G5, LA5, LHA5)
    k.barrier()

    k.begin_phase("F_")
    io_f = IOK3Fused(1, d, d["xh0"], X2, LA5, LHA5, LH, SND3, yT)
    io_f.X1D, io_f.AFFD = X1D, AFFD
    io_f.wb = WB[1]
    emit_k3(k, nc, T, True, io_f, 512, "a")
    k.end_phase()
    k.begin_phase("Fb_")
    emit_k3(k, nc, T, True, io_f, 512, "b")
    k.barrier()
    k.finish()
    print("[fused] semaphores used:", k.nsem)
    return nc


def fused_inputs(x, norm_mix_g, norm_ffn_g, final_norm_g,
                 nsa_w_in, nsa_cmp_k_pe, nsa_cmp_k_w1, nsa_cmp_k_w2,
                 nsa_cmp_v_pe, nsa_cmp_v_w1, nsa_cmp_v_w2, nsa_w_out,
                 diff_w_in, diff_lam_q1, diff_lam_k1, diff_lam_q2, diff_lam_k2,
                 diff_subln_g, diff_w_out,
                 ffn_w_up, ffn_conv_w, ffn_conv_b, ffn_w_down):
    f32 = lambda a: np.ascontiguousarray(np.asarray(a, dtype=np.float32))
    x = f32(x)
    xT = [np.ascontiguousarray(x[b].T) for b in range(BATCH)]
    consts, per_g = k2a_consts(SEQ)
    wts = k2a_weights(nsa_cmp_k_pe[0], nsa_cmp_k_w1[0], nsa_cmp_k_w2[0],
                      nsa_cmp_v_pe[0], nsa_cmp_v_w1[0], nsa_cmp_v_w2[0])
    sl8 = alibi_slopes(8)
    lam = np.stack([f32(diff_lam_q1[0]), f32(diff_lam_k1[0]), f32(diff_lam_q2[0]), f32(diff_lam_k2[0])], 0)
    lam = np.ascontiguousarray(np.broadcast_to(lam[None], (128, 4, 64)))
    common = {"g_mix0": g_layout(norm_mix_g[0]), "g_mix1": g_layout(norm_mix_g[1]),
              "g_ffn0": g_layout(norm_ffn_g[0]), "g_ffn1": g_layout(norm_ffn_g[1]),
              "g_fin": g_layout(final_norm_g), "b_lam": lam, "b_sg": f32(diff_subln_g[0]).reshape(128, 1),
              "wo0": f32(nsa_w_out[0]), "wo1": f32(diff_w_out[0])}
    for l in range(2):
        cw, cb = conv_layouts(ffn_conv_w[l], ffn_conv_b[l])
        common.update({"wu%d" % l: f32(ffn_w_up[l]), "wd%d" % l: f32(ffn_w_down[l]), "cw%d" % l: cw,
                       "cb%d" % l: cb})
    for n_, v_ in list(consts.items()) + list(wts.items()):
        common["a_" + n_] = v_
    in_maps = []
    for c in range(NCORES):
        r = c % CPB
        m = dict(common)
        for n_, v_ in per_g[r].items():
            m["a_" + n_] = v_
        m["xh0"] = _with_halo(xT, c)
        w0, w1 = f32(nsa_w_in[0]), f32(diff_w_in[0])
        cols0 = (list(range(r * 256, (r + 1) * 256)) + [o_ + r * 64 + i for o_ in (1024, 1280, 1536, 2048)
                                                         for i in range(64)]
                 + list(range(2560 + r * 12, 2560 + (r + 1) * 12))
                 + [o_ + r * 64 + i for o_ in (1792, 2304) for i in range(64)])
        m["w_in0g"] = np.ascontiguousarray(w0[:, cols0])
        cols1 = [o_ + r * 256 + i for o_ in (0, 1024, 2048) for i in range(256)]
        m["w_in1g"] = np.ascontiguousarray(w1[:, cols1])
        m["m0"] = np.full((128, 1), 1.0 if r > 0 else 0.0, np.float32)
        m["b_qaug"] = np.stack([q_aug_rows(sl8[r * 2 + hh], 512) for hh in range(2)], 0)
        m["b_bt"] = np.stack([bt_table(sl8[r * 2 + hh]) for hh in range(2)], 0)
        in_maps.append(m)
    return in_maps


def kernel(**inputs):
    lambda_init = 0.8 - 0.6 * float(np.exp(-0.3 * 1))
    nc = build_fused(lambda_init)
    in_maps = fused_inputs(**inputs)
    res = run_bass_kernel_spmd(nc, in_maps, core_ids=list(range(NCORES))).results
    out = np.empty((BATCH, SEQ, D_MODEL), np.float32)
    for c in range(NCORES):
        b, j = divmod(c, CPB)
        out[b, j * TPC:(j + 1) * TPC, :] = res[c]["yT"].T
    return out
```

```python
import numpy as np
import ml_dtypes
import concourse.bass as bass
import concourse.mybir as mybir
from concourse.bass_utils import run_bass_kernel_spmd

F32 = mybir.dt.float32
BF16 = mybir.dt.bfloat16
AF = mybir.ActivationFunctionType
ALU = mybir.AluOpType
AX = mybir.AxisListType
NPBF = ml_dtypes.bfloat16

NCORES = 8
D_MODEL = 1024
BATCH = 2
SEQ = 16384
EPS = 1e-6


class Buf:
    __slots__ = ("name", "w", "r", "sem", "cnt")

    def __init__(self, name):
        self.name = name
        self.w = None
        self.r = {}
        self.sem = None
        self.cnt = 0


class Ctx:
    SEM_ROLL = 30000

    def __init__(self, nc, same_engine_sync=True):
        self.nc = nc
        self.same = same_engine_sync
        self.nsem = 0
        self.E = {}
        for n, e in [("pe", nc.tensor), ("act", nc.scalar), ("dve", nc.vector),
                     ("pool", nc.gpsimd), ("sp", nc.sync)]:
            self.E[n] = {"eng": e, "sem": self._newsem("e_" + n), "cnt": 0, "waited": {}}
        self.stores = []
        self.es = None
        self.pfx = ""
        self.phase_bufs = []
        self.sempool = []
        self.ccbuf = None
        self.dummy = self.nc.alloc_sbuf_tensor("bar_dummy", [128, 8], F32)
        self.bdummy = Buf("bar_dummy")

    def _newsem(self, name):
        self.nsem += 1
        s = self.nc.alloc_semaphore("%s_%d" % (name, self.nsem))
        return (s, self.nsem)

    def buf(self, name):
        return Buf(name)

    def begin_phase(self, pfx):
        from contextlib import ExitStack
        self.es = ExitStack()
        self.pfx = pfx

    def sb(self, name, shape, dt):
        return self.es.enter_context(self.nc.sbuf_tensor(self.pfx + name, shape, dt))

    def ps(self, name, shape, dt=F32):
        return self.es.enter_context(self.nc.psum_tensor(self.pfx + name, shape, dt))

    def _own_sem(self, own):
        if own.sem is None or own.cnt >= self.SEM_ROLL:
            if self.sempool and own.sem is None:
                sem, cnt = self.sempool.pop()
                own.sem = sem
                own.cnt = cnt
            else:
                own.sem = self._newsem("d")
                own.cnt = 0
            self.phase_bufs.append(own)

    def barrier(self):
        toks = []
        for n, E in self.E.items():
            if E["cnt"] > 0:
                toks.append((E["sem"], E["cnt"], n))
        for b in self.phase_bufs:
            toks.append((b.sem, b.cnt, "dma"))
        if self.ccbuf is not None and self.ccbuf.cnt > 0:
            toks.append((self.ccbuf.sem, self.ccbuf.cnt, "dma"))
        self._wait("pool", toks)
        self.op("pool", lambda e: e.memset(self.dummy[:], 0.0), w=[self.bdummy])
        for n in self.E:
            if n != "pool":
                self._wait(n, [self.bdummy.w])

    def end_phase(self):
        self.barrier()
        self.es.close()
        self.es = None
        seen = set()
        for b in self.phase_bufs:
            if b.sem[1] not in seen and b.cnt < self.SEM_ROLL // 2:
                seen.add(b.sem[1])
                self.sempool.append((b.sem, b.cnt))
        self.phase_bufs = []

    def collective_async(self, src_ap, dst_ap, groups, deps):
        if self.ccbuf is None:
            self.ccbuf = Buf("cc_async")
            self.ccbuf.sem = self._newsem("cca")
        self._wait("pool", deps)
        inst = self.nc.gpsimd.collective_compute("AllGather", ALU.bypass, replica_groups=groups,
                                                 ins=[src_ap.opt()], outs=[dst_ap.opt()])
        inst.then_inc(self.ccbuf.sem[0], 1)
        self.ccbuf.cnt += 1

    def all_gather_chunks(self, pairs, groups):
        self.barrier()
        cb = Buf("cc")
        cb.sem = self._newsem("cc")
        for (src_ap, dst_ap) in pairs:
            inst = self.nc.gpsimd.collective_compute("AllGather", ALU.bypass, replica_groups=groups,
                                                     ins=[src_ap.opt()], outs=[dst_ap.opt()])
            inst.then_inc(cb.sem[0], 1)
            cb.cnt += 1
        self.phase_bufs.append(cb)
        self.barrier()
        self.phase_bufs.remove(cb)

    def all_gather(self, src_ap, dst_ap, groups):
        self.barrier()
        cb = Buf("cc")
        cb.sem = self._newsem("cc")
        inst = self.nc.gpsimd.collective_compute("AllGather", ALU.bypass, replica_groups=groups,
                                                 ins=[src_ap.opt()], outs=[dst_ap.opt()])
        inst.then_inc(cb.sem[0], 1)
        cb.cnt = 1
        self.phase_bufs.append(cb)
        self.barrier()
        self.phase_bufs.remove(cb)

    def bufs(self, name, n):
        return [Buf("%s%d" % (name, i)) for i in range(n)]

    def _wait(self, en, toks):
        E = self.E[en]
        need = {}
        for t in toks:
            if t is None:
                continue
            (sem, key), val, src = t
            if src == en and (en == "pe" or not self.same):
                continue
            if need.get(key, (None, 0))[1] < val:
                need[key] = (sem, val)
        for key, (sem, val) in need.items():
            if E["waited"].get(key, 0) < val:
                E["eng"].wait_ge(sem, val)
                E["waited"][key] = val

    def _collect(self, r, w):
        toks = []
        for b in r:
            toks.append(b.w)
        for b in w:
            toks.append(b.w)
            toks.extend(b.r.values())
        return toks

    def op(self, en, fn, r=(), w=()):
        E = self.E[en]
        if E["cnt"] >= self.SEM_ROLL:
            E["sem"] = self._newsem("e_" + en)
            E["cnt"] = 0
        self._wait(en, self._collect(r, w))
        inst = fn(E["eng"])
        E["cnt"] += 1
        inst.then_inc(E["sem"][0], 1)
        tok = (E["sem"], E["cnt"], en)
        for b in r:
            b.r[en] = tok
        for b in w:
            b.w = tok
            b.r = {}
        return inst

    def dma(self, qn, out, in_, r=(), w=(), own=None, **kw):
        E = self.E[qn]
        if own is None:
            own = w[0] if len(w) else r[0]
        self._own_sem(own)
        self._wait(qn, self._collect(r, w))
        inst = E["eng"].dma_start(out=out, in_=in_, **kw)
        own.cnt += 16
        inst.then_inc(own.sem[0], 16)
        tok = (own.sem, own.cnt, "dma")
        for b in r:
            b.r["dma_%d" % own.sem[1]] = tok
        for b in w:
            b.w = tok
            b.r = {}
        return tok

    def store(self, qn, out, in_, r, **kw):
        tok = self.dma(qn, out, in_, r=r, w=(), **kw)
        self.stores.append(tok)

    def finish(self):
        self._wait("sp", self.stores)
        self.stores = []
        if self.es is not None:
            self.es.close()
            self.es = None


def new_nc():
    return bass.Bass("TRN2", target_bir_lowering=False)


def emit_k1(k, nc, T, C, xv, g_in, w_in, fm_specs, fm_route, tok_cols, tok_route):
    NT = T // 512
    W = k.sb("W", [128, 8, C], BF16)
    G = k.sb("G", [128, 8], F32)
    ONES = k.sb("ONES", [128, 128], F32)
    X = [k.sb("X%d" % i, [128, 8, 512], F32) for i in range(2)]
    SQ = k.sb("SQ", [128, 8, 512], F32)
    RS = k.sb("RS", [128, 512], F32)
    HT = k.sb("HT", [128, 8, 512], BF16)
    NOB = 4
    OB = [k.sb("OB%d" % i, [128, 512], BF16) for i in range(NOB)]
    PS = [k.ps("PS%d" % i, [128, 512], F32) for i in range(6)]
    PSS = k.ps("PSS", [128, 512], F32)

    bW = k.bufs("W", 8)
    bG = k.buf("G")
    bONES = k.buf("ONES")
    bX = k.bufs("X", 2)
    bSQ = k.buf("SQ")
    bRS = k.buf("RS")
    bHT = k.buf("HT")
    bOB = k.bufs("OB", NOB)
    bPS = k.bufs("PS", 6)
    bPSS = k.buf("PSS")

    wv = w_in.rearrange("(dc p) c -> p dc c", p=128)
    for dc in range(8):
        for c0 in range(0, C, 1024):
            c1 = min(C, c0 + 1024)
            k.dma("pool", W[:, dc, c0:c1], wv[:, dc, c0:c1], w=[bW[dc]])
    k.dma("sp", G[:], g_in, w=[bG])
    k.op("dve", lambda e: e.memset(ONES[:], 1.0), w=[bONES])

    pi = 0
    oi = 0
    for it in range(NT):
        t0 = it * 512
        xb = it % 2
        k.dma("sp", X[xb][:], xv[:, :, t0:t0 + 512], w=[bX[xb]])
        k.op("act", lambda e: e.activation(out=SQ[:], in_=X[xb][:], func=AF.Square),
             r=[bX[xb]], w=[bSQ])
        for dc in range(8):
            k.op("pe", lambda e: e.matmul(PSS[:], lhsT=ONES[:], rhs=SQ[:, dc, :],
                                           start=(dc == 0), stop=(dc == 7)),
                 r=[bONES, bSQ], w=[bPSS])
        k.op("act", lambda e: e.activation(out=RS[:], in_=PSS[:], func=AF.Sqrt,
                                            scale=1.0 / D_MODEL, bias=EPS),
             r=[bPSS], w=[bRS])
        k.op("dve", lambda e: e.reciprocal(out=RS[:], in_=RS[:]), r=[bRS], w=[bRS])
        for dc in range(8):
            k.op("dve", lambda e: e.scalar_tensor_tensor(
                out=HT[:, dc, :], in0=X[xb][:, dc, :], scalar=G[:, dc:dc + 1], in1=RS[:],
                op0=ALU.mult, op1=ALU.mult), r=[bX[xb], bG, bRS], w=[bHT])
        for (s0, s1, scale, func) in fm_specs:
            for c0 in range(s0, s1, 128):
                c1 = min(s1, c0 + 128)
                m = c1 - c0
                p = pi % 6
                pi += 1
                for dc in range(8):
                    k.op("pe", lambda e: e.matmul(PS[p][:m, :], lhsT=W[:, dc, c0:c1],
                                                   rhs=HT[:, dc, :], start=(dc == 0),
                                                   stop=(dc == 7)),
                         r=[bW[dc], bHT], w=[bPS[p]])
                o = oi % NOB
                oi += 1
                k.op("act", lambda e: e.activation(out=OB[o][:m, :], in_=PS[p][:m, :],
                                                    func=func, scale=scale),
                     r=[bPS[p]], w=[bOB[o]])
                for (ro, nr, dst) in fm_route(c0, c1, t0):
                    k.store("sp", dst, OB[o][ro:ro + nr, :], r=[bOB[o]])
        for (c0, c1) in tok_cols:
            m = c1 - c0
            for tb in range(4):
                p = pi % 6
                pi += 1
                for dc in range(8):
                    k.op("pe", lambda e: e.matmul(PS[p][:, :m],
                                                   lhsT=HT[:, dc, tb * 128:(tb + 1) * 128],
                                                   rhs=W[:, dc, c0:c1], start=(dc == 0),
                                                   stop=(dc == 7)),
                         r=[bW[dc], bHT], w=[bPS[p]])
                o = oi % NOB
                oi += 1
                k.op("dve", lambda e: e.tensor_copy(out=OB[o][:, :m], in_=PS[p][:, :m]),
                     r=[bPS[p]], w=[bOB[o]])
                for (co, ncl, dst) in tok_route(c0, c1, t0, tb):
                    k.store("sp", dst, OB[o][:, co:co + ncl], r=[bOB[o]])


def emit_norm(k, nc, T, xv, g_in, h_dst, tile_done=None):
    NT = T // 512
    G = k.sb("G", [128, 8], F32)
    ONES = k.sb("ONES", [128, 128], F32)
    X = [k.sb("X%d" % i, [128, 8, 512], F32) for i in range(2)]
    SQs = [k.sb("SQ%d" % i, [128, 8, 512], F32) for i in range(2)]
    RSs = [k.sb("RS%d" % i, [128, 512], F32) for i in range(2)]
    PSSs = [k.ps("PSS%d" % i, [128, 512], F32) for i in range(2)]
    HT = [k.sb("HT%d" % i, [128, 8, 512], BF16) for i in range(2)]
    bG = k.buf("G"); bONES = k.buf("ONES"); bX = k.bufs("X", 2); bSQs = k.bufs("SQ", 2); bRSs = k.bufs("RS", 2)
    bHT = k.bufs("HT", 2); bPSSs = k.bufs("PSS", 2)
    k.dma("sp", G[:], g_in, w=[bG])
    k.op("dve", lambda e: e.memset(ONES[:], 1.0), w=[bONES])
    for it in range(NT):
        t0 = it * 512
        xb = it % 2
        SQ, RS, PSS, bSQ, bRS, bPSS = SQs[xb], RSs[xb], PSSs[xb], bSQs[xb], bRSs[xb], bPSSs[xb]
        k.dma("sp", X[xb][:], xv[:, :, t0:t0 + 512], w=[bX[xb]])
        k.op("act", lambda e: e.activation(out=SQ[:], in_=X[xb][:], func=AF.Square), r=[bX[xb]], w=[bSQ])
        for dc in range(8):
            k.op("pe", lambda e: e.matmul(PSS[:], lhsT=ONES[:], rhs=SQ[:, dc, :], start=(dc == 0), stop=(dc == 7)),
                 r=[bONES, bSQ], w=[bPSS])
        k.op("act", lambda e: e.activation(out=RS[:], in_=PSS[:], func=AF.Sqrt, scale=1.0 / D_MODEL, bias=EPS),
             r=[bPSS], w=[bRS])
        k.op("dve", lambda e: e.reciprocal(out=RS[:], in_=RS[:]), r=[bRS], w=[bRS])
        for dc in range(8):
            k.op("dve", lambda e: e.scalar_tensor_tensor(
                out=HT[xb][:, dc, :], in0=X[xb][:, dc, :], scalar=G[:, dc:dc + 1], in1=RS[:],
                op0=ALU.mult, op1=ALU.mult), r=[bX[xb], bG, bRS], w=[bHT[xb]])
        n0 = len(k.stores)
        for dc in range(8):
            k.store("pool", h_dst(dc, t0), HT[xb][:, dc, :], r=[bHT[xb]])
        if tile_done is not None:
            tile_done(it, k.stores[n0:])


def emit_proj(k, nc, T, C, ht_src, w_in, fm_specs, fm_route, tok_cols, tok_route):
    NT = T // 512
    W = k.sb("W", [128, 8, C], BF16)
    HT = [k.sb("HT%d" % i, [128, 8, 512], BF16) for i in range(2)]
    NOB = 4
    OB = [k.sb("OB%d" % i, [128, 512], BF16) for i in range(NOB)]
    PS = [k.ps("PS%d" % i, [128, 512], F32) for i in range(6)]
    bW = k.bufs("W", 8); bHT = k.bufs("HT", 2); bOB = k.bufs("OB", NOB); bPS = k.bufs("PS", 6)
    wv = w_in.rearrange("(dc p) c -> p dc c", p=128)
    for dc in range(8):
        for c0 in range(0, C, 1024):
            c1 = min(C, c0 + 1024)
            k.dma("pool", W[:, dc, c0:c1], wv[:, dc, c0:c1], w=[bW[dc]])
    pi = 0
    oi = 0
    for it in range(NT):
        t0 = it * 512
        hb = it % 2
        k.dma("sp", HT[hb][:], ht_src(t0), w=[bHT[hb]])
        for (s0, s1, scale, func) in fm_specs:
            for c0 in range(s0, s1, 128):
                c1 = min(s1, c0 + 128)
                m = c1 - c0
                p = pi % 6
                pi += 1
                for dc in range(8):
                    k.op("pe", lambda e: e.matmul(PS[p][:m, :], lhsT=W[:, dc, c0:c1], rhs=HT[hb][:, dc, :],
                                                   start=(dc == 0), stop=(dc == 7)),
                         r=[bW[dc], bHT[hb]], w=[bPS[p]])
                o = oi % NOB
                oi += 1
                k.op("act", lambda e: e.activation(out=OB[o][:m, :], in_=PS[p][:m, :], func=func, scale=scale),
                     r=[bPS[p]], w=[bOB[o]])
                for (ro, nr, dst) in fm_route(c0, c1, t0):
                    k.store("pool", dst, OB[o][ro:ro + nr, :], r=[bOB[o]])
        for (c0, c1) in tok_cols:
            m = c1 - c0
            for tb in range(4):
                p = pi % 6
                pi += 1
                for dc in range(8):
                    k.op("pe", lambda e: e.matmul(PS[p][:, :m], lhsT=HT[hb][:, dc, tb * 128:(tb + 1) * 128],
                                                   rhs=W[:, dc, c0:c1], start=(dc == 0), stop=(dc == 7)),
                         r=[bW[dc], bHT[hb]], w=[bPS[p]])
                o = oi % NOB
                oi += 1
                k.op("dve", lambda e: e.tensor_copy(out=OB[o][:, :m], in_=PS[p][:, :m]), r=[bPS[p]], w=[bOB[o]])
                for (co, ncl, dst) in tok_route(c0, c1, t0, tb):
                    k.store("pool", dst, OB[o][:, co:co + ncl], r=[bOB[o]])


def build_k1(T, C, fm_specs, tok_cols):
    nc = new_nc()
    CV = sum(c1 - c0 for c0, c1 in tok_cols)
    xT = nc.dram_tensor("xT", [D_MODEL, T], F32, kind="ExternalInput").ap()
    g_in = nc.dram_tensor("g", [128, 8], F32, kind="ExternalInput").ap()
    w_in = nc.dram_tensor("w", [D_MODEL, C], F32, kind="ExternalInput").ap()
    projT = nc.dram_tensor("projT", [C, T], BF16, kind="ExternalOutput").ap()
    vtok = nc.dram_tensor("vtok", [T, max(CV, 1)], BF16, kind="ExternalOutput").ap()
    voff = {}
    vo = 0
    for (c0, c1) in tok_cols:
        voff[c0] = vo
        vo += c1 - c0
    k = Ctx(nc)
    k.begin_phase("")
    emit_k1(k, nc, T, C, xT.rearrange("(dc p) t -> p dc t", p=128), g_in, w_in, fm_specs,
            lambda c0, c1, t0: [(0, c1 - c0, projT[c0:c1, t0:t0 + 512])],
            tok_cols,
            lambda c0, c1, t0, tb: [(0, c1 - c0, vtok[t0 + tb * 128:t0 + (tb + 1) * 128,
                                                     voff[c0]:voff[c0] + c1 - c0])])
    k.finish()
    return nc


def g_layout(g):
    return np.ascontiguousarray(np.asarray(g, np.float32).reshape(8, 128).T)


def run_k1(xT_shards, g, w, fm_specs, tok_cols):
    T = xT_shards[0].shape[1]
    C = w.shape[1]
    nc = build_k1(T, C, fm_specs, tok_cols)
    gl = g_layout(g)
    w = np.ascontiguousarray(w, dtype=np.float32)
    in_maps = [{"xT": np.ascontiguousarray(s), "g": gl, "w": w} for s in xT_shards]
    res = run_bass_kernel_spmd(nc, in_maps, core_ids=list(range(NCORES)))
    return res.results


BIG = 29952.0


def alibi_slopes(n):
    return np.exp2(-8.0 * np.arange(1, n + 1, dtype=np.float64) / n)


def split3(v):
    v = np.asarray(v, np.float64)
    a = v.astype(NPBF)
    r = v - a.astype(np.float64)
    b = r.astype(NPBF)
    r = r - b.astype(np.float64)
    c = r.astype(NPBF)
    return np.stack([a, b, c], 0)


def q_aug_rows(m, S):
    tl = np.arange(S) % 512
    return split3(-m * tl)


def bt_table(m):
    sl = np.arange(128)[:, None]
    delta = np.arange(-3, 125)[None, :]
    return (m * sl - m * 128.0 * delta).astype(np.float32)


def btc_table(m):
    jl = np.arange(128)[:, None]
    dd = np.arange(-28, 32)[None, :]
    return (16.0 * m * jl - m * (512.0 * dd - 31.0)).astype(np.float32)


def dm_table():
    sl = np.arange(128)[:, None, None]
    dd = np.arange(4)[None, :, None]
    tl = np.arange(512)[None, None, :]
    return np.where(128 * dd + sl > tl, -1.0, 0.0).astype(NPBF)


def wm_table():
    sl = np.arange(128)[:, None, None]
    dd = np.arange(4)[None, :, None]
    tl = np.arange(512)[None, None, :]
    return np.where(tl - 128 * dd - sl >= 0, -1.0, 0.0).astype(NPBF)


def cm_table():
    jl = np.arange(128)[:, None, None]
    dd = np.arange(5)[None, :, None]
    tl = np.arange(512)[None, None, :]
    return np.where(16 * jl + 31 > 512 * dd + tl, -1.0, 0.0).astype(NPBF)


def bigi_table():
    return (np.eye(128) * BIG).astype(NPBF)


def bcast128(v):
    v = np.asarray(v, np.float32).reshape(1, -1)
    return np.ascontiguousarray(np.broadcast_to(v, (128, v.shape[1])))


def emit_sqmax(k, nc, fetch, ntiles, SQT, bSQT, ONESB, bONESB, psM, bpsM, MX, bMX, OUT, bOUT):
    nxt = fetch(0)
    for i in range(ntiles):
        rb, src = nxt
        if i + 1 < ntiles:
            nxt = fetch(i + 1)
        a = i % 2
        w_ = src.shape[-1]
        k.op("dve", lambda e: e.tensor_tensor(out=SQT[a][0:64, :w_], in0=src, in1=src, op=ALU.mult),
             r=rb, w=[bSQT[a]])
        k.op("pe", lambda e: e.matmul(psM[a][:, :w_], lhsT=ONESB[0:64, :], rhs=SQT[a][0:64, :w_],
                                       start=True, stop=True), r=[bSQT[a], bONESB], w=[bpsM[a]])
        k.op("dve", lambda e: e.reduce_max(out=MX[:, i:i + 1], in_=psM[a][:, :w_], axis=AX.X),
             r=[bpsM[a]], w=[bMX])
    k.op("dve", lambda e: e.reduce_max(out=OUT, in_=MX[:, 0:ntiles], axis=AX.X),
         r=[bMX], w=[bOUT])


class IOK2bStandalone:
    def __init__(self, nc, S, NH):
        d = lambda n, sh, dt: nc.dram_tensor(n, sh, dt, kind="ExternalInput").ap()
        self.qa = d("qa", [NH * 2, 64, S], BF16)
        self.ka = d("ka", [NH * 2, 64, S], BF16)
        self.v_in = d("v", [S, NH * 128], BF16)
        self.qaug = d("qaug", [NH, 3, 512], BF16)
        self.bt_in = d("bt", [NH, 128, 128], F32)
        self.dm_in = d("dm", [128, 4, 512], BF16)
        self.bigi_in = d("bigi", [128, 128], BF16)
        self.lam_in = d("lam", [128, 4, 64], F32)
        self.sg_in = d("sg", [128, 1], F32)
        self.oT = nc.dram_tensor("oT", [NH * 128, S], BF16, kind="ExternalOutput").ap()

    def q(self, hh, c, t0, t1):
        return [(self.qa[hh * 2 + c, :, t0:t1], t0, t1)]

    def k(self, hh, c, t0, t1):
        return [(self.ka[hh * 2 + c, :, t0:t1], t0, t1)]

    def v(self, hh, t0, t1):
        return [(self.v_in[t0:t1, hh * 128:(hh + 1) * 128], t0, t1)]

    def out(self, hh, I):
        return self.oT[hh * 128:(hh + 1) * 128, I * 512:(I + 1) * 512]


def build_k2b(S, lambda_init, NH=2):
    nc = new_nc()
    io = IOK2bStandalone(nc, S, NH)
    k = Ctx(nc)
    k.begin_phase("")
    emit_k2b(k, nc, S, lambda_init, io, NH)
    k.finish()
    return nc


def emit_k2b(k, nc, S, lambda_init, io, NH=2):
    NQ = S // 512
    NB = S // 128
    bt_in, dm_in, bigi_in, lam_in, sg_in = io.bt_in, io.dm_in, io.bigi_in, io.lam_in, io.sg_in

    KA = [k.sb("KA%d" % c, [67, S], BF16) for c in range(2)]
    V = k.sb("V", [128, NB, 128], BF16)
    QT = [k.sb("QT%d" % i, [67, 512], BF16) for i in range(4)]
    BT = k.sb("BT", [128, 128], F32)
    BTC = [k.sb("BTC%d" % c, [128, 128], F32) for c in range(2)]
    DM = k.sb("DM", [128, 4, 512], BF16)
    BIGI = k.sb("BIGI", [128, 128], BF16)
    ONESB = k.sb("ONESB", [128, 128], BF16)
    ONESF = k.sb("ONESF", [128, 128], F32)
    LAM = k.sb("LAM", [128, 4, 64], F32)
    LT = k.sb("LT", [128, 2, 64], F32)
    LS = k.sb("LS", [128, 4], F32)
    SG = k.sb("SG", [128, 1], F32)
    SQT = [k.sb("SQT%d" % i, [128, 512], BF16) for i in range(2)]
    MX = k.sb("MX", [128, 64], F32)
    Q2 = k.sb("Q2", [128, 4], F32)
    NPT = 6
    PT = [k.sb("PT%d" % i, [128, 512], BF16) for i in range(NPT)]
    RL = k.sb("RL", [128, 512], F32)
    OC = [k.sb("OC%d" % c, [128, 512], F32) for c in range(2)]
    OD = k.sb("OD", [128, 512], F32)
    ACP = [k.sb("ACP%d" % c, [128, 512], F32) for c in range(2)]
    OSQ = k.sb("OSQ", [128, 512], F32)
    RS = k.sb("RS", [128, 512], F32)
    OUT = [k.sb("OUT%d" % i, [128, 512], BF16) for i in range(2)]
    psS = [k.ps("psS%d" % i, [128, 512], F32) for i in range(4)]
    psO = [k.ps("psO%d" % i, [128, 512], F32) for i in range(2)]
    psL = [k.ps("psL%d" % i, [128, 512], F32) for i in range(2)]
    psM = psS[0]

    bKA = k.bufs("KA", 2); bV = k.buf("V"); bQT = k.bufs("QT", 4); bBT = k.buf("BT")
    bBTC = k.bufs("BTC", 2); bDM = k.buf("DM"); bBIGI = k.buf("BIGI"); bONESB = k.buf("ONESB")
    bONESF = k.buf("ONESF"); bLAM = k.buf("LAM"); bLT = k.buf("LT"); bLS = k.buf("LS")
    bSG = k.buf("SG"); bSQT = k.bufs("SQT", 2); bMX = k.buf("MX"); bQ2 = k.buf("Q2")
    bPT = k.bufs("PT", NPT); bRL = k.buf("RL"); bOC = k.bufs("OC", 2); bOD = k.buf("OD")
    bOSQ = k.buf("OSQ"); bRS = k.buf("RS"); bOUT = k.bufs("OUT", 2); bACP = k.bufs("ACP", 2)
    bpsS = k.bufs("psS", 4); bpsO = k.bufs("psO", 2); bpsL = k.bufs("psL", 2); bpsM = bpsS[0]

    k.dma("sp", DM[:], dm_in, w=[bDM])
    k.dma("sp", BIGI[:], bigi_in, w=[bBIGI])
    k.dma("sp", LAM[:], lam_in, w=[bLAM])
    k.dma("sp", SG[:], sg_in, w=[bSG])
    k.op("dve", lambda e: e.memset(ONESB[:], 1.0), w=[bONESB])
    k.op("dve", lambda e: e.memset(ONESF[:], 1.0), w=[bONESF])
    for j in range(2):
        k.op("dve", lambda e: e.tensor_tensor(out=LT[:, j, :], in0=LAM[:, 2 * j, :],
                                               in1=LAM[:, 2 * j + 1, :], op=ALU.mult),
             r=[bLAM], w=[bLT])
        k.op("dve", lambda e: e.reduce_sum(out=LS[:, j:j + 1], in_=LT[:, j, :], axis=AX.X),
             r=[bLT], w=[bLS])
    k.op("act", lambda e: e.activation(out=LS[:, 0:2], in_=LS[:, 0:2], func=AF.Exp),
         r=[bLS], w=[bLS])
    k.op("dve", lambda e: e.tensor_tensor(out=LS[:, 2:3], in0=LS[:, 1:2], in1=LS[:, 0:1],
                                           op=ALU.subtract), r=[bLS], w=[bLS])
    k.op("dve", lambda e: e.tensor_scalar(out=LS[:, 3:4], in0=LS[:, 2:3], scalar1=-lambda_init,
                                           scalar2=None, op0=ALU.add), r=[bLS], w=[bLS])
    k.op("dve", lambda e: e.tensor_scalar(out=SG[:], in0=SG[:], scalar1=1.0 - lambda_init,
                                           scalar2=None, op0=ALU.mult), r=[bSG], w=[bSG])

    qti = 0
    pti = 0
    psi = 0
    oi = 0
    deferred = []
    for hh in range(NH):
        for c in range(2):
            for s0 in range(0, S, 4096):
                s1 = min(S, s0 + 4096)
                for (ap, lo, hi) in io.k(hh, c, s0, s1):
                    k.dma("act", KA[c][0:64, lo:hi], ap, w=[bKA[c]])
            k.op("pool", lambda e: e.memset(KA[c][64:67, :], 1.0), w=[bKA[c]])
        for b0 in range(0, NB, 32):
            b1 = min(NB, b0 + 32)
            for (ap, lo, hi) in io.v(hh, b0 * 128, b1 * 128):
                k.dma("act", V[:, lo // 128:hi // 128, :], ap.rearrange("(nb p) c -> p nb c", p=128), w=[bV])
        k.dma("sp", BT[:], bt_in[hh], w=[bBT])
        for qi_ in range(4):
            k.dma("sp", QT[qi_][64:67, :], io.qaug[hh], w=[bQT[qi_]])
        for c in range(2):
            def fetch_q(i, c=c):
                nonlocal qti
                qb = qti % 4
                qti += 1
                for (ap, lo, hi) in io.q(hh, c, i * 512, (i + 1) * 512):
                    k.dma("sp", QT[qb][0:64, :], ap, w=[bQT[qb]])
                return [bQT[qb]], QT[qb][0:64, :]
            emit_sqmax(k, nc, fetch_q, NQ, SQT, bSQT, ONESB, bONESB, psS[0:2], bpsS[0:2], MX, bMX,
                       Q2[:, 0:1], bQ2)
            emit_sqmax(k, nc, lambda i, c=c: ([bKA[c]], KA[c][0:64, i * 512:(i + 1) * 512]), NQ,
                       SQT, bSQT, ONESB, bONESB, psS[0:2], bpsS[0:2], MX, bMX, Q2[:, 1:2], bQ2)
            k.op("dve", lambda e: e.tensor_tensor(out=Q2[:, 2:3], in0=Q2[:, 0:1], in1=Q2[:, 1:2],
                                                   op=ALU.mult), r=[bQ2], w=[bQ2])
            k.op("act", lambda e: e.activation(out=Q2[:, 3:4], in_=Q2[:, 2:3], func=AF.Sqrt),
                 r=[bQ2], w=[bQ2])
            k.op("dve", lambda e: e.tensor_scalar(out=BTC[c][:], in0=BT[:], scalar1=Q2[:, 3:4],
                                                   scalar2=None, op0=ALU.subtract),
                 r=[bBT, bQ2], w=[bBTC[c]])
        for I in range(NQ):
            nkb = 4 * I + 4
            qbs = []
            for c in range(2):
                qb = qti % 4
                qti += 1
                for (ap, lo, hi) in io.q(hh, c, I * 512, (I + 1) * 512):
                    k.dma("sp", QT[qb][0:64, :], ap, w=[bQT[qb]])
                qbs.append(qb)
            staged = []

            def stage_a(jb):
                nonlocal psi, pti
                pts = []
                diag = jb >= 4 * I
                idx = 4 * I - jb + 3
                for c in range(2):
                    ps = psi % 4
                    psi += 1
                    k.op("pe", lambda e: e.matmul(psS[ps][:], lhsT=KA[c][:, jb * 128:(jb + 1) * 128],
                                                   rhs=QT[qbs[c]][:], start=True, stop=not diag),
                         r=[bKA[c], bQT[qbs[c]]], w=[bpsS[ps]])
                    if diag:
                        dd = jb - 4 * I
                        k.op("pe", lambda e: e.matmul(psS[ps][:], lhsT=BIGI[:], rhs=DM[:, dd, :],
                                                       start=False, stop=True),
                             r=[bBIGI, bDM], w=[bpsS[ps]])
                    pt = pti % NPT
                    pti += 1
                    k.op("act", lambda e: e.activation(out=PT[pt][:], in_=psS[ps][:], func=AF.Exp,
                                                        bias=BTC[c][:, idx:idx + 1], scale=1.0),
                         r=[bpsS[ps], bBTC[c]], w=[bPT[pt]])
                    pts.append(pt)
                return pts

            def stage_b(jb, pts):
                for c in range(2):
                    k.op("pe", lambda e: e.matmul(psO[c][:], lhsT=V[:, jb, :], rhs=PT[pts[c]][:],
                                                   start=(jb == 0), stop=(jb == nkb - 1)),
                         r=[bV, bPT[pts[c]]], w=[bpsO[c]])
                for c in range(2):
                    if jb % 3 == 2:
                        if jb == 2:
                            k.op("pool", lambda e: e.tensor_copy(out=ACP[c][:], in_=PT[pts[c]][:]),
                                 r=[bPT[pts[c]]], w=[bACP[c]])
                        else:
                            k.op("pool", lambda e: e.tensor_tensor(out=ACP[c][:], in0=ACP[c][:],
                                                                    in1=PT[pts[c]][:], op=ALU.add),
                                 r=[bACP[c], bPT[pts[c]]], w=[bACP[c]])
                    elif jb == 0:
                        k.op("dve", lambda e: e.tensor_copy(out=psL[c][:], in_=PT[pts[c]][:]),
                             r=[bPT[pts[c]]], w=[bpsL[c]])
                    else:
                        k.op("dve", lambda e: e.tensor_tensor(out=psL[c][:], in0=psL[c][:], in1=PT[pts[c]][:],
                                                               op=ALU.add),
                             r=[bpsL[c], bPT[pts[c]]], w=[bpsL[c]])

            for jb in range(nkb):
                staged.append((jb, stage_a(jb)))
                if jb == 1:
                    while deferred:
                        deferred.pop(0)()
                if len(staged) > 1:
                    stage_b(*staged.pop(0))
            while staged:
                stage_b(*staged.pop(0))
            def tile_epilogue(hh=hh, I=I):
                nonlocal psi, oi
                for c in range(2):
                    k.op("dve", lambda e: e.tensor_tensor(out=OSQ[:], in0=psL[c][:], in1=ACP[c][:], op=ALU.add),
                         r=[bpsL[c], bACP[c]], w=[bOSQ])
                    ps = psi % 4
                    psi += 1
                    k.op("pe", lambda e: e.matmul(psS[ps][:], lhsT=ONESF[:], rhs=OSQ[:], start=True, stop=True),
                         r=[bONESF, bOSQ], w=[bpsS[ps]])
                    k.op("dve", lambda e: e.tensor_scalar(out=RL[:], in0=psS[ps][:], scalar1=1e-30,
                                                           scalar2=None, op0=ALU.max),
                         r=[bpsS[ps]], w=[bRL])
                    k.op("dve", lambda e: e.reciprocal(out=RL[:], in_=RL[:]), r=[bRL], w=[bRL])
                    k.op("dve", lambda e: e.tensor_tensor(out=OC[c][:], in0=psO[c][:], in1=RL[:], op=ALU.mult),
                         r=[bpsO[c], bRL], w=[bOC[c]])
                k.op("dve", lambda e: e.scalar_tensor_tensor(out=OD[:], in0=OC[1][:], scalar=LS[:, 3:4],
                                                              in1=OC[0][:], op0=ALU.mult, op1=ALU.add),
                     r=[bOC[0], bOC[1], bLS], w=[bOD])
                k.op("act", lambda e: e.activation(out=OSQ[:], in_=OD[:], func=AF.Square),
                     r=[bOD], w=[bOSQ])
                k.op("pe", lambda e: e.matmul(psM[:], lhsT=ONESF[:], rhs=OSQ[:], start=True, stop=True),
                     r=[bONESF, bOSQ], w=[bpsM])
                k.op("act", lambda e: e.activation(out=RS[:], in_=psM[:], func=AF.Sqrt,
                                                    scale=1.0 / 128.0, bias=EPS), r=[bpsM], w=[bRS])
                k.op("dve", lambda e: e.reciprocal(out=RS[:], in_=RS[:]), r=[bRS], w=[bRS])
                ob = oi % 2
                oi += 1
                k.op("dve", lambda e: e.scalar_tensor_tensor(out=OUT[ob][:], in0=OD[:], scalar=SG[:, 0:1],
                                                              in1=RS[:], op0=ALU.mult, op1=ALU.mult),
                     r=[bOD, bSG, bRS], w=[bOUT[ob]])
                n0 = len(k.stores)
                k.store("sp", io.out(hh, I), OUT[ob][:], r=[bOUT[ob]])
                if getattr(io, "out_done", None) is not None:
                    io.out_done(I, hh, k.stores[n0:])
            deferred.append(tile_epilogue)
    while deferred:
        deferred.pop(0)()


D_FF = 2816


class IOK3Standalone:
    def __init__(self, nc, T):
        d = lambda n, sh, dt: nc.dram_tensor(n, sh, dt, kind="ExternalInput").ap()
        NCH = 2 * D_FF // 128
        self.xT = d("xT", [D_MODEL, 2 + T], F32)
        self.aT = d("aT", [D_MODEL, 2 + T], BF16)
        self.wo_in = d("wo", [D_MODEL, D_MODEL], F32)
        self.wu_in = d("wu", [D_MODEL, 2 * D_FF], F32)
        self.wd_in = d("wd", [D_FF, D_MODEL], F32)
        self.cw_in = d("cw", [128, NCH, 3], F32)
        self.cb_in = d("cb", [128, NCH], F32)
        self.g_in = d("g", [128, 8], F32)
        self.gf_in = d("gf", [128, 8], F32)
        self.yT = nc.dram_tensor("yT", [D_MODEL, T], F32, kind="ExternalOutput").ap()
        self.halo_scale = None
        self.tail_dst = None

    def x_src(self, col0, n):
        return self.xT.rearrange("(dc p) t -> p dc t", p=128)[:, :, col0:col0 + n]

    def a_src(self, col0, n):
        return [(0, 1, self.aT.rearrange("(dc p) t -> p dc t", p=128)[:, :, col0:col0 + n])]

    def y_dst(self, oc, tcol, n):
        return self.yT[oc * 128:(oc + 1) * 128, tcol:tcol + n]

    def make_scratch(self, nc, T):
        self.X1D = nc.dram_tensor("X1D", [D_MODEL, T], F32).ap()
        self.AFFD = nc.dram_tensor("AFFD", [D_FF, T], BF16).ap()

    def x1_dst(self, dc, tcol, n):
        return self.X1D[dc * 128:(dc + 1) * 128, tcol:tcol + n]

    def x1_src(self, tcol, n):
        return self.X1D.rearrange("(dc p) t -> p dc t", p=128)[:, :, tcol:tcol + n]

    def aff_dst(self, gch, tcol, n):
        return self.AFFD[gch * 128:(gch + 1) * 128, tcol:tcol + n]

    def aff_src(self, tcol, n):
        return self.AFFD.rearrange("(kc p) t -> p kc t", p=128)[:, :, tcol:tcol + n]


def build_k3_split(T, final_norm):
    nc = new_nc()
    io = IOK3Standalone(nc, T)
    io.make_scratch(nc, T)
    k = Ctx(nc)
    k.begin_phase("a_")
    emit_k3(k, nc, T, final_norm, io, 512, "a")
    k.end_phase()
    k.begin_phase("b_")
    emit_k3(k, nc, T, final_norm, io, 512, "b")
    k.finish()
    return nc


def build_k3(T, final_norm, N=256):
    nc = new_nc()
    io = IOK3Standalone(nc, T)
    k = Ctx(nc)
    k.begin_phase("")
    emit_k3(k, nc, T, final_norm, io, N)
    k.finish()
    return nc


def emit_k3(k, nc, T, final_norm, io, N=256, part="ab"):
    A_, B_ = ("a" in part), ("b" in part)
    NT = T // N
    NCH = 2 * D_FF // 128
    NG = NCH // 2
    wo_in, wu_in, wd_in, cw_in, cb_in, g_in, gf_in = (io.wo_in, io.wu_in, io.wd_in, io.cw_in, io.cb_in,
                                                      io.g_in, io.gf_in)

    WO = k.sb("WO", [128, 8, D_MODEL], BF16) if A_ else None
    WU = k.sb("WU", [128, 8, 2 * D_FF], BF16) if A_ else None
    WD = k.sb("WD", [128, NG, D_MODEL], BF16) if B_ else None
    CW = k.sb("CW", [128, NCH, 3], F32)
    CB = k.sb("CB", [128, NCH], F32)
    G = k.sb("G", [128, 8], F32)
    GF = k.sb("GF", [128, 8], F32)
    ONES = k.sb("ONES", [128, 128], F32)
    HALO = k.sb("HALO", [128, NCH, 2], F32) if A_ else None
    NX1 = 2 if part == "b" else 1
    X1s = [k.sb("X1_%d" % i, [128, 8, N], F32) for i in range(NX1)]
    X1 = X1s[0]
    AT = k.sb("AT", [128, 8, N], BF16) if A_ else None
    SQ = [k.sb("SQ%d" % i, [128, N], F32) for i in range(2)]
    RS = k.sb("RS", [128, N], F32)
    HT = k.sb("HT", [128, 8, N], BF16) if A_ else None
    NAF = {"ab": 1, "a": 1, "b": 2}[part]
    AFFs = [k.sb("AFF%d" % i, [128, NG if B_ else 2, N], BF16) for i in range(NAF)]
    AFF = AFFs[0]
    NU, NY, NSG = 3, 4, 2
    U = [k.sb("U%d" % i, [128, 2 + N], F32) for i in range(NU)] if A_ else None
    Y = [k.sb("Y%d" % i, [128, N], F32) for i in range(NY)] if A_ else None
    SGT = [k.sb("SGT%d" % i, [128, N], F32) for i in range(NSG)] if A_ else None
    OUT = [k.sb("OUT%d" % i, [128, N], F32) for i in range(2)] if B_ else None
    mid = B_ and getattr(io, "h_dst", None) is not None
    assert not (mid and final_norm)
    if mid:
        OUTH = [k.sb("OUTH%d" % i, [128, N], BF16) for i in range(2)]
        bOUTH = k.bufs("OUTH", 2)
    PS = [k.ps("PS%d" % i, [128, 512], F32) for i in range(6)]
    PSS = k.ps("PSS", [128, 512], F32)

    bWO = k.buf("WO"); bWU = k.bufs("WU", 8); bWD = k.buf("WD"); bCW = k.buf("CW"); bCB = k.buf("CB")
    bG = k.buf("G"); bGF = k.buf("GF"); bONES = k.buf("ONES"); bHALO = k.bufs("HALO", NCH)
    bX1s = [k.bufs("X1", 8) for _ in range(NX1)]; bX1 = bX1s[0]; bAT = k.buf("AT"); bSQ = k.bufs("SQ", 2); bRS = k.buf("RS"); bHT = k.buf("HT")
    bAFFs = [k.bufs("AFF", NG) for _ in range(NAF)]; bAFF = bAFFs[0]; bU = k.bufs("U", NU); bY = k.bufs("Y", NY); bSGT = k.bufs("SGT", NSG)
    bOUT = k.bufs("OUT", 2); bPS = k.bufs("PS", 6); bPSS = k.buf("PSS")

    if A_:
        k.dma("sp", CW[:], cw_in, w=[bCW])
        k.dma("sp", CB[:], cb_in, w=[bCB])
        k.dma("sp", G[:], g_in, w=[bG])
    k.dma("sp", GF[:], gf_in, w=[bGF])
    k.op("dve", lambda e: e.memset(ONES[:], 1.0), w=[bONES])
    pre = getattr(io, "wb", None)
    if pre is not None:
        wov = pre[0].rearrange("(dc p) c -> p dc c", p=128)
        wuv = pre[1].rearrange("(dc p) c -> p dc c", p=128)
        wdv = pre[2].rearrange("(kc p) c -> p kc c", p=128)
        for dc in range(8 if A_ else 0):
            k.dma("sp", WU[:, dc, :], wuv[:, dc, :], w=[bWU[dc]])
        if A_:
            k.dma("sp", WO[:], wov, w=[bWO])
        for kc in range(0, NG if B_ else 0, 2):
            k.dma("sp", WD[:, kc:kc + 2, :], wdv[:, kc:kc + 2, :], w=[bWD])
    else:
        wov = wo_in.rearrange("(dc p) c -> p dc c", p=128)
        for dc in range(8 if A_ else 0):
            k.dma("pool", WO[:, dc, :], wov[:, dc, :], w=[bWO])
        wuv = wu_in.rearrange("(dc p) c -> p dc c", p=128)
        for dc in range(8 if A_ else 0):
            for c0 in range(0, 2 * D_FF, 1024):
                c1 = min(2 * D_FF, c0 + 1024)
                k.dma("pool", WU[:, dc, c0:c1], wuv[:, dc, c0:c1], w=[bWU[dc]])
        wdv = wd_in.rearrange("(kc p) c -> p kc c", p=128)
        for kc in range(NG if B_ else 0):
            k.dma("pool", WD[:, kc, :], wdv[:, kc, :], w=[bWD])

    st = {"pi": 0, "ui": 0, "yi": 0, "oi": 0, "sq": 0}
    if A_ and io.halo_scale is not None:
        M0 = k.sb("M0", [128, 1], F32)
        bM0 = k.buf("M0")
        k.dma("sp", M0[:], io.halo_scale, w=[bM0])

    def rms(n, gtile, inv_d):
        for dc in range(8):
            s = st["sq"] % 2
            st["sq"] += 1
            k.op("act", lambda e: e.activation(out=SQ[s][:, :n], in_=X1[:, dc, :n], func=AF.Square),
                 r=[bX1[dc]], w=[bSQ[s]])
            k.op("pe", lambda e: e.matmul(PSS[:, :n], lhsT=ONES[:], rhs=SQ[s][:, :n],
                                           start=(dc == 0), stop=(dc == 7)),
                 r=[bONES, bSQ[s]], w=[bPSS])
        k.op("act", lambda e: e.activation(out=RS[:, :n], in_=PSS[:, :n], func=AF.Sqrt,
                                            scale=inv_d, bias=EPS), r=[bPSS], w=[bRS])
        k.op("dve", lambda e: e.reciprocal(out=RS[:, :n], in_=RS[:, :n]), r=[bRS], w=[bRS])

    def tile(col0, n, halo_only, it=0):
        nonlocal X1, bX1, AFF, bAFF
        X1, bX1 = X1s[it % NX1], bX1s[it % NX1]
        AFF, bAFF = AFFs[it % NAF], bAFFs[it % NAF]
        if part == "b":
            afv = io.aff_src(col0 - 2, n)
            for h0 in (0, NG // 2):
                k.dma("act", AFF[:, h0:h0 + NG // 2, :n], afv[:, h0:h0 + NG // 2, :], w=bAFF[h0:h0 + NG // 2])
            k.dma("sp", X1[:, :, :n], io.x1_src(col0 - 2, n), w=bX1)
        else:
            tile_a(col0, n, halo_only)
        if halo_only or part == "a":
            return
        tile_b(col0, n)

    def tile_a(col0, n, halo_only):
        k.dma("sp", X1[:, :, :n], io.x_src(col0, n), w=bX1)
        for (dc0, dstep, ap_) in io.a_src(col0, n):
            ndc = ap_.shape[1]
            k.dma("sp", AT[:, dc0:dc0 + (ndc - 1) * dstep + 1:dstep, :n], ap_, w=[bAT])
        if halo_only and io.halo_scale is not None:
            k.op("dve", lambda e: e.tensor_scalar(out=X1[:, :, :n], in0=X1[:, :, :n], scalar1=M0[:, 0:1],
                                                   scalar2=None, op0=ALU.mult), r=bX1 + [bM0], w=bX1)
            k.op("dve", lambda e: e.tensor_scalar(out=AT[:, :, :n], in0=AT[:, :, :n], scalar1=M0[:, 0:1],
                                                   scalar2=None, op0=ALU.mult), r=[bAT, bM0], w=[bAT])
        for oc in range(8):
            p = st["pi"] % 6
            st["pi"] += 1
            for dc in range(8):
                k.op("pe", lambda e: e.matmul(PS[p][:, :n], lhsT=WO[:, dc, oc * 128:(oc + 1) * 128],
                                               rhs=AT[:, dc, :n], start=(dc == 0), stop=(dc == 7)),
                     r=[bWO, bAT], w=[bPS[p]])
            k.op("dve", lambda e: e.tensor_tensor(out=X1[:, oc, :n], in0=X1[:, oc, :n], in1=PS[p][:, :n],
                                                   op=ALU.add), r=[bX1[oc], bPS[p]], w=[bX1[oc]])
        if part == "a" and not halo_only:
            for dc in range(8):
                k.store("pool", io.x1_dst(dc, col0 - 2, n), X1[:, dc, :n], r=[bX1[dc]])
        rms(n, None, 1.0 / D_MODEL)
        for dc in range(8):
            k.op("dve", lambda e: e.scalar_tensor_tensor(
                out=HT[:, dc, :n], in0=X1[:, dc, :n], scalar=G[:, dc:dc + 1], in1=RS[:, :n],
                op0=ALU.mult, op1=ALU.mult), r=[bX1[dc], bG, bRS], w=[bHT])
        for cc in range(NCH):
            c = (cc // 2) + (NG if cc % 2 else 0)
            p = st["pi"] % 6
            st["pi"] += 1
            for dc in range(8):
                k.op("pe", lambda e: e.matmul(PS[p][:, :n], lhsT=WU[:, dc, c * 128:(c + 1) * 128],
                                               rhs=HT[:, dc, :n], start=(dc == 0), stop=(dc == 7)),
                     r=[bWU[dc], bHT], w=[bPS[p]])
            if halo_only:
                k.op("act", lambda e: e.copy(out=HALO[:, c, :], in_=PS[p][:, :n]),
                     r=[bPS[p]], w=[bHALO[c]])
                continue
            u = st["ui"] % NU
            st["ui"] += 1
            k.op("pool", lambda e: e.tensor_copy(out=U[u][:, 0:2], in_=HALO[:, c, :]),
                 r=[bHALO[c]], w=[bU[u]])
            k.op("act", lambda e: e.copy(out=U[u][:, 2:2 + n], in_=PS[p][:, :n]),
                 r=[bPS[p]], w=[bU[u]])
            k.op("pool", lambda e: e.tensor_copy(out=HALO[:, c, :], in_=U[u][:, n:n + 2]),
                 r=[bU[u]], w=[bHALO[c]])
            y = st["yi"] % NY
            st["yi"] += 1
            k.op("dve", lambda e: e.tensor_scalar(out=Y[y][:, :n], in0=U[u][:, 2:2 + n],
                                                   scalar1=CW[:, c, 2:3], scalar2=CB[:, c:c + 1],
                                                   op0=ALU.mult, op1=ALU.add),
                 r=[bU[u], bCW, bCB], w=[bY[y]])
            k.op("dve", lambda e: e.scalar_tensor_tensor(out=Y[y][:, :n], in0=U[u][:, 1:1 + n],
                                                          scalar=CW[:, c, 1:2], in1=Y[y][:, :n],
                                                          op0=ALU.mult, op1=ALU.add),
                 r=[bU[u], bCW, bY[y]], w=[bY[y]])
            k.op("dve", lambda e: e.scalar_tensor_tensor(out=Y[y][:, :n], in0=U[u][:, 0:n],
                                                          scalar=CW[:, c, 0:1], in1=Y[y][:, :n],
                                                          op0=ALU.mult, op1=ALU.add),
                 r=[bU[u], bCW, bY[y]], w=[bY[y]])
            sg = (cc // 2) % NSG
            if cc % 2 == 0:
                k.op("act", lambda e: e.activation(out=SGT[sg][:, :n], in_=Y[y][:, :n], func=AF.Silu),
                     r=[bY[y]], w=[bSGT[sg]])
            else:
                gch = cc // 2
                asl = gch if part == "ab" else gch % 2
                k.op("pool", lambda e: e.tensor_tensor(out=AFF[:, asl, :n], in0=SGT[sg][:, :n],
                                                        in1=Y[y][:, :n], op=ALU.mult),
                     r=[bSGT[sg], bY[y]], w=[bAFF[asl]])
                if part == "a":
                    k.store("pool", io.aff_dst(gch, col0 - 2, n), AFF[:, asl, :n], r=[bAFF[asl]])

    def tile_b(col0, n):
        for oc in range(8):
            p = st["pi"] % 6
            st["pi"] += 1
            for kc in range(NG):
                k.op("pe", lambda e: e.matmul(PS[p][:, :n], lhsT=WD[:, kc, oc * 128:(oc + 1) * 128],
                                               rhs=AFF[:, kc, :n], start=(kc == 0), stop=(kc == NG - 1)),
                     r=[bWD, bAFF[kc]], w=[bPS[p]])
            if final_norm or mid:
                k.op("dve", lambda e: e.tensor_tensor(out=X1[:, oc, :n], in0=X1[:, oc, :n],
                                                       in1=PS[p][:, :n], op=ALU.add),
                     r=[bX1[oc], bPS[p]], w=[bX1[oc]])
            else:
                o = st["oi"] % 2
                st["oi"] += 1
                k.op("dve", lambda e: e.tensor_tensor(out=OUT[o][:, :n], in0=X1[:, oc, :n],
                                                       in1=PS[p][:, :n], op=ALU.add),
                     r=[bX1[oc], bPS[p]], w=[bOUT[o]])
                k.store("pool", io.y_dst(oc, col0 - 2, n), OUT[o][:, :n], r=[bOUT[o]])
                if io.tail_dst is not None and col0 - 2 + n == T:
                    k.store("pool", io.tail_dst(oc), OUT[o][:, n - 2:n], r=[bOUT[o]])
        if mid:
            rms(n, None, 1.0 / D_MODEL)
            for oc in range(8):
                k.store("pool", io.y_dst(oc, col0 - 2, n), X1[:, oc, :n], r=[bX1[oc]])
                if col0 - 2 + n == T:
                    k.store("pool", io.tail_dst(oc), X1[:, oc, n - 2:n], r=[bX1[oc]])
                o = st["oi"] % 2
                st["oi"] += 1
                k.op("dve", lambda e: e.scalar_tensor_tensor(
                    out=OUTH[o][:, :n], in0=X1[:, oc, :n], scalar=GF[:, oc:oc + 1], in1=RS[:, :n],
                    op0=ALU.mult, op1=ALU.mult), r=[bX1[oc], bGF, bRS], w=[bOUTH[o]])
                k.store("pool", io.h_dst(oc, col0 - 2, n), OUTH[o][:, :n], r=[bOUTH[o]])
        if final_norm:
            rms(n, None, 1.0 / D_MODEL)
            for oc in range(8):
                o = st["oi"] % 2
                st["oi"] += 1
                k.op("dve", lambda e: e.scalar_tensor_tensor(
                    out=OUT[o][:, :n], in0=X1[:, oc, :n], scalar=GF[:, oc:oc + 1], in1=RS[:, :n],
                    op0=ALU.mult, op1=ALU.mult), r=[bX1[oc], bGF, bRS], w=[bOUT[o]])
                k.store("pool", io.y_dst(oc, col0 - 2, n), OUT[o][:, :n], r=[bOUT[o]])

    if A_:
        tile(0, 2, True)
    for it in range(NT):
        n0 = len(k.stores)
        tile(2 + it * N, N, False, it)
        if B_ and getattr(io, "tile_done", None) is not None:
            io.tile_done(it, k.stores[n0:], N)


def conv_layouts(conv_w, conv_b):
    cw = np.ascontiguousarray(np.asarray(conv_w, np.float32).reshape(3, 44, 128).transpose(2, 1, 0))
    cb = np.ascontiguousarray(np.asarray(conv_b, np.float32).reshape(44, 128).T)
    return cw, cb


def esel_table():
    n = np.arange(128)[:, None, None]
    jj = np.arange(64)[None, :, None]
    s = np.arange(128)[None, None, :]
    return np.where(n == 2 * jj + (s >= 64), BIG, 0.0).astype(NPBF)


def itab_table(m):
    tl = np.arange(128)[:, None]
    jp = np.arange(1024)[None, :] - 1016
    d = tl - 16 * jp - 31
    return np.where(d >= 0, -m * d, -1e30).astype(np.float32)


def ab_tables():
    tl = np.arange(128)[:, None]
    npr = np.arange(256)[None, :] - 254
    cc = (tl >= 64).astype(np.int64)
    V = npr <= cc
    Fn = V & (npr >= cc - 1)
    A = V.astype(np.float32)
    B = (V.astype(np.float32) - 1.0) + 1e6 * Fn.astype(np.float32)
    return A, B.astype(np.float32)


def selg_table():
    t = np.zeros((12, 12, 64), np.float32)
    for r in range(12):
        t[r, r, :] = 1.0
    return t.astype(NPBF)


K2A_STATIC = [("pek", [64, 32], F32), ("pev", [64, 32], F32), ("w1k", [2048, 256], F32),
              ("w2k", [256, 64], F32), ("w1v", [2048, 256], F32), ("w2v", [256, 64], F32),
              ("bt", [128, 4, 128], F32), ("btc", [128, 4, 60], F32), ("dm", [128, 4, 512], BF16),
              ("wm", [128, 4, 512], BF16), ("cm", [128, 5, 512], BF16), ("bigi", [128, 128], BF16),
              ("idb", [128, 128], F32), ("esel", [128, 64, 128], BF16), ("itab", [128, 4, 1024], F32),
              ("atab", [128, 256], F32), ("btab", [128, 256], F32), ("selg", [12, 12, 64], BF16),
              ("qaug", [4, 3, 512], BF16)]


class IOK2aStandalone:
    def __init__(self, nc, S):
        d = lambda n, sh, dt: nc.dram_tensor(n, sh, dt, kind="ExternalInput").ap()
        self.t = {"q%d" % p: None for p in range(4)}
        qa = d("qa", [4, 64, S], BF16)
        for p in range(4):
            self.t["q%d" % p] = qa[p]
        self.t["kc"] = d("kca", [64, S], BF16)
        self.t["vc"] = d("vca", [64, S], BF16)
        self.t["ks"] = d("ksa", [64, S], BF16)
        self.t["kw"] = d("kwa", [64, S], BF16)
        self.t["gt"] = d("gt", [12, S], BF16)
        self.t["vs"] = d("vs", [S, 64], BF16)
        self.t["vw"] = d("vw", [S, 64], BF16)
        self.st = {n: d(n, sh, dt) for (n, sh, dt) in K2A_STATIC}
        self.oT = nc.dram_tensor("oT", [256, S], BF16, kind="ExternalOutput").ap()

    def fm(self, name, t0, t1):
        return [(self.t[name][:, t0:t1], t0, t1)]

    def tok(self, name, t0, t1):
        return [(self.t[name][t0:t1, :], t0, t1)]

    def out(self, p, I):
        return self.oT[p * 64:(p + 1) * 64, I * 512:(I + 1) * 512]


def build_k2a(S):
    nc = new_nc()
    io = IOK2aStandalone(nc, S)
    k = Ctx(nc)
    k.begin_phase("")
    emit_k2a(k, nc, S, io)
    k.finish()
    return nc


def emit_k2a(k, nc, S, io):
    NQ = S // 512
    NB = S // 128
    NCMP = S // 16 - 1
    CW_ = S // 16
    NCB = (CW_ + 127) // 128
    stc = io.st
    pek_in, pev_in, w1k_in, w2k_in, w1v_in, w2v_in = (stc["pek"], stc["pev"], stc["w1k"], stc["w2k"],
                                                      stc["w1v"], stc["w2v"])
    bt_in, btc_in, dm_in, wm_in, cm_in, bigi_in, idb_in = (stc["bt"], stc["btc"], stc["dm"], stc["wm"],
                                                           stc["cm"], stc["bigi"], stc["idb"])
    esel_in, itab_in, atab_in, btab_in, selg_in, qaug_in = (stc["esel"], stc["itab"], stc["atab"],
                                                            stc["btab"], stc["selg"], stc["qaug"])
    A = k.sb
    P = k.ps

    def ld_fm(dst_fn, name, t0, t1, w, q="sp"):
        for (ap, lo, hi) in io.fm(name, t0, t1):
            k.dma(q, dst_fn(lo - t0, hi - t0), ap, w=w)

    def ld_tok(dst_fn, name, t0, t1, w, q="sp"):
        for (ap, lo, hi) in io.tok(name, t0, t1):
            k.dma(q, dst_fn((lo - t0) // 128, (hi - t0) // 128),
                  ap.rearrange("(nb p) d -> p nb d", p=128), w=w)

    KS = A("KS", [67, S], BF16); bKS = k.buf("KS")
    VS = A("VS", [128, NB, 65], BF16); bVS = k.buf("VS")
    KCMP = A("KCMP", [67, NCB * 128], BF16); bKCMP = k.buf("KCMP")
    VC = A("VC", [128, NCB, 65], BF16); bVC = k.buf("VC")
    BT = A("BT", [128, 4, 128], F32); bBT = k.buf("BT")
    BTCs = A("BTCs", [128, 4, 128], F32); bBTCs = k.buf("BTCs")
    BTCw = A("BTCw", [128, 4, 8], F32); bBTCw = k.buf("BTCw")
    BTC0 = A("BTC0", [128, 4, 60], F32); bBTC0 = k.buf("BTC0")
    BTCc = A("BTCc", [128, 4, 60], F32); bBTCc = k.buf("BTCc")
    DM = A("DM", [128, 4, 512], BF16); bDM = k.buf("DM")
    WM = A("WM", [128, 4, 512], BF16); bWM = k.buf("WM")
    CM = A("CM", [128, 5, 512], BF16); bCM = k.buf("CM")
    BIGI = A("BIGI", [128, 128], BF16); bBIGI = k.buf("BIGI")
    IDB = A("IDB", [128, 128], F32); bIDB = k.buf("IDB")
    ESEL = A("ESEL", [128, 64, 128], BF16); bESEL = k.buf("ESEL")
    ITAB = A("ITAB", [128, 4, 1024], F32); bITAB = k.buf("ITAB")
    ATAB = A("ATAB", [128, 256], F32); bATAB = k.buf("ATAB")
    BTAB = A("BTAB", [128, 256], F32); bBTAB = k.buf("BTAB")
    SELG = A("SELG", [12, 12, 64], BF16); bSELG = k.buf("SELG")
    ONESB = A("ONESB", [128, 128], BF16); bONESB = k.buf("ONESB")
    ONESF = A("ONESF", [128, 64], F32); bONESF = k.buf("ONESF")
    QT = [A("QT%d" % i, [67, 4, 512], BF16) for i in range(2)]; bQT = k.bufs("QT", 2)
    KW = [A("KW%d" % i, [67, 1024], BF16) for i in range(2)]; bKW = k.bufs("KW", 2)
    VW = [A("VW%d" % i, [128, 8, 65], BF16) for i in range(2)]; bVW = k.bufs("VW", 2)
    GT = [A("GT%d" % i, [12, 512], BF16) for i in range(2)]; bGT = k.bufs("GT", 2)
    SQT = [A("SQT%d" % i, [128, 512], BF16) for i in range(2)]; bSQT = k.bufs("SQT", 2)
    MX = A("MX", [128, 64], F32); bMX = k.buf("MX")
    ST = A("ST", [128, 16], F32); bST = k.buf("ST")
    SX = A("SX", [128, 1024], F32); bSX = k.buf("SX")
    EX = A("EX", [128, 1024], F32); bEX = k.buf("EX")
    PG = A("PG", [128, 1028], F32); bPG = k.buf("PG")
    SC = A("SC", [128, 8], F32); bSC = k.buf("SC")
    IMP = A("IMP", [128, 256], F32); bIMP = k.buf("IMP")
    I2 = A("I2", [128, 256], F32); bI2 = k.buf("I2")
    I3 = A("I3", [128, 256], F32); bI3 = k.buf("I3")
    M8 = A("M8", [128, 16], F32); bM8 = k.buf("M8")
    MQ = A("MQ", [128, 256], F32); bMQ = k.buf("MQ")
    MT = [A("MT%d" % i, [128, 2, 512], BF16) for i in range(2)]; bMT = k.bufs("MT", 2)
    NPT = 6
    PT = [A("PT%d" % i, [128, 512], BF16) for i in range(NPT)]; bPT = k.bufs("PT", NPT)
    RL = A("RL", [128, 512], F32); bRL = k.buf("RL")
    OBF = A("OBF", [64, 512], F32); bOBF = k.buf("OBF")
    TT = A("TT", [64, 512], F32); bTT = k.buf("TT")
    ACC = [A("ACC%d" % i, [64, 512], F32) for i in range(2)]; bACC = k.bufs("ACC", 2)
    OUTB = [A("OUTB%d" % i, [64, 512], BF16) for i in range(2)]; bOUTB = k.bufs("OUTB", 2)
    psS = [P("psS%d" % i, [128, 512], F32) for i in range(4)]; bpsS = k.bufs("psS", 4)
    psO = [P("psO%d" % i, [128, 512], F32) for i in range(2)]; bpsO = k.bufs("psO", 2)
    psA = P("psA", [128, 512], F32); bpsA = k.buf("psA")
    psX = [P("psX0", [128, 512], F32), psS[0]]; bpsX = [k.buf("psX0"), bpsS[0]]

    for (dst, src, b) in [(BT, bt_in, bBT), (BTC0, btc_in, bBTC0), (DM, dm_in, bDM), (WM, wm_in, bWM),
                          (CM, cm_in, bCM), (BIGI, bigi_in, bBIGI), (IDB, idb_in, bIDB),
                          (ITAB, itab_in, bITAB), (ATAB, atab_in, bATAB), (BTAB, btab_in, bBTAB),
                          (SELG, selg_in, bSELG)]:
        k.dma("sp", dst[:], src, w=[b])
    for j0 in range(0, 64, 16):
        k.dma("sp", ESEL[:, j0:j0 + 16, :], esel_in[:, j0:j0 + 16, :], w=[bESEL])
    k.op("dve", lambda e: e.memset(ONESB[:], 1.0), w=[bONESB])
    k.op("dve", lambda e: e.memset(ONESF[:], 1.0), w=[bONESF])
    k.op("dve", lambda e: e.memset(PG[:], 0.0), w=[bPG])
    k.op("dve", lambda e: e.memset(I2[:], -1.0), w=[bI2])
    k.op("dve", lambda e: e.memset(RL[:], 1.0), w=[bRL])
    k.op("pool", lambda e: e.memset(KCMP[:], 0.0), w=[bKCMP])
    k.op("pool", lambda e: e.memset(KCMP[64:67, :], 1.0), w=[bKCMP])
    k.op("pool", lambda e: e.memset(VC[:], 0.0), w=[bVC])
    for s0 in range(0, S, 4096):
        s1 = min(S, s0 + 4096)
        ld_fm(lambda a, b, s0=s0: KS[0:64, s0 + a:s0 + b], "ks", s0, s1, [bKS], "act")
    k.op("pool", lambda e: e.memset(KS[64:67, :], 1.0), w=[bKS])
    for b0 in range(0, NB, 8):
        ld_tok(lambda a, b, b0=b0: VS[:, b0 + a:b0 + b, 0:64], "vs", b0 * 128, (b0 + 8) * 128, [bVS], "act")
    k.op("pool", lambda e: e.memset(VS[:, :, 64:65], 1.0), w=[bVS])
    for i in range(2):
        k.op("pool", lambda e: e.memset(VW[i][:, :, 64:65], 1.0), w=[bVW[i]])
        k.op("pool", lambda e: e.memset(KW[i][64:67, :], 1.0), w=[bKW[i]])
        k.dma("sp", QT[i][64:67, :, :], qaug_in.rearrange("h r t -> r h t"), w=[bQT[i]])

    with nc.sbuf_tensor("KCH", [64, 8208], BF16) as KCH, \
            nc.sbuf_tensor("W1", [64, 32, 256], BF16) as W1, \
            nc.sbuf_tensor("W2", [128, 2, 64], BF16) as W2, \
            nc.sbuf_tensor("PEF", [64, 32], F32) as PEF, \
            nc.sbuf_tensor("PEB", [64, 32], BF16) as PEB, \
            nc.sbuf_tensor("BH", [128, 2], F32) as BH, \
            nc.sbuf_tensor("HX", [128, 512], F32) as HX, \
            nc.sbuf_tensor("H2", [128, 512], F32) as H2, \
            nc.sbuf_tensor("HID", [128, 2, 512], BF16) as HID:
        bKCH = k.buf("KCH"); bW1 = k.buf("W1"); bW2 = k.buf("W2"); bPEF = k.buf("PEF"); bPEB = k.buf("PEB")
        bBH = k.buf("BH"); bHX = k.buf("HX"); bH2 = k.buf("H2"); bHID = k.bufs("HID", 2)
        for which, (src, pe_in, w1_in, w2_in) in enumerate([("kc", pek_in, w1k_in, w2k_in),
                                                           ("vc", pev_in, w1v_in, w2v_in)]):
            w1v_ = w1_in.rearrange("(p d) h -> d p h", d=64)
            for p0 in range(0, 32, 8):
                k.dma("pool", W1[:, p0:p0 + 8, :], w1v_[:, p0:p0 + 8, :], w=[bW1])
            k.dma("pool", W2[:], w2_in.rearrange("(c p) d -> p c d", p=128), w=[bW2])
            k.dma("sp", PEF[:], pe_in, w=[bPEF])
            k.op("dve", lambda e: e.tensor_copy(out=PEB[:], in_=PEF[:]), r=[bPEF], w=[bPEB])
            for hc in range(2):
                for pos in range(32):
                    k.op("pe", lambda e: e.matmul(psX[0][:, hc:hc + 1], lhsT=W1[:, pos, hc * 128:(hc + 1) * 128],
                                                   rhs=PEB[:, pos:pos + 1], start=(pos == 0), stop=(pos == 31)),
                         r=[bW1, bPEB], w=[bpsX[0]])
            k.op("dve", lambda e: e.tensor_copy(out=BH[:], in_=psX[0][:, 0:2]), r=[bpsX[0]], w=[bBH])
            for j0 in range(0, NCMP, 512):
                n = min(512, NCMP - j0)
                t0 = 16 * j0
                t1 = min(S, t0 + 16 * n + 16)
                ld_fm(lambda a, b: KCH[:, a:b], src, t0, t1, [bKCH])
                for hc in range(2):
                    px = psX[1]
                    for pos in range(32):
                        k.op("pe", lambda e: e.matmul(px[:, :n], lhsT=W1[:, pos, hc * 128:(hc + 1) * 128],
                                                       rhs=KCH[:, pos:pos + 16 * (n - 1) + 1:16],
                                                       start=(pos == 0), stop=(pos == 31)),
                             r=[bW1, bKCH], w=[bpsX[1]])
                    k.op("act", lambda e: e.activation(out=HX[:, :n], in_=px[:, :n], func=AF.Identity,
                                                        bias=BH[:, hc:hc + 1], scale=1.0),
                         r=[bpsX[1], bBH], w=[bHX])
                    k.op("dve", lambda e: e.tensor_tensor(out=H2[:, :n], in0=HX[:, :n], in1=HX[:, :n],
                                                           op=ALU.mult), r=[bHX], w=[bH2])
                    k.op("dve", lambda e: e.tensor_scalar(out=H2[:, :n], in0=H2[:, :n], scalar1=0.044715,
                                                           scalar2=1.0, op0=ALU.mult, op1=ALU.add),
                         r=[bH2], w=[bH2])
                    k.op("dve", lambda e: e.tensor_tensor(out=H2[:, :n], in0=H2[:, :n], in1=HX[:, :n],
                                                           op=ALU.mult), r=[bH2, bHX], w=[bH2])
                    k.op("act", lambda e: e.activation(out=H2[:, :n], in_=H2[:, :n], func=AF.Tanh,
                                                        scale=0.7978845608028654), r=[bH2], w=[bH2])
                    k.op("dve", lambda e: e.tensor_scalar(out=H2[:, :n], in0=H2[:, :n], scalar1=0.5,
                                                           scalar2=0.5, op0=ALU.mult, op1=ALU.add),
                         r=[bH2], w=[bH2])
                    k.op("dve", lambda e: e.tensor_tensor(out=HID[:, hc, :n], in0=H2[:, :n], in1=HX[:, :n],
                                                           op=ALU.mult), r=[bH2, bHX], w=[bHID[hc]])
                if which == 0:
                    for hc in range(2):
                        k.op("pe", lambda e: e.matmul(psX[0][0:64, :n], lhsT=W2[:, hc, :], rhs=HID[:, hc, :n],
                                                       start=(hc == 0), stop=(hc == 1)),
                             r=[bW2, bHID[hc]], w=[bpsX[0]])
                    k.op("act", lambda e: e.copy(out=KCMP[0:64, j0:j0 + n], in_=psX[0][0:64, :n]),
                         r=[bpsX[0]], w=[bKCMP])
                else:
                    for jb in range((n + 127) // 128):
                        m = min(128, n - jb * 128)
                        for hc in range(2):
                            k.op("pe", lambda e: e.matmul(psX[0][0:m, 0:64], lhsT=HID[:, hc, jb * 128:jb * 128 + m],
                                                           rhs=W2[:, hc, :], start=(hc == 0), stop=(hc == 1)),
                                 r=[bW2, bHID[hc]], w=[bpsX[0]])
                        gb = j0 // 128 + jb
                        k.op("act", lambda e: e.copy(out=VC[0:m, gb, 0:64], in_=psX[0][0:m, 0:64]),
                             r=[bpsX[0]], w=[bVC])
                        k.op("pool", lambda e: e.memset(VC[0:m, gb, 64:65], 1.0), w=[bVC])

    st = {"q": 0, "kw": 0, "ps": 0, "pt": 0, "out": 0}
    NT5 = S // 512

    def load_q(I):
        qb = st["q"] % 2
        st["q"] += 1
        for p_ in range(4):
            ld_fm(lambda a, b, p_=p_: QT[qb][0:64, p_, a:b], "q%d" % p_, I * 512, (I + 1) * 512, [bQT[qb]])
        return qb

    ncw = (NCB * 128 + 511) // 512
    emit_sqmax(k, nc, lambda i: ([bKCMP], KCMP[0:64, i * 512:min(NCB * 128, (i + 1) * 512)]), ncw,
               SQT, bSQT, ONESB, bONESB, psS[1:3], bpsS[1:3], MX, bMX, ST[:, 4:5], bST)
    emit_sqmax(k, nc, lambda i: ([bKS], KS[0:64, i * 512:(i + 1) * 512]), NT5,
               SQT, bSQT, ONESB, bONESB, psS[1:3], bpsS[1:3], MX, bMX, ST[:, 5:6], bST)

    def fetch_kw(i):
        wb = st["kw"] % 2
        st["kw"] += 1
        ld_fm(lambda a, b: KW[wb][0:64, a:b], "kw", i * 512, (i + 1) * 512, [bKW[wb]])
        return [bKW[wb]], KW[wb][0:64, 0:512]
    emit_sqmax(k, nc, fetch_kw, NT5, SQT, bSQT, ONESB, bONESB, psS[1:3], bpsS[1:3], MX, bMX, ST[:, 6:7], bST)
    MXQ = A("MXQ", [128, 4, 32], F32); bMXQ = k.buf("MXQ")
    it_ = 0
    qb_nx = load_q(0)
    for i in range(NT5):
        qb_ = qb_nx
        if i + 1 < NT5:
            qb_nx = load_q(i + 1)
        for p in range(4):
            a = it_ % 2
            it_ += 1
            k.op("dve", lambda e: e.tensor_tensor(out=SQT[a][0:64, :], in0=QT[qb_][0:64, p, :],
                                                   in1=QT[qb_][0:64, p, :], op=ALU.mult),
                 r=[bQT[qb_]], w=[bSQT[a]])
            k.op("pe", lambda e: e.matmul(psS[1 + a][:], lhsT=ONESB[0:64, :], rhs=SQT[a][0:64, :],
                                           start=True, stop=True), r=[bSQT[a], bONESB], w=[bpsS[1 + a]])
            k.op("dve", lambda e: e.reduce_max(out=MXQ[:, p, i:i + 1], in_=psS[1 + a][:], axis=AX.X),
                 r=[bpsS[1 + a]], w=[bMXQ])
    for p in range(4):
        k.op("dve", lambda e: e.reduce_max(out=ST[:, p:p + 1], in_=MXQ[:, p, 0:NT5], axis=AX.X),
             r=[bMXQ], w=[bST])
    for p in range(4):
        k.op("dve", lambda e: e.tensor_scalar(out=ST[:, 8:11], in0=ST[:, 4:7], scalar1=ST[:, p:p + 1],
                                               scalar2=None, op0=ALU.mult), r=[bST], w=[bST])
        k.op("act", lambda e: e.activation(out=ST[:, 8:11], in_=ST[:, 8:11], func=AF.Sqrt), r=[bST], w=[bST])
        k.op("dve", lambda e: e.tensor_scalar(out=BTCc[:, p, :], in0=BTC0[:, p, :], scalar1=ST[:, 8:9],
                                               scalar2=None, op0=ALU.subtract), r=[bST, bBTC0], w=[bBTCc])
        k.op("dve", lambda e: e.tensor_scalar(out=BTCs[:, p, :], in0=BT[:, p, :], scalar1=ST[:, 9:10],
                                               scalar2=None, op0=ALU.subtract), r=[bST, bBT], w=[bBTCs])
        k.op("dve", lambda e: e.tensor_scalar(out=BTCw[:, p, :], in0=BT[:, p, 0:8], scalar1=ST[:, 10:11],
                                               scalar2=None, op0=ALU.subtract), r=[bST, bBT], w=[bBTCw])


    def importance_block(I, qb, mb, qi):
        for u in importance_units(I, qb, mb, qi):
            u()

    def importance_units(I, qb, mb, qi):
        return [lambda p=p: importance_head(I, qb, qi, p) for p in range(4)] + [lambda: importance_topk(I, mb, qi)]

    def importance_head(I, qb, qi, p):
        i = 4 * I + qi
        ncols = min(8 * (i + 1), NCB * 128)
        nb = 2 * (i + 1)
        if True:
            io_ = 1016 - 8 * i
            for c0 in range(0, ncols, 512):
                c1 = min(ncols, c0 + 512)
                k.op("pe", lambda e: e.matmul(psA[:, 0:c1 - c0], lhsT=QT[qb][0:64, p, qi * 128:(qi + 1) * 128],
                                               rhs=KCMP[0:64, c0:c1], start=True, stop=True),
                     r=[bQT[qb], bKCMP], w=[bpsA])
                k.op("dve", lambda e: e.tensor_tensor(out=SX[:, c0:c1], in0=psA[:, 0:c1 - c0],
                                                       in1=ITAB[:, p, io_ + c0:io_ + c1], op=ALU.add),
                     r=[bpsA, bITAB], w=[bSX])
            k.op("dve", lambda e: e.reduce_max(out=SC[:, 0:1], in_=SX[:, :ncols], axis=AX.X),
                 r=[bSX], w=[bSC])
            k.op("dve", lambda e: e.tensor_scalar(out=SC[:, 1:2], in0=SC[:, 0:1], scalar1=-1e20,
                                                   scalar2=-1.0, op0=ALU.max, op1=ALU.mult),
                 r=[bSC], w=[bSC])
            k.op("act", lambda e: e.activation(out=EX[:, :ncols], in_=SX[:, :ncols], func=AF.Exp,
                                                bias=SC[:, 1:2], scale=1.0, accum_out=SC[:, 2:3]),
                 r=[bSX, bSC], w=[bEX, bSC])
            k.op("dve", lambda e: e.tensor_scalar(out=SC[:, 3:4], in0=SC[:, 2:3], scalar1=1e-30,
                                                   scalar2=None, op0=ALU.max), r=[bSC], w=[bSC])
            k.op("dve", lambda e: e.reciprocal(out=SC[:, 3:4], in_=SC[:, 3:4]), r=[bSC], w=[bSC])
            if p == 0:
                k.op("dve", lambda e: e.tensor_scalar(out=PG[:, 1:1 + ncols], in0=EX[:, :ncols],
                                                       scalar1=SC[:, 3:4], scalar2=None, op0=ALU.mult),
                     r=[bEX, bSC], w=[bPG])
            else:
                k.op("dve", lambda e: e.scalar_tensor_tensor(out=PG[:, 1:1 + ncols], in0=EX[:, :ncols],
                                                              scalar=SC[:, 3:4], in1=PG[:, 1:1 + ncols],
                                                              op0=ALU.mult, op1=ALU.add),
                     r=[bEX, bSC, bPG], w=[bPG])

    def importance_topk(I, mb, qi):
        i = 4 * I + qi
        nb = 2 * (i + 1)
        k.op("dve", lambda e: e.reduce_sum(out=IMP[:, :nb],
                                            in_=PG[:, 0:4 * nb].rearrange("p (n r) -> p n r", r=4),
                                            axis=AX.X), r=[bPG], w=[bIMP])
        k.op("dve", lambda e: e.tensor_tensor(out=IMP[:, :nb], in0=IMP[:, :nb],
                                               in1=PG[:, 4:4 * nb + 1:4], op=ALU.add),
             r=[bPG, bIMP], w=[bIMP])
        k.op("dve", lambda e: e.tensor_tensor(out=I2[:, :nb], in0=IMP[:, :nb], in1=ATAB[:, 256 - nb:256],
                                               op=ALU.mult), r=[bIMP, bATAB], w=[bI2])
        k.op("dve", lambda e: e.tensor_tensor(out=I2[:, :nb], in0=I2[:, :nb], in1=BTAB[:, 256 - nb:256],
                                               op=ALU.add), r=[bI2, bBTAB], w=[bI2])
        k.op("dve", lambda e: e.memset(I2[:, 0:1], 1e6), w=[bI2])
        k.op("dve", lambda e: e.max(out=M8[:, 0:8], in_=I2[:]), r=[bI2], w=[bM8])
        k.op("dve", lambda e: e.match_replace(out=I3[:], in_to_replace=M8[:, 0:8], in_values=I2[:],
                                               imm_value=-2.0), r=[bI2, bM8], w=[bI3])
        k.op("dve", lambda e: e.max(out=M8[:, 8:16], in_=I3[:]), r=[bI3], w=[bM8])
        k.op("dve", lambda e: e.tensor_scalar(out=MQ[:], in0=I2[:], scalar1=M8[:, 15:16], scalar2=1.0,
                                               op0=ALU.is_ge, op1=ALU.subtract), r=[bI2, bM8], w=[bMQ])
        nch = 2 if nb > 128 else 1
        for ch in range(nch):
            k.op("pe", lambda e: e.transpose(out=psX[0][:, 0:128], in_=MQ[:, ch * 128:(ch + 1) * 128],
                                              identity=IDB[:]), r=[bMQ, bIDB], w=[bpsX[0]])
            k.op("act", lambda e: e.copy(out=MT[mb][:, ch, qi * 128:(qi + 1) * 128], in_=psX[0][:, 0:128]),
                 r=[bpsX[0]], w=[bMT[mb]])

    def branch(I, qb, heads, br, steps, bias_fn, first, hook=None):
        nst = len(steps)
        staged = []

        def stage_a(n_):
            lk, kb, extra, bidx, vl, vb = steps[n_]
            pts = []
            for hi_, p in enumerate(heads):
                ps = st["ps"] % 4
                st["ps"] += 1
                pts.append((ps, None))
            for hi_, p in enumerate(heads):
                ps = pts[hi_][0]
                k.op("pe", lambda e: e.matmul(psS[ps][:], lhsT=lk, rhs=QT[qb][:, p, :], start=True,
                                               stop=(len(extra) == 0)), r=kb + [bQT[qb]], w=[bpsS[ps]])
            for xi, (xl, xr, xb) in enumerate(extra):
                for hi_, p in enumerate(heads):
                    ps = pts[hi_][0]
                    k.op("pe", lambda e: e.matmul(psS[ps][:], lhsT=xl, rhs=xr, start=False,
                                                   stop=(xi == len(extra) - 1)), r=xb, w=[bpsS[ps]])
            out = []
            for hi_, p in enumerate(heads):
                ps = pts[hi_][0]
                pt = st["pt"] % NPT
                st["pt"] += 1
                bias, bbuf = bias_fn(p, bidx)
                k.op("act", lambda e: e.activation(out=PT[pt][:], in_=psS[ps][:], func=AF.Exp, bias=bias,
                                                    scale=1.0), r=[bpsS[ps], bbuf], w=[bPT[pt]])
                out.append(pt)
            return out

        def stage_b(n_, pts):
            lk, kb, extra, bidx, vl, vb = steps[n_]
            for hi_, p in enumerate(heads):
                k.op("pe", lambda e: e.matmul(psO[hi_][0:65, :], lhsT=vl, rhs=PT[pts[hi_]][:], start=(n_ == 0),
                                               stop=(n_ == nst - 1)), r=[vb, bPT[pts[hi_]]], w=[bpsO[hi_]])

        for n_ in range(nst):
            staged.append((n_, stage_a(n_)))
            if n_ == min(1, nst - 1):
                while deferred:
                    deferred.pop(0)()
            if hook is not None:
                hook(n_, nst)
            if len(staged) > 1:
                stage_b(*staged.pop(0))
        while staged:
            stage_b(*staged.pop(0))
        deferred.append(lambda: epilogue(qb, heads, br, first))

    def epilogue(qb, heads, br, first):
        for hi_, p in enumerate(heads):
            po = psO[hi_]
            k.op("dve", lambda e: e.tensor_scalar(out=RL[64:65, :], in0=po[64:65, :], scalar1=1e-30, scalar2=None,
                                                   op0=ALU.max), r=[bpsO[hi_]], w=[bRL])
            k.op("dve", lambda e: e.reciprocal(out=RL[64:65, :], in_=RL[64:65, :]), r=[bRL], w=[bRL])
            k.op("act", lambda e: e.copy(out=OBF[:], in_=po[0:64, :]), r=[bpsO[hi_]], w=[bOBF])
            k.op("pe", lambda e: e.matmul(psX[0][0:64, :], lhsT=ONESF[64:65, :], rhs=RL[64:65, :], start=True,
                                           stop=True), r=[bONESF, bRL], w=[bpsX[0]])
            k.op("dve", lambda e: e.tensor_tensor(out=TT[:], in0=OBF[:], in1=psX[0][0:64, :], op=ALU.mult),
                 r=[bOBF, bpsX[0]], w=[bTT])
            gr = p * 3 + br
            k.op("pe", lambda e: e.matmul(psX[0][0:64, :], lhsT=SELG[:, gr, :], rhs=GT[qb][:, :], start=True,
                                           stop=True), r=[bSELG, bGT[qb]], w=[bpsX[0]])
            if first:
                k.op("dve", lambda e: e.tensor_tensor(out=ACC[hi_][:], in0=TT[:], in1=psX[0][0:64, :], op=ALU.mult),
                     r=[bTT, bpsX[0]], w=[bACC[hi_]])
            else:
                k.op("dve", lambda e: e.tensor_tensor(out=TT[:], in0=TT[:], in1=psX[0][0:64, :], op=ALU.mult),
                     r=[bTT, bpsX[0]], w=[bTT])
                k.op("dve", lambda e: e.tensor_tensor(out=ACC[hi_][:], in0=ACC[hi_][:], in1=TT[:], op=ALU.add),
                     r=[bTT, bACC[hi_]], w=[bACC[hi_]])

    def load_tile(I):
        qb = load_q(I)
        ld_fm(lambda a, b: GT[qb][:, a:b], "gt", I * 512, (I + 1) * 512, [bGT[qb]])
        wb = I % 2
        jlo = max(0, 4 * I - 4)
        lo = jlo - (4 * I - 4)
        ld_fm(lambda a, b: KW[wb][0:64, lo * 128 + a:lo * 128 + b], "kw", jlo * 128, (4 * I + 4) * 128,
              [bKW[wb]])
        ld_tok(lambda a, b: VW[wb][:, lo + a:lo + b, 0:64], "vw", jlo * 128, (4 * I + 4) * 128, [bVW[wb]])
        return qb

    if getattr(io, "after_prologue", None) is not None:
        io.after_prologue()
    deferred = []
    qb_next = load_tile(0)
    for qi in range(4):
        importance_block(0, qb_next, 0, qi)
    for I in range(NQ):
        qb = qb_next
        mb = I % 2
        wb = I % 2
        jlo = max(0, 4 * I - 4)
        pend = []
        while deferred:
            deferred.pop(0)()
        if I + 1 < NQ:
            qb_next = load_tile(I + 1)
            for qi in range(4):
                pend += importance_units(I + 1, qb_next, (I + 1) % 2, qi)
        gap = max(1, (2 * (4 * I + 4)) // 22)
        half = [10]

        def hook(n_, nst):
            if pend and half[0] > 0 and n_ >= 1 and (n_ - 1) % gap == 0:
                pend.pop(0)()
                half[0] -= 1

        for hp in range(2):
            heads = (2 * hp, 2 * hp + 1)
            steps = []
            for jb in range(NCB):
                dd = I - 4 * jb
                if dd < 0:
                    continue
                extra = []
                if dd <= 4:
                    extra.append((BIGI[:], CM[:, dd, :], [bBIGI, bCM]))
                steps.append((KCMP[:, jb * 128:(jb + 1) * 128], [bKCMP], extra, dd + 28, VC[:, jb, :], bVC))
            branch(I, qb, heads, 0, steps, lambda p, ix: (BTCc[:, p, ix:ix + 1], bBTCc), True)
            steps = []
            for jb in range(4 * I + 4):
                extra = [(ESEL[:, jb % 64, :], MT[mb][:, jb // 64, :], [bESEL, bMT[mb]])]
                if jb >= 4 * I:
                    extra.append((BIGI[:], DM[:, jb - 4 * I, :], [bBIGI, bDM]))
                steps.append((KS[:, jb * 128:(jb + 1) * 128], [bKS], extra, 4 * I - jb + 3, VS[:, jb, :], bVS))
            half[0] = 10
            branch(I, qb, heads, 1, steps, lambda p, ix: (BTCs[:, p, ix:ix + 1], bBTCs), False, hook)
            steps = []
            for jb in range(jlo, 4 * I + 4):
                lw = jb - (4 * I - 4)
                if lw < 4:
                    extra = [(BIGI[:], WM[:, lw, :], [bBIGI, bWM])]
                else:
                    extra = [(BIGI[:], DM[:, lw - 4, :], [bBIGI, bDM])]
                steps.append((KW[wb][:, lw * 128:(lw + 1) * 128], [bKW[wb]], extra, 4 * I - jb + 3,
                              VW[wb][:, lw, :], bVW[wb]))
            branch(I, qb, heads, 2, steps, lambda p, ix: (BTCw[:, p, ix:ix + 1], bBTCw), False)
            def emit_out(I=I, hp=hp, heads=heads):
                n0 = len(k.stores)
                for hi_, p in enumerate(heads):
                    ob = st["out"] % 2
                    st["out"] += 1
                    k.op("act", lambda e: e.copy(out=OUTB[ob][:], in_=ACC[hi_][:]), r=[bACC[hi_]], w=[bOUTB[ob]])
                    k.store("sp", io.out(p, I), OUTB[ob][:], r=[bOUTB[ob]])
                if getattr(io, "out_done", None) is not None:
                    io.out_done(I, hp, k.stores[n0:])
            deferred.append(emit_out)
        while pend:
            pend.pop(0)()
    while deferred:
        deferred.pop(0)()


def k2a_consts(S):
    sl16 = alibi_slopes(16)
    A_, B_ = ab_tables()
    c = {"dm": dm_table(), "wm": wm_table(), "cm": cm_table(), "bigi": bigi_table(),
         "idb": np.eye(128).astype(np.float32), "esel": esel_table(), "atab": A_, "btab": B_,
         "selg": selg_table()}
    per_g = []
    for g in range(4):
        ms = sl16[g * 4:(g + 1) * 4]
        per_g.append({
            "bt": np.ascontiguousarray(np.stack([bt_table(m) for m in ms], 1)),
            "btc": np.ascontiguousarray(np.stack([btc_table(m) for m in ms], 1)),
            "itab": np.ascontiguousarray(np.stack([itab_table(m) for m in ms], 1)),
            "qaug": np.stack([q_aug_rows(m, 512) for m in ms], 0),
        })
    return c, per_g


def prep_k2a(pT, vt, g, consts, per_g, wts, S):
    d = dict(consts)
    d.update(per_g[g])
    d.update(wts)
    d["qa"] = np.ascontiguousarray(pT[g * 256:(g + 1) * 256].reshape(4, 64, S))
    d["kca"] = np.ascontiguousarray(pT[1024 + g * 64:1024 + (g + 1) * 64])
    d["vca"] = np.ascontiguousarray(pT[1280 + g * 64:1280 + (g + 1) * 64])
    d["ksa"] = np.ascontiguousarray(pT[1536 + g * 64:1536 + (g + 1) * 64])
    d["kwa"] = np.ascontiguousarray(pT[2048 + g * 64:2048 + (g + 1) * 64])
    d["vs"] = np.ascontiguousarray(vt[:, g * 64:(g + 1) * 64])
    d["vw"] = np.ascontiguousarray(vt[:, 256 + g * 64:256 + (g + 1) * 64])
    d["gt"] = np.ascontiguousarray(pT[2560 + g * 12:2560 + (g + 1) * 12])
    return d


def k2a_weights(pek, w1k, w2k, pev, w1v, w2v):
    f = lambda a: np.ascontiguousarray(np.asarray(a, np.float32))
    return {"pek": f(np.asarray(pek).T), "pev": f(np.asarray(pev).T), "w1k": f(w1k), "w2k": f(w2k),
            "w1v": f(w1v), "w2v": f(w2v)}


TPC = BATCH * SEQ // NCORES
CPB = NCORES // BATCH


def _run(nc, in_maps):
    res = run_bass_kernel_spmd(nc, in_maps, core_ids=list(range(NCORES)))
    return res.results


def _with_halo(full_T, c):
    b, j = divmod(c, CPB)
    a = full_T[b]
    out = np.zeros((a.shape[0], 2 + TPC), a.dtype)
    lo = j * TPC
    if j > 0:
        out[:, 0:2] = a[:, lo - 2:lo]
    out[:, 2:] = a[:, lo:lo + TPC]
    return out


def kernel_unfused(x, norm_mix_g, norm_ffn_g, final_norm_g,
           nsa_w_in, nsa_cmp_k_pe, nsa_cmp_k_w1, nsa_cmp_k_w2,
           nsa_cmp_v_pe, nsa_cmp_v_w1, nsa_cmp_v_w2, nsa_w_out,
           diff_w_in, diff_lam_q1, diff_lam_k1, diff_lam_q2, diff_lam_k2,
           diff_subln_g, diff_w_out,
           ffn_w_up, ffn_conv_w, ffn_conv_b, ffn_w_down):
    f32 = lambda a: np.ascontiguousarray(np.asarray(a, dtype=np.float32))
    x = f32(x)
    S = SEQ
    xT = [np.ascontiguousarray(x[b].T) for b in range(BATCH)]

    def tok_shards(full_T):
        return [np.ascontiguousarray(full_T[c // CPB][:, (c % CPB) * TPC:(c % CPB + 1) * TPC])
                for c in range(NCORES)]

    fm = [(0, 1024, 0.125, AF.Copy), (1024, 2560, 1.0, AF.Copy), (2560, 2608, 1.0, AF.Sigmoid)]
    tok = [(1792, 2048), (2304, 2560)]
    nc1 = build_k1(TPC, 2608, fm, tok)
    gl = g_layout(norm_mix_g[0])
    w = f32(nsa_w_in[0])
    r1 = _run(nc1, [{"xT": s, "g": gl, "w": w} for s in tok_shards(xT)])
    pT = [np.concatenate([r1[b * CPB + j]["projT"] for j in range(CPB)], axis=1) for b in range(BATCH)]
    vt = [np.concatenate([r1[b * CPB + j]["vtok"] for j in range(CPB)], axis=0) for b in range(BATCH)]
    del r1
    consts, per_g = k2a_consts(S)
    wts = k2a_weights(nsa_cmp_k_pe[0], nsa_cmp_k_w1[0], nsa_cmp_k_w2[0],
                      nsa_cmp_v_pe[0], nsa_cmp_v_w1[0], nsa_cmp_v_w2[0])
    nc2 = build_k2a(S)
    r2 = _run(nc2, [prep_k2a(pT[c // CPB], vt[c // CPB], c % CPB, consts, per_g, wts, S)
                    for c in range(NCORES)])
    aT = [np.concatenate([r2[b * CPB + g]["oT"] for g in range(CPB)], axis=0) for b in range(BATCH)]
    del r2, pT, vt
    cw, cb = conv_layouts(ffn_conv_w[0], ffn_conv_b[0])
    nc3 = build_k3(TPC, False)
    ins = [{"xT": _with_halo(xT, c), "aT": _with_halo(aT, c), "wo": f32(nsa_w_out[0]),
            "wu": f32(ffn_w_up[0]), "wd": f32(ffn_w_down[0]), "cw": cw, "cb": cb,
            "g": g_layout(norm_ffn_g[0]), "gf": g_layout(final_norm_g)} for c in range(NCORES)]
    r3 = _run(nc3, ins)
    xT = [np.concatenate([r3[b * CPB + j]["yT"] for j in range(CPB)], axis=1) for b in range(BATCH)]
    del r3, ins, aT

    lambda_init = 0.8 - 0.6 * float(np.exp(-0.3 * 1))
    fm = [(0, 1024, 0.125, AF.Copy), (1024, 2048, 1.0, AF.Copy)]
    tok = [(2048, 2560), (2560, 3072)]
    nc4 = build_k1(TPC, 3072, fm, tok)
    gl = g_layout(norm_mix_g[1])
    w = f32(diff_w_in[0])
    r4 = _run(nc4, [{"xT": s, "g": gl, "w": w} for s in tok_shards(xT)])
    pT = [np.concatenate([r4[b * CPB + j]["projT"] for j in range(CPB)], axis=1) for b in range(BATCH)]
    vt = [np.concatenate([r4[b * CPB + j]["vtok"] for j in range(CPB)], axis=0) for b in range(BATCH)]
    del r4
    sl8 = alibi_slopes(8)
    dmt, bigit = dm_table(), bigi_table()
    lam = np.stack([f32(diff_lam_q1[0]), f32(diff_lam_k1[0]), f32(diff_lam_q2[0]), f32(diff_lam_k2[0])], 0)
    lam = np.ascontiguousarray(np.broadcast_to(lam[None], (128, 4, 64)))
    sg = f32(diff_subln_g[0]).reshape(128, 1)
    ins = []
    for c in range(NCORES):
        b, hp = divmod(c, CPB)
        qa = np.ascontiguousarray(pT[b][hp * 256:(hp + 1) * 256].reshape(4, 64, S))
        ka = np.ascontiguousarray(pT[b][1024 + hp * 256:1024 + (hp + 1) * 256].reshape(4, 64, S))
        bt = np.stack([bt_table(sl8[hp * 2 + hh]) for hh in range(2)], 0)
        qaug = np.stack([q_aug_rows(sl8[hp * 2 + hh], 512) for hh in range(2)], 0)
        ins.append({"qa": qa, "ka": ka, "qaug": qaug,
                    "v": np.ascontiguousarray(vt[b][:, hp * 256:(hp + 1) * 256]),
                    "bt": bt, "dm": dmt, "bigi": bigit, "lam": lam, "sg": sg})
    nc5 = build_k2b(S, lambda_init)
    r5 = _run(nc5, ins)
    aT = [np.concatenate([r5[b * CPB + hp]["oT"] for hp in range(CPB)], axis=0) for b in range(BATCH)]
    del r5, ins, pT, vt
    cw, cb = conv_layouts(ffn_conv_w[1], ffn_conv_b[1])
    nc6 = build_k3(TPC, True)
    ins = [{"xT": _with_halo(xT, c), "aT": _with_halo(aT, c), "wo": f32(diff_w_out[0]),
            "wu": f32(ffn_w_up[1]), "wd": f32(ffn_w_down[1]), "cw": cw, "cb": cb,
            "g": g_layout(norm_ffn_g[1]), "gf": g_layout(final_norm_g)} for c in range(NCORES)]
    r6 = _run(nc6, ins)
    out = np.empty((BATCH, SEQ, D_MODEL), np.float32)
    for c in range(NCORES):
        b, j = divmod(c, CPB)
        out[b, j * TPC:(j + 1) * TPC, :] = r6[c]["yT"].T
    return out

TPC = BATCH * SEQ // NCORES
CPB = NCORES // BATCH
GROUPS = [[0, 1, 2, 3], [4, 5, 6, 7]]
RB1 = 640
RB4 = 512


def _chunks(t0, t1):
    j = t0 // TPC
    while j * TPC < t1:
        lo, hi = max(t0, j * TPC), min(t1, (j + 1) * TPC)
        yield j, lo, hi
        j += 1


class IOK2aFused:
    ROW = {"q0": 0, "q1": 64, "q2": 128, "q3": 192, "kc": 256, "vc": 320, "ks": 384, "kw": 448, "gt": 512}

    def __init__(self, L1F, L1T, SND2, st, k=None, G2=None):
        self.WF, self.WT, self.SND2, self.st = L1F, L1T, SND2, st
        self.k, self.G2, self.pend = k, G2, {}

    def fm(self, name, t0, t1):
        row0 = self.ROW[name]
        nr = 12 if name == "gt" else 64
        rr = lambda j: ((row0 // 128) * 4 + j) * 128 + row0 % 128
        return [(self.WF[rr(j):rr(j) + nr, lo - j * TPC:hi - j * TPC], lo, hi)
                for (j, lo, hi) in _chunks(t0, t1)]

    def tok(self, name, t0, t1):
        c0 = 0 if name == "vs" else 64
        return [(self.WT[lo:hi, c0:c0 + 64], lo, hi) for (j, lo, hi) in _chunks(t0, t1)]

    def out(self, p, I):
        j, i8 = divmod(I, 8)
        return self.SND2[j * 256 + p * 64:j * 256 + (p + 1) * 64, i8 * 512:(i8 + 1) * 512]

    def out_done(self, I, hp, toks):
        self.pend.setdefault(hp, []).extend(toks)
        if I % 8 == 7:
            i = (I // 8) * 2 + hp
            self.k.collective_async(self.SND2[i * 128:(i + 1) * 128, :], self.G2[i * 512:(i + 1) * 512, :],
                                    GROUPS, self.pend.pop(hp))


class IOK2bFused:
    def __init__(self, L4F, L4T, SND5, d, k=None, G5=None):
        self.WF, self.WT, self.SND5 = L4F, L4T, SND5
        self.ctx, self.G5, self.pend = k, G5, {}
        self.qaug, self.bt_in, self.dm_in, self.bigi_in, self.lam_in, self.sg_in = (
            d["b_qaug"], d["b_bt"], d["a_dm"], d["a_bigi"], d["b_lam"], d["b_sg"])

    def _fm(self, row0, t0, t1):
        rr = lambda j: ((row0 // 128) * 4 + j) * 128 + row0 % 128
        return [(self.WF[rr(j):rr(j) + 64, lo - j * TPC:hi - j * TPC], lo, hi)
                for (j, lo, hi) in _chunks(t0, t1)]

    def q(self, hh, c, t0, t1):
        return self._fm((hh * 2 + c) * 64, t0, t1)

    def k(self, hh, c, t0, t1):
        return self._fm(256 + (hh * 2 + c) * 64, t0, t1)

    def v(self, hh, t0, t1):
        out = []
        for (j, lo, hi) in _chunks(t0, t1):
            a = lo
            while a < hi:
                tl = a - j * TPC
                b = min(hi, j * TPC + (tl // 2048 + 1) * 2048)
                row = ((tl // 2048) * 4 + j) * 2048 + tl % 2048
                out.append((self.WT[row:row + (b - a), hh * 128:(hh + 1) * 128], a, b))
                a = b
        return out

    def out(self, hh, I):
        j, i8 = divmod(I, 8)
        return self.SND5[j * 256 + hh * 128:j * 256 + (hh + 1) * 128, i8 * 512:(i8 + 1) * 512]

    def out_done(self, I, hh, toks):
        self.pend.setdefault(hh, []).extend(toks)
        if I % 8 == 7:
            i = (I // 8) * 2 + hh
            self.ctx.collective_async(self.SND5[i * 128:(i + 1) * 128, :], self.G5[i * 512:(i + 1) * 512, :],
                                    GROUPS, self.pend.pop(hh))


class IOK3Fused:
    def __init__(self, layer, d, xh0, X2, LA, LHA, LH, SND3, yT, SNDH=None, k=None, GH=None):
        self.layer, self.xh0, self.X2, self.SND3, self.yT = layer, xh0, X2, SND3, yT
        self.WA = LA.rearrange("(h g p) t -> h p g t", h=2, g=4, p=128)
        self.WHA = LHA.rearrange("(h g p) t -> h p g t", h=2, g=4, p=128)
        self.WH = LH.rearrange("(dc p) t -> p dc t", p=128)
        sfx = str(layer)
        self.wo_in, self.wu_in, self.wd_in = d["wo" + sfx], d["wu" + sfx], d["wd" + sfx]
        self.cw_in, self.cb_in, self.g_in, self.gf_in = d["cw" + sfx], d["cb" + sfx], d["g_ffn" + sfx], d["g_fin"]
        self.halo_scale = d["m0"]
        self.tail_dst = (lambda oc: SND3[oc * 128:(oc + 1) * 128, 0:2]) if layer == 0 else None
        if layer == 0:
            self.gf_in = d["g_mix1"]
            self.h_dst = lambda oc, tcol, n: SNDH[(tcol // 512) * 1024 + oc * 128:(tcol // 512) * 1024 + (oc + 1) * 128,
                                                  tcol % 512:tcol % 512 + n]
            self.k, self.GH, self.SNDH, self.pend = k, GH, SNDH, []

    def tile_done(self, it, toks, N=256):
        if self.layer != 0:
            return
        self.pend.extend(toks)
        per = 512 // N
        if it % per == per - 1:
            c = it // per
            self.k.collective_async(self.SNDH[c * 1024:(c + 1) * 1024, :], self.GH[c * 4096:(c + 1) * 4096, :],
                                    GROUPS, self.pend)
            self.pend = []

    def x_src(self, col0, n):
        if self.layer == 0:
            return self.xh0.rearrange("(dc p) t -> p dc t", p=128)[:, :, col0:col0 + n]
        if col0 == 0:
            assert n == 2
            return self.WH
        return self.X2.rearrange("(dc p) t -> p dc t", p=128)[:, :, col0 - 2:col0 - 2 + n]

    def a_src(self, col0, n):
        if col0 == 0:
            assert n == 2
            return [(h, 2, self.WHA[h]) for h in range(2)]
        return [(h, 2, self.WA[h][:, :, col0 - 2:col0 - 2 + n]) for h in range(2)]

    def y_dst(self, oc, tcol, n):
        dst = self.X2 if self.layer == 0 else self.yT
        return dst[oc * 128:(oc + 1) * 128, tcol:tcol + n]

    def x1_dst(self, dc, tcol, n):
        return self.X1D[dc * 128:(dc + 1) * 128, tcol:tcol + n]

    def x1_src(self, tcol, n):
        return self.X1D.rearrange("(dc p) t -> p dc t", p=128)[:, :, tcol:tcol + n]

    def aff_dst(self, gch, tcol, n):
        return self.AFFD[gch * 128:(gch + 1) * 128, tcol:tcol + n]

    def aff_src(self, tcol, n):
        return self.AFFD.rearrange("(kc p) t -> p kc t", p=128)[:, :, tcol:tcol + n]


FUSED_INPUTS = [("xh0", [D_MODEL, 2 + TPC], F32), ("w_in0g", [D_MODEL, 652], F32), ("w_in1g", [D_MODEL, 768], F32),
                ("g_mix0", [128, 8], F32), ("g_mix1", [128, 8], F32), ("g_ffn0", [128, 8], F32),
                ("g_ffn1", [128, 8], F32), ("g_fin", [128, 8], F32), ("m0", [128, 1], F32),
                ("b_qaug", [2, 3, 512], BF16), ("b_bt", [2, 128, 128], F32), ("b_lam", [128, 4, 64], F32),
                ("b_sg", [128, 1], F32)]
for _l in range(2):
    FUSED_INPUTS += [("wo%d" % _l, [D_MODEL, D_MODEL], F32), ("wu%d" % _l, [D_MODEL, 2 * D_FF], F32),
                     ("wd%d" % _l, [D_FF, D_MODEL], F32), ("cw%d" % _l, [128, 44, 3], F32),
                     ("cb%d" % _l, [128, 44], F32)]
FUSED_INPUTS += [("a_" + n, sh, dt) for (n, sh, dt) in K2A_STATIC]


def build_fused(lambda_init, upto=None):
    nc = new_nc()
    S, T = SEQ, TPC
    d = {n: nc.dram_tensor(n, sh, dt, kind="ExternalInput").ap() for (n, sh, dt) in FUSED_INPUTS}
    yT = nc.dram_tensor("yT", [D_MODEL, T], F32, kind="ExternalOutput").ap()
    sc = lambda n, sh, dt: nc.dram_tensor(n, sh, dt).ap()
    SND1F = sc("SND1F", [4 * RB1, T], BF16); G1F = sc("G1F", [16 * RB1, T], BF16)
    SND1T = sc("SND1T", [4 * T, 128], BF16); G1T = sc("G1T", [16 * T, 128], BF16)
    SND2 = sc("SND2", [4 * 256, T], BF16); G2 = sc("G2", [16 * 256, T], BF16)
    X2 = sc("X2", [D_MODEL, T], F32)
    SND3 = sc("SND3", [D_MODEL, 2], F32); G3 = sc("G3", [4 * D_MODEL, 2], F32)
    SND4F = sc("SND4F", [4 * RB4, T], BF16); G4F = sc("G4F", [16 * RB4, T], BF16)
    SND4T = sc("SND4T", [4 * T, 256], BF16); G4T = sc("G4T", [16 * T, 256], BF16)
    SND5 = sc("SND5", [4 * 256, T], BF16); G5 = sc("G5", [16 * 256, T], BF16)
    L1F = sc("L1F", [4 * RB1, T], BF16); L1T = sc("L1T", [4 * T, 128], BF16)
    L4F = sc("L4F", [4 * RB4, T], BF16); L4T = sc("L4T", [4 * T, 256], BF16)
    LA2 = sc("LA2", [1024, T], BF16); LA5 = sc("LA5", [1024, T], BF16)
    LHA2 = sc("LHA2", [1024, 2], BF16); LHA5 = sc("LHA5", [1024, 2], BF16)
    LH = sc("LH", [D_MODEL, 2], F32)
    k = Ctx(nc)
    r = nc.sync.partition_id() % 4

    dbg = nc.dram_tensor("dbg", [2560, 4096], BF16, kind="ExternalOutput").ap() if upto else None

    def stop_here(tag, src_ap):
        if upto != tag:
            return False
        k.barrier()
        db = Buf("dbg")
        k.dma("sp", dbg[0:src_ap.shape[0], 0:src_ap.shape[1]], src_ap, w=[db], own=db)
        k.stores.append(db.w)
        k.finish()
        return True

    def gather_rows(SND, G, rows):
        n = SND.shape[0] // rows
        k.all_gather_chunks([(SND[i * rows:(i + 1) * rows, :], G[i * 4 * rows:(i + 1) * 4 * rows, :])
                             for i in range(n)], GROUPS)

    def extract_window(G, L):
        n = L.shape[0] * L.shape[1]
        a = n // 16384
        assert a * 16384 == n and G.shape[0] * G.shape[1] == 4 * n
        gf = G.rearrange("r t -> (r t)").rearrange("(q a l) -> q a l", q=4, a=a, l=16384)
        lf = L.rearrange("r t -> (r t)").rearrange("(a l) -> a l", a=a, l=16384)
        bX = Buf("extract")
        k.dma("sp", lf, gf[bass.ds(r, 1)].rearrange("o a l -> (o a) l"), w=[bX], own=bX)

    def extract_att(G, L, LHA_):
        extract_window(G, L)
        gq = G.rearrange("(q x) t -> q x t", q=4)
        bX2 = Buf("extract")
        with nc.allow_non_contiguous_dma(reason="2-column conv halo"):
            k.dma("sp", LHA_, gq[bass.ds((r + 3) % 4, 1), :, T - 2:T].rearrange("o x t -> (o x) t"), w=[bX2],
                  own=bX2)

    SNDH = sc("SNDH", [8 * D_MODEL, 512], BF16)
    GH = sc("GH", [32 * D_MODEL, 512], BF16)
    GHv = GH.rearrange("(it j dc p) t -> p it j dc t", it=8, j=4, dc=8, p=128)

    def ht_src(t0):
        j, tl = divmod(t0, T)
        return GHv[:, tl // 512, j, :, :]

    k.begin_phase("A_")
    emit_norm(k, nc, T, d["xh0"].rearrange("(dc p) t -> p dc t", p=128)[:, :, 2:2 + T], d["g_mix0"],
              lambda dc, t0: SNDH[(t0 // 512) * 1024 + dc * 128:(t0 // 512) * 1024 + (dc + 1) * 128, 0:512],
              lambda it, toks: k.collective_async(SNDH[it * 1024:(it + 1) * 1024, :],
                                                  GH[it * 4096:(it + 1) * 4096, :], GROUPS, toks))
    k.end_phase()

    k.begin_phase("P_")

    def fm_route_a(c0, c1, t0):
        j, tl = divmod(t0, T)
        row = ((c0 // 128) * 4 + j) * 128 + c0 % 128
        return [(0, c1 - c0, L1F[row:row + c1 - c0, tl:tl + 512])]

    def tok_route_a(c0, c1, t0, tb):
        return [(0, 128, L1T[t0 + tb * 128:t0 + (tb + 1) * 128, 0:128])]

    emit_proj(k, nc, S, 652, ht_src, d["w_in0g"],
              [(0, 256, 0.125, AF.Copy), (256, 512, 1.0, AF.Copy), (512, 524, 1.0, AF.Sigmoid)], fm_route_a,
              [(524, 652)], tok_route_a)
    k.end_phase()

    WB = [(sc("WOB%d" % l, [D_MODEL, D_MODEL], BF16), sc("WUB%d" % l, [D_MODEL, 2 * D_FF], BF16),
           sc("WDB%d" % l, [D_FF, D_MODEL], BF16)) for l in range(2)]

    def precast_weights():
        for l in range(2):
            for (src, dst) in ((d["wo%d" % l], WB[l][0]), (d["wu%d" % l], WB[l][1]), (d["wd%d" % l], WB[l][2])):
                R_, C_ = src.shape
                bw = Buf("wcast")
                for r0 in range(0, R_, 128):
                    for c0 in range(0, C_, 1024):
                        c1 = min(C_, c0 + 1024)
                        k.dma("pool", dst[r0:r0 + 128, c0:c1], src[r0:r0 + 128, c0:c1], w=[bw], own=bw)

    k.begin_phase("B_")
    io_b = IOK2aFused(L1F, L1T, SND2, {n: d["a_" + n] for (n, _, _) in K2A_STATIC}, k, G2)
    io_b.after_prologue = precast_weights
    emit_k2a(k, nc, S, io_b)
    k.end_phase()
    extract_att(G2, LA2, LHA2)
    k.barrier()
    if stop_here("B2", LA2) or stop_here("B2H", LHA2):
        return nc

    k.begin_phase("C_")
    X1D = sc("X1D", [D_MODEL, T], F32)
    AFFD = sc("AFFD", [D_FF, T], BF16)
    io_c = IOK3Fused(0, d, d["xh0"], X2, LA2, LHA2, LH, SND3, yT, SNDH, k, GH)
    io_c.X1D, io_c.AFFD = X1D, AFFD
    io_c.wb = WB[0]
    emit_k3(k, nc, T, False, io_c, 512, "a")
    k.end_phase()
    k.begin_phase("Cb_")
    emit_k3(k, nc, T, False, io_c, 512, "b")
    k.end_phase()
    if upto == "C":
        k.barrier()
        k.finish()
        return nc
    k.all_gather_chunks([(SND3, G3)], GROUPS)
    bXh = Buf("extract")
    k.dma("sp", LH, G3[bass.ds(((r + 3) % 4) * D_MODEL, D_MODEL), :], w=[bXh], own=bXh)

    k.begin_phase("D_")

    def fm_route_d(c0, c1, t0):
        j, tl = divmod(t0, T)
        row = ((c0 // 128) * 4 + j) * 128
        return [(0, 128, L4F[row:row + 128, tl:tl + 512])]

    def tok_route_d(c0, c1, t0, tb):
        j, tl = divmod(t0 + tb * 128, T)
        row = ((tl // 2048) * 4 + j) * 2048 + tl % 2048
        return [(0, 256, L4T[row:row + 128, 0:256])]

    emit_proj(k, nc, S, 768, ht_src, d["w_in1g"],
              [(0, 256, 0.125, AF.Copy), (256, 512, 1.0, AF.Copy)], fm_route_d, [(512, 768)], tok_route_d)
    k.end_phase()

    k.begin_phase("E_")
    emit_k2b(k, nc, S, lambda_init, IOK2bFused(L4F, L4T, SND5, d, k, G5))
    k.end_phase()
    if upto == "E":
        k.barrier()
        k.finish()
        return nc
    extract_att(G5, LA5, LHA5)
    k.barrier()

    k.begin_phase("F_")
    io_f = IOK3Fused(1, d, d["xh0"], X2, LA5, LHA5, LH, SND3, yT)
    io_f.X1D, io_f.AFFD = X1D, AFFD
    io_f.wb = WB[1]
    emit_k3(k, nc, T, True, io_f, 512, "a")
    k.end_phase()
    k.begin_phase("Fb_")
    emit_k3(k, nc, T, True, io_f, 512, "b")
    k.barrier()
    k.finish()
    print("[fused] semaphores used:", k.nsem)
    return nc


def fused_inputs(x, norm_mix_g, norm_ffn_g, final_norm_g,
                 nsa_w_in, nsa_cmp_k_pe, nsa_cmp_k_w1, nsa_cmp_k_w2,
                 nsa_cmp_v_pe, nsa_cmp_v_w1, nsa_cmp_v_w2, nsa_w_out,
                 diff_w_in, diff_lam_q1, diff_lam_k1, diff_lam_q2, diff_lam_k2,
                 diff_subln_g, diff_w_out,
                 ffn_w_up, ffn_conv_w, ffn_conv_b, ffn_w_down):
    f32 = lambda a: np.ascontiguousarray(np.asarray(a, dtype=np.float32))
    x = f32(x)
    xT = [np.ascontiguousarray(x[b].T) for b in range(BATCH)]
    consts, per_g = k2a_consts(SEQ)
    wts = k2a_weights(nsa_cmp_k_pe[0], nsa_cmp_k_w1[0], nsa_cmp_k_w2[0],
                      nsa_cmp_v_pe[0], nsa_cmp_v_w1[0], nsa_cmp_v_w2[0])
    sl8 = alibi_slopes(8)
    lam = np.stack([f32(diff_lam_q1[0]), f32(diff_lam_k1[0]), f32(diff_lam_q2[0]), f32(diff_lam_k2[0])], 0)
    lam = np.ascontiguousarray(np.broadcast_to(lam[None], (128, 4, 64)))
    common = {"g_mix0": g_layout(norm_mix_g[0]), "g_mix1": g_layout(norm_mix_g[1]),
              "g_ffn0": g_layout(norm_ffn_g[0]), "g_ffn1": g_layout(norm_ffn_g[1]),
              "g_fin": g_layout(final_norm_g), "b_lam": lam, "b_sg": f32(diff_subln_g[0]).reshape(128, 1),
              "wo0": f32(nsa_w_out[0]), "wo1": f32(diff_w_out[0])}
    for l in range(2):
        cw, cb = conv_layouts(ffn_conv_w[l], ffn_conv_b[l])
        common.update({"wu%d" % l: f32(ffn_w_up[l]), "wd%d" % l: f32(ffn_w_down[l]), "cw%d" % l: cw,
                       "cb%d" % l: cb})
    for n_, v_ in list(consts.items()) + list(wts.items()):
        common["a_" + n_] = v_
    in_maps = []
    for c in range(NCORES):
        r = c % CPB
        m = dict(common)
        for n_, v_ in per_g[r].items():
            m["a_" + n_] = v_
        m["xh0"] = _with_halo(xT, c)
        w0, w1 = f32(nsa_w_in[0]), f32(diff_w_in[0])
        cols0 = (list(range(r * 256, (r + 1) * 256)) + [o_ + r * 64 + i for o_ in (1024, 1280, 1536, 2048)
                                                         for i in range(64)]
                 + list(range(2560 + r * 12, 2560 + (r + 1) * 12))
                 + [o_ + r * 64 + i for o_ in (1792, 2304) for i in range(64)])
        m["w_in0g"] = np.ascontiguousarray(w0[:, cols0])
        cols1 = [o_ + r * 256 + i for o_ in (0, 1024, 2048) for i in range(256)]
        m["w_in1g"] = np.ascontiguousarray(w1[:, cols1])
        m["m0"] = np.full((128, 1), 1.0 if r > 0 else 0.0, np.float32)
        m["b_qaug"] = np.stack([q_aug_rows(sl8[r * 2 + hh], 512) for hh in range(2)], 0)
        m["b_bt"] = np.stack([bt_table(sl8[r * 2 + hh]) for hh in range(2)], 0)
        in_maps.append(m)
    return in_maps


def kernel(**inputs):
    lambda_init = 0.8 - 0.6 * float(np.exp(-0.3 * 1))
    nc = build_fused(lambda_init)
    in_maps = fused_inputs(**inputs)
    res = run_bass_kernel_spmd(nc, in_maps, core_ids=list(range(NCORES))).results
    out = np.empty((BATCH, SEQ, D_MODEL), np.float32)
    for c in range(NCORES):
        b, j = divmod(c, CPB)
        out[b, j * TPC:(j + 1) * TPC, :] = res[c]["yT"].T
    return out
```
